# Optimizing a Trainium2 kernel written in Bass

```python
import math
import jax, jax.numpy as jnp
from jax import lax
import numpy as np

D_MODEL = 2048
BATCH = 1
SEQ = 8192
DEPTH = 4

N_META = 16
CHUNK = 128
PAD = CHUNK - N_META
D_FF = 5632
EPS = 1e-6
ROPE_THETA = 10000.0
NEG_INF = -1e30

MLA_HEADS = 4
MLA_NOPE = 128
MLA_ROPE = 64
MLA_QK = MLA_NOPE + MLA_ROPE
MLA_V = 128
MLA_Q_LORA = 384
MLA_KV_LORA = 128
MLA_WIDTH = MLA_HEADS * MLA_V

SSD_HEADS = 8
SSD_HEAD_DIM = 64
SSD_WIDTH = SSD_HEADS * SSD_HEAD_DIM
SSD_GROUPS = 2
SSD_STATE = 128
SSD_CONV = 4
SSD_CONV_CH = SSD_WIDTH + 2 * SSD_GROUPS * SSD_STATE

RET_HEADS = 4
RET_DK = 64
RET_DV = 128
RET_WIDTH = RET_HEADS * RET_DV

RWKV_HEADS = 8
RWKV_HEAD_DIM = 64
RWKV_WIDTH = RWKV_HEADS * RWKV_HEAD_DIM
RWKV_DECAY_LORA = 32
RWKV_A_LORA = 32
RWKV_GATE_LORA = 64
RWKV_LN_EPS = 64e-5

MIX_WIDTH = MLA_WIDTH + SSD_WIDTH + RET_WIDTH + RWKV_WIDTH
MLA_IN = MLA_Q_LORA + MLA_KV_LORA + MLA_ROPE
SSD_IN = SSD_WIDTH + SSD_CONV_CH + SSD_HEADS
RET_IN = 2 * RET_HEADS * RET_DK + 2 * RET_WIDTH
RWKV_IN = 3 * RWKV_WIDTH + RWKV_DECAY_LORA + RWKV_A_LORA + RWKV_GATE_LORA
IN_WIDTH = MLA_IN + SSD_IN + RET_IN + RWKV_IN

kernel_name = 'hybrid_parallel_heads_mla_ssd_ret_rwkv7'


def rms_norm(x, g, eps=EPS):
    xf = x.astype(jnp.float32)
    y = xf * lax.rsqrt(jnp.mean(xf * xf, axis=-1, keepdims=True) + eps)
    return (y * g.astype(jnp.float32)).astype(x.dtype)


def layer_norm_heads(x, g, eps):
    xf = x.astype(jnp.float32)
    mu = jnp.mean(xf, axis=-1, keepdims=True)
    var = jnp.mean(jnp.square(xf - mu), axis=-1, keepdims=True)
    return ((xf - mu) * lax.rsqrt(var + eps) * g.astype(jnp.float32)).astype(x.dtype)


def swiglu(h, w_gate, w_up, w_down):
    return (jax.nn.silu(h @ w_gate) * (h @ w_up)) @ w_down


def rope(x, pos):
    half = x.shape[-1] // 2
    inv = ROPE_THETA ** (-jnp.arange(half, dtype=jnp.float32) / half)
    ang = pos[:, None] * inv[None, :]
    cos = jnp.cos(ang)[None, :, None, :]
    sin = jnp.sin(ang)[None, :, None, :]
    xf = x.astype(jnp.float32)
    x1, x2 = xf[..., :half], xf[..., half:]
    return jnp.concatenate([x1 * cos - x2 * sin, x1 * sin + x2 * cos], axis=-1).astype(x.dtype)


def causal_tri():
    return jnp.arange(CHUNK)[:, None] >= jnp.arange(CHUNK)[None, :]


def chunk_state_scan(decay, states):
    def step(s, inp):
        dcy, st = inp
        return dcy[:, :, None, None] * s + st, s
    s0 = jnp.zeros_like(states[:, 0])
    _, prev = lax.scan(step, s0, (jnp.moveaxis(decay, 1, 0), jnp.moveaxis(states, 1, 0)))
    return jnp.moveaxis(prev, 0, 1)


def mla_mixer(u, pos, valid, q_norm, w_q_up, kv_norm, w_kv_up, qk_norm_q, qk_norm_k):
    b, lp, _ = u.shape
    cq, ckv, k_pe = jnp.split(u, [MLA_Q_LORA, MLA_Q_LORA + MLA_KV_LORA], axis=-1)
    q = (rms_norm(cq, q_norm) @ w_q_up).reshape(b, lp, MLA_HEADS, MLA_QK)
    kv = (rms_norm(ckv, kv_norm) @ w_kv_up).reshape(b, lp, MLA_HEADS, MLA_NOPE + MLA_V)
    k_nope, v = kv[..., :MLA_NOPE], kv[..., MLA_NOPE:]
    k_pe = jnp.broadcast_to(k_pe[:, :, None, :], (b, lp, MLA_HEADS, MLA_ROPE))
    k = jnp.concatenate([k_nope, k_pe], axis=-1)
    q = rms_norm(q, qk_norm_q)
    k = rms_norm(k, qk_norm_k)
    q = jnp.concatenate([q[..., :MLA_NOPE], rope(q[..., MLA_NOPE:], pos)], axis=-1)
    k = jnp.concatenate([k[..., :MLA_NOPE], rope(k[..., MLA_NOPE:], pos)], axis=-1)
    nb = lp // CHUNK
    qb = jnp.moveaxis(q.reshape(b, nb, CHUNK, MLA_HEADS, MLA_QK), 1, 0)
    kpos = jnp.arange(lp)
    scale = MLA_QK ** -0.5

    def block(args):
        qi, i = args
        s = jnp.einsum('bqhd,bkhd->bhqk', qi, k, preferred_element_type=jnp.float32) * scale
        qpos = i * CHUNK + jnp.arange(CHUNK)
        mask = (kpos[None, :] <= qpos[:, None]) & valid[None, :]
        p = jax.nn.softmax(jnp.where(mask, s, NEG_INF), axis=-1).astype(v.dtype)
        return jnp.einsum('bhqk,bkhd->bqhd', p, v)

    o = lax.map(block, (qb, jnp.arange(nb)))
    return jnp.moveaxis(o, 0, 1).reshape(b, lp, MLA_WIDTH)


def ssd_chunked(x, log_a, bm, cm):
    b, lp, h, p = x.shape
    c = lp // CHUNK
    xf = x.astype(jnp.float32).reshape(b, c, CHUNK, h, p)
    bf = bm.astype(jnp.float32).reshape(b, c, CHUNK, h, -1)
    cf = cm.astype(jnp.float32).reshape(b, c, CHUNK, h, -1)
    cs = jnp.cumsum(log_a.astype(jnp.float32).reshape(b, c, CHUNK, h), axis=2)
    seg = cs[:, :, :, None, :] - cs[:, :, None, :, :]
    lmat = jnp.exp(jnp.where(causal_tri()[None, None, :, :, None], seg, -jnp.inf))
    scores = jnp.einsum('bcihn,bcjhn->bcijh', cf, bf) * lmat
    y_diag = jnp.einsum('bcijh,bcjhp->bcihp', scores, xf)
    decay_to_end = jnp.exp(cs[:, :, -1:, :] - cs)
    states = jnp.einsum('bcjhn,bcjh,bcjhp->bchpn', bf, decay_to_end, xf)
    s_prev = chunk_state_scan(jnp.exp(cs[:, :, -1, :]), states)
    y_off = jnp.einsum('bcihn,bchpn,bcih->bcihp', cf, s_prev, jnp.exp(cs))
    return (y_diag + y_off).reshape(b, lp, h, p).astype(x.dtype)


def ssd_mixer(u, valid, conv_w, conv_b, dt_bias, a_log, d_skip, norm_g):
    b, lp, _ = u.shape
    z, xbc, dt = jnp.split(u, [SSD_WIDTH, SSD_WIDTH + SSD_CONV_CH], axis=-1)
    xbc = lax.conv_general_dilated(xbc, conv_w[:, None, :], window_strides=(1,),
                                   padding=[(SSD_CONV - 1, 0)],
                                   dimension_numbers=('NWC', 'WIO', 'NWC'),
                                   feature_group_count=SSD_CONV_CH)
    xbc = jax.nn.silu(xbc + conv_b)
    xs, bm, cm = jnp.split(xbc, [SSD_WIDTH, SSD_WIDTH + SSD_GROUPS * SSD_STATE], axis=-1)
    xs = xs.reshape(b, lp, SSD_HEADS, SSD_HEAD_DIM)
    rep = SSD_HEADS // SSD_GROUPS
    bm = jnp.repeat(bm.reshape(b, lp, SSD_GROUPS, SSD_STATE), rep, axis=2)
    cm = jnp.repeat(cm.reshape(b, lp, SSD_GROUPS, SSD_STATE), rep, axis=2)
    dt = jax.nn.softplus(dt.astype(jnp.float32) + dt_bias.astype(jnp.float32))
    dt = dt * valid.astype(jnp.float32)[None, :, None]
    a = -jnp.exp(a_log.astype(jnp.float32))
    y = ssd_chunked(xs * dt[..., None].astype(xs.dtype), dt * a, bm, cm)
    y = y + d_skip[:, None] * xs
    y = y.reshape(b, lp, SSD_WIDTH) * jax.nn.silu(z)
    return rms_norm(y, norm_g)


def retention_mixer(u, pos, norm_g):
    b, lp, _ = u.shape
    qkw = RET_HEADS * RET_DK
    q, k, v, g = jnp.split(u, [qkw, 2 * qkw, 2 * qkw + RET_WIDTH], axis=-1)
    q = rope(q.reshape(b, lp, RET_HEADS, RET_DK), pos)
    k = rope(k.reshape(b, lp, RET_HEADS, RET_DK), pos) * (RET_DK ** -0.5)
    c = lp // CHUNK
    qf = q.astype(jnp.float32).reshape(b, c, CHUNK, RET_HEADS, RET_DK)
    kf = k.astype(jnp.float32).reshape(b, c, CHUNK, RET_HEADS, RET_DK)
    vf = v.astype(jnp.float32).reshape(b, c, CHUNK, RET_HEADS, RET_DV)
    log_g = jnp.log(1.0 - 2.0 ** (-5.0 - jnp.arange(RET_HEADS, dtype=jnp.float32)))
    idx = jnp.arange(CHUNK, dtype=jnp.float32)
    diff = idx[:, None] - idx[None, :]
    dmat = jnp.exp(jnp.where(causal_tri()[..., None], diff[..., None] * log_g, -jnp.inf))
    y_in = jnp.einsum('bcijh,bcjhv->bcihv', jnp.einsum('bcihk,bcjhk->bcijh', qf, kf) * dmat, vf)
    k_dec = jnp.exp((CHUNK - 1 - idx)[:, None] * log_g)
    states = jnp.einsum('bcjhk,jh,bcjhv->bchvk', kf, k_dec, vf)
    chunk_dec = jnp.broadcast_to(jnp.exp(CHUNK * log_g), (b, c, RET_HEADS))
    s_prev = chunk_state_scan(chunk_dec, states)
    q_dec = jnp.exp((idx + 1.0)[:, None] * log_g)
    y_cross = jnp.einsum('bcihk,bchvk,ih->bcihv', qf, s_prev, q_dec)
    y = (y_in + y_cross).reshape(b, lp, RET_HEADS, RET_DV)
    y = layer_norm_heads(y, norm_g, EPS).astype(u.dtype).reshape(b, lp, RET_WIDTH)
    return jax.nn.silu(g) * y


def rwkv7_mixer(u, mu, w0, w2, a0, a2, g2, k_k, k_a, r_k, ln_g):
    b, lp, _ = u.shape
    u_prev = jnp.pad(u, ((0, 0), (1, 0), (0, 0)))[:, :-1]
    u = u + (u_prev - u) * mu
    r, k, v, wd, ad, gd = jnp.split(
        u, [RWKV_WIDTH, 2 * RWKV_WIDTH, 3 * RWKV_WIDTH, 3 * RWKV_WIDTH + RWKV_DECAY_LORA,
            3 * RWKV_WIDTH + RWKV_DECAY_LORA + RWKV_A_LORA], axis=-1)
    w = (w0 + jnp.tanh(wd) @ w2).astype(jnp.float32)
    w = -jax.nn.softplus(-w) - 0.5
    decay = jnp.exp(-jnp.exp(w))
    a = jax.nn.sigmoid(a0 + ad @ a2)
    g = jax.nn.sigmoid(gd) @ g2
    kk = k * k_k
    k = k * (1.0 + (a - 1.0) * k_a)

    def heads(t):
        return t.astype(jnp.float32).reshape(b, lp, RWKV_HEADS, RWKV_HEAD_DIM)

    rh, kh, vh, ah, dh, kkh = heads(r), heads(k), heads(v), heads(a), heads(decay), heads(kk)
    kkh = kkh / jnp.maximum(jnp.linalg.norm(kkh, axis=-1, keepdims=True), 1e-12)

    def step(s, inp):
        r_t, w_t, k_t, v_t, kk_t, b_t = inp
        sa = jnp.einsum('bhvk,bhk->bhv', s, -kk_t)
        s = s * w_t[:, :, None, :] + sa[..., None] * b_t[:, :, None, :] + v_t[..., None] * k_t[:, :, None, :]
        return s, jnp.einsum('bhvk,bhk->bhv', s, r_t)

    s0 = jnp.zeros((b, RWKV_HEADS, RWKV_HEAD_DIM, RWKV_HEAD_DIM), jnp.float32)
    xs = tuple(jnp.moveaxis(t, 1, 0) for t in (rh, dh, kh, vh, kkh, kkh * ah))
    _, out = lax.scan(step, s0, xs)
    out = jnp.moveaxis(out, 0, 1)
    out = layer_norm_heads(out, ln_g, RWKV_LN_EPS)
    bonus = jnp.sum(rh * kh * r_k.astype(jnp.float32), axis=-1, keepdims=True) * vh
    out = (out + bonus).astype(u.dtype).reshape(b, lp, RWKV_WIDTH)
    return out * g


def setup_inputs(seed: int = 0) -> dict:
    key = jax.random.key(seed)
    ks = iter(jax.random.split(key, 48))
    L = DEPTH

    def nrm(shape, scale):
        return jax.random.normal(next(ks), shape, jnp.float32) * scale

    def gain(shape):
        return 1.0 + nrm(shape, 0.02)

    def unif(shape, lo, hi):
        return jax.random.uniform(next(ks), shape, jnp.float32, minval=lo, maxval=hi)

    dt0 = jnp.exp(unif((L, SSD_HEADS), math.log(1e-3), math.log(1e-1)))
    return {
        'x': nrm((BATCH, SEQ, D_MODEL), 1.0),
        'meta_tokens': nrm((N_META, D_MODEL), 1.0),
        'ffn1_norm': gain((L, D_MODEL)),
        'ffn1_w_gate': nrm((L, D_MODEL, D_FF), D_MODEL ** -0.5),
        'ffn1_w_up': nrm((L, D_MODEL, D_FF), D_MODEL ** -0.5),
        'ffn1_w_down': nrm((L, D_FF, D_MODEL), D_FF ** -0.5),
        'mix_norm': gain((L, D_MODEL)),
        'w_in': nrm((L, D_MODEL, IN_WIDTH), D_MODEL ** -0.5),
        'w_out': nrm((L, MIX_WIDTH, D_MODEL), MIX_WIDTH ** -0.5),
        'mla_q_norm': gain((L, MLA_Q_LORA)),
        'mla_w_q_up': nrm((L, MLA_Q_LORA, MLA_HEADS * MLA_QK), MLA_Q_LORA ** -0.5),
        'mla_kv_norm': gain((L, MLA_KV_LORA)),
        'mla_w_kv_up': nrm((L, MLA_KV_LORA, MLA_HEADS * (MLA_NOPE + MLA_V)), MLA_KV_LORA ** -0.5),
        'mla_qk_norm_q': gain((L, MLA_QK)),
        'mla_qk_norm_k': gain((L, MLA_QK)),
        'ssd_conv_w': nrm((L, SSD_CONV, SSD_CONV_CH), SSD_CONV ** -0.5),
        'ssd_conv_b': nrm((L, SSD_CONV_CH), 0.01),
        'ssd_dt_bias': dt0 + jnp.log(-jnp.expm1(-dt0)),
        'ssd_a_log': jnp.log(unif((L, SSD_HEADS), 1.0, 16.0)),
        'ssd_d': gain((L, SSD_HEADS)),
        'ssd_norm': gain((L, SSD_WIDTH)),
        'ret_norm': gain((L, RET_HEADS, RET_DV)),
        'rwkv_mu': unif((L, RWKV_IN), 0.0, 1.0),
        'rwkv_w0': unif((L, RWKV_WIDTH), -6.0, 0.0),
        'rwkv_w2': nrm((L, RWKV_DECAY_LORA, RWKV_WIDTH), RWKV_DECAY_LORA ** -0.5),
        'rwkv_a0': nrm((L, RWKV_WIDTH), 0.1),
        'rwkv_a2': nrm((L, RWKV_A_LORA, RWKV_WIDTH), RWKV_A_LORA ** -0.5),
        'rwkv_g2': nrm((L, RWKV_GATE_LORA, RWKV_WIDTH), RWKV_GATE_LORA ** -0.5),
        'rwkv_k_k': 0.85 + nrm((L, RWKV_WIDTH), 0.02),
        'rwkv_k_a': gain((L, RWKV_WIDTH)),
        'rwkv_r_k': nrm((L, RWKV_HEADS, RWKV_HEAD_DIM), 0.1),
        'rwkv_ln': gain((L, RWKV_HEADS, RWKV_HEAD_DIM)),
        'ffn2_norm': gain((L, D_MODEL)),
        'ffn2_w_gate': nrm((L, D_MODEL, D_FF), D_MODEL ** -0.5),
        'ffn2_w_up': nrm((L, D_MODEL, D_FF), D_MODEL ** -0.5),
        'ffn2_w_down': nrm((L, D_FF, D_MODEL), D_FF ** -0.5),
    }


def reference(x, meta_tokens, ffn1_norm, ffn1_w_gate, ffn1_w_up, ffn1_w_down, mix_norm, w_in, w_out,
              mla_q_norm, mla_w_q_up, mla_kv_norm, mla_w_kv_up, mla_qk_norm_q, mla_qk_norm_k,
              ssd_conv_w, ssd_conv_b, ssd_dt_bias, ssd_a_log, ssd_d, ssd_norm, ret_norm,
              rwkv_mu, rwkv_w0, rwkv_w2, rwkv_a0, rwkv_a2, rwkv_g2, rwkv_k_k, rwkv_k_a, rwkv_r_k, rwkv_ln,
              ffn2_norm, ffn2_w_gate, ffn2_w_up, ffn2_w_down):
    b, seq, _ = x.shape
    meta = jnp.broadcast_to(meta_tokens.astype(x.dtype)[None], (b, N_META, D_MODEL))
    h = jnp.concatenate([meta, x], axis=1)
    lp = seq + CHUNK
    pos = jnp.arange(lp, dtype=jnp.float32) - PAD
    valid = jnp.arange(lp) >= PAD
    for l in range(DEPTH):
        h = h + 0.5 * swiglu(rms_norm(h, ffn1_norm[l]), ffn1_w_gate[l], ffn1_w_up[l], ffn1_w_down[l])
        u = rms_norm(h, mix_norm[l]) @ w_in[l]
        u = jnp.pad(u, ((0, 0), (PAD, 0), (0, 0)))
        u_mla, u_ssd, u_ret, u_rwkv = jnp.split(
            u, [MLA_IN, MLA_IN + SSD_IN, MLA_IN + SSD_IN + RET_IN], axis=-1)
        y = jnp.concatenate([
            mla_mixer(u_mla, pos, valid, mla_q_norm[l], mla_w_q_up[l], mla_kv_norm[l], mla_w_kv_up[l],
                      mla_qk_norm_q[l], mla_qk_norm_k[l]),
            ssd_mixer(u_ssd, valid, ssd_conv_w[l], ssd_conv_b[l], ssd_dt_bias[l], ssd_a_log[l],
                      ssd_d[l], ssd_norm[l]),
            retention_mixer(u_ret, pos, ret_norm[l]),
            rwkv7_mixer(u_rwkv, rwkv_mu[l], rwkv_w0[l], rwkv_w2[l], rwkv_a0[l], rwkv_a2[l], rwkv_g2[l],
                        rwkv_k_k[l], rwkv_k_a[l], rwkv_r_k[l], rwkv_ln[l]),
        ], axis=-1)[:, PAD:]
        h = h + y @ w_out[l]
        h = h + 0.5 * swiglu(rms_norm(h, ffn2_norm[l]), ffn2_w_gate[l], ffn2_w_up[l], ffn2_w_down[l])
    return h[:, N_META:]
```

```python
import numpy as np
import concourse.bass as bass
import concourse.mybir as mybir
from concourse.bass_utils import run_bass_kernel_spmd

F32 = mybir.dt.float32
BF16 = mybir.dt.bfloat16
AF = mybir.ActivationFunctionType
ALU = mybir.AluOpType
AX = mybir.AxisListType

NCORES = 8
D = 2048
KC = 16
DFF = 5632
FC = 44
NT = 1040
CGS = [(0, 16), (16, 528), (528, 1040)]
INW = 5320
EPS = 1e-6
EPOCH = 30000


class Buf:
    __slots__ = ("ap", "w", "r", "name", "dsem")

    def __init__(self, ap, name=""):
        self.ap = ap
        self.w = None
        self.r = {}
        self.name = name
        self.dsem = None

    def __getitem__(self, idx):
        return self.ap[idx]


class Sched:
    def __init__(self, nc):
        self.nc = nc
        self.engs = {"pe": nc.tensor, "dve": nc.vector, "act": nc.scalar,
                     "pool": nc.gpsimd, "sp": nc.sync}
        self.sems = []
        self.owner = {}
        self.cur = {}
        self.cnt = {}
        self.seen = {e: {} for e in self.engs}
        self.ninst = {e: 0 for e in self.engs}
        for e in ("pe", "dve", "act", "pool"):
            self._new_epoch(e)
        self.uid = 0

    def _alloc_sem(self, name, owner=None):
        h = self.nc.alloc_semaphore(name)
        self.sems.append(h)
        k = len(self.sems) - 1
        self.cnt[k] = 0
        self.owner[k] = owner
        return k

    def _new_epoch(self, e):
        self.cur[e] = self._alloc_sem("c_%s_%d" % (e, len(self.sems)), e)

    def new_dsem(self, name="d"):
        return self._alloc_sem("%s_%d" % (name, len(self.sems)))

    def _waits(self, e, reads, writes):
        need = {}
        for b in reads:
            if b.w is not None:
                k, v = b.w
                if need.get(k, 0) < v:
                    need[k] = v
        for b in writes:
            if b.w is not None:
                k, v = b.w
                if need.get(k, 0) < v:
                    need[k] = v
            for k, v in b.r.items():
                if need.get(k, 0) < v:
                    need[k] = v
        eng = self.engs[e]
        seen = self.seen[e]
        for k, v in need.items():
            if e == "pe" and self.owner[k] == "pe":
                continue
            if seen.get(k, 0) >= v:
                continue
            eng.wait_ge(self.sems[k], v)
            seen[k] = v

    def op(self, e, fn, reads=(), writes=(), inc=True):
        self._waits(e, reads, writes)
        ins = fn(self.engs[e])
        k = self.cur[e]
        self.ninst[e] += 1
        if inc:
            self.cnt[k] += 1
            ins.then_inc(self.sems[k], 1)
            tok = (k, self.cnt[k])
        else:
            tok = (k, self.cnt[k] + 1)
        for b in reads:
            if b.r.get(tok[0], 0) < tok[1]:
                b.r[tok[0]] = tok[1]
        for b in writes:
            b.w = tok
            b.r = {}
        if inc and self.cnt[k] >= EPOCH:
            self._new_epoch(e)
        return ins

    def dma(self, q, out_ap, in_ap, reads=(), writes=(), sem=None, **kw):
        self._waits(q, reads, writes)
        if sem is None:
            for b in list(writes) + list(reads):
                if b.dsem is None:
                    b.dsem = self.new_dsem()
                sem = b.dsem
                break
        ins = self.engs[q].dma_start(out=out_ap, in_=in_ap, **kw)
        ins.then_inc(self.sems[sem], 16)
        self.cnt[sem] += 16
        self.ninst[q] += 1
        tok = (sem, self.cnt[sem])
        for b in reads:
            if b.r.get(tok[0], 0) < tok[1]:
                b.r[tok[0]] = tok[1]
        for b in writes:
            b.w = tok
            b.r = {}
        return sem

    def wait_all(self, e, semkeys):
        eng = self.engs[e]
        for k in semkeys:
            if self.cnt[k] > 0:
                eng.wait_ge(self.sems[k], self.cnt[k])

    def sb(self, shape, dt, name=None):
        self.uid += 1
        name = "%s_%d" % (name or "sb", self.uid)
        return Buf(self.nc.alloc_sbuf_tensor(name, list(shape), dt).ap(), name)

    def ps(self, shape, dt=F32, name=None):
        self.uid += 1
        name = "%s_%d" % (name or "ps", self.uid)
        return Buf(self.nc.alloc_psum_tensor(name, list(shape), dt).ap(), name)

    def sub(self, buf, ap):
        return Buf(ap, buf.name + "_s")


class Ring:
    def __init__(self, bufs):
        self.bufs = bufs
        self.i = 0

    def next(self):
        b = self.bufs[self.i % len(self.bufs)]
        self.i += 1
        return b


class TCtx:
    def __init__(self, S):
        self.S = S
        nc = S.nc
        self.hT = [S.sb([128, NT], F32, "hT") for _ in range(KC)]
        self.xnT = [S.sb([128, NT], BF16, "xnT") for _ in range(KC)]
        self.actT = Ring([S.sb([128, NT], BF16, "actT") for _ in range(8)])
        self.wst = Ring([S.sb([128, KC, 128], F32, "wst") for _ in range(4)])
        self.wbf = Ring([S.sb([128, KC, 128], BF16, "wbf") for _ in range(4)])
        self.wdst = Ring([S.sb([128, 4, 128], F32, "wdst") for _ in range(2)])
        self.wdbf = Ring([S.sb([128, 4, 128], BF16, "wdbf") for _ in range(2)])
        self.tmp = Ring([S.sb([128, 512], F32, "tmp") for _ in range(3)])
        self.ost = Ring([S.sb([128, NT], F32, "ost") for _ in range(2)])
        self.rstd = S.sb([128, NT], F32, "rstd")
        self.gcol = S.sb([128, KC], F32, "gcol")
        self.ones = S.sb([128, 128], F32, "ones")
        S.op("pool", lambda e: e.memset(self.ones[:], 1.0), writes=[self.ones])
        banks = [S.ps([128, 512], F32, "bank") for _ in range(8)]
        self.G = [banks[0], banks[1]]
        self.U = [banks[2], banks[3]]
        self.Dn = [banks[4], banks[5]]
        self.N = banks[6]
        self.M = banks[7]

    def pG(self, ci):
        return (self.M, self.M.ap[:, 0:16]) if ci == 0 else (self.G[ci - 1], self.G[ci - 1].ap[:, :])

    def pU(self, ci):
        return (self.M, self.M.ap[:, 16:32]) if ci == 0 else (self.U[ci - 1], self.U[ci - 1].ap[:, :])

    def pD(self, ci):
        return (self.M, self.M.ap[:, 32:48]) if ci == 0 else (self.Dn[ci - 1], self.Dn[ci - 1].ap[:, :])

    def pN(self, ci):
        return (self.M, self.M.ap[:, 48:64]) if ci == 0 else (self.N, self.N.ap[:, :])


def t_load_h(T, h_dram):
    S = T.S
    for kc in range(KC):
        S.dma("sp", T.hT[kc][:], h_dram[kc * 128:(kc + 1) * 128, :], writes=[T.hT[kc]])


def t_store_h(T, h_dram, sems):
    S = T.S
    for kc in range(KC):
        sems.append(S.dma("sp", h_dram[kc * 128:(kc + 1) * 128, :], T.hT[kc][:], reads=[T.hT[kc]]))


def t_rmsnorm(T, g_dram, src=None, nk=KC, dst=None, width=D):
    S = T.S
    src = src or T.hT
    dst = dst or T.xnT
    S.dma("sp", T.gcol[:, 0:nk], g_dram[:, 0:nk], writes=[T.gcol])
    for ci, (c0, c1) in enumerate(CGS):
        pb, pap = T.pN(ci)
        for kc in range(nk):
            t = T.tmp.next()
            S.op("act", lambda e, t=t, kc=kc: e.activation(out=t[:, 0:c1 - c0], in_=src[kc][:, c0:c1], func=AF.Square),
                 reads=[src[kc]], writes=[t])
            S.op("pe", lambda e, t=t, kc=kc: e.matmul(pap[:, 0:c1 - c0], lhsT=T.ones[:], rhs=t[:, 0:c1 - c0],
                                                       start=(kc == 0), stop=(kc == nk - 1)),
                 reads=[T.ones, t], writes=[pb], inc=True)
        S.op("act", lambda e: e.activation(out=T.rstd[:, c0:c1], in_=pap[:, 0:c1 - c0], func=AF.Sqrt,
                                           scale=1.0 / width, bias=EPS),
             writes=[T.rstd, pb])
    S.op("dve", lambda e: e.reciprocal(T.rstd[:], T.rstd[:]), reads=[T.rstd], writes=[T.rstd])
    for kc in range(nk):
        S.op("dve", lambda e, kc=kc: e.scalar_tensor_tensor(out=dst[kc][:], in0=src[kc][:], scalar=T.gcol[:, kc:kc + 1],
                                                            in1=T.rstd[:], op0=ALU.mult, op1=ALU.mult),
             reads=[src[kc], T.gcol, T.rstd], writes=[dst[kc]])


def t_load_w(T, w_dram, c0, ncol, nk=KC):
    S = T.S
    st = T.wst.next()
    bf = T.wbf.next()
    S.dma("sp", st[:, 0:nk, 0:ncol], w_dram[:, c0:c0 + ncol].rearrange("(kc p) f -> p kc f", p=128), writes=[st])
    S.op("pool", lambda e: e.tensor_copy(bf[:, 0:nk, 0:ncol], st[:, 0:nk, 0:ncol]), reads=[st], writes=[bf])
    return bf


def t_ffn(T, g_dram, wg, wu, wd):
    S = T.S
    t_rmsnorm(T, g_dram)
    SG = 4
    for sg in range(FC // SG):
        acts = []
        for fi in range(SG):
            fc = sg * SG + fi
            bg = t_load_w(T, wg, fc * 128, 128)
            bu = t_load_w(T, wu, fc * 128, 128)
            a = T.actT.next()
            acts.append(a)
            for ci, (c0, c1) in enumerate(CGS):
                n = c1 - c0
                gb, gap = T.pG(ci)
                ub, uap = T.pU(ci)
                for kc in range(KC):
                    S.op("pe", lambda e, kc=kc: e.matmul(gap[:, 0:n], lhsT=bg[:, kc, :], rhs=T.xnT[kc][:, c0:c1],
                                                          start=(kc == 0), stop=(kc == KC - 1)),
                         reads=[bg, T.xnT[kc]], writes=[gb], inc=(kc == KC - 1))
                for kc in range(KC):
                    S.op("pe", lambda e, kc=kc: e.matmul(uap[:, 0:n], lhsT=bu[:, kc, :], rhs=T.xnT[kc][:, c0:c1],
                                                          start=(kc == 0), stop=(kc == KC - 1)),
                         reads=[bu, T.xnT[kc]], writes=[ub], inc=(kc == KC - 1))
                t = T.tmp.next()
                S.op("act", lambda e, t=t: e.activation(out=t[:, 0:n], in_=gap[:, 0:n], func=AF.Silu),
                     writes=[t, gb])
                S.op("dve", lambda e, t=t: e.tensor_tensor(out=a[:, c0:c1], in0=t[:, 0:n], in1=uap[:, 0:n], op=ALU.mult),
                     reads=[t], writes=[a, ub])
        for dc in range(KC):
            st = T.wdst.next()
            bf = T.wdbf.next()
            S.dma("sp", st[:], wd[sg * SG * 128:(sg + 1) * SG * 128, dc * 128:(dc + 1) * 128]
                  .rearrange("(fi p) d -> p fi d", p=128), writes=[st])
            S.op("pool", lambda e: e.tensor_copy(bf[:], st[:]), reads=[st], writes=[bf])
            for ci, (c0, c1) in enumerate(CGS):
                n = c1 - c0
                db, dap = T.pD(ci)
                for fi in range(SG):
                    S.op("pe", lambda e, fi=fi: e.matmul(dap[:, 0:n], lhsT=bf[:, fi, :], rhs=acts[fi][:, c0:c1],
                                                          start=(fi == 0), stop=(fi == SG - 1)),
                         reads=[bf, acts[fi]], writes=[db], inc=(fi == SG - 1))
                S.op("dve", lambda e: e.scalar_tensor_tensor(out=T.hT[dc][:, c0:c1], in0=dap[:, 0:n], scalar=0.5,
                                                            in1=T.hT[dc][:, c0:c1], op0=ALU.mult, op1=ALU.add),
                     writes=[T.hT[dc], db])


def t_proj(T, w_dram, ncols, rhsT, nk, sink):
    S = T.S
    nchunks = (ncols + 127) // 128
    for uc in range(nchunks):
        m = min(128, ncols - uc * 128)
        bw = t_load_w(T, w_dram, uc * 128, m, nk=nk)
        for ci, (c0, c1) in enumerate(CGS):
            n = c1 - c0
            gb, gap = T.pG(ci)
            for kc in range(nk):
                S.op("pe", lambda e, kc=kc: e.matmul(gap[0:m, 0:n], lhsT=bw[:, kc, 0:m], rhs=rhsT[kc][:, c0:c1],
                                                      start=(kc == 0), stop=(kc == nk - 1)),
                     reads=[bw, rhsT[kc]], writes=[gb], inc=(kc == nk - 1))
            sink(ci, c0, c1, uc, m, gb, gap)


def build_Ta():
    nc = bass.Bass("TRN2", target_bir_lowering=False)
    dt = lambda n, s, k: nc.dram_tensor(n, list(s), F32, kind=k).ap()
    h_in = dt("h_in", [D, NT], "ExternalInput")
    g1 = dt("g1", [128, KC], "ExternalInput")
    wg = dt("wg", [D, DFF], "ExternalInput")
    wu = dt("wu", [D, DFF], "ExternalInput")
    wd = dt("wd", [DFF, D], "ExternalInput")
    gm = dt("gm", [128, KC], "ExternalInput")
    win = dt("win", [D, INW], "ExternalInput")
    h_out = dt("h_out", [D, NT], "ExternalOutput")
    uT = dt("uT", [INW, NT], "ExternalOutput")
    S = Sched(nc)
    T = TCtx(S)
    outs = []
    t_load_h(T, h_in)
    t_ffn(T, g1, wg, wu, wd)
    t_store_h(T, h_out, outs)
    t_rmsnorm(T, gm)
    cur = {}

    def sink(ci, c0, c1, uc, m, pb, pap):
        if ci == 0:
            cur["o"] = T.ost.next()
        o = cur["o"]
        n = c1 - c0
        S.op("act", lambda e: e.activation(out=o[0:m, c0:c1], in_=pap[0:m, 0:n], func=AF.Copy), writes=[o, pb])
        if ci == len(CGS) - 1:
            outs.append(S.dma("sp", uT[uc * 128:uc * 128 + m, :], o[0:m, :], reads=[o]))

    t_proj(T, win, INW, T.xnT, KC, sink)
    S.wait_all("sp", sorted(set(outs)))
    return nc, S


def build_Tb():
    nc = bass.Bass("TRN2", target_bir_lowering=False)
    dt = lambda n, s, k: nc.dram_tensor(n, list(s), F32, kind=k).ap()
    h_in = dt("h_in", [D, NT], "ExternalInput")
    yT = dt("yT", [D, NT], "ExternalInput")
    gs = dt("gs", [128, 4], "ExternalInput")
    wout = dt("wout", [D, D], "ExternalInput")
    g2 = dt("g2", [128, KC], "ExternalInput")
    wg = dt("wg", [D, DFF], "ExternalInput")
    wu = dt("wu", [D, DFF], "ExternalInput")
    wd = dt("wd", [DFF, D], "ExternalInput")
    h_out = dt("h_out", [D, NT], "ExternalOutput")
    S = Sched(nc)
    T = TCtx(S)
    outs = []
    t_load_h(T, h_in)
    yst = [S.sb([128, NT], F32, "yst") for _ in range(4)]
    for j in range(4):
        S.dma("sp", yst[j][:], yT[(4 + j) * 128:(5 + j) * 128, :], writes=[yst[j]])
    t_rmsnorm(T, gs, src=yst, nk=4, dst=T.xnT[4:8], width=512)
    for kc in list(range(0, 4)) + list(range(8, 16)):
        st = yst[kc % 4]
        S.dma("sp", st[:], yT[kc * 128:(kc + 1) * 128, :], writes=[st])
        S.op("pool", lambda e, kc=kc, st=st: e.tensor_copy(T.xnT[kc][:], st[:]), reads=[st], writes=[T.xnT[kc]])

    def sink(ci, c0, c1, dc, m, pb, pap):
        n = c1 - c0
        S.op("dve", lambda e: e.tensor_tensor(out=T.hT[dc][:, c0:c1], in0=pap[:, 0:n], in1=T.hT[dc][:, c0:c1], op=ALU.add),
             writes=[T.hT[dc], pb])

    t_proj(T, wout, D, T.xnT, KC, sink)
    t_ffn(T, g2, wg, wu, wd)
    t_store_h(T, h_out, outs)
    S.wait_all("sp", sorted(set(outs)))
    return nc, S


LP = 8320
NCH = 65


class MProg:
    def __init__(self):
        self.nc = bass.Bass("TRN2", target_bir_lowering=False)
        self.S = Sched(self.nc)
        self.outs = []
        self.banks = [self.S.ps([128, 512], F32, "bank") for _ in range(8)]

    def din(self, name, shape):
        return self.nc.dram_tensor(name, list(shape), F32, kind="ExternalInput").ap()

    def dout(self, name, shape):
        return self.nc.dram_tensor(name, list(shape), F32, kind="ExternalOutput").ap()

    def load(self, name, shape, dt=F32, dram=None):
        d = dram if dram is not None else self.din(name, shape)
        b = self.S.sb(shape, F32, name)
        if len(shape) == 2 and shape[1] > 2048:
            step = 2080
            for c0 in range(0, shape[1], step):
                c1 = min(shape[1], c0 + step)
                self.S.dma("sp", b[:, c0:c1], d[:, c0:c1], writes=[b])
        else:
            self.S.dma("sp", b[:], d, writes=[b])
        return b

    def store(self, dram_ap, buf, ap):
        self.outs.append(self.S.dma("sp", dram_ap, ap, reads=[buf]))

    def finish(self):
        self.S.wait_all("sp", sorted(set(self.outs)))
        return self.nc


class Stream:
    def __init__(self, P, name, rows, total, width, nbuf=2):
        self.P = P
        self.d = P.din(name, [rows, total])
        self.rows = rows
        self.ring = Ring([P.S.sb([rows, width], F32, name) for _ in range(nbuf)])

    def get(self, c0, w):
        b = self.ring.next()
        self.P.S.dma("sp", b[:, 0:w], self.d[:, c0:c0 + w], writes=[b])
        return b


def build_ret():
    P = MProg()
    S = P.S
    B = P.banks
    qT_s = Stream(P, "qT", 64, LP, 512); qsT_s = Stream(P, "qsT", 64, LP, 512)
    kT_s = Stream(P, "kT", 64, LP, 512); ksT_s = Stream(P, "ksT", 64, LP, 512)
    cosT_s = Stream(P, "cosT", 64, LP, 512); sinT_s = Stream(P, "sinT", 64, LP, 512)
    ktok_s = Stream(P, "k_tok", 128, NCH * 64, 256); kstok_s = Stream(P, "ks_tok", 128, NCH * 64, 256)
    costok_s = Stream(P, "cos_tok", 128, NCH * 64, 256); sintok_s = Stream(P, "sin_tok", 128, NCH * 64, 256)
    vtok_s = Stream(P, "v_tok", 128, NCH * 128, 512)
    gT_s = Stream(P, "gT", 128, LP, 512)
    qdec = P.load("qdecT", [64, 512])
    kdec = P.load("kdec", [128, 1])
    dmatT = P.load("dmatT", [128, 128])
    cdec = P.load("cdec", [64, 1])
    gcol = P.load("gcol", [128, 1])
    yT_d = P.dout("yT", [128, LP])
    tmp64 = S.sb([64, 512], F32, "tmp64")
    qd = S.sb([64, 512], F32, "qd")

    def mul(o, a, b_, w, rows=64):
        S.op("dve", lambda e: e.tensor_tensor(out=o[0:rows, 0:w], in0=a[0:rows, 0:w], in1=b_[0:rows, 0:w], op=ALU.mult),
             reads=[a, b_], writes=[o])

    def add(o, a, b_, w, rows=64):
        S.op("dve", lambda e: e.tensor_tensor(out=o[0:rows, 0:w], in0=a[0:rows, 0:w], in1=b_[0:rows, 0:w], op=ALU.add),
             reads=[a, b_], writes=[o])
    ones = S.sb([128, 128], F32, "ones")
    S.op("pool", lambda e: e.memset(ones[:], 1.0 / 128), writes=[ones])
    Sst = [S.sb([64, 128], F32, "Sst") for _ in range(2)]
    S.op("pool", lambda e: e.memset(Sst[0][:], 0.0), writes=[Sst[0]])
    atm = Ring([S.sb([128, 128], F32, "atm") for _ in range(2)])
    ybuf = Ring([S.sb([128, 512], F32, "ybuf") for _ in range(2)])
    yc = S.sb([128, 512], F32, "yc"); sq = S.sb([128, 512], F32, "sq"); rs = S.sb([128, 512], F32, "rs")
    AT = Ring([B[0], B[1]]); YT = Ring([B[2], B[3]]); SP_ = B[4]; LN1 = B[5]; LN2 = B[6]
    blocks = [(0, 1)] + [(1 + 4 * i, 4) for i in range(16)]
    for (cb, ncb) in blocks:
        yb = ybuf.next()
        W = ncb * 128
        p0 = cb * 128
        qT = qT_s.get(p0, W); qsT = qsT_s.get(p0, W); kT = kT_s.get(p0, W); ksT = ksT_s.get(p0, W)
        cosT = cosT_s.get(p0, W); sinT = sinT_s.get(p0, W)
        k_tok = ktok_s.get(cb * 64, ncb * 64); ks_tok = kstok_s.get(cb * 64, ncb * 64)
        cos_tok = costok_s.get(cb * 64, ncb * 64); sin_tok = sintok_s.get(cb * 64, ncb * 64)
        v_tok = vtok_s.get(p0, W)
        mul(qT, qT, cosT, W); mul(tmp64, qsT, sinT, W); add(qT, qT, tmp64, W)
        mul(kT, kT, cosT, W); mul(tmp64, ksT, sinT, W); add(kT, kT, tmp64, W)
        S.op("dve", lambda e: e.tensor_scalar(out=kT[:, 0:W], in0=kT[:, 0:W], scalar1=float(64 ** -0.5), scalar2=None,
                                              op0=ALU.mult), reads=[kT], writes=[kT])
        mul(qd, qT, qdec, W)
        w2 = ncb * 64
        mul(k_tok, k_tok, cos_tok, w2, 128); mul(ks_tok, ks_tok, sin_tok, w2, 128); add(k_tok, k_tok, ks_tok, w2, 128)
        S.op("dve", lambda e: e.tensor_scalar(out=k_tok[:, 0:w2], in0=k_tok[:, 0:w2], scalar1=kdec[:, 0:1], scalar2=None,
                                              op0=ALU.mult), reads=[k_tok, kdec], writes=[k_tok])
        for ci in range(ncb):
            c = cb + ci
            sl = slice(ci * 128, (ci + 1) * 128)
            Sp = Sst[c % 2]; Sn = Sst[(c + 1) % 2]
            at = AT.next(); yt = YT.next(); am = atm.next()
            S.op("pe", lambda e: e.matmul(at[:, 0:128], lhsT=kT[:, sl], rhs=qT[:, sl], start=True, stop=True),
                 reads=[kT, qT], writes=[at])
            S.op("dve", lambda e: e.tensor_tensor(out=am[:], in0=at[:, 0:128], in1=dmatT[:], op=ALU.mult),
                 reads=[dmatT], writes=[am, at])
            S.op("pe", lambda e: e.matmul(yt[:, 0:128], lhsT=v_tok[:, sl], rhs=am[:], start=True, stop=False),
                 reads=[v_tok, am], writes=[yt], inc=False)
            S.op("pe", lambda e: e.matmul(yt[:, 0:128], lhsT=Sp[:], rhs=qd[:, sl], start=False, stop=True),
                 reads=[Sp, qd], writes=[yt])
            S.op("act", lambda e: e.activation(out=yb[:, sl], in_=yt[:, 0:128], func=AF.Copy),
                 writes=[yb, yt])
            S.op("pe", lambda e: e.matmul(SP_[0:64, 0:128], lhsT=k_tok[:, ci * 64:(ci + 1) * 64], rhs=v_tok[:, sl],
                                          start=True, stop=True), reads=[k_tok, v_tok], writes=[SP_])
            S.op("dve", lambda e: e.scalar_tensor_tensor(out=Sn[:], in0=Sp[:], scalar=cdec[:, 0:1], in1=SP_[0:64, 0:128],
                                                         op0=ALU.mult, op1=ALU.add),
                 reads=[Sp, cdec], writes=[Sn, SP_])
        gb = gT_s.get(p0, W)
        S.op("act", lambda e: e.activation(out=gb[:, 0:W], in_=gb[:, 0:W], func=AF.Silu), reads=[gb], writes=[gb])
        S.op("pe", lambda e: e.matmul(LN1[:, 0:W], lhsT=ones[:], rhs=yb[:, 0:W], start=True, stop=True),
             reads=[ones, yb], writes=[LN1])
        S.op("dve", lambda e: e.tensor_tensor(out=yc[:, 0:W], in0=yb[:, 0:W], in1=LN1[:, 0:W], op=ALU.subtract),
             reads=[yb], writes=[yc, LN1])
        S.op("act", lambda e: e.activation(out=sq[:, 0:W], in_=yc[:, 0:W], func=AF.Square), reads=[yc], writes=[sq])
        S.op("pe", lambda e: e.matmul(LN2[:, 0:W], lhsT=ones[:], rhs=sq[:, 0:W], start=True, stop=True),
             reads=[ones, sq], writes=[LN2])
        S.op("act", lambda e: e.activation(out=rs[:, 0:W], in_=LN2[:, 0:W], func=AF.Sqrt, scale=1.0, bias=EPS),
             writes=[rs, LN2])
        S.op("dve", lambda e: e.reciprocal(rs[:, 0:W], rs[:, 0:W]), reads=[rs], writes=[rs])
        S.op("dve", lambda e: e.scalar_tensor_tensor(out=yc[:, 0:W], in0=yc[:, 0:W], scalar=gcol[:, 0:1], in1=rs[:, 0:W],
                                                     op0=ALU.mult, op1=ALU.mult), reads=[yc, gcol, rs], writes=[yc])
        S.op("dve", lambda e: e.tensor_tensor(out=gb[:, 0:W], in0=yc[:, 0:W], in1=gb[:, 0:W], op=ALU.mult),
             reads=[yc, gb], writes=[gb])
        P.store(yT_d[:, p0:p0 + W], gb, gb[:, 0:W])
    return P.finish()


PAD = 112
C_ = np.ascontiguousarray


def tok_layout(a):
    F_ = a.shape[1]
    return C_(a.reshape(NCH, 128, F_).transpose(1, 0, 2).reshape(128, NCH * F_))


def rope_tables(dim):
    half = dim // 2
    inv = (10000.0 ** (-np.arange(half, dtype=np.float32) / half)).astype(np.float32)
    pos = (np.arange(LP, dtype=np.float32) - PAD).astype(np.float32)
    ang = (pos[:, None] * inv[None, :]).astype(np.float32)
    cos = np.cos(ang).astype(np.float32)
    sin = np.sin(ang).astype(np.float32)
    cos2 = np.concatenate([cos, cos], 1)
    sin2 = np.concatenate([-sin, sin], 1)
    return cos2, sin2


def swap_halves(a):
    h = a.shape[1] // 2
    return np.concatenate([a[:, h:], a[:, :h]], 1)


def prep_ret(up, ret_norm_l, core):
    h = core % 4
    o = 576 + 1544
    q = up[:, o + h * 64:o + (h + 1) * 64]
    k = up[:, o + 256 + h * 64:o + 256 + (h + 1) * 64]
    v = up[:, o + 512 + h * 128:o + 512 + (h + 1) * 128]
    g = up[:, o + 1024 + h * 128:o + 1024 + (h + 1) * 128]
    cos2, sin2 = rope_tables(64)
    log_g = np.log(np.float32(1.0) - np.float32(2.0) ** np.float32(-5.0 - h)).astype(np.float32)
    idx = np.arange(128, dtype=np.float32)
    qdec = np.exp((idx + 1.0) * log_g).astype(np.float32)
    kdec = (np.exp((127.0 - idx) * log_g) * (64.0 ** -0.5)).astype(np.float32)
    diff = idx[None, :] - idx[:, None]
    dmatT = np.where(diff >= 0, np.exp(diff * log_g), 0.0).astype(np.float32)
    return {
        "qT": C_(q.T), "qsT": C_(swap_halves(q).T), "kT": C_(k.T), "ksT": C_(swap_halves(k).T),
        "cosT": C_(cos2.T), "sinT": C_(sin2.T),
        "k_tok": tok_layout(k), "ks_tok": tok_layout(swap_halves(k)),
        "cos_tok": tok_layout(cos2), "sin_tok": tok_layout(sin2),
        "v_tok": tok_layout(v), "gT": C_(g.T),
        "qdecT": C_(np.tile(np.tile(qdec, 4)[None, :], (64, 1))),
        "kdec": C_(kdec[:, None]), "dmatT": C_(dmatT),
        "cdec": np.full((64, 1), np.exp(128.0 * log_g), np.float32),
        "gcol": C_(ret_norm_l[h][:, None].astype(np.float32)),
    }


def build_ssd():
    P = MProg()
    S = P.S
    B = P.banks
    xsT_s = Stream(P, "xsT_pad", 64, LP + 3, 515); BT_s = Stream(P, "BT_pad", 128, LP + 3, 515)
    CT_s = Stream(P, "CT_pad", 128, LP + 3, 515); zT_s = Stream(P, "zT", 64, LP, 512)
    xtap = [Stream(P, "xs_tok%d" % j, 128, NCH * 64, 256) for j in range(4)]
    btap = [Stream(P, "B_tok%d" % j, 128, NCH * 128, 512) for j in range(4)]
    cw_xs = P.load("cw_xs", [64, 4]); cb_xs = P.load("cb_xs", [64, 1])
    cw_B = P.load("cw_B", [128, 4]); cb_B = P.load("cb_B", [128, 1])
    cw_C = P.load("cw_C", [128, 4]); cb_C = P.load("cb_C", [128, 1])
    cwt_xs = P.load("cwt_xs", [128, 4 * 256]); cbt_xs = P.load("cbt_xs", [128, 256])
    cwt_B = P.load("cwt_B", [128, 4 * 512]); cbt_B = P.load("cbt_B", [128, 512])
    dt = P.load("dt_tok", [128, NCH]); dtb = P.load("dtb", [128, 1]); alog = P.load("alog", [128, 1])
    valid = P.load("valid_tok", [128, NCH]); dcol = P.load("dcol", [64, 1])
    TriU = P.load("TriU", [128, 128]); UTs = P.load("UTs", [128, 128])
    yT_d = P.dout("yT", [64, LP])
    onesF = S.sb([128, 128], F32, "onesF")
    S.op("pool", lambda e: e.memset(onesF[:], 1.0), writes=[onesF])
    S.op("act", lambda e: e.activation(out=dt[:], in_=dt[:], func=AF.Exp, bias=dtb[:, 0:1]), reads=[dt, dtb], writes=[dt])
    S.op("act", lambda e: e.activation(out=dt[:], in_=dt[:], func=AF.Ln, bias=1.0), reads=[dt], writes=[dt])
    S.op("dve", lambda e: e.tensor_tensor(out=dt[:], in0=dt[:], in1=valid[:], op=ALU.mult), reads=[dt, valid], writes=[dt])
    S.op("act", lambda e: e.activation(out=alog[:], in_=alog[:], func=AF.Exp), reads=[alog], writes=[alog])
    la = S.sb([128, NCH], F32, "la"); cs = S.sb([128, NCH], F32, "cs"); dte = S.sb([128, NCH], F32, "dte")
    S.op("dve", lambda e: e.tensor_scalar(out=la[:], in0=dt[:], scalar1=alog[:, 0:1], scalar2=-1.0, op0=ALU.mult,
                                          op1=ALU.mult), reads=[dt, alog], writes=[la])
    S.op("pe", lambda e: e.matmul(B[5][:, 0:NCH], lhsT=TriU[:], rhs=la[:], start=True, stop=True),
         reads=[TriU, la], writes=[B[5]])
    S.op("act", lambda e: e.activation(out=cs[:], in_=B[5][:, 0:NCH], func=AF.Copy), writes=[cs, B[5]])
    S.op("pe", lambda e: e.matmul(B[6][:, 0:NCH], lhsT=onesF[:], rhs=la[:], start=True, stop=True),
         reads=[onesF, la], writes=[B[6]])
    S.op("dve", lambda e: e.tensor_tensor(out=dte[:], in0=B[6][:, 0:NCH], in1=cs[:], op=ALU.subtract),
         reads=[cs], writes=[dte, B[6]])
    S.op("act", lambda e: e.activation(out=dte[:], in_=dte[:], func=AF.Exp), reads=[dte], writes=[dte])

    def conv_fm(src, rows, W, cw, cb, dst):
        S.op("dve", lambda e: e.tensor_scalar(out=dst[0:rows, 0:W], in0=src[0:rows, 0:W], scalar1=cw[:, 0:1], scalar2=None,
                                              op0=ALU.mult), reads=[src, cw], writes=[dst])
        for j in range(1, 4):
            S.op("dve", lambda e, j=j: e.scalar_tensor_tensor(out=dst[0:rows, 0:W], in0=src[0:rows, j:W + j],
                                                              scalar=cw[:, j:j + 1], in1=dst[0:rows, 0:W],
                                                              op0=ALU.mult, op1=ALU.add), reads=[src, cw, dst], writes=[dst])
        S.op("act", lambda e: e.activation(out=dst[0:rows, 0:W], in_=dst[0:rows, 0:W], func=AF.Silu, bias=cb[:, 0:1]),
             reads=[dst, cb], writes=[dst])

    def conv_tm(taps, w, cwt, cbt, full, dst, tmp):
        for j in range(4):
            o = dst if j == 0 else tmp
            S.op("dve", lambda e, j=j, o=o: e.tensor_tensor(out=o[:, 0:w], in0=taps[j][:, 0:w],
                                                            in1=cwt[:, j * full:j * full + w], op=ALU.mult),
                 reads=[taps[j], cwt], writes=[o])
            if j > 0:
                S.op("dve", lambda e: e.tensor_tensor(out=dst[:, 0:w], in0=dst[:, 0:w], in1=tmp[:, 0:w], op=ALU.add),
                     reads=[dst, tmp], writes=[dst])
        S.op("dve", lambda e: e.tensor_tensor(out=dst[:, 0:w], in0=dst[:, 0:w], in1=cbt[:, 0:w], op=ALU.add),
             reads=[dst, cbt], writes=[dst])
        S.op("act", lambda e: e.activation(out=dst[:, 0:w], in_=dst[:, 0:w], func=AF.Silu), reads=[dst], writes=[dst])

    BTc = Ring([S.sb([128, 512], F32, "BTc") for _ in range(2)])
    CTc = Ring([S.sb([128, 512], F32, "CTc") for _ in range(2)])
    xsTc = Ring([S.sb([64, 512], F32, "xsTc") for _ in range(2)])
    xtok = Ring([S.sb([128, 256], F32, "xtok") for _ in range(2)])
    btok = Ring([S.sb([128, 512], F32, "btok") for _ in range(2)])
    tmpx = S.sb([128, 256], F32, "tmpx"); tmpb = S.sb([128, 512], F32, "tmpb")
    lam = Ring([S.sb([128, 128], F32, "lam") for _ in range(2)])
    laf = Ring([S.sb([128, 128], F32, "laf") for _ in range(2)])
    LT = Ring([S.sb([128, 128], F32, "LT") for _ in range(2)])
    Er = Ring([S.sb([128, 128], F32, "Er") for _ in range(2)])
    CsT = Ring([S.sb([128, 128], F32, "CsT") for _ in range(2)])
    xdt = Ring([S.sb([128, 64], F32, "xdt") for _ in range(2)])
    bd = Ring([S.sb([128, 128], F32, "bd") for _ in range(2)])
    ybuf = Ring([S.sb([64, 512], F32, "ybuf") for _ in range(2)])
    Sst = [S.sb([128, 64], F32, "Sst") for _ in range(2)]
    S.op("pool", lambda e: e.memset(Sst[0][:], 0.0), writes=[Sst[0]])
    GT = B[0]; SEG = B[1]; CSR = B[2]; YT = B[3]; SC = B[4]
    blocks = [(0, 1)] + [(1 + 4 * i, 4) for i in range(16)]
    for (cb, ncb) in blocks:
        W = ncb * 128
        p0 = cb * 128
        xs_in = xsT_s.get(p0, W + 3); B_in = BT_s.get(p0, W + 3); C_in = CT_s.get(p0, W + 3); zT = zT_s.get(p0, W)
        xt = [xtap[j].get(cb * 64, ncb * 64) for j in range(4)]
        bt = [btap[j].get(cb * 128, ncb * 128) for j in range(4)]
        BT = BTc.next(); CT = CTc.next(); xsT = xsTc.next(); xk = xtok.next(); bk = btok.next(); yb = ybuf.next()
        conv_fm(B_in, 128, W, cw_B, cb_B, BT)
        conv_fm(C_in, 128, W, cw_C, cb_C, CT)
        conv_fm(xs_in, 64, W, cw_xs, cb_xs, xsT)
        conv_tm(xt, ncb * 64, cwt_xs, cbt_xs, 256, xk, tmpx)
        conv_tm(bt, ncb * 128, cwt_B, cbt_B, 512, bk, tmpb)
        S.op("act", lambda e: e.activation(out=zT[:, 0:W], in_=zT[:, 0:W], func=AF.Silu), reads=[zT], writes=[zT])
        for ci in range(ncb):
            c = cb + ci
            sl = slice(ci * 128, (ci + 1) * 128)
            Sp = Sst[c % 2]; Sn = Sst[(c + 1) % 2]
            lm = lam.next(); lf = laf.next(); lt = LT.next(); er = Er.next(); cst = CsT.next(); xd = xdt.next(); bdd = bd.next()
            lac = la[:, c:c + 1]
            S.op("pe", lambda e: e.matmul(GT[:, 0:128], lhsT=BT[:, sl], rhs=CT[:, sl], start=True, stop=True),
                 reads=[BT, CT], writes=[GT])
            S.op("dve", lambda e: e.tensor_scalar(out=lm[:], in0=UTs[:], scalar1=lac, scalar2=None, op0=ALU.mult),
                 reads=[UTs, la], writes=[lm])
            S.op("dve", lambda e: e.tensor_scalar(out=lf[:], in0=onesF[:], scalar1=lac, scalar2=None, op0=ALU.mult),
                 reads=[onesF, la], writes=[lf])
            S.op("pe", lambda e: e.matmul(SEG[:, 0:128], lhsT=lm[:], rhs=TriU[:], start=True, stop=True),
                 reads=[lm, TriU], writes=[SEG])
            S.op("pe", lambda e: e.matmul(CSR[:, 0:128], lhsT=lf[:], rhs=TriU[:], start=True, stop=True),
                 reads=[lf, TriU], writes=[CSR])
            S.op("act", lambda e: e.activation(out=lt[:], in_=SEG[:, 0:128], func=AF.Exp), writes=[lt, SEG])
            S.op("dve", lambda e: e.tensor_tensor(out=lt[:], in0=lt[:], in1=TriU[:], op=ALU.mult), reads=[lt, TriU], writes=[lt])
            S.op("dve", lambda e: e.tensor_tensor(out=lt[:], in0=lt[:], in1=GT[:, 0:128], op=ALU.mult), reads=[lt], writes=[lt, GT])
            S.op("act", lambda e: e.activation(out=er[:], in_=CSR[:, 0:128], func=AF.Exp), writes=[er, CSR])
            S.op("dve", lambda e: e.tensor_tensor(out=cst[:], in0=CT[:, sl], in1=er[:], op=ALU.mult), reads=[CT, er], writes=[cst])
            S.op("dve", lambda e: e.tensor_scalar(out=xd[:], in0=xk[:, ci * 64:(ci + 1) * 64], scalar1=dt[:, c:c + 1],
                                                  scalar2=None, op0=ALU.mult), reads=[xk, dt], writes=[xd])
            S.op("dve", lambda e: e.tensor_scalar(out=bdd[:], in0=bk[:, sl], scalar1=dte[:, c:c + 1], scalar2=None,
                                                  op0=ALU.mult), reads=[bk, dte], writes=[bdd])
            S.op("pe", lambda e: e.matmul(YT[0:64, 0:128], lhsT=xd[:], rhs=lt[:], start=True, stop=False),
                 reads=[xd, lt], writes=[YT], inc=False)
            S.op("pe", lambda e: e.matmul(YT[0:64, 0:128], lhsT=Sp[:], rhs=cst[:], start=False, stop=True),
                 reads=[Sp, cst], writes=[YT])
            S.op("dve", lambda e: e.scalar_tensor_tensor(out=yb[:, sl], in0=xsT[:, sl], scalar=dcol[:, 0:1],
                                                         in1=YT[0:64, 0:128], op0=ALU.mult, op1=ALU.add),
                 reads=[xsT, dcol], writes=[yb, YT])
            S.op("pe", lambda e: e.matmul(SC[:, 0:64], lhsT=bdd[:], rhs=xd[:], start=True, stop=True),
                 reads=[bdd, xd], writes=[SC])
            S.op("dve", lambda e: e.scalar_tensor_tensor(out=Sn[:], in0=Sp[:], scalar=er[:, 127:128], in1=SC[:, 0:64],
                                                         op0=ALU.mult, op1=ALU.add), reads=[Sp, er], writes=[Sn, SC])
        S.op("dve", lambda e: e.tensor_tensor(out=yb[:, 0:W], in0=yb[:, 0:W], in1=zT[:, 0:W], op=ALU.mult),
             reads=[yb, zT], writes=[yb])
        P.store(yT_d[:, p0:p0 + W], yb, yb[:, 0:W])
    return P.finish()


def prep_ssd(up, p, l, core):
    j = core
    g = j // 4
    o = 576
    z = up[:, o + j * 64:o + (j + 1) * 64]
    xo = o + 512
    xs = up[:, xo + j * 64:xo + (j + 1) * 64]
    Bm = up[:, xo + 512 + g * 128:xo + 512 + (g + 1) * 128]
    Cm = up[:, xo + 768 + g * 128:xo + 768 + (g + 1) * 128]
    dtr = up[:, xo + 1024 + j:xo + 1024 + j + 1]
    cw = p['ssd_conv_w'][l]
    cb = p['ssd_conv_b'][l]
    ch_xs = slice(j * 64, (j + 1) * 64)
    ch_B = slice(512 + g * 128, 512 + (g + 1) * 128)
    ch_C = slice(768 + g * 128, 768 + (g + 1) * 128)
    pad3 = lambda a: np.concatenate([np.zeros((3, a.shape[1]), np.float32), a], 0)
    xs_p = pad3(xs); B_p = pad3(Bm); C_p = pad3(Cm)
    valid = np.ones((LP, 1), np.float32); valid[:PAD] = 0
    idx = np.arange(128)
    TriU = (idx[:, None] <= idx[None, :]).astype(np.float32)
    m = {
        "xsT_pad": C_(xs_p.T), "BT_pad": C_(B_p.T), "CT_pad": C_(C_p.T), "zT": C_(z.T),
        "cw_xs": C_(cw[:, ch_xs].T), "cb_xs": C_(cb[ch_xs][:, None]),
        "cw_B": C_(cw[:, ch_B].T), "cb_B": C_(cb[ch_B][:, None]),
        "cw_C": C_(cw[:, ch_C].T), "cb_C": C_(cb[ch_C][:, None]),
        "cwt_xs": C_(np.tile(np.tile(cw[:, ch_xs], (1, 4)).reshape(1, 4 * 256), (128, 1))),
        "cbt_xs": C_(np.tile(np.tile(cb[ch_xs], 4)[None, :], (128, 1))),
        "cwt_B": C_(np.tile(np.tile(cw[:, ch_B], (1, 4)).reshape(1, 4 * 512), (128, 1))),
        "cbt_B": C_(np.tile(np.tile(cb[ch_B], 4)[None, :], (128, 1))),
        "dt_tok": tok_layout(dtr), "dtb": np.full((128, 1), p['ssd_dt_bias'][l][j], np.float32),
        "alog": np.full((128, 1), p['ssd_a_log'][l][j], np.float32),
        "valid_tok": tok_layout(valid), "dcol": np.full((64, 1), p['ssd_d'][l][j], np.float32),
        "TriU": C_(TriU), "UTs": C_(1.0 - TriU),
    }
    for t in range(4):
        m["xs_tok%d" % t] = tok_layout(xs_p[t:t + LP])
        m["B_tok%d" % t] = tok_layout(B_p[t:t + LP])
    return m


def build_mla():
    P = MProg()
    S = P.S
    B = P.banks
    cq_s = [Stream(P, "cqT%d" % i, 128, LP, 512) for i in range(3)]
    ckv_s = Stream(P, "ckvT", 128, LP, 512)
    kpe_s = Stream(P, "kpeT", 64, LP, 512); kpes_s = Stream(P, "kpesT", 64, LP, 512)
    cos_s = Stream(P, "cosT", 64, LP, 512); sin_s = Stream(P, "sinT", 64, LP, 512)
    yT_d = P.dout("yT", [128, LP])

    def loadbf(name, shape):
        f = P.load(name, shape)
        b = S.sb(shape, BF16, name + "b")
        S.op("dve", lambda e: e.tensor_copy(b[:], f[:]), reads=[f], writes=[b])
        return b
    wqn = loadbf("wq_n", [128, 3 * 128]); wqr = loadbf("wq_r", [128, 3 * 64]); wqrs = loadbf("wq_rs", [128, 3 * 64])
    wk = loadbf("wk", [128, 128]); wv = loadbf("wv", [128, 128])
    ones0b = loadbf("ones0", [128, 128])
    gqn = P.load("gqn", [128, 3]); gkv = P.load("gkv", [128, 1])
    gq_n = P.load("gq_n", [128, 1]); gq_r = P.load("gq_r", [64, 1]); gq_rs = P.load("gq_rs", [64, 1])
    gk_n = P.load("gk_n", [128, 1]); gk_r = P.load("gk_r", [64, 1]); gk_rs = P.load("gk_rs", [64, 1])
    mblk = P.load("mblk", [128, 4 * 512])
    onesF = S.sb([128, 128], F32, "onesF"); onesb = S.sb([128, 128], BF16, "onesb")
    S.op("pool", lambda e: e.memset(onesF[:], 1.0), writes=[onesF])
    S.op("pool", lambda e: e.memset(onesb[:], 1.0), writes=[onesb])
    KnT = S.sb([128, LP], BF16, "KnT"); KrT = S.sb([64, LP], BF16, "KrT"); Vt = S.sb([128, LP], BF16, "Vt")
    sqa = Ring([S.sb([128, 512], F32, "sqa") for _ in range(2)])
    rstd = S.sb([128, 512], F32, "rstd"); rstdq = S.sb([128, 512], F32, "rstdq"); rstdk = S.sb([128, 512], F32, "rstdk")
    cqn = [S.sb([128, 512], BF16, "cqn") for _ in range(3)]
    ckvn = S.sb([128, 512], BF16, "ckvn")
    qn_f = S.sb([128, 512], BF16, "qn_f"); qr_f = S.sb([64, 512], BF16, "qr_f")
    t1 = S.sb([64, 512], F32, "t1"); t2 = S.sb([64, 512], F32, "t2")
    PTr = Ring([S.sb([128, 512], BF16, "PT") for _ in range(3)])
    rden = S.sb([128, 512], F32, "rden"); yo = Ring([S.sb([128, 512], F32, "yo") for _ in range(2)])
    STb = Ring([B[0], B[1]]); OB = B[2]; DB = B[3]; NB = B[4]; Q1 = B[5]; Q2 = B[6]; Q3 = B[7]
    SCALE = float(192 ** -0.5)

    def rms(parts, W, width, dst, post_scale=1.0):
        n = len(parts)
        for i, (buf, ap, rows, is_ps) in enumerate(parts):
            sq = sqa.next()
            if is_ps:
                S.op("act", lambda e: e.activation(out=sq[0:rows, 0:W], in_=ap, func=AF.Square), writes=[sq, buf])
            else:
                S.op("act", lambda e: e.activation(out=sq[0:rows, 0:W], in_=ap, func=AF.Square), reads=[buf], writes=[sq])
            S.op("pe", lambda e: e.matmul(NB[:, 0:W], lhsT=onesF[0:rows, :], rhs=sq[0:rows, 0:W], start=(i == 0),
                                          stop=(i == n - 1)), reads=[onesF, sq], writes=[NB])
        S.op("act", lambda e: e.activation(out=dst[:, 0:W], in_=NB[:, 0:W], func=AF.Sqrt, scale=1.0 / width, bias=EPS),
             writes=[dst, NB])
        S.op("dve", lambda e: e.reciprocal(dst[:, 0:W], dst[:, 0:W]), reads=[dst], writes=[dst])
        if post_scale != 1.0:
            S.op("dve", lambda e: e.tensor_scalar(out=dst[:, 0:W], in0=dst[:, 0:W], scalar1=post_scale, scalar2=None,
                                                  op0=ALU.mult), reads=[dst], writes=[dst])

    blocks = [(0, 1)] + [(1 + 4 * i, 4) for i in range(16)]
    for (cb, ncb) in blocks:
        W = ncb * 128
        p0 = cb * 128
        cq = [s.get(p0, W) for s in cq_s]
        ckv = ckv_s.get(p0, W); kpe = kpe_s.get(p0, W); kpes = kpes_s.get(p0, W)
        cosb = cos_s.get(p0, W); sinb = sin_s.get(p0, W)
        rms([(cq[i], cq[i][:, 0:W], 128, False) for i in range(3)], W, 384.0, rstd)
        for i in range(3):
            S.op("dve", lambda e, i=i: e.scalar_tensor_tensor(out=cqn[i][:, 0:W], in0=cq[i][:, 0:W], scalar=gqn[:, i:i + 1],
                                                              in1=rstd[:, 0:W], op0=ALU.mult, op1=ALU.mult),
                 reads=[cq[i], gqn, rstd], writes=[cqn[i]])
        for (pb, wt, m) in ((Q1, wqn, 128), (Q2, wqr, 64), (Q3, wqrs, 64)):
            for i in range(3):
                S.op("pe", lambda e, i=i: e.matmul(pb[0:m, 0:W], lhsT=wt[:, i * m:(i + 1) * m], rhs=cqn[i][:, 0:W],
                                                   start=(i == 0), stop=(i == 2)), reads=[wt, cqn[i]], writes=[pb],
                     inc=(i == 2))
        rms([(Q1, Q1[:, 0:W], 128, True), (Q2, Q2[0:64, 0:W], 64, True)], W, 192.0, rstdq, SCALE)
        S.op("dve", lambda e: e.scalar_tensor_tensor(out=qn_f[:, 0:W], in0=Q1[:, 0:W], scalar=gq_n[:, 0:1], in1=rstdq[:, 0:W],
                                                     op0=ALU.mult, op1=ALU.mult), reads=[gq_n, rstdq], writes=[qn_f, Q1])
        S.op("dve", lambda e: e.scalar_tensor_tensor(out=t1[:, 0:W], in0=Q2[0:64, 0:W], scalar=gq_r[:, 0:1], in1=cosb[:, 0:W],
                                                     op0=ALU.mult, op1=ALU.mult), reads=[gq_r, cosb], writes=[t1, Q2])
        S.op("dve", lambda e: e.scalar_tensor_tensor(out=t2[:, 0:W], in0=Q3[0:64, 0:W], scalar=gq_rs[:, 0:1], in1=sinb[:, 0:W],
                                                     op0=ALU.mult, op1=ALU.mult), reads=[gq_rs, sinb], writes=[t2, Q3])
        S.op("dve", lambda e: e.tensor_tensor(out=t1[:, 0:W], in0=t1[:, 0:W], in1=t2[:, 0:W], op=ALU.add), reads=[t1, t2], writes=[t1])
        S.op("dve", lambda e: e.tensor_tensor(out=qr_f[:, 0:W], in0=t1[:, 0:W], in1=rstdq[0:64, 0:W], op=ALU.mult),
             reads=[t1, rstdq], writes=[qr_f])
        rms([(ckv, ckv[:, 0:W], 128, False)], W, 128.0, rstd)
        S.op("dve", lambda e: e.scalar_tensor_tensor(out=ckvn[:, 0:W], in0=ckv[:, 0:W], scalar=gkv[:, 0:1], in1=rstd[:, 0:W],
                                                     op0=ALU.mult, op1=ALU.mult), reads=[ckv, gkv, rstd], writes=[ckvn])
        S.op("pe", lambda e: e.matmul(Q1[:, 0:W], lhsT=wk[:], rhs=ckvn[:, 0:W], start=True, stop=True),
             reads=[wk, ckvn], writes=[Q1])
        for ci in range(ncb):
            c = cb + ci
            S.op("pe", lambda e: e.matmul(Q3[:, 0:128], lhsT=ckvn[:, ci * 128:(ci + 1) * 128], rhs=wv[:], start=True, stop=True),
                 reads=[ckvn, wv], writes=[Q3])
            S.op("act", lambda e: e.activation(out=Vt[:, c * 128:(c + 1) * 128], in_=Q3[:, 0:128], func=AF.Copy),
                 writes=[Vt, Q3])
        rms([(Q1, Q1[:, 0:W], 128, True), (kpe, kpe[:, 0:W], 64, False)], W, 192.0, rstdk)
        S.op("dve", lambda e: e.scalar_tensor_tensor(out=KnT[:, p0:p0 + W], in0=Q1[:, 0:W], scalar=gk_n[:, 0:1], in1=rstdk[:, 0:W],
                                                     op0=ALU.mult, op1=ALU.mult), reads=[gk_n, rstdk], writes=[KnT, Q1])
        S.op("dve", lambda e: e.scalar_tensor_tensor(out=t1[:, 0:W], in0=kpe[:, 0:W], scalar=gk_r[:, 0:1], in1=cosb[:, 0:W],
                                                     op0=ALU.mult, op1=ALU.mult), reads=[kpe, gk_r, cosb], writes=[t1])
        S.op("dve", lambda e: e.scalar_tensor_tensor(out=t2[:, 0:W], in0=kpes[:, 0:W], scalar=gk_rs[:, 0:1], in1=sinb[:, 0:W],
                                                     op0=ALU.mult, op1=ALU.mult), reads=[kpes, gk_rs, sinb], writes=[t2])
        S.op("dve", lambda e: e.tensor_tensor(out=t1[:, 0:W], in0=t1[:, 0:W], in1=t2[:, 0:W], op=ALU.add), reads=[t1, t2], writes=[t1])
        S.op("dve", lambda e: e.tensor_tensor(out=KrT[:, p0:p0 + W], in0=t1[:, 0:W], in1=rstdk[0:64, 0:W], op=ALU.mult),
             reads=[t1, rstdk], writes=[KrT])
        nk = cb + ncb
        for kc in range(nk):
            ks = slice(kc * 128, (kc + 1) * 128)
            st = STb.next(); pt = PTr.next()
            S.op("pe", lambda e: e.matmul(st[:, 0:W], lhsT=KnT[:, ks], rhs=qn_f[:, 0:W], start=True, stop=False),
                 reads=[KnT, qn_f], writes=[st], inc=False)
            S.op("pe", lambda e: e.matmul(st[:, 0:W], lhsT=KrT[:, ks], rhs=qr_f[:, 0:W], start=False, stop=True),
                 reads=[KrT, qr_f], writes=[st])
            S.op("act", lambda e: e.activation(out=pt[:, 0:W], in_=st[:, 0:W], func=AF.Exp), writes=[pt, st])
            if kc >= cb:
                k_ = kc - cb
                S.op("pool", lambda e: e.tensor_tensor(out=pt[:, 0:W], in0=pt[:, 0:W], in1=mblk[:, k_ * 512:k_ * 512 + W],
                                                       op=ALU.mult), reads=[pt, mblk], writes=[pt])
            S.op("pe", lambda e: e.matmul(OB[:, 0:W], lhsT=Vt[:, ks], rhs=pt[:, 0:W], start=(kc == 0), stop=(kc == nk - 1)),
                 reads=[Vt, pt], writes=[OB], inc=False)
            S.op("pe", lambda e: e.matmul(DB[:, 0:W], lhsT=(ones0b if kc == 0 else onesb)[:], rhs=pt[:, 0:W],
                                          start=(kc == 0), stop=(kc == nk - 1)), reads=[ones0b, onesb, pt], writes=[DB])
        y = yo.next()
        S.op("dve", lambda e: e.tensor_scalar(out=rden[:, 0:W], in0=DB[:, 0:W], scalar1=1e-30, scalar2=None, op0=ALU.max),
             writes=[rden, DB])
        S.op("dve", lambda e: e.reciprocal(rden[:, 0:W], rden[:, 0:W]), reads=[rden], writes=[rden])
        S.op("dve", lambda e: e.tensor_tensor(out=y[:, 0:W], in0=OB[:, 0:W], in1=rden[:, 0:W], op=ALU.mult),
             reads=[rden], writes=[y, OB])
        P.store(yT_d[:, p0:p0 + W], y, y[:, 0:W])
    return P.finish()


def prep_mla(up, p, l, core):
    h = core % 4
    cq = up[:, 0:384]; ckv = up[:, 384:512]; kpe = up[:, 512:576]
    cos2, sin2 = rope_tables(64)
    wq = p['mla_w_q_up'][l][:, h * 192:(h + 1) * 192]
    wkv = p['mla_w_kv_up'][l][:, h * 256:(h + 1) * 256]
    kcl = lambda w: C_(w.reshape(3, 128, w.shape[1]).transpose(1, 0, 2).reshape(128, 3 * w.shape[1]))
    gq = p['mla_qk_norm_q'][l]; gk = p['mla_qk_norm_k'][l]
    col = lambda v: C_(v[:, None].astype(np.float32))
    idx = np.arange(128)
    tri = (idx[:, None] <= idx[None, :]).astype(np.float32)
    mblk = np.zeros((4, 128, 4, 128), np.float32)
    for k in range(4):
        for qi in range(4):
            if qi > k:
                mblk[k, :, qi, :] = 1.0
            elif qi == k:
                mblk[k, :, qi, :] = tri
    mblk = mblk.reshape(4, 128, 512).transpose(1, 0, 2).reshape(128, 2048)
    ones0 = np.ones((128, 128), np.float32); ones0[:PAD] = 0
    return {
        "cqT0": C_(cq[:, 0:128].T), "cqT1": C_(cq[:, 128:256].T), "cqT2": C_(cq[:, 256:384].T),
        "ckvT": C_(ckv.T), "kpeT": C_(kpe.T), "kpesT": C_(swap_halves(kpe).T),
        "cosT": C_(cos2.T), "sinT": C_(sin2.T),
        "wq_n": kcl(wq[:, 0:128]), "wq_r": kcl(wq[:, 128:192]), "wq_rs": kcl(swap_halves(wq[:, 128:192])),
        "wk": C_(wkv[:, 0:128]), "wv": C_(wkv[:, 128:256]), "ones0": ones0,
        "gqn": C_(p['mla_q_norm'][l].reshape(3, 128).T), "gkv": col(p['mla_kv_norm'][l]),
        "gq_n": col(gq[0:128]), "gq_r": col(gq[128:192]), "gq_rs": col(swap_halves(gq[None, 128:192])[0]),
        "gk_n": col(gk[0:128]), "gk_r": col(gk[128:192]), "gk_rs": col(swap_halves(gk[None, 128:192])[0]),
        "mblk": C_(mblk),
    }


def build_rwkv():
    P = MProg()
    S = P.S
    B = P.banks
    st = {}
    for nm, rows, tot, w in (("r", 128, NCH * 64, 256), ("k", 128, NCH * 64, 256), ("v", 128, NCH * 64, 256)):
        st[nm] = Stream(P, nm + "_tok", rows, tot, w)
        st[nm + "p"] = Stream(P, nm + "p_tok", rows, tot, w)
    for nm, rows in (("wd", 32), ("ad", 32), ("gd", 64)):
        st[nm] = Stream(P, nm + "T", rows, LP, 512)
        st[nm + "p"] = Stream(P, nm + "pT", rows, LP, 512)
    mu_r = P.load("mu_r", [128, 256]); mu_k = P.load("mu_k", [128, 256]); mu_v = P.load("mu_v", [128, 256])
    mu_wd = P.load("mu_wd", [32, 1]); mu_ad = P.load("mu_ad", [32, 1]); mu_gd = P.load("mu_gd", [64, 1])
    w2h = P.load("w2h", [32, 64]); a2h = P.load("a2h", [32, 64]); g2h = P.load("g2h", [64, 64])
    w0t = P.load("w0t", [128, 64]); a0t = P.load("a0t", [128, 64]); kkt = P.load("kkt", [128, 64])
    kat = P.load("kat", [128, 64]); rkt = P.load("rkt", [128, 64]); lng = P.load("lng", [64, 1])
    TriU = P.load("TriU", [128, 128]); SL = P.load("SL", [128, 128]); SU = P.load("SU", [128, 128])
    Id = P.load("Ident", [128, 128])
    yT_d = P.dout("yT", [64, LP])
    ones64 = S.sb([64, 64], F32, "ones64")
    S.op("pool", lambda e: e.memset(ones64[:], 1.0 / 64), writes=[ones64])
    X = [S.sb([64, 64], F32, "X") for _ in range(2)]
    S.op("pool", lambda e: e.memset(X[0][:], 0.0), writes=[X[0]])
    cnt = [0]

    def T(shape, name):
        cnt[0] += 1
        return S.sb(shape, F32, name)
    rs_, ks_, vs_ = T([128, 256], "rs"), T([128, 256], "ks"), T([128, 256], "vs")
    dtm = T([128, 256], "dtm")
    tw = T([32, 512], "tw"); ads = T([32, 512], "ads"); sg = T([64, 512], "sg"); d32 = T([64, 512], "d32")
    gT = T([64, 512], "gT"); yblk = Ring([T([64, 512], "yblk") for _ in range(2)])
    names = ["ld", "a", "kk", "kkn", "kmod", "bb", "cs_e", "Eg", "Eneg", "Ege", "Kt", "Bh", "Kh", "Rt", "t3", "Vs", "SA", "U_"]
    c64 = {n: T([128, 64], n) for n in names}
    col = {n: T([128, 1], n) for n in ["ss", "rn", "sbon"]}
    fm = {n: T([64, 128], n) for n in ["KtT", "BhT", "KhT", "RtT", "WT", "bonT", "oT", "oc", "osq", "ors"]}
    sq = {n: T([128, 128], n) for n in ["Pa", "PTa", "Pb", "PTb", "A", "MakT", "MrbT", "MrkT"]}
    gam = T([64, 1], "gam"); Xg = T([64, 64], "Xg")
    G0, G1, G2, G3, G4, G5, G6, G7 = B

    def mm(pb, pap, lhsT, rhs, reads, start=True, stop=True, inc=True):
        S.op("pe", lambda e: e.matmul(pap, lhsT=lhsT, rhs=rhs, start=start, stop=stop), reads=reads, writes=[pb], inc=inc)

    def tt(o, oap, a, aap, b_, bap, op, extra_w=()):
        S.op("dve", lambda e: e.tensor_tensor(out=oap, in0=aap, in1=bap, op=op), reads=[a, b_], writes=[o] + list(extra_w))

    blocks = [(0, 1)] + [(1 + 4 * i, 4) for i in range(16)]
    for (cb, ncb) in blocks:
        W = ncb * 128
        w2 = ncb * 64
        p0 = cb * 128
        for nm, dst, mu in (("r", rs_, mu_r), ("k", ks_, mu_k), ("v", vs_, mu_v)):
            x = st[nm].get(cb * 64, w2); xp = st[nm + "p"].get(cb * 64, w2)
            tt(dtm, dtm[:, 0:w2], xp, xp[:, 0:w2], x, x[:, 0:w2], ALU.subtract)
            tt(dtm, dtm[:, 0:w2], dtm, dtm[:, 0:w2], mu, mu[:, 0:w2], ALU.mult)
            tt(dst, dst[:, 0:w2], dtm, dtm[:, 0:w2], x, x[:, 0:w2], ALU.add)
        for nm, dst, mu, rows, fn in (("wd", tw, mu_wd, 32, AF.Tanh), ("ad", ads, mu_ad, 32, None), ("gd", sg, mu_gd, 64, AF.Sigmoid)):
            x = st[nm].get(p0, W); xp = st[nm + "p"].get(p0, W)
            tt(d32, d32[0:rows, 0:W], xp, xp[:, 0:W], x, x[:, 0:W], ALU.subtract)
            S.op("dve", lambda e: e.scalar_tensor_tensor(out=dst[:, 0:W], in0=d32[0:rows, 0:W], scalar=mu[:, 0:1], in1=x[:, 0:W],
                                                         op0=ALU.mult, op1=ALU.add), reads=[d32, mu, x], writes=[dst])
            if fn is not None:
                S.op("act", lambda e: e.activation(out=dst[:, 0:W], in_=dst[:, 0:W], func=fn), reads=[dst], writes=[dst])
        mm(G0, G0[0:64, 0:W], g2h[:], sg[:, 0:W], [g2h, sg])
        S.op("act", lambda e: e.activation(out=gT[:, 0:W], in_=G0[0:64, 0:W], func=AF.Copy), writes=[gT, G0])
        yb = yblk.next()
        for ci in range(ncb):
            c = cb + ci
            s64 = slice(ci * 64, (ci + 1) * 64)
            sl = slice(ci * 128, (ci + 1) * 128)
            r_ap, k_ap, v_ap = rs_[:, s64], ks_[:, s64], vs_[:, s64]
            D_ = c64
            mm(G0, G0[:, 0:64], tw[:, sl], w2h[:], [tw, w2h])
            tt(D_["ld"], D_["ld"][:], w0t, w0t[:], w0t, G0[:, 0:64], ALU.add, extra_w=[G0])
            S.op("act", lambda e: e.activation(out=D_["ld"][:], in_=D_["ld"][:], func=AF.Sigmoid), reads=[D_["ld"]], writes=[D_["ld"]])
            S.op("dve", lambda e: e.tensor_scalar(out=D_["ld"][:], in0=D_["ld"][:], scalar1=float(-np.exp(-0.5)), scalar2=None,
                                                  op0=ALU.mult), reads=[D_["ld"]], writes=[D_["ld"]])
            mm(G0, G0[:, 64:128], ads[:, sl], a2h[:], [ads, a2h])
            tt(D_["a"], D_["a"][:], a0t, a0t[:], a0t, G0[:, 64:128], ALU.add, extra_w=[G0])
            S.op("act", lambda e: e.activation(out=D_["a"][:], in_=D_["a"][:], func=AF.Sigmoid), reads=[D_["a"]], writes=[D_["a"]])
            tt(D_["kk"], D_["kk"][:], ks_, k_ap, kkt, kkt[:], ALU.mult)
            S.op("act", lambda e: e.activation(out=D_["t3"][:], in_=D_["kk"][:], func=AF.Square, accum_out=col["ss"][:, 0:1]),
                 reads=[D_["kk"]], writes=[D_["t3"], col["ss"]])
            S.op("act", lambda e: e.activation(out=col["rn"][:], in_=col["ss"][:], func=AF.Sqrt), reads=[col["ss"]], writes=[col["rn"]])
            S.op("dve", lambda e: e.tensor_scalar(out=col["rn"][:], in0=col["rn"][:], scalar1=1e-12, scalar2=None, op0=ALU.max),
                 reads=[col["rn"]], writes=[col["rn"]])
            S.op("dve", lambda e: e.reciprocal(col["rn"][:], col["rn"][:]), reads=[col["rn"]], writes=[col["rn"]])
            S.op("dve", lambda e: e.tensor_scalar(out=D_["kkn"][:], in0=D_["kk"][:], scalar1=col["rn"][:, 0:1], scalar2=None,
                                                  op0=ALU.mult), reads=[D_["kk"], col["rn"]], writes=[D_["kkn"]])
            S.op("dve", lambda e: e.scalar_tensor_tensor(out=D_["kmod"][:], in0=D_["a"][:], scalar=-1.0, in1=kat[:],
                                                         op0=ALU.add, op1=ALU.mult), reads=[D_["a"], kat], writes=[D_["kmod"]])
            S.op("dve", lambda e: e.scalar_tensor_tensor(out=D_["kmod"][:], in0=D_["kmod"][:], scalar=1.0, in1=k_ap,
                                                         op0=ALU.add, op1=ALU.mult), reads=[D_["kmod"], ks_], writes=[D_["kmod"]])
            tt(D_["bb"], D_["bb"][:], D_["kkn"], D_["kkn"][:], D_["a"], D_["a"][:], ALU.mult)
            tt(D_["t3"], D_["t3"][:], rs_, r_ap, rkt, rkt[:], ALU.mult)
            S.op("dve", lambda e: e.scalar_tensor_tensor(out=D_["t3"][:], in0=D_["t3"][:], scalar=1.0, in1=D_["kmod"][:],
                                                         op0=ALU.mult, op1=ALU.mult, accum_out=col["sbon"][:, 0:1]),
                 reads=[D_["t3"], D_["kmod"]], writes=[D_["t3"], col["sbon"]])
            S.op("dve", lambda e: e.tensor_scalar(out=D_["Vs"][:], in0=v_ap, scalar1=col["sbon"][:, 0:1], scalar2=None,
                                                  op0=ALU.mult), reads=[vs_, col["sbon"]], writes=[D_["Vs"]])
            mm(G0, G0[:, 128:192], TriU[:], D_["ld"][:], [TriU, D_["ld"]])
            mm(G0, G0[0:64, 192:193], D_["ld"][:], TriU[:, 127:128], [TriU, D_["ld"]])
            S.op("act", lambda e: e.activation(out=D_["Eg"][:], in_=G0[:, 128:192], func=AF.Exp), writes=[D_["Eg"], G0])
            S.op("act", lambda e: e.activation(out=D_["Eneg"][:], in_=G0[:, 128:192], func=AF.Exp, scale=-1.0), writes=[D_["Eneg"], G0])
            S.op("act", lambda e: e.activation(out=gam[:], in_=G0[0:64, 192:193], func=AF.Exp), writes=[gam, G0])
            tt(D_["cs_e"], D_["cs_e"][:], D_["ld"], G0[:, 128:192], D_["ld"], D_["ld"][:], ALU.subtract, extra_w=[G0])
            S.op("act", lambda e: e.activation(out=D_["Ege"][:], in_=D_["cs_e"][:], func=AF.Exp), reads=[D_["cs_e"]], writes=[D_["Ege"]])
            tt(D_["Kt"], D_["Kt"][:], D_["kkn"], D_["kkn"][:], D_["Ege"], D_["Ege"][:], ALU.mult)
            tt(D_["Bh"], D_["Bh"][:], D_["bb"], D_["bb"][:], D_["Eneg"], D_["Eneg"][:], ALU.mult)
            tt(D_["Kh"], D_["Kh"][:], D_["kmod"], D_["kmod"][:], D_["Eneg"], D_["Eneg"][:], ALU.mult)
            tt(D_["Rt"], D_["Rt"][:], rs_, r_ap, D_["Eg"], D_["Eg"][:], ALU.mult)
            for i, (src, dst) in enumerate((("Kt", "KtT"), ("Bh", "BhT"), ("Kh", "KhT"), ("Rt", "RtT"))):
                mm(G1, G1[0:64, i * 128:(i + 1) * 128], D_[src][:], Id[:], [D_[src], Id])
            for i, dst in enumerate(("KtT", "BhT", "KhT", "RtT")):
                S.op("act", lambda e, i=i, dst=dst: e.activation(out=fm[dst][:], in_=G1[0:64, i * 128:(i + 1) * 128], func=AF.Copy),
                     writes=[fm[dst], G1])
            mm(G1, G1[0:64, 0:128], D_["Vs"][:], Id[:], [D_["Vs"], Id])
            S.op("act", lambda e: e.activation(out=fm["bonT"][:], in_=G1[0:64, 0:128], func=AF.Copy), writes=[fm["bonT"], G1])
            mm(G2, G2[:, 0:128], fm["BhT"][:], fm["KtT"][:], [fm["BhT"], fm["KtT"]])
            mm(G2, G2[:, 128:256], fm["KtT"][:], fm["BhT"][:], [fm["BhT"], fm["KtT"]])
            mm(G2, G2[:, 256:384], fm["KhT"][:], fm["KtT"][:], [fm["KhT"], fm["KtT"]])
            S.op("dve", lambda e: e.scalar_tensor_tensor(out=sq["Pa"][:], in0=G2[:, 0:128], scalar=-1.0, in1=SU[:],
                                                         op0=ALU.mult, op1=ALU.mult), reads=[SU], writes=[sq["Pa"], G2])
            S.op("dve", lambda e: e.scalar_tensor_tensor(out=sq["PTa"][:], in0=G2[:, 128:256], scalar=-1.0, in1=SL[:],
                                                         op0=ALU.mult, op1=ALU.mult), reads=[SL], writes=[sq["PTa"], G2])
            tt(sq["MakT"], sq["MakT"][:], SU, G2[:, 256:384], SU, SU[:], ALU.mult, extra_w=[G2])
            mm(G3, G3[:, 0:128], fm["BhT"][:], fm["RtT"][:], [fm["BhT"], fm["RtT"]])
            mm(G3, G3[:, 128:256], fm["KhT"][:], fm["RtT"][:], [fm["KhT"], fm["RtT"]])
            tt(sq["MrbT"], sq["MrbT"][:], TriU, G3[:, 0:128], TriU, TriU[:], ALU.mult, extra_w=[G3])
            tt(sq["MrkT"], sq["MrkT"][:], TriU, G3[:, 128:256], TriU, TriU[:], ALU.mult, extra_w=[G3])
            tt(sq["A"], sq["A"][:], Id, Id[:], sq["Pa"], sq["Pa"][:], ALU.add)
            Pc, PTc, Pn, PTn = "Pa", "PTa", "Pb", "PTb"
            for lvl in range(6):
                mm(G4, G4[:, 0:128], sq[PTc][:], sq[Pc][:], [sq[PTc], sq[Pc]])
                mm(G4, G4[:, 128:256], sq[Pc][:], sq[PTc][:], [sq[PTc], sq[Pc]])
                S.op("act", lambda e, Pn=Pn: e.activation(out=sq[Pn][:], in_=G4[:, 0:128], func=AF.Copy), writes=[sq[Pn], G4])
                S.op("act", lambda e, PTn=PTn: e.activation(out=sq[PTn][:], in_=G4[:, 128:256], func=AF.Copy), writes=[sq[PTn], G4])
                mm(G4, G4[:, 256:384], sq[PTn][:], sq["A"][:], [sq[PTn], sq["A"]])
                tt(sq["A"], sq["A"][:], sq["A"], sq["A"][:], sq["A"], G4[:, 256:384], ALU.add, extra_w=[G4])
                Pc, PTc, Pn, PTn = Pn, PTn, Pc, PTc
            mm(G5, G5[:, 0:64], sq["MakT"][:], v_ap, [sq["MakT"], vs_])
            S.op("act", lambda e: e.activation(out=D_["t3"][:], in_=G5[:, 0:64], func=AF.Copy), writes=[D_["t3"], G5])
            mm(G5, G5[:, 64:128], sq["A"][:], D_["t3"][:], [sq["A"], D_["t3"]])
            S.op("act", lambda e: e.activation(out=D_["U_"][:], in_=G5[:, 64:128], func=AF.Copy, scale=-1.0), writes=[D_["U_"], G5])
            mm(G5, G5[0:64, 128:256], D_["Kt"][:], sq["A"][:], [D_["Kt"], sq["A"]])
            S.op("act", lambda e: e.activation(out=fm["WT"][:], in_=G5[0:64, 128:256], func=AF.Copy), writes=[fm["WT"], G5])
            Xp = X[c % 2]; Xn = X[(c + 1) % 2]
            mm(G6, G6[:, 0:64], fm["WT"][:], Xp[:], [fm["WT"], Xp])
            tt(D_["SA"], D_["SA"][:], D_["U_"], D_["U_"][:], D_["U_"], G6[:, 0:64], ALU.subtract, extra_w=[G6])
            S.op("dve", lambda e: e.tensor_scalar(out=Xg[:], in0=Xp[:], scalar1=gam[:, 0:1], scalar2=None, op0=ALU.mult),
                 reads=[Xp, gam], writes=[Xg])
            mm(G6, G6[0:64, 64:128], D_["Kh"][:], v_ap, [D_["Kh"], vs_], start=True, stop=False, inc=False)
            mm(G6, G6[0:64, 64:128], D_["Bh"][:], D_["SA"][:], [D_["Bh"], D_["SA"]], start=False, stop=True)
            mm(G7, G7[0:64, 0:128], Xp[:], fm["RtT"][:], [Xp, fm["RtT"]], start=True, stop=False, inc=False)
            mm(G7, G7[0:64, 0:128], D_["SA"][:], sq["MrbT"][:], [D_["SA"], sq["MrbT"]], start=False, stop=False, inc=False)
            mm(G7, G7[0:64, 0:128], v_ap, sq["MrkT"][:], [vs_, sq["MrkT"]], start=False, stop=True)
            S.op("dve", lambda e: e.scalar_tensor_tensor(out=Xn[:], in0=G6[0:64, 64:128], scalar=gam[:, 0:1], in1=Xg[:],
                                                         op0=ALU.mult, op1=ALU.add), reads=[gam, Xg], writes=[Xn, G6])
            S.op("act", lambda e: e.activation(out=fm["oT"][:], in_=G7[0:64, 0:128], func=AF.Copy), writes=[fm["oT"], G7])
            mm(G7, G7[0:64, 128:256], ones64[:], fm["oT"][:], [ones64, fm["oT"]])
            tt(fm["oc"], fm["oc"][:], fm["oT"], fm["oT"][:], fm["oT"], G7[0:64, 128:256], ALU.subtract, extra_w=[G7])
            S.op("act", lambda e: e.activation(out=fm["osq"][:], in_=fm["oc"][:], func=AF.Square), reads=[fm["oc"]], writes=[fm["osq"]])
            mm(G7, G7[0:64, 256:384], ones64[:], fm["osq"][:], [ones64, fm["osq"]])
            S.op("act", lambda e: e.activation(out=fm["ors"][:], in_=G7[0:64, 256:384], func=AF.Sqrt, bias=64e-5), writes=[fm["ors"], G7])
            S.op("dve", lambda e: e.reciprocal(fm["ors"][:], fm["ors"][:]), reads=[fm["ors"]], writes=[fm["ors"]])
            S.op("dve", lambda e: e.scalar_tensor_tensor(out=fm["oc"][:], in0=fm["oc"][:], scalar=lng[:, 0:1], in1=fm["ors"][:],
                                                         op0=ALU.mult, op1=ALU.mult), reads=[fm["oc"], lng, fm["ors"]], writes=[fm["oc"]])
            tt(fm["oc"], fm["oc"][:], fm["oc"], fm["oc"][:], fm["bonT"], fm["bonT"][:], ALU.add)
            tt(yb, yb[:, sl], fm["oc"], fm["oc"][:], gT, gT[:, sl], ALU.mult)
        P.store(yT_d[:, p0:p0 + W], yb, yb[:, 0:W])
    return P.finish()


def prep_rwkv(up, p, l, core):
    j = core
    o = 576 + 1544 + 1536
    prev = np.concatenate([np.zeros((1, up.shape[1]), np.float32), up[:-1]], 0)
    hs = slice(j * 64, (j + 1) * 64)
    mu = p['rwkv_mu'][l]
    rep = lambda v, n=128: C_(np.tile(v[None, :].astype(np.float32), (n, 1)))
    idx = np.arange(128)
    TriU = (idx[:, None] <= idx[None, :]).astype(np.float32)
    m = {}
    for nm, off in (("r", 0), ("k", 512), ("v", 1024)):
        cs_ = slice(o + off + j * 64, o + off + (j + 1) * 64)
        m[nm + "_tok"] = tok_layout(up[:, cs_]); m[nm + "p_tok"] = tok_layout(prev[:, cs_])
        m["mu_" + nm] = rep(np.tile(mu[off + j * 64: off + (j + 1) * 64], 4))
    for nm, off, wdt in (("wd", 1536, 32), ("ad", 1568, 32), ("gd", 1600, 64)):
        cs_ = slice(o + off, o + off + wdt)
        m[nm + "T"] = C_(up[:, cs_].T); m[nm + "pT"] = C_(prev[:, cs_].T)
        m["mu_" + nm] = C_(mu[off:off + wdt][:, None].astype(np.float32))
    m["w2h"] = C_(p['rwkv_w2'][l][:, hs]); m["a2h"] = C_(p['rwkv_a2'][l][:, hs]); m["g2h"] = C_(p['rwkv_g2'][l][:, hs])
    m["w0t"] = rep(p['rwkv_w0'][l][hs]); m["a0t"] = rep(p['rwkv_a0'][l][hs]); m["kkt"] = rep(p['rwkv_k_k'][l][hs])
    m["kat"] = rep(p['rwkv_k_a'][l][hs]); m["rkt"] = rep(p['rwkv_r_k'][l][j]); m["lng"] = C_(p['rwkv_ln'][l][j][:, None].astype(np.float32))
    m["TriU"] = C_(TriU); m["SL"] = C_((idx[:, None] > idx[None, :]).astype(np.float32))
    m["SU"] = C_((idx[:, None] < idx[None, :]).astype(np.float32)); m["Ident"] = np.eye(128, dtype=np.float32)
    return m


_PROGS = {}


def _prog(name, fn):
    if name not in _PROGS:
        r = fn()
        _PROGS[name] = r[0] if isinstance(r, tuple) else r
    return _PROGS[name]


def _run(nc, maps):
    res = run_bass_kernel_spmd(nc, maps, core_ids=list(range(NCORES)))
    return res.results


def _gl(v, n):
    return C_(np.asarray(v, np.float32).reshape(n, 128).T)


def kernel(**inp):
    p = {k: np.asarray(v, np.float32) for k, v in inp.items()}
    x = p['x'][0]
    meta = p['meta_tokens']
    depth = p['w_in'].shape[0]
    hT = [C_(np.concatenate([meta, x[c * 1024:(c + 1) * 1024]], 0).T) for c in range(NCORES)]
    nc_a = _prog("Ta", build_Ta)
    nc_b = _prog("Tb", build_Tb)
    nc_mla = _prog("mla", build_mla)
    nc_ssd = _prog("ssd", build_ssd)
    nc_ret = _prog("ret", build_ret)
    nc_rwkv = _prog("rwkv", build_rwkv)
    for l in range(depth):
        g1 = _gl(p['ffn1_norm'][l], KC); gm = _gl(p['mix_norm'][l], KC)
        res = _run(nc_a, [{"h_in": hT[c], "g1": g1, "wg": p['ffn1_w_gate'][l], "wu": p['ffn1_w_up'][l],
                           "wd": p['ffn1_w_down'][l], "gm": gm, "win": p['w_in'][l]} for c in range(NCORES)])
        hT = [C_(res[c]['h_out']) for c in range(NCORES)]
        up = np.zeros((LP, INW), np.float32)
        up[PAD:PAD + 16] = res[0]['uT'][:, 0:16].T
        for c in range(NCORES):
            up[PAD + 16 + c * 1024:PAD + 16 + (c + 1) * 1024] = res[c]['uT'][:, 16:].T
        del res
        y = np.zeros((LP, D), np.float32)
        r = _run(nc_mla, [prep_mla(up, p, l, c) for c in range(NCORES)])
        for h in range(4):
            y[:, h * 128:(h + 1) * 128] = r[h]['yT'].T
        r = _run(nc_ssd, [prep_ssd(up, p, l, c) for c in range(NCORES)])
        for j in range(8):
            y[:, 512 + j * 64:512 + (j + 1) * 64] = r[j]['yT'].T
        r = _run(nc_ret, [prep_ret(up, p['ret_norm'][l], c) for c in range(NCORES)])
        for h in range(4):
            y[:, 1024 + h * 128:1024 + (h + 1) * 128] = r[h]['yT'].T
        r = _run(nc_rwkv, [prep_rwkv(up, p, l, c) for c in range(NCORES)])
        for j in range(8):
            y[:, 1536 + j * 64:1536 + (j + 1) * 64] = r[j]['yT'].T
        del r, up
        g2 = _gl(p['ffn2_norm'][l], KC); gs = _gl(p['ssd_norm'][l], 4)
        maps = []
        for c in range(NCORES):
            yc = np.concatenate([y[PAD:PAD + 16], y[PAD + 16 + c * 1024:PAD + 16 + (c + 1) * 1024]], 0)
            maps.append({"h_in": hT[c], "yT": C_(yc.T), "gs": gs, "wout": p['w_out'][l], "g2": g2,
                         "wg": p['ffn2_w_gate'][l], "wu": p['ffn2_w_up'][l], "wd": p['ffn2_w_down'][l]})
        res = _run(nc_b, maps)
        hT = [C_(res[c]['h_out']) for c in range(NCORES)]
        del res, maps, y
    out = np.concatenate([hT[c][:, 16:].T for c in range(NCORES)], 0)
    return C_(out[None].astype(np.float32))
```

```python
import numpy as np
import concourse.bass as bass
import concourse.mybir as mybir
from concourse.bass_utils import run_bass_kernel_spmd

F32 = mybir.dt.float32
BF16 = mybir.dt.bfloat16
AF = mybir.ActivationFunctionType
ALU = mybir.AluOpType
AX = mybir.AxisListType

NCORES = 8
D = 2048
KC = 16
DFF = 5632
FC = 44
NT = 1040
CGS = [(0, 16), (16, 528), (528, 1040)]
INW = 5320
EPS = 1e-6
EPOCH = 30000


class Buf:
    __slots__ = ("ap", "w", "r", "name", "dsem")

    def __init__(self, ap, name=""):
        self.ap = ap
        self.w = None
        self.r = {}
        self.name = name
        self.dsem = None

    def __getitem__(self, idx):
        return self.ap[idx]


class Sched:
    def __init__(self, nc):
        self.nc = nc
        self.engs = {"pe": nc.tensor, "dve": nc.vector, "act": nc.scalar,
                     "pool": nc.gpsimd, "sp": nc.sync}
        self.sems = []
        self.owner = {}
        self.cur = {}
        self.cnt = {}
        self.seen = {e: {} for e in self.engs}
        self.ninst = {e: 0 for e in self.engs}
        for e in ("pe", "dve", "act", "pool"):
            self._new_epoch(e)
        self.uid = 0

    def _alloc_sem(self, name, owner=None):
        h = self.nc.alloc_semaphore(name)
        self.sems.append(h)
        k = len(self.sems) - 1
        self.cnt[k] = 0
        self.owner[k] = owner
        return k

    def _new_epoch(self, e):
        self.cur[e] = self._alloc_sem("c_%s_%d" % (e, len(self.sems)), e)

    def new_dsem(self, name="d"):
        return self._alloc_sem("%s_%d" % (name, len(self.sems)))

    def _waits(self, e, reads, writes):
        need = {}
        for b in reads:
            if b.w is not None:
                k, v = b.w
                if need.get(k, 0) < v:
                    need[k] = v
        for b in writes:
            if b.w is not None:
                k, v = b.w
                if need.get(k, 0) < v:
                    need[k] = v
            for k, v in b.r.items():
                if need.get(k, 0) < v:
                    need[k] = v
        eng = self.engs[e]
        seen = self.seen[e]
        for k, v in need.items():
            if e == "pe" and self.owner[k] == "pe":
                continue
            if seen.get(k, 0) >= v:
                continue
            eng.wait_ge(self.sems[k], v)
            seen[k] = v

    def record(self):
        self._rec = []
        return self._rec

    def stop_record(self):
        self._rec = None

    def replay(self, lanes, skew=0):
        self._rec = None
        n = max(len(l) + k * skew for k, l in enumerate(lanes)) if lanes else 0
        for i in range(n):
            for k, l in enumerate(lanes):
                j = i - k * skew
                if 0 <= j < len(l):
                    kind, a, kw = l[j]
                    (self.op if kind == "op" else self.dma)(*a, **kw)

    def op(self, e, fn, reads=(), writes=(), inc=True):
        if getattr(self, "_rec", None) is not None:
            self._rec.append(("op", (e, fn, tuple(reads), tuple(writes), inc), {}))
            return None
        self._waits(e, reads, writes)
        ins = fn(self.engs[e])
        k = self.cur[e]
        self.ninst[e] += 1
        if inc:
            self.cnt[k] += 1
            ins.then_inc(self.sems[k], 1)
            tok = (k, self.cnt[k])
        else:
            tok = (k, self.cnt[k] + 1)
        for b in reads:
            if b.r.get(tok[0], 0) < tok[1]:
                b.r[tok[0]] = tok[1]
        for b in writes:
            b.w = tok
            b.r = {}
        if inc and self.cnt[k] >= EPOCH:
            self._new_epoch(e)
        return ins

    def dma(self, q, out_ap, in_ap, reads=(), writes=(), sem=None, **kw):
        if getattr(self, "_rec", None) is not None:
            kw2 = dict(kw); kw2.update(reads=tuple(reads), writes=tuple(writes), sem=sem)
            self._rec.append(("dma", (q, out_ap, in_ap), kw2))
            return None
        self._waits(q, reads, writes)
        if sem is None:
            for b in list(writes) + list(reads):
                if b.dsem is None:
                    b.dsem = self.new_dsem()
                sem = b.dsem
                break
        ins = self.engs[q].dma_start(out=out_ap, in_=in_ap, **kw)
        ins.then_inc(self.sems[sem], 16)
        self.cnt[sem] += 16
        self.ninst[q] += 1
        tok = (sem, self.cnt[sem])
        for b in reads:
            if b.r.get(tok[0], 0) < tok[1]:
                b.r[tok[0]] = tok[1]
        for b in writes:
            b.w = tok
            b.r = {}
        return sem

    def wait_all(self, e, semkeys):
        eng = self.engs[e]
        for k in semkeys:
            if self.cnt[k] > 0:
                eng.wait_ge(self.sems[k], self.cnt[k])

    def sb(self, shape, dt, name=None):
        self.uid += 1
        name = "%s_%d" % (name or "sb", self.uid)
        return Buf(self.nc.alloc_sbuf_tensor(name, list(shape), dt).ap(), name)

    def ps(self, shape, dt=F32, name=None):
        self.uid += 1
        name = "%s_%d" % (name or "ps", self.uid)
        return Buf(self.nc.alloc_psum_tensor(name, list(shape), dt).ap(), name)

    def sub(self, buf, ap):
        return Buf(ap, buf.name + "_s")


class Ring:
    def __init__(self, bufs):
        self.bufs = bufs
        self.i = 0

    def next(self):
        b = self.bufs[self.i % len(self.bufs)]
        self.i += 1
        return b


class TCtx:
    def __init__(self, S, n_ost=2):
        self.S = S
        nc = S.nc
        self.hT = [S.sb([128, NT], F32, "hT") for _ in range(KC)]
        self.xnT = [S.sb([128, NT], BF16, "xnT") for _ in range(KC)]
        self.actT = Ring([S.sb([128, NT], BF16, "actT") for _ in range(8)])
        self.wst = Ring([S.sb([128, KC, 128], F32, "wst") for _ in range(4)])
        self.wbf = Ring([S.sb([128, KC, 128], BF16, "wbf") for _ in range(6)])
        self.wdst = Ring([S.sb([128, 4, 128], F32, "wdst") for _ in range(3)])
        self.wdbf = Ring([S.sb([128, 4, 128], BF16, "wdbf") for _ in range(3)])
        self.tmp = Ring([S.sb([128, 512], F32, "tmp") for _ in range(3)])
        self.ost = Ring([S.sb([128, NT], F32, "ost") for _ in range(n_ost)])
        self.rstd = S.sb([128, NT], F32, "rstd")
        self.gcol = S.sb([128, KC], F32, "gcol")
        self.ones = S.sb([128, 128], F32, "ones")
        S.op("pool", lambda e: e.memset(self.ones[:], 1.0), writes=[self.ones])
        banks = [S.ps([128, 512], F32, "bank") for _ in range(8)]
        self.G = [banks[0], banks[1]]
        self.U = [banks[2], banks[3]]
        self.Dn = [banks[4], banks[5]]
        self.N = banks[6]
        self.M = banks[7]

    def pG(self, ci):
        return (self.M, self.M.ap[:, 0:16]) if ci == 0 else (self.G[ci - 1], self.G[ci - 1].ap[:, :])

    def pU(self, ci):
        return (self.M, self.M.ap[:, 16:32]) if ci == 0 else (self.U[ci - 1], self.U[ci - 1].ap[:, :])

    def pD(self, ci):
        return (self.M, self.M.ap[:, 32:48]) if ci == 0 else (self.Dn[ci - 1], self.Dn[ci - 1].ap[:, :])

    def pN(self, ci):
        return (self.M, self.M.ap[:, 48:64]) if ci == 0 else (self.N, self.N.ap[:, :])


def t_load_h(T, h_dram):
    S = T.S
    for kc in range(KC):
        S.dma("sp", T.hT[kc][:], h_dram[kc * 128:(kc + 1) * 128, :], writes=[T.hT[kc]])


def t_store_h(T, h_dram, sems):
    S = T.S
    for kc in range(KC):
        sems.append(S.dma("sp", h_dram[kc * 128:(kc + 1) * 128, :], T.hT[kc][:], reads=[T.hT[kc]]))


def t_rmsnorm(T, g_dram, src=None, nk=KC, dst=None, width=D):
    S = T.S
    src = src or T.hT
    dst = dst or T.xnT
    S.dma("sp", T.gcol[:, 0:nk], g_dram[:, 0:nk], writes=[T.gcol])
    for ci, (c0, c1) in enumerate(CGS):
        pb, pap = T.pN(ci)
        for kc in range(nk):
            t = T.tmp.next()
            S.op("act", lambda e, t=t, kc=kc: e.activation(out=t[:, 0:c1 - c0], in_=src[kc][:, c0:c1], func=AF.Square),
                 reads=[src[kc]], writes=[t])
            S.op("pe", lambda e, t=t, kc=kc: e.matmul(pap[:, 0:c1 - c0], lhsT=T.ones[:], rhs=t[:, 0:c1 - c0],
                                                       start=(kc == 0), stop=(kc == nk - 1)),
                 reads=[T.ones, t], writes=[pb], inc=True)
        S.op("act", lambda e: e.activation(out=T.rstd[:, c0:c1], in_=pap[:, 0:c1 - c0], func=AF.Sqrt,
                                           scale=1.0 / width, bias=EPS),
             writes=[T.rstd, pb])
    S.op("dve", lambda e: e.reciprocal(T.rstd[:], T.rstd[:]), reads=[T.rstd], writes=[T.rstd])
    for kc in range(nk):
        S.op("dve", lambda e, kc=kc: e.scalar_tensor_tensor(out=dst[kc][:], in0=src[kc][:], scalar=T.gcol[:, kc:kc + 1],
                                                            in1=T.rstd[:], op0=ALU.mult, op1=ALU.mult),
             reads=[src[kc], T.gcol, T.rstd], writes=[dst[kc]])


PROBE_NODMA = False
W_TWO_QUEUES = True
PROBE_NOMM = False


def t_load_w(T, w_dram, idx, ncol, nk=KC, cast_eng="act"):
    S = T.S
    if PROBE_NODMA and getattr(T, "_w0", None) is not None:
        return T._w0
    st = T.wst.next()
    bf = T.wbf.next()
    T._wq = getattr(T, "_wq", 0) + 1
    q = "sp" if (T._wq % 2 == 0 or not W_TWO_QUEUES) else "pool"
    S.dma(q, st[:, 0:nk, :], w_dram[idx].rearrange("p (kc f) -> p kc f", kc=nk), writes=[st])
    if cast_eng == "act":
        S.op("act", lambda e: e.activation(out=bf[:, 0:nk, :], in_=st[:, 0:nk, :], func=AF.Copy), reads=[st], writes=[bf])
    else:
        S.op(cast_eng, lambda e: e.tensor_copy(bf[:, 0:nk, :], st[:, 0:nk, :]), reads=[st], writes=[bf])
    T._w0 = bf
    return bf


def t_ffn(T, g_dram, wg, wu, wd):
    S = T.S
    t_rmsnorm(T, g_dram)
    SG = 4
    for sg in range(FC // SG):
        acts = []
        for fi in range(SG):
            fc = sg * SG + fi
            bg = t_load_w(T, wg, fc, 128)
            bu = t_load_w(T, wu, fc, 128)
            a = T.actT.next()
            acts.append(a)
            for ci, (c0, c1) in enumerate(CGS):
                n = c1 - c0
                gb, gap = T.pG(ci)
                ub, uap = T.pU(ci)
                for kc in range(KC):
                    S.op("pe", lambda e, kc=kc: e.matmul(gap[:, 0:n], lhsT=bg[:, kc, :], rhs=T.xnT[kc][:, c0:c1],
                                                          start=(kc == 0), stop=(kc == KC - 1)),
                         reads=[bg, T.xnT[kc]], writes=[gb], inc=(kc == KC - 1))
                for kc in range(KC):
                    S.op("pe", lambda e, kc=kc: e.matmul(uap[:, 0:n], lhsT=bu[:, kc, :], rhs=T.xnT[kc][:, c0:c1],
                                                          start=(kc == 0), stop=(kc == KC - 1)),
                         reads=[bu, T.xnT[kc]], writes=[ub], inc=(kc == KC - 1))
                t = T.tmp.next()
                S.op("act", lambda e, t=t: e.activation(out=t[:, 0:n], in_=gap[:, 0:n], func=AF.Silu),
                     writes=[t, gb])
                S.op("dve", lambda e, t=t: e.tensor_tensor(out=a[:, c0:c1], in0=t[:, 0:n], in1=uap[:, 0:n], op=ALU.mult),
                     reads=[t], writes=[a, ub])
        for dc in range(KC):
            st = T.wdst.next()
            bf = T.wdbf.next()
            S.dma("sp", st[:], wd[sg * KC + dc].rearrange("p (fi d) -> p fi d", fi=SG), writes=[st])
            S.op("pool", lambda e: e.tensor_copy(bf[:], st[:]), reads=[st], writes=[bf])
            for ci, (c0, c1) in enumerate(CGS):
                n = c1 - c0
                db, dap = T.pD(ci)
                for fi in range(SG):
                    S.op("pe", lambda e, fi=fi: e.matmul(dap[:, 0:n], lhsT=bf[:, fi, :], rhs=acts[fi][:, c0:c1],
                                                          start=(fi == 0), stop=(fi == SG - 1)),
                         reads=[bf, acts[fi]], writes=[db], inc=(fi == SG - 1))
                S.op("dve", lambda e: e.scalar_tensor_tensor(out=T.hT[dc][:, c0:c1], in0=dap[:, 0:n], scalar=0.5,
                                                            in1=T.hT[dc][:, c0:c1], op0=ALU.mult, op1=ALU.add),
                     writes=[T.hT[dc], db])


def t_proj(T, w_dram, ncols, rhsT, nk, sink):
    S = T.S
    nchunks = (ncols + 127) // 128
    for uc in range(nchunks):
        m = min(128, ncols - uc * 128)
        bw = t_load_w(T, w_dram, uc, m, nk=nk, cast_eng="pool")
        for ci, (c0, c1) in enumerate(CGS):
            n = c1 - c0
            gb, gap = T.pG(ci)
            for kc in range(nk):
                S.op("pe", lambda e, kc=kc: e.matmul(gap[0:m, 0:n], lhsT=bw[:, kc, 0:m], rhs=rhsT[kc][:, c0:c1],
                                                      start=(kc == 0), stop=(kc == nk - 1)),
                     reads=[bw, rhsT[kc]], writes=[gb], inc=(kc == nk - 1))
            sink(ci, c0, c1, uc, m, gb, gap)


def build_Ta():
    nc = bass.Bass("TRN2", target_bir_lowering=False)
    dt = lambda n, s, k: nc.dram_tensor(n, list(s), F32, kind=k).ap()
    h_in = dt("h_in", [D, NT], "ExternalInput")
    g1 = dt("g1", [128, KC], "ExternalInput")
    wg = dt("wg", [FC, 128, KC * 128], "ExternalInput")
    wu = dt("wu", [FC, 128, KC * 128], "ExternalInput")
    wd = dt("wd", [(FC // 4) * KC, 128, 4 * 128], "ExternalInput")
    gm = dt("gm", [128, KC], "ExternalInput")
    win = dt("win", [42, 128, KC * 128], "ExternalInput")
    h_out = dt("h_out", [D, NT], "ExternalOutput")
    uT = dt("uT", [INW, NT], "ExternalOutput")
    S = Sched(nc)
    T = TCtx(S)
    outs = []
    t_load_h(T, h_in)
    t_ffn(T, g1, wg, wu, wd)
    t_store_h(T, h_out, outs)
    t_rmsnorm(T, gm)
    cur = {}

    def sink(ci, c0, c1, uc, m, pb, pap):
        if ci == 0:
            cur["o"] = T.ost.next()
        o = cur["o"]
        n = c1 - c0
        S.op("act", lambda e: e.activation(out=o[0:m, c0:c1], in_=pap[0:m, 0:n], func=AF.Copy), writes=[o, pb])
        if ci == len(CGS) - 1:
            outs.append(S.dma("act", uT[uc * 128:uc * 128 + m, :], o[0:m, :], reads=[o]))

    t_proj(T, win, INW, T.xnT, KC, sink)
    S.wait_all("sp", sorted(set(outs)))
    return nc, S


def build_Tb():
    nc = bass.Bass("TRN2", target_bir_lowering=False)
    dt = lambda n, s, k: nc.dram_tensor(n, list(s), F32, kind=k).ap()
    h_in = dt("h_in", [D, NT], "ExternalInput")
    yT = dt("yT", [D, NT], "ExternalInput")
    gs = dt("gs", [128, 4], "ExternalInput")
    wout = dt("wout", [KC, 128, KC * 128], "ExternalInput")
    g2 = dt("g2", [128, KC], "ExternalInput")
    wg = dt("wg", [FC, 128, KC * 128], "ExternalInput")
    wu = dt("wu", [FC, 128, KC * 128], "ExternalInput")
    wd = dt("wd", [(FC // 4) * KC, 128, 4 * 128], "ExternalInput")
    h_out = dt("h_out", [D, NT], "ExternalOutput")
    S = Sched(nc)
    T = TCtx(S, n_ost=0)
    outs = []
    t_load_h(T, h_in)
    yst = [S.sb([128, NT], F32, "yst") for _ in range(4)]
    for j in range(4):
        S.dma("sp", yst[j][:], yT[(4 + j) * 128:(5 + j) * 128, :], writes=[yst[j]])
    t_rmsnorm(T, gs, src=yst, nk=4, dst=T.xnT[4:8], width=512)
    for kc in list(range(0, 4)) + list(range(8, 16)):
        st = yst[kc % 4]
        S.dma("sp", st[:], yT[kc * 128:(kc + 1) * 128, :], writes=[st])
        S.op("pool", lambda e, kc=kc, st=st: e.tensor_copy(T.xnT[kc][:], st[:]), reads=[st], writes=[T.xnT[kc]])

    def sink(ci, c0, c1, dc, m, pb, pap):
        n = c1 - c0
        S.op("dve", lambda e: e.tensor_tensor(out=T.hT[dc][:, c0:c1], in0=pap[:, 0:n], in1=T.hT[dc][:, c0:c1], op=ALU.add),
             writes=[T.hT[dc], pb])

    t_proj(T, wout, D, T.xnT, KC, sink)
    t_ffn(T, g2, wg, wu, wd)
    t_store_h(T, h_out, outs)
    S.wait_all("sp", sorted(set(outs)))
    return nc, S


LP = 8320
NCH = 65


class MProg:
    def __init__(self):
        self.nc = bass.Bass("TRN2", target_bir_lowering=False)
        self.S = Sched(self.nc)
        self.outs = []
        self.banks = [self.S.ps([128, 512], F32, "bank") for _ in range(8)]

    def din(self, name, shape):
        return self.nc.dram_tensor(name, list(shape), F32, kind="ExternalInput").ap()

    def dout(self, name, shape):
        return self.nc.dram_tensor(name, list(shape), F32, kind="ExternalOutput").ap()

    def load(self, name, shape, dt=F32, dram=None):
        d = dram if dram is not None else self.din(name, shape)
        b = self.S.sb(shape, F32, name)
        if len(shape) == 2 and shape[1] > 2048:
            step = 2080
            for c0 in range(0, shape[1], step):
                c1 = min(shape[1], c0 + step)
                self.S.dma("sp", b[:, c0:c1], d[:, c0:c1], writes=[b])
        else:
            self.S.dma("sp", b[:], d, writes=[b])
        return b

    def store(self, dram_ap, buf, ap):
        self.outs.append(self.S.dma("pool", dram_ap, ap, reads=[buf]))

    def finish(self):
        self.S.wait_all("sp", sorted(set(self.outs)))
        return self.nc


class Stream:
    def __init__(self, P, name, rows, total, width, nbuf=2):
        self.P = P
        self.d = P.din(name, [rows, total])
        self.rows = rows
        self.ring = Ring([P.S.sb([rows, width], F32, name) for _ in range(nbuf)])

    def get(self, c0, w):
        b = self.ring.next()
        self.P.S.dma("sp", b[:, 0:w], self.d[:, c0:c0 + w], writes=[b])
        return b


def build_ret():
    P = MProg()
    S = P.S
    B = P.banks
    qT_s = Stream(P, "qT", 64, LP, 512); qsT_s = Stream(P, "qsT", 64, LP, 512)
    kT_s = Stream(P, "kT", 64, LP, 512); ksT_s = Stream(P, "ksT", 64, LP, 512)
    cosT_s = Stream(P, "cosT", 64, LP, 512); sinT_s = Stream(P, "sinT", 64, LP, 512)
    ktok_s = Stream(P, "k_tok", 128, NCH * 64, 256); kstok_s = Stream(P, "ks_tok", 128, NCH * 64, 256)
    costok_s = Stream(P, "cos_tok", 128, NCH * 64, 256); sintok_s = Stream(P, "sin_tok", 128, NCH * 64, 256)
    vtok_s = Stream(P, "v_tok", 128, NCH * 128, 512)
    gT_s = Stream(P, "gT", 128, LP, 512)
    qdec = P.load("qdecT", [64, 512])
    kdec = P.load("kdec", [128, 1])
    dmatT = P.load("dmatT", [128, 128])
    cdec = P.load("cdec", [64, 1])
    gcol = P.load("gcol", [128, 1])
    yT_d = P.dout("yT", [128, LP])
    tmp64 = S.sb([64, 512], F32, "tmp64")
    qd = S.sb([64, 512], F32, "qd")

    def mul(o, a, b_, w, rows=64):
        S.op("dve", lambda e: e.tensor_tensor(out=o[0:rows, 0:w], in0=a[0:rows, 0:w], in1=b_[0:rows, 0:w], op=ALU.mult),
             reads=[a, b_], writes=[o])

    def add(o, a, b_, w, rows=64):
        S.op("dve", lambda e: e.tensor_tensor(out=o[0:rows, 0:w], in0=a[0:rows, 0:w], in1=b_[0:rows, 0:w], op=ALU.add),
             reads=[a, b_], writes=[o])
    ones = S.sb([128, 128], F32, "ones")
    S.op("pool", lambda e: e.memset(ones[:], 1.0 / 128), writes=[ones])
    Sst = [S.sb([64, 128], F32, "Sst") for _ in range(2)]
    S.op("pool", lambda e: e.memset(Sst[0][:], 0.0), writes=[Sst[0]])
    atm = Ring([S.sb([128, 128], F32, "atm") for _ in range(2)])
    ybuf = Ring([S.sb([128, 512], F32, "ybuf") for _ in range(2)])
    yc = S.sb([128, 512], F32, "yc"); sq = S.sb([128, 512], F32, "sq"); rs = S.sb([128, 512], F32, "rs")
    AT = Ring([B[0], B[1]]); YT = Ring([B[2], B[3]]); SP_ = B[4]; LN1 = B[5]; LN2 = B[6]
    blocks = [(0, 1)] + [(1 + 4 * i, 4) for i in range(16)]
    for (cb, ncb) in blocks:
        yb = ybuf.next()
        W = ncb * 128
        p0 = cb * 128
        qT = qT_s.get(p0, W); qsT = qsT_s.get(p0, W); kT = kT_s.get(p0, W); ksT = ksT_s.get(p0, W)
        cosT = cosT_s.get(p0, W); sinT = sinT_s.get(p0, W)
        k_tok = ktok_s.get(cb * 64, ncb * 64); ks_tok = kstok_s.get(cb * 64, ncb * 64)
        cos_tok = costok_s.get(cb * 64, ncb * 64); sin_tok = sintok_s.get(cb * 64, ncb * 64)
        v_tok = vtok_s.get(p0, W)
        mul(qT, qT, cosT, W); mul(tmp64, qsT, sinT, W); add(qT, qT, tmp64, W)
        mul(kT, kT, cosT, W); mul(tmp64, ksT, sinT, W); add(kT, kT, tmp64, W)
        S.op("dve", lambda e: e.tensor_scalar(out=kT[:, 0:W], in0=kT[:, 0:W], scalar1=float(64 ** -0.5), scalar2=None,
                                              op0=ALU.mult), reads=[kT], writes=[kT])
        mul(qd, qT, qdec, W)
        w2 = ncb * 64
        mul(k_tok, k_tok, cos_tok, w2, 128); mul(ks_tok, ks_tok, sin_tok, w2, 128); add(k_tok, k_tok, ks_tok, w2, 128)
        S.op("dve", lambda e: e.tensor_scalar(out=k_tok[:, 0:w2], in0=k_tok[:, 0:w2], scalar1=kdec[:, 0:1], scalar2=None,
                                              op0=ALU.mult), reads=[k_tok, kdec], writes=[k_tok])
        for ci in range(ncb):
            c = cb + ci
            sl = slice(ci * 128, (ci + 1) * 128)
            Sp = Sst[c % 2]; Sn = Sst[(c + 1) % 2]
            at = AT.next(); yt = YT.next(); am = atm.next()
            S.op("pe", lambda e: e.matmul(at[:, 0:128], lhsT=kT[:, sl], rhs=qT[:, sl], start=True, stop=True),
                 reads=[kT, qT], writes=[at])
            S.op("dve", lambda e: e.tensor_tensor(out=am[:], in0=at[:, 0:128], in1=dmatT[:], op=ALU.mult),
                 reads=[dmatT], writes=[am, at])
            S.op("pe", lambda e: e.matmul(yt[:, 0:128], lhsT=v_tok[:, sl], rhs=am[:], start=True, stop=False),
                 reads=[v_tok, am], writes=[yt], inc=False)
            S.op("pe", lambda e: e.matmul(yt[:, 0:128], lhsT=Sp[:], rhs=qd[:, sl], start=False, stop=True),
                 reads=[Sp, qd], writes=[yt])
            S.op("act", lambda e: e.activation(out=yb[:, sl], in_=yt[:, 0:128], func=AF.Copy),
                 writes=[yb, yt])
            S.op("pe", lambda e: e.matmul(SP_[0:64, 0:128], lhsT=k_tok[:, ci * 64:(ci + 1) * 64], rhs=v_tok[:, sl],
                                          start=True, stop=True), reads=[k_tok, v_tok], writes=[SP_])
            S.op("dve", lambda e: e.scalar_tensor_tensor(out=Sn[:], in0=Sp[:], scalar=cdec[:, 0:1], in1=SP_[0:64, 0:128],
                                                         op0=ALU.mult, op1=ALU.add),
                 reads=[Sp, cdec], writes=[Sn, SP_])
        gb = gT_s.get(p0, W)
        S.op("act", lambda e: e.activation(out=gb[:, 0:W], in_=gb[:, 0:W], func=AF.Silu), reads=[gb], writes=[gb])
        S.op("pe", lambda e: e.matmul(LN1[:, 0:W], lhsT=ones[:], rhs=yb[:, 0:W], start=True, stop=True),
             reads=[ones, yb], writes=[LN1])
        S.op("dve", lambda e: e.tensor_tensor(out=yc[:, 0:W], in0=yb[:, 0:W], in1=LN1[:, 0:W], op=ALU.subtract),
             reads=[yb], writes=[yc, LN1])
        S.op("act", lambda e: e.activation(out=sq[:, 0:W], in_=yc[:, 0:W], func=AF.Square), reads=[yc], writes=[sq])
        S.op("pe", lambda e: e.matmul(LN2[:, 0:W], lhsT=ones[:], rhs=sq[:, 0:W], start=True, stop=True),
             reads=[ones, sq], writes=[LN2])
        S.op("act", lambda e: e.activation(out=rs[:, 0:W], in_=LN2[:, 0:W], func=AF.Sqrt, scale=1.0, bias=EPS),
             writes=[rs, LN2])
        S.op("dve", lambda e: e.reciprocal(rs[:, 0:W], rs[:, 0:W]), reads=[rs], writes=[rs])
        S.op("dve", lambda e: e.scalar_tensor_tensor(out=yc[:, 0:W], in0=yc[:, 0:W], scalar=gcol[:, 0:1], in1=rs[:, 0:W],
                                                     op0=ALU.mult, op1=ALU.mult), reads=[yc, gcol, rs], writes=[yc])
        S.op("dve", lambda e: e.tensor_tensor(out=gb[:, 0:W], in0=yc[:, 0:W], in1=gb[:, 0:W], op=ALU.mult),
             reads=[yc, gb], writes=[gb])
        P.store(yT_d[:, p0:p0 + W], gb, gb[:, 0:W])
    return P.finish()


PAD = 112
C_ = np.ascontiguousarray


def tok_layout(a):
    F_ = a.shape[1]
    return C_(a.reshape(NCH, 128, F_).transpose(1, 0, 2).reshape(128, NCH * F_))


def rope_tables(dim):
    half = dim // 2
    inv = (10000.0 ** (-np.arange(half, dtype=np.float32) / half)).astype(np.float32)
    pos = (np.arange(LP, dtype=np.float32) - PAD).astype(np.float32)
    ang = (pos[:, None] * inv[None, :]).astype(np.float32)
    cos = np.cos(ang).astype(np.float32)
    sin = np.sin(ang).astype(np.float32)
    cos2 = np.concatenate([cos, cos], 1)
    sin2 = np.concatenate([-sin, sin], 1)
    return cos2, sin2


def swap_halves(a):
    h = a.shape[1] // 2
    return np.concatenate([a[:, h:], a[:, :h]], 1)


def prep_ret(up, ret_norm_l, core):
    h = core % 4
    o = 576 + 1544
    q = up[:, o + h * 64:o + (h + 1) * 64]
    k = up[:, o + 256 + h * 64:o + 256 + (h + 1) * 64]
    v = up[:, o + 512 + h * 128:o + 512 + (h + 1) * 128]
    g = up[:, o + 1024 + h * 128:o + 1024 + (h + 1) * 128]
    cos2, sin2 = rope_tables(64)
    log_g = np.log(np.float32(1.0) - np.float32(2.0) ** np.float32(-5.0 - h)).astype(np.float32)
    idx = np.arange(128, dtype=np.float32)
    qdec = np.exp((idx + 1.0) * log_g).astype(np.float32)
    kdec = (np.exp((127.0 - idx) * log_g) * (64.0 ** -0.5)).astype(np.float32)
    diff = idx[None, :] - idx[:, None]
    dmatT = np.where(diff >= 0, np.exp(diff * log_g), 0.0).astype(np.float32)
    return {
        "qT": C_(q.T), "qsT": C_(swap_halves(q).T), "kT": C_(k.T), "ksT": C_(swap_halves(k).T),
        "cosT": C_(cos2.T), "sinT": C_(sin2.T),
        "k_tok": tok_layout(k), "ks_tok": tok_layout(swap_halves(k)),
        "cos_tok": tok_layout(cos2), "sin_tok": tok_layout(sin2),
        "v_tok": tok_layout(v), "gT": C_(g.T),
        "qdecT": C_(np.tile(np.tile(qdec, 4)[None, :], (64, 1))),
        "kdec": C_(kdec[:, None]), "dmatT": C_(dmatT),
        "cdec": np.full((64, 1), np.exp(128.0 * log_g), np.float32),
        "gcol": C_(ret_norm_l[h][:, None].astype(np.float32)),
    }


def build_ssd():
    P = MProg()
    S = P.S
    B = P.banks
    xsT_s = Stream(P, "xsT_pad", 64, LP + 3, 515); BT_s = Stream(P, "BT_pad", 128, LP + 3, 515)
    CT_s = Stream(P, "CT_pad", 128, LP + 3, 515); zT_s = Stream(P, "zT", 64, LP, 512)
    xtap = [Stream(P, "xs_tok%d" % j, 128, NCH * 64, 256) for j in range(4)]
    btap = [Stream(P, "B_tok%d" % j, 128, NCH * 128, 512) for j in range(4)]
    cw_xs = P.load("cw_xs", [64, 4]); cb_xs = P.load("cb_xs", [64, 1])
    cw_B = P.load("cw_B", [128, 4]); cb_B = P.load("cb_B", [128, 1])
    cw_C = P.load("cw_C", [128, 4]); cb_C = P.load("cb_C", [128, 1])
    cwt_xs = P.load("cwt_xs", [128, 4 * 256]); cbt_xs = P.load("cbt_xs", [128, 256])
    cwt_B = P.load("cwt_B", [128, 4 * 512]); cbt_B = P.load("cbt_B", [128, 512])
    dt = P.load("dt_tok", [128, NCH]); dtb = P.load("dtb", [128, 1]); alog = P.load("alog", [128, 1])
    valid = P.load("valid_tok", [128, NCH]); dcol = P.load("dcol", [64, 1])
    TriU = P.load("TriU", [128, 128]); UTs = P.load("UTs", [128, 128])
    yT_d = P.dout("yT", [64, LP])
    onesF = S.sb([128, 128], F32, "onesF")
    S.op("pool", lambda e: e.memset(onesF[:], 1.0), writes=[onesF])
    S.op("act", lambda e: e.activation(out=dt[:], in_=dt[:], func=AF.Exp, bias=dtb[:, 0:1]), reads=[dt, dtb], writes=[dt])
    S.op("act", lambda e: e.activation(out=dt[:], in_=dt[:], func=AF.Ln, bias=1.0), reads=[dt], writes=[dt])
    S.op("dve", lambda e: e.tensor_tensor(out=dt[:], in0=dt[:], in1=valid[:], op=ALU.mult), reads=[dt, valid], writes=[dt])
    S.op("act", lambda e: e.activation(out=alog[:], in_=alog[:], func=AF.Exp), reads=[alog], writes=[alog])
    la = S.sb([128, NCH], F32, "la"); cs = S.sb([128, NCH], F32, "cs"); dte = S.sb([128, NCH], F32, "dte")
    S.op("dve", lambda e: e.tensor_scalar(out=la[:], in0=dt[:], scalar1=alog[:, 0:1], scalar2=-1.0, op0=ALU.mult,
                                          op1=ALU.mult), reads=[dt, alog], writes=[la])
    S.op("pe", lambda e: e.matmul(B[5][:, 0:NCH], lhsT=TriU[:], rhs=la[:], start=True, stop=True),
         reads=[TriU, la], writes=[B[5]])
    S.op("act", lambda e: e.activation(out=cs[:], in_=B[5][:, 0:NCH], func=AF.Copy), writes=[cs, B[5]])
    S.op("pe", lambda e: e.matmul(B[6][:, 0:NCH], lhsT=onesF[:], rhs=la[:], start=True, stop=True),
         reads=[onesF, la], writes=[B[6]])
    S.op("dve", lambda e: e.tensor_tensor(out=dte[:], in0=B[6][:, 0:NCH], in1=cs[:], op=ALU.subtract),
         reads=[cs], writes=[dte, B[6]])
    S.op("act", lambda e: e.activation(out=dte[:], in_=dte[:], func=AF.Exp), reads=[dte], writes=[dte])

    def conv_fm(src, rows, W, cw, cb, dst):
        S.op("dve", lambda e: e.tensor_scalar(out=dst[0:rows, 0:W], in0=src[0:rows, 0:W], scalar1=cw[:, 0:1], scalar2=None,
                                              op0=ALU.mult), reads=[src, cw], writes=[dst])
        for j in range(1, 4):
            S.op("dve", lambda e, j=j: e.scalar_tensor_tensor(out=dst[0:rows, 0:W], in0=src[0:rows, j:W + j],
                                                              scalar=cw[:, j:j + 1], in1=dst[0:rows, 0:W],
                                                              op0=ALU.mult, op1=ALU.add), reads=[src, cw, dst], writes=[dst])
        S.op("act", lambda e: e.activation(out=dst[0:rows, 0:W], in_=dst[0:rows, 0:W], func=AF.Silu, bias=cb[:, 0:1]),
             reads=[dst, cb], writes=[dst])

    def conv_tm(taps, w, cwt, cbt, full, dst, tmp):
        for j in range(4):
            o = dst if j == 0 else tmp
            S.op("dve", lambda e, j=j, o=o: e.tensor_tensor(out=o[:, 0:w], in0=taps[j][:, 0:w],
                                                            in1=cwt[:, j * full:j * full + w], op=ALU.mult),
                 reads=[taps[j], cwt], writes=[o])
            if j > 0:
                S.op("dve", lambda e: e.tensor_tensor(out=dst[:, 0:w], in0=dst[:, 0:w], in1=tmp[:, 0:w], op=ALU.add),
                     reads=[dst, tmp], writes=[dst])
        S.op("dve", lambda e: e.tensor_tensor(out=dst[:, 0:w], in0=dst[:, 0:w], in1=cbt[:, 0:w], op=ALU.add),
             reads=[dst, cbt], writes=[dst])
        S.op("act", lambda e: e.activation(out=dst[:, 0:w], in_=dst[:, 0:w], func=AF.Silu), reads=[dst], writes=[dst])

    BTc = Ring([S.sb([128, 512], F32, "BTc") for _ in range(2)])
    CTc = Ring([S.sb([128, 512], F32, "CTc") for _ in range(2)])
    xsTc = Ring([S.sb([64, 512], F32, "xsTc") for _ in range(2)])
    xtok = Ring([S.sb([128, 256], F32, "xtok") for _ in range(2)])
    btok = Ring([S.sb([128, 512], F32, "btok") for _ in range(2)])
    tmpx = S.sb([128, 256], F32, "tmpx"); tmpb = S.sb([128, 512], F32, "tmpb")
    lam = Ring([S.sb([128, 128], F32, "lam") for _ in range(2)])
    laf = Ring([S.sb([128, 128], F32, "laf") for _ in range(2)])
    LT = Ring([S.sb([128, 128], F32, "LT") for _ in range(2)])
    Er = Ring([S.sb([128, 128], F32, "Er") for _ in range(2)])
    CsT = Ring([S.sb([128, 128], F32, "CsT") for _ in range(2)])
    xdt = Ring([S.sb([128, 64], F32, "xdt") for _ in range(2)])
    bd = Ring([S.sb([128, 128], F32, "bd") for _ in range(2)])
    ybuf = Ring([S.sb([64, 512], F32, "ybuf") for _ in range(2)])
    Sst = [S.sb([128, 64], F32, "Sst") for _ in range(2)]
    S.op("pool", lambda e: e.memset(Sst[0][:], 0.0), writes=[Sst[0]])
    GT = B[0]; SEG = B[1]; CSR = B[2]; YT = B[3]; SC = B[4]
    blocks = [(0, 1)] + [(1 + 4 * i, 4) for i in range(16)]
    for (cb, ncb) in blocks:
        W = ncb * 128
        p0 = cb * 128
        xs_in = xsT_s.get(p0, W + 3); B_in = BT_s.get(p0, W + 3); C_in = CT_s.get(p0, W + 3); zT = zT_s.get(p0, W)
        xt = [xtap[j].get(cb * 64, ncb * 64) for j in range(4)]
        bt = [btap[j].get(cb * 128, ncb * 128) for j in range(4)]
        BT = BTc.next(); CT = CTc.next(); xsT = xsTc.next(); xk = xtok.next(); bk = btok.next(); yb = ybuf.next()
        conv_fm(B_in, 128, W, cw_B, cb_B, BT)
        conv_fm(C_in, 128, W, cw_C, cb_C, CT)
        conv_fm(xs_in, 64, W, cw_xs, cb_xs, xsT)
        conv_tm(xt, ncb * 64, cwt_xs, cbt_xs, 256, xk, tmpx)
        conv_tm(bt, ncb * 128, cwt_B, cbt_B, 512, bk, tmpb)
        S.op("act", lambda e: e.activation(out=zT[:, 0:W], in_=zT[:, 0:W], func=AF.Silu), reads=[zT], writes=[zT])
        for ci in range(ncb):
            c = cb + ci
            sl = slice(ci * 128, (ci + 1) * 128)
            Sp = Sst[c % 2]; Sn = Sst[(c + 1) % 2]
            lm = lam.next(); lf = laf.next(); lt = LT.next(); er = Er.next(); cst = CsT.next(); xd = xdt.next(); bdd = bd.next()
            lac = la[:, c:c + 1]
            S.op("pe", lambda e: e.matmul(GT[:, 0:128], lhsT=BT[:, sl], rhs=CT[:, sl], start=True, stop=True),
                 reads=[BT, CT], writes=[GT])
            S.op("dve", lambda e: e.tensor_scalar(out=lm[:], in0=UTs[:], scalar1=lac, scalar2=None, op0=ALU.mult),
                 reads=[UTs, la], writes=[lm])
            S.op("dve", lambda e: e.tensor_scalar(out=lf[:], in0=onesF[:], scalar1=lac, scalar2=None, op0=ALU.mult),
                 reads=[onesF, la], writes=[lf])
            S.op("pe", lambda e: e.matmul(SEG[:, 0:128], lhsT=lm[:], rhs=TriU[:], start=True, stop=True),
                 reads=[lm, TriU], writes=[SEG])
            S.op("pe", lambda e: e.matmul(CSR[:, 0:128], lhsT=lf[:], rhs=TriU[:], start=True, stop=True),
                 reads=[lf, TriU], writes=[CSR])
            S.op("act", lambda e: e.activation(out=lt[:], in_=SEG[:, 0:128], func=AF.Exp), writes=[lt, SEG])
            S.op("dve", lambda e: e.tensor_tensor(out=lt[:], in0=lt[:], in1=TriU[:], op=ALU.mult), reads=[lt, TriU], writes=[lt])
            S.op("dve", lambda e: e.tensor_tensor(out=lt[:], in0=lt[:], in1=GT[:, 0:128], op=ALU.mult), reads=[lt], writes=[lt, GT])
            S.op("act", lambda e: e.activation(out=er[:], in_=CSR[:, 0:128], func=AF.Exp), writes=[er, CSR])
            S.op("dve", lambda e: e.tensor_tensor(out=cst[:], in0=CT[:, sl], in1=er[:], op=ALU.mult), reads=[CT, er], writes=[cst])
            S.op("dve", lambda e: e.tensor_scalar(out=xd[:], in0=xk[:, ci * 64:(ci + 1) * 64], scalar1=dt[:, c:c + 1],
                                                  scalar2=None, op0=ALU.mult), reads=[xk, dt], writes=[xd])
            S.op("dve", lambda e: e.tensor_scalar(out=bdd[:], in0=bk[:, sl], scalar1=dte[:, c:c + 1], scalar2=None,
                                                  op0=ALU.mult), reads=[bk, dte], writes=[bdd])
            S.op("pe", lambda e: e.matmul(YT[0:64, 0:128], lhsT=xd[:], rhs=lt[:], start=True, stop=False),
                 reads=[xd, lt], writes=[YT], inc=False)
            S.op("pe", lambda e: e.matmul(YT[0:64, 0:128], lhsT=Sp[:], rhs=cst[:], start=False, stop=True),
                 reads=[Sp, cst], writes=[YT])
            S.op("dve", lambda e: e.scalar_tensor_tensor(out=yb[:, sl], in0=xsT[:, sl], scalar=dcol[:, 0:1],
                                                         in1=YT[0:64, 0:128], op0=ALU.mult, op1=ALU.add),
                 reads=[xsT, dcol], writes=[yb, YT])
            S.op("pe", lambda e: e.matmul(SC[:, 0:64], lhsT=bdd[:], rhs=xd[:], start=True, stop=True),
                 reads=[bdd, xd], writes=[SC])
            S.op("dve", lambda e: e.scalar_tensor_tensor(out=Sn[:], in0=Sp[:], scalar=er[:, 127:128], in1=SC[:, 0:64],
                                                         op0=ALU.mult, op1=ALU.add), reads=[Sp, er], writes=[Sn, SC])
        S.op("dve", lambda e: e.tensor_tensor(out=yb[:, 0:W], in0=yb[:, 0:W], in1=zT[:, 0:W], op=ALU.mult),
             reads=[yb, zT], writes=[yb])
        P.store(yT_d[:, p0:p0 + W], yb, yb[:, 0:W])
    return P.finish()


def prep_ssd(up, p, l, core):
    j = core
    g = j // 4
    o = 576
    z = up[:, o + j * 64:o + (j + 1) * 64]
    xo = o + 512
    xs = up[:, xo + j * 64:xo + (j + 1) * 64]
    Bm = up[:, xo + 512 + g * 128:xo + 512 + (g + 1) * 128]
    Cm = up[:, xo + 768 + g * 128:xo + 768 + (g + 1) * 128]
    dtr = up[:, xo + 1024 + j:xo + 1024 + j + 1]
    cw = p['ssd_conv_w'][l]
    cb = p['ssd_conv_b'][l]
    ch_xs = slice(j * 64, (j + 1) * 64)
    ch_B = slice(512 + g * 128, 512 + (g + 1) * 128)
    ch_C = slice(768 + g * 128, 768 + (g + 1) * 128)
    pad3 = lambda a: np.concatenate([np.zeros((3, a.shape[1]), np.float32), a], 0)
    xs_p = pad3(xs); B_p = pad3(Bm); C_p = pad3(Cm)
    valid = np.ones((LP, 1), np.float32); valid[:PAD] = 0
    idx = np.arange(128)
    TriU = (idx[:, None] <= idx[None, :]).astype(np.float32)
    m = {
        "xsT_pad": C_(xs_p.T), "BT_pad": C_(B_p.T), "CT_pad": C_(C_p.T), "zT": C_(z.T),
        "cw_xs": C_(cw[:, ch_xs].T), "cb_xs": C_(cb[ch_xs][:, None]),
        "cw_B": C_(cw[:, ch_B].T), "cb_B": C_(cb[ch_B][:, None]),
        "cw_C": C_(cw[:, ch_C].T), "cb_C": C_(cb[ch_C][:, None]),
        "cwt_xs": C_(np.tile(np.tile(cw[:, ch_xs], (1, 4)).reshape(1, 4 * 256), (128, 1))),
        "cbt_xs": C_(np.tile(np.tile(cb[ch_xs], 4)[None, :], (128, 1))),
        "cwt_B": C_(np.tile(np.tile(cw[:, ch_B], (1, 4)).reshape(1, 4 * 512), (128, 1))),
        "cbt_B": C_(np.tile(np.tile(cb[ch_B], 4)[None, :], (128, 1))),
        "dt_tok": tok_layout(dtr), "dtb": np.full((128, 1), p['ssd_dt_bias'][l][j], np.float32),
        "alog": np.full((128, 1), p['ssd_a_log'][l][j], np.float32),
        "valid_tok": tok_layout(valid), "dcol": np.full((64, 1), p['ssd_d'][l][j], np.float32),
        "TriU": C_(TriU), "UTs": C_(1.0 - TriU),
    }
    for t in range(4):
        m["xs_tok%d" % t] = tok_layout(xs_p[t:t + LP])
        m["B_tok%d" % t] = tok_layout(B_p[t:t + LP])
    return m


def build_mla():
    P = MProg()
    S = P.S
    B = P.banks
    cq_s = [Stream(P, "cqT%d" % i, 128, LP, 512) for i in range(3)]
    ckv_s = Stream(P, "ckvT", 128, LP, 512)
    kpe_s = Stream(P, "kpeT", 64, LP, 512); kpes_s = Stream(P, "kpesT", 64, LP, 512)
    cos_s = Stream(P, "cosT", 64, LP, 512); sin_s = Stream(P, "sinT", 64, LP, 512)
    yT_d = P.dout("yT", [128, LP])

    def loadbf(name, shape):
        f = P.load(name, shape)
        b = S.sb(shape, BF16, name + "b")
        S.op("dve", lambda e: e.tensor_copy(b[:], f[:]), reads=[f], writes=[b])
        return b
    wqn = loadbf("wq_n", [128, 3 * 128]); wqr = loadbf("wq_r", [128, 3 * 64]); wqrs = loadbf("wq_rs", [128, 3 * 64])
    wk = loadbf("wk", [128, 128]); wv = loadbf("wv", [128, 128])
    ones0b = loadbf("ones0", [128, 128])
    gqn = P.load("gqn", [128, 3]); gkv = P.load("gkv", [128, 1])
    gq_n = P.load("gq_n", [128, 1]); gq_r = P.load("gq_r", [64, 1]); gq_rs = P.load("gq_rs", [64, 1])
    gk_n = P.load("gk_n", [128, 1]); gk_r = P.load("gk_r", [64, 1]); gk_rs = P.load("gk_rs", [64, 1])
    mblk = P.load("mblk", [128, 4 * 512])
    onesF = S.sb([128, 128], F32, "onesF"); onesb = S.sb([128, 128], BF16, "onesb")
    S.op("pool", lambda e: e.memset(onesF[:], 1.0), writes=[onesF])
    S.op("pool", lambda e: e.memset(onesb[:], 1.0), writes=[onesb])
    KnT = S.sb([128, LP], BF16, "KnT"); KrT = S.sb([64, LP], BF16, "KrT"); Vt = S.sb([128, LP], BF16, "Vt")
    sqa = Ring([S.sb([128, 512], F32, "sqa") for _ in range(2)])
    rstd = S.sb([128, 512], F32, "rstd"); rstdq = S.sb([128, 512], F32, "rstdq"); rstdk = S.sb([128, 512], F32, "rstdk")
    cqn = [S.sb([128, 512], BF16, "cqn") for _ in range(3)]
    ckvn = S.sb([128, 512], BF16, "ckvn")
    qn_f = S.sb([128, 512], BF16, "qn_f"); qr_f = S.sb([64, 512], BF16, "qr_f")
    t1 = S.sb([64, 512], F32, "t1"); t2 = S.sb([64, 512], F32, "t2")
    PTr = Ring([S.sb([128, 512], BF16, "PT") for _ in range(3)])
    rden = S.sb([128, 512], F32, "rden"); yo = Ring([S.sb([128, 512], F32, "yo") for _ in range(2)])
    STb = Ring([B[0], B[1]]); OB = B[2]; DB = B[3]; NB = B[4]; Q1 = B[5]; Q2 = B[6]; Q3 = B[7]
    SCALE = float(192 ** -0.5)

    def rms(parts, W, width, dst, post_scale=1.0):
        n = len(parts)
        for i, (buf, ap, rows, is_ps) in enumerate(parts):
            sq = sqa.next()
            if is_ps:
                S.op("act", lambda e: e.activation(out=sq[0:rows, 0:W], in_=ap, func=AF.Square), writes=[sq, buf])
            else:
                S.op("act", lambda e: e.activation(out=sq[0:rows, 0:W], in_=ap, func=AF.Square), reads=[buf], writes=[sq])
            S.op("pe", lambda e: e.matmul(NB[:, 0:W], lhsT=onesF[0:rows, :], rhs=sq[0:rows, 0:W], start=(i == 0),
                                          stop=(i == n - 1)), reads=[onesF, sq], writes=[NB])
        S.op("act", lambda e: e.activation(out=dst[:, 0:W], in_=NB[:, 0:W], func=AF.Sqrt, scale=1.0 / width, bias=EPS),
             writes=[dst, NB])
        S.op("dve", lambda e: e.reciprocal(dst[:, 0:W], dst[:, 0:W]), reads=[dst], writes=[dst])
        if post_scale != 1.0:
            S.op("dve", lambda e: e.tensor_scalar(out=dst[:, 0:W], in0=dst[:, 0:W], scalar1=post_scale, scalar2=None,
                                                  op0=ALU.mult), reads=[dst], writes=[dst])

    blocks = [(0, 1)] + [(1 + 4 * i, 4) for i in range(16)]
    for (cb, ncb) in blocks:
        W = ncb * 128
        p0 = cb * 128
        cq = [s.get(p0, W) for s in cq_s]
        ckv = ckv_s.get(p0, W); kpe = kpe_s.get(p0, W); kpes = kpes_s.get(p0, W)
        cosb = cos_s.get(p0, W); sinb = sin_s.get(p0, W)
        rms([(cq[i], cq[i][:, 0:W], 128, False) for i in range(3)], W, 384.0, rstd)
        for i in range(3):
            S.op("dve", lambda e, i=i: e.scalar_tensor_tensor(out=cqn[i][:, 0:W], in0=cq[i][:, 0:W], scalar=gqn[:, i:i + 1],
                                                              in1=rstd[:, 0:W], op0=ALU.mult, op1=ALU.mult),
                 reads=[cq[i], gqn, rstd], writes=[cqn[i]])
        for (pb, wt, m) in ((Q1, wqn, 128), (Q2, wqr, 64), (Q3, wqrs, 64)):
            for i in range(3):
                S.op("pe", lambda e, i=i: e.matmul(pb[0:m, 0:W], lhsT=wt[:, i * m:(i + 1) * m], rhs=cqn[i][:, 0:W],
                                                   start=(i == 0), stop=(i == 2)), reads=[wt, cqn[i]], writes=[pb],
                     inc=(i == 2))
        rms([(Q1, Q1[:, 0:W], 128, True), (Q2, Q2[0:64, 0:W], 64, True)], W, 192.0, rstdq, SCALE)
        S.op("dve", lambda e: e.scalar_tensor_tensor(out=qn_f[:, 0:W], in0=Q1[:, 0:W], scalar=gq_n[:, 0:1], in1=rstdq[:, 0:W],
                                                     op0=ALU.mult, op1=ALU.mult), reads=[gq_n, rstdq], writes=[qn_f, Q1])
        S.op("dve", lambda e: e.scalar_tensor_tensor(out=t1[:, 0:W], in0=Q2[0:64, 0:W], scalar=gq_r[:, 0:1], in1=cosb[:, 0:W],
                                                     op0=ALU.mult, op1=ALU.mult), reads=[gq_r, cosb], writes=[t1, Q2])
        S.op("dve", lambda e: e.scalar_tensor_tensor(out=t2[:, 0:W], in0=Q3[0:64, 0:W], scalar=gq_rs[:, 0:1], in1=sinb[:, 0:W],
                                                     op0=ALU.mult, op1=ALU.mult), reads=[gq_rs, sinb], writes=[t2, Q3])
        S.op("dve", lambda e: e.tensor_tensor(out=t1[:, 0:W], in0=t1[:, 0:W], in1=t2[:, 0:W], op=ALU.add), reads=[t1, t2], writes=[t1])
        S.op("dve", lambda e: e.tensor_tensor(out=qr_f[:, 0:W], in0=t1[:, 0:W], in1=rstdq[0:64, 0:W], op=ALU.mult),
             reads=[t1, rstdq], writes=[qr_f])
        rms([(ckv, ckv[:, 0:W], 128, False)], W, 128.0, rstd)
        S.op("dve", lambda e: e.scalar_tensor_tensor(out=ckvn[:, 0:W], in0=ckv[:, 0:W], scalar=gkv[:, 0:1], in1=rstd[:, 0:W],
                                                     op0=ALU.mult, op1=ALU.mult), reads=[ckv, gkv, rstd], writes=[ckvn])
        S.op("pe", lambda e: e.matmul(Q1[:, 0:W], lhsT=wk[:], rhs=ckvn[:, 0:W], start=True, stop=True),
             reads=[wk, ckvn], writes=[Q1])
        for ci in range(ncb):
            c = cb + ci
            S.op("pe", lambda e: e.matmul(Q3[:, 0:128], lhsT=ckvn[:, ci * 128:(ci + 1) * 128], rhs=wv[:], start=True, stop=True),
                 reads=[ckvn, wv], writes=[Q3])
            S.op("act", lambda e: e.activation(out=Vt[:, c * 128:(c + 1) * 128], in_=Q3[:, 0:128], func=AF.Copy),
                 writes=[Vt, Q3])
        rms([(Q1, Q1[:, 0:W], 128, True), (kpe, kpe[:, 0:W], 64, False)], W, 192.0, rstdk)
        S.op("dve", lambda e: e.scalar_tensor_tensor(out=KnT[:, p0:p0 + W], in0=Q1[:, 0:W], scalar=gk_n[:, 0:1], in1=rstdk[:, 0:W],
                                                     op0=ALU.mult, op1=ALU.mult), reads=[gk_n, rstdk], writes=[KnT, Q1])
        S.op("dve", lambda e: e.scalar_tensor_tensor(out=t1[:, 0:W], in0=kpe[:, 0:W], scalar=gk_r[:, 0:1], in1=cosb[:, 0:W],
                                                     op0=ALU.mult, op1=ALU.mult), reads=[kpe, gk_r, cosb], writes=[t1])
        S.op("dve", lambda e: e.scalar_tensor_tensor(out=t2[:, 0:W], in0=kpes[:, 0:W], scalar=gk_rs[:, 0:1], in1=sinb[:, 0:W],
                                                     op0=ALU.mult, op1=ALU.mult), reads=[kpes, gk_rs, sinb], writes=[t2])
        S.op("dve", lambda e: e.tensor_tensor(out=t1[:, 0:W], in0=t1[:, 0:W], in1=t2[:, 0:W], op=ALU.add), reads=[t1, t2], writes=[t1])
        S.op("dve", lambda e: e.tensor_tensor(out=KrT[:, p0:p0 + W], in0=t1[:, 0:W], in1=rstdk[0:64, 0:W], op=ALU.mult),
             reads=[t1, rstdk], writes=[KrT])
        nk = cb + ncb
        def scores(kc):
            ks = slice(kc * 128, (kc + 1) * 128)
            st = STb.next(); pt = PTr.next()
            S.op("pe", lambda e: e.matmul(st[:, 0:W], lhsT=KnT[:, ks], rhs=qn_f[:, 0:W], start=True, stop=False),
                 reads=[KnT, qn_f], writes=[st], inc=False)
            S.op("pe", lambda e: e.matmul(st[:, 0:W], lhsT=KrT[:, ks], rhs=qr_f[:, 0:W], start=False, stop=True),
                 reads=[KrT, qr_f], writes=[st])
            S.op("act", lambda e: e.activation(out=pt[:, 0:W], in_=st[:, 0:W], func=AF.Exp), writes=[pt, st])
            if kc >= cb:
                k_ = kc - cb
                S.op("pool", lambda e: e.tensor_tensor(out=pt[:, 0:W], in0=pt[:, 0:W], in1=mblk[:, k_ * 512:k_ * 512 + W],
                                                       op=ALU.mult), reads=[pt, mblk], writes=[pt])
            return pt

        def pv(kc, pt):
            ks = slice(kc * 128, (kc + 1) * 128)
            S.op("pe", lambda e: e.matmul(OB[:, 0:W], lhsT=Vt[:, ks], rhs=pt[:, 0:W], start=(kc == 0), stop=(kc == nk - 1)),
                 reads=[Vt, pt], writes=[OB], inc=False)
            S.op("pe", lambda e: e.matmul(DB[:, 0:W], lhsT=(ones0b if kc == 0 else onesb)[:], rhs=pt[:, 0:W],
                                          start=(kc == 0), stop=(kc == nk - 1)), reads=[ones0b, onesb, pt], writes=[DB])
        pend = scores(0)
        for kc in range(nk):
            nxt = scores(kc + 1) if kc + 1 < nk else None
            pv(kc, pend)
            pend = nxt
        y = yo.next()
        S.op("dve", lambda e: e.tensor_scalar(out=rden[:, 0:W], in0=DB[:, 0:W], scalar1=1e-30, scalar2=None, op0=ALU.max),
             writes=[rden, DB])
        S.op("dve", lambda e: e.reciprocal(rden[:, 0:W], rden[:, 0:W]), reads=[rden], writes=[rden])
        S.op("dve", lambda e: e.tensor_tensor(out=y[:, 0:W], in0=OB[:, 0:W], in1=rden[:, 0:W], op=ALU.mult),
             reads=[rden], writes=[y, OB])
        P.store(yT_d[:, p0:p0 + W], y, y[:, 0:W])
    return P.finish()


def prep_mla(up, p, l, core):
    h = core % 4
    cq = up[:, 0:384]; ckv = up[:, 384:512]; kpe = up[:, 512:576]
    cos2, sin2 = rope_tables(64)
    wq = p['mla_w_q_up'][l][:, h * 192:(h + 1) * 192]
    wkv = p['mla_w_kv_up'][l][:, h * 256:(h + 1) * 256]
    kcl = lambda w: C_(w.reshape(3, 128, w.shape[1]).transpose(1, 0, 2).reshape(128, 3 * w.shape[1]))
    gq = p['mla_qk_norm_q'][l]; gk = p['mla_qk_norm_k'][l]
    col = lambda v: C_(v[:, None].astype(np.float32))
    idx = np.arange(128)
    tri = (idx[:, None] <= idx[None, :]).astype(np.float32)
    mblk = np.zeros((4, 128, 4, 128), np.float32)
    for k in range(4):
        for qi in range(4):
            if qi > k:
                mblk[k, :, qi, :] = 1.0
            elif qi == k:
                mblk[k, :, qi, :] = tri
    mblk = mblk.reshape(4, 128, 512).transpose(1, 0, 2).reshape(128, 2048)
    ones0 = np.ones((128, 128), np.float32); ones0[:PAD] = 0
    return {
        "cqT0": C_(cq[:, 0:128].T), "cqT1": C_(cq[:, 128:256].T), "cqT2": C_(cq[:, 256:384].T),
        "ckvT": C_(ckv.T), "kpeT": C_(kpe.T), "kpesT": C_(swap_halves(kpe).T),
        "cosT": C_(cos2.T), "sinT": C_(sin2.T),
        "wq_n": kcl(wq[:, 0:128]), "wq_r": kcl(wq[:, 128:192]), "wq_rs": kcl(swap_halves(wq[:, 128:192])),
        "wk": C_(wkv[:, 0:128]), "wv": C_(wkv[:, 128:256]), "ones0": ones0,
        "gqn": C_(p['mla_q_norm'][l].reshape(3, 128).T), "gkv": col(p['mla_kv_norm'][l]),
        "gq_n": col(gq[0:128]), "gq_r": col(gq[128:192]), "gq_rs": col(swap_halves(gq[None, 128:192])[0]),
        "gk_n": col(gk[0:128]), "gk_r": col(gk[128:192]), "gk_rs": col(swap_halves(gk[None, 128:192])[0]),
        "mblk": C_(mblk),
    }


def build_rwkv():
    P = MProg()
    S = P.S
    B = P.banks
    st = {}
    for nm, rows, tot, w in (("r", 128, NCH * 64, 256), ("k", 128, NCH * 64, 256), ("v", 128, NCH * 64, 256)):
        st[nm] = Stream(P, nm + "_tok", rows, tot, w)
        st[nm + "p"] = Stream(P, nm + "p_tok", rows, tot, w)
    for nm, rows in (("wd", 32), ("ad", 32), ("gd", 64)):
        st[nm] = Stream(P, nm + "T", rows, LP, 512)
        st[nm + "p"] = Stream(P, nm + "pT", rows, LP, 512)
    mu_r = P.load("mu_r", [128, 256]); mu_k = P.load("mu_k", [128, 256]); mu_v = P.load("mu_v", [128, 256])
    mu_wd = P.load("mu_wd", [32, 1]); mu_ad = P.load("mu_ad", [32, 1]); mu_gd = P.load("mu_gd", [64, 1])
    w2h = P.load("w2h", [32, 64]); a2h = P.load("a2h", [32, 64]); g2h = P.load("g2h", [64, 64])
    w0t = P.load("w0t", [128, 64]); a0t = P.load("a0t", [128, 64]); kkt = P.load("kkt", [128, 64])
    kat = P.load("kat", [128, 64]); rkt = P.load("rkt", [128, 64]); lng = P.load("lng", [64, 1])
    TriU = P.load("TriU", [128, 128]); SL = P.load("SL", [128, 128]); SU = P.load("SU", [128, 128])
    Id = P.load("Ident", [128, 128])
    yT_d = P.dout("yT", [64, LP])
    ones64 = S.sb([64, 64], F32, "ones64")
    S.op("pool", lambda e: e.memset(ones64[:], 1.0 / 64), writes=[ones64])
    NX = 6
    X = [S.sb([64, 64], F32, "X") for _ in range(NX)]
    S.op("pool", lambda e: e.memset(X[0][:], 0.0), writes=[X[0]])

    def T(shape, name):
        return S.sb(shape, F32, name)
    blkbufs = []
    for _ in range(2):
        blkbufs.append(dict(rs=T([128, 256], "rs"), ks=T([128, 256], "ks"), vs=T([128, 256], "vs"),
                            tw=T([32, 512], "tw"), ads=T([32, 512], "ads"), sg=T([64, 512], "sg"),
                            gT=T([64, 512], "gT"), yb=T([64, 512], "yblk")))
    dtm = T([128, 256], "dtm"); d32 = T([64, 512], "d32")
    names = ["ld", "a", "kk", "kkn", "kmod", "bb", "cs_e", "Eg", "Eneg", "Ege", "Kt", "Bh", "Kh", "Rt", "t3", "Vs", "SA", "U_"]
    lanes_buf = []
    for ln in range(4):
        L = dict(c64={n: T([128, 64], n) for n in names},
                 col={n: T([128, 1], n) for n in ["ss", "rn", "sbon"]},
                 fm={n: T([64, 128], n) for n in ["KtT", "BhT", "KhT", "RtT", "WT", "bonT", "oT", "oc", "osq", "ors"]},
                 sq={n: T([128, 128], n) for n in ["Pa", "PTa", "Pb", "PTb", "A", "MakT", "MrbT", "MrkT"]},
                 gam=T([64, 1], "gam"), Xg=T([64, 64], "Xg"), a=B[2 * ln], b=B[2 * ln + 1])
        lanes_buf.append(L)

    def mm(pb, pap, lhsT, rhs, reads, start=True, stop=True, inc=True):
        S.op("pe", lambda e: e.matmul(pap, lhsT=lhsT, rhs=rhs, start=start, stop=stop), reads=reads, writes=[pb], inc=inc)

    def tt(o, oap, a, aap, b_, bap, op, extra_w=()):
        S.op("dve", lambda e: e.tensor_tensor(out=oap, in0=aap, in1=bap, op=op), reads=[a, b_], writes=[o] + list(extra_w))

    def chunk(L, bb, ci, c):
        D_ = L["c64"]; col = L["col"]; fm = L["fm"]; sq = L["sq"]; gam = L["gam"]; Xg = L["Xg"]
        Ga = L["a"]; Gb = L["b"]
        rs_, ks_, vs_, tw, ads, gT, yb = bb["rs"], bb["ks"], bb["vs"], bb["tw"], bb["ads"], bb["gT"], bb["yb"]
        s64 = slice(ci * 64, (ci + 1) * 64)
        sl = slice(ci * 128, (ci + 1) * 128)
        r_ap, k_ap, v_ap = rs_[:, s64], ks_[:, s64], vs_[:, s64]
        mm(Ga, Ga[:, 0:64], tw[:, sl], w2h[:], [tw, w2h])
        tt(D_["ld"], D_["ld"][:], w0t, w0t[:], w0t, Ga[:, 0:64], ALU.add, extra_w=[Ga])
        S.op("act", lambda e: e.activation(out=D_["ld"][:], in_=D_["ld"][:], func=AF.Sigmoid), reads=[D_["ld"]], writes=[D_["ld"]])
        S.op("dve", lambda e: e.tensor_scalar(out=D_["ld"][:], in0=D_["ld"][:], scalar1=float(-np.exp(-0.5)), scalar2=None,
                                              op0=ALU.mult), reads=[D_["ld"]], writes=[D_["ld"]])
        mm(Ga, Ga[:, 64:128], ads[:, sl], a2h[:], [ads, a2h])
        tt(D_["a"], D_["a"][:], a0t, a0t[:], a0t, Ga[:, 64:128], ALU.add, extra_w=[Ga])
        S.op("act", lambda e: e.activation(out=D_["a"][:], in_=D_["a"][:], func=AF.Sigmoid), reads=[D_["a"]], writes=[D_["a"]])
        tt(D_["kk"], D_["kk"][:], ks_, k_ap, kkt, kkt[:], ALU.mult)
        S.op("act", lambda e: e.activation(out=D_["t3"][:], in_=D_["kk"][:], func=AF.Square, accum_out=col["ss"][:, 0:1]),
             reads=[D_["kk"]], writes=[D_["t3"], col["ss"]])
        S.op("act", lambda e: e.activation(out=col["rn"][:], in_=col["ss"][:], func=AF.Sqrt), reads=[col["ss"]], writes=[col["rn"]])
        S.op("dve", lambda e: e.tensor_scalar(out=col["rn"][:], in0=col["rn"][:], scalar1=1e-12, scalar2=None, op0=ALU.max),
             reads=[col["rn"]], writes=[col["rn"]])
        S.op("dve", lambda e: e.reciprocal(col["rn"][:], col["rn"][:]), reads=[col["rn"]], writes=[col["rn"]])
        S.op("dve", lambda e: e.tensor_scalar(out=D_["kkn"][:], in0=D_["kk"][:], scalar1=col["rn"][:, 0:1], scalar2=None,
                                              op0=ALU.mult), reads=[D_["kk"], col["rn"]], writes=[D_["kkn"]])
        S.op("dve", lambda e: e.scalar_tensor_tensor(out=D_["kmod"][:], in0=D_["a"][:], scalar=-1.0, in1=kat[:],
                                                     op0=ALU.add, op1=ALU.mult), reads=[D_["a"], kat], writes=[D_["kmod"]])
        S.op("dve", lambda e: e.scalar_tensor_tensor(out=D_["kmod"][:], in0=D_["kmod"][:], scalar=1.0, in1=k_ap,
                                                     op0=ALU.add, op1=ALU.mult), reads=[D_["kmod"], ks_], writes=[D_["kmod"]])
        tt(D_["bb"], D_["bb"][:], D_["kkn"], D_["kkn"][:], D_["a"], D_["a"][:], ALU.mult)
        tt(D_["t3"], D_["t3"][:], rs_, r_ap, rkt, rkt[:], ALU.mult)
        S.op("dve", lambda e: e.scalar_tensor_tensor(out=D_["t3"][:], in0=D_["t3"][:], scalar=1.0, in1=D_["kmod"][:],
                                                     op0=ALU.mult, op1=ALU.mult, accum_out=col["sbon"][:, 0:1]),
             reads=[D_["t3"], D_["kmod"]], writes=[D_["t3"], col["sbon"]])
        S.op("dve", lambda e: e.tensor_scalar(out=D_["Vs"][:], in0=v_ap, scalar1=col["sbon"][:, 0:1], scalar2=None,
                                              op0=ALU.mult), reads=[vs_, col["sbon"]], writes=[D_["Vs"]])
        mm(Ga, Ga[:, 128:192], TriU[:], D_["ld"][:], [TriU, D_["ld"]])
        mm(Ga, Ga[0:64, 192:193], D_["ld"][:], TriU[:, 127:128], [TriU, D_["ld"]])
        S.op("act", lambda e: e.activation(out=D_["Eg"][:], in_=Ga[:, 128:192], func=AF.Exp), writes=[D_["Eg"], Ga])
        S.op("act", lambda e: e.activation(out=D_["Eneg"][:], in_=Ga[:, 128:192], func=AF.Exp, scale=-1.0), writes=[D_["Eneg"], Ga])
        S.op("act", lambda e: e.activation(out=gam[:], in_=Ga[0:64, 192:193], func=AF.Exp), writes=[gam, Ga])
        tt(D_["cs_e"], D_["cs_e"][:], D_["ld"], Ga[:, 128:192], D_["ld"], D_["ld"][:], ALU.subtract, extra_w=[Ga])
        S.op("act", lambda e: e.activation(out=D_["Ege"][:], in_=D_["cs_e"][:], func=AF.Exp), reads=[D_["cs_e"]], writes=[D_["Ege"]])
        tt(D_["Kt"], D_["Kt"][:], D_["kkn"], D_["kkn"][:], D_["Ege"], D_["Ege"][:], ALU.mult)
        tt(D_["Bh"], D_["Bh"][:], D_["bb"], D_["bb"][:], D_["Eneg"], D_["Eneg"][:], ALU.mult)
        tt(D_["Kh"], D_["Kh"][:], D_["kmod"], D_["kmod"][:], D_["Eneg"], D_["Eneg"][:], ALU.mult)
        tt(D_["Rt"], D_["Rt"][:], rs_, r_ap, D_["Eg"], D_["Eg"][:], ALU.mult)
        for i, src in enumerate(("Kt", "Bh", "Kh", "Rt")):
            mm(Gb, Gb[0:64, i * 128:(i + 1) * 128], D_[src][:], Id[:], [D_[src], Id])
        for i, dst in enumerate(("KtT", "BhT", "KhT", "RtT")):
            S.op("act", lambda e, i=i, dst=dst: e.activation(out=fm[dst][:], in_=Gb[0:64, i * 128:(i + 1) * 128], func=AF.Copy),
                 writes=[fm[dst], Gb])
        mm(Ga, Ga[:, 0:128], fm["BhT"][:], fm["KtT"][:], [fm["BhT"], fm["KtT"]])
        mm(Ga, Ga[:, 128:256], fm["KtT"][:], fm["BhT"][:], [fm["BhT"], fm["KtT"]])
        mm(Ga, Ga[:, 256:384], fm["KhT"][:], fm["KtT"][:], [fm["KhT"], fm["KtT"]])
        mm(Gb, Gb[:, 0:128], fm["BhT"][:], fm["RtT"][:], [fm["BhT"], fm["RtT"]])
        mm(Gb, Gb[:, 128:256], fm["KhT"][:], fm["RtT"][:], [fm["KhT"], fm["RtT"]])
        mm(Gb, Gb[0:64, 256:384], D_["Vs"][:], Id[:], [D_["Vs"], Id])
        S.op("dve", lambda e: e.scalar_tensor_tensor(out=sq["Pa"][:], in0=Ga[:, 0:128], scalar=-1.0, in1=SU[:],
                                                     op0=ALU.mult, op1=ALU.mult), reads=[SU], writes=[sq["Pa"], Ga])
        S.op("dve", lambda e: e.scalar_tensor_tensor(out=sq["PTa"][:], in0=Ga[:, 128:256], scalar=-1.0, in1=SL[:],
                                                     op0=ALU.mult, op1=ALU.mult), reads=[SL], writes=[sq["PTa"], Ga])
        tt(sq["MakT"], sq["MakT"][:], SU, Ga[:, 256:384], SU, SU[:], ALU.mult, extra_w=[Ga])
        tt(sq["MrbT"], sq["MrbT"][:], TriU, Gb[:, 0:128], TriU, TriU[:], ALU.mult, extra_w=[Gb])
        tt(sq["MrkT"], sq["MrkT"][:], TriU, Gb[:, 128:256], TriU, TriU[:], ALU.mult, extra_w=[Gb])
        S.op("act", lambda e: e.activation(out=fm["bonT"][:], in_=Gb[0:64, 256:384], func=AF.Copy), writes=[fm["bonT"], Gb])
        tt(sq["A"], sq["A"][:], Id, Id[:], sq["Pa"], sq["Pa"][:], ALU.add)
        Pc, PTc, Pn, PTn = "Pa", "PTa", "Pb", "PTb"
        for lvl in range(6):
            G = Ga if lvl % 2 == 0 else Gb
            mm(G, G[:, 0:128], sq[PTc][:], sq[Pc][:], [sq[PTc], sq[Pc]])
            mm(G, G[:, 128:256], sq[Pc][:], sq[PTc][:], [sq[PTc], sq[Pc]])
            S.op("act", lambda e, Pn=Pn, G=G: e.activation(out=sq[Pn][:], in_=G[:, 0:128], func=AF.Copy), writes=[sq[Pn], G])
            S.op("act", lambda e, PTn=PTn, G=G: e.activation(out=sq[PTn][:], in_=G[:, 128:256], func=AF.Copy), writes=[sq[PTn], G])
            mm(G, G[:, 256:384], sq[PTn][:], sq["A"][:], [sq[PTn], sq["A"]])
            tt(sq["A"], sq["A"][:], sq["A"], sq["A"][:], sq["A"], G[:, 256:384], ALU.add, extra_w=[G])
            Pc, PTc, Pn, PTn = Pn, PTn, Pc, PTc
        mm(Gb, Gb[:, 0:64], sq["MakT"][:], v_ap, [sq["MakT"], vs_])
        S.op("act", lambda e: e.activation(out=D_["t3"][:], in_=Gb[:, 0:64], func=AF.Copy), writes=[D_["t3"], Gb])
        mm(Gb, Gb[:, 64:128], sq["A"][:], D_["t3"][:], [sq["A"], D_["t3"]])
        S.op("act", lambda e: e.activation(out=D_["U_"][:], in_=Gb[:, 64:128], func=AF.Copy, scale=-1.0), writes=[D_["U_"], Gb])
        mm(Gb, Gb[0:64, 128:256], D_["Kt"][:], sq["A"][:], [D_["Kt"], sq["A"]])
        S.op("act", lambda e: e.activation(out=fm["WT"][:], in_=Gb[0:64, 128:256], func=AF.Copy), writes=[fm["WT"], Gb])
        Xp = X[c % NX]; Xn = X[(c + 1) % NX]
        mm(Ga, Ga[:, 0:64], fm["WT"][:], Xp[:], [fm["WT"], Xp])
        tt(D_["SA"], D_["SA"][:], D_["U_"], D_["U_"][:], D_["U_"], Ga[:, 0:64], ALU.subtract, extra_w=[Ga])
        S.op("dve", lambda e: e.tensor_scalar(out=Xg[:], in0=Xp[:], scalar1=gam[:, 0:1], scalar2=None, op0=ALU.mult),
             reads=[Xp, gam], writes=[Xg])
        mm(Ga, Ga[0:64, 64:128], D_["Kh"][:], v_ap, [D_["Kh"], vs_], start=True, stop=False, inc=False)
        mm(Ga, Ga[0:64, 64:128], D_["Bh"][:], D_["SA"][:], [D_["Bh"], D_["SA"]], start=False, stop=True)
        S.op("dve", lambda e: e.scalar_tensor_tensor(out=Xn[:], in0=Ga[0:64, 64:128], scalar=gam[:, 0:1], in1=Xg[:],
                                                     op0=ALU.mult, op1=ALU.add), reads=[gam, Xg], writes=[Xn, Ga])
        mm(Gb, Gb[0:64, 0:128], Xp[:], fm["RtT"][:], [Xp, fm["RtT"]], start=True, stop=False, inc=False)
        mm(Gb, Gb[0:64, 0:128], D_["SA"][:], sq["MrbT"][:], [D_["SA"], sq["MrbT"]], start=False, stop=False, inc=False)
        mm(Gb, Gb[0:64, 0:128], v_ap, sq["MrkT"][:], [vs_, sq["MrkT"]], start=False, stop=True)
        S.op("act", lambda e: e.activation(out=fm["oT"][:], in_=Gb[0:64, 0:128], func=AF.Copy), writes=[fm["oT"], Gb])
        mm(Gb, Gb[0:64, 128:256], ones64[:], fm["oT"][:], [ones64, fm["oT"]])
        tt(fm["oc"], fm["oc"][:], fm["oT"], fm["oT"][:], fm["oT"], Gb[0:64, 128:256], ALU.subtract, extra_w=[Gb])
        S.op("act", lambda e: e.activation(out=fm["osq"][:], in_=fm["oc"][:], func=AF.Square), reads=[fm["oc"]], writes=[fm["osq"]])
        mm(Gb, Gb[0:64, 256:384], ones64[:], fm["osq"][:], [ones64, fm["osq"]])
        S.op("act", lambda e: e.activation(out=fm["ors"][:], in_=Gb[0:64, 256:384], func=AF.Sqrt, bias=64e-5), writes=[fm["ors"], Gb])
        S.op("dve", lambda e: e.reciprocal(fm["ors"][:], fm["ors"][:]), reads=[fm["ors"]], writes=[fm["ors"]])
        S.op("dve", lambda e: e.scalar_tensor_tensor(out=fm["oc"][:], in0=fm["oc"][:], scalar=lng[:, 0:1], in1=fm["ors"][:],
                                                     op0=ALU.mult, op1=ALU.mult), reads=[fm["oc"], lng, fm["ors"]], writes=[fm["oc"]])
        tt(fm["oc"], fm["oc"][:], fm["oc"], fm["oc"][:], fm["bonT"], fm["bonT"][:], ALU.add)
        tt(fm["oc"], fm["oc"][:], fm["oc"], fm["oc"][:], gT, gT[:, sl], ALU.mult)
        S.op("pool", lambda e: e.tensor_copy(yb[:, sl], fm["oc"][:]), reads=[fm["oc"]], writes=[yb])

    blocks = [(0, 1)] + [(1 + 4 * i, 4) for i in range(16)]
    for bi, (cb, ncb) in enumerate(blocks):
        W = ncb * 128
        w2 = ncb * 64
        p0 = cb * 128
        bb = blkbufs[bi % 2]
        for nm, dst, mu in (("r", bb["rs"], mu_r), ("k", bb["ks"], mu_k), ("v", bb["vs"], mu_v)):
            x = st[nm].get(cb * 64, w2); xp = st[nm + "p"].get(cb * 64, w2)
            tt(dtm, dtm[:, 0:w2], xp, xp[:, 0:w2], x, x[:, 0:w2], ALU.subtract)
            tt(dtm, dtm[:, 0:w2], dtm, dtm[:, 0:w2], mu, mu[:, 0:w2], ALU.mult)
            tt(dst, dst[:, 0:w2], dtm, dtm[:, 0:w2], x, x[:, 0:w2], ALU.add)
        for nm, dst, mu, rows, fn in (("wd", bb["tw"], mu_wd, 32, AF.Tanh), ("ad", bb["ads"], mu_ad, 32, None),
                                      ("gd", bb["sg"], mu_gd, 64, AF.Sigmoid)):
            x = st[nm].get(p0, W); xp = st[nm + "p"].get(p0, W)
            tt(d32, d32[0:rows, 0:W], xp, xp[:, 0:W], x, x[:, 0:W], ALU.subtract)
            S.op("dve", lambda e, dst=dst, mu=mu, x=x, rows=rows: e.scalar_tensor_tensor(
                out=dst[:, 0:W], in0=d32[0:rows, 0:W], scalar=mu[:, 0:1], in1=x[:, 0:W], op0=ALU.mult, op1=ALU.add),
                reads=[d32, mu, x], writes=[dst])
            if fn is not None:
                S.op("act", lambda e, dst=dst, fn=fn: e.activation(out=dst[:, 0:W], in_=dst[:, 0:W], func=fn), reads=[dst], writes=[dst])
        G0 = lanes_buf[0]["a"]
        mm(G0, G0[0:64, 0:W], g2h[:], bb["sg"][:, 0:W], [g2h, bb["sg"]])
        S.op("act", lambda e: e.activation(out=bb["gT"][:, 0:W], in_=G0[0:64, 0:W], func=AF.Copy), writes=[bb["gT"], G0])
        lanes = []
        for ci in range(ncb):
            rec = S.record()
            chunk(lanes_buf[ci], bb, ci, cb + ci)
            S.stop_record()
            lanes.append(rec)
        S.replay(lanes, skew=10)
        P.store(yT_d[:, p0:p0 + W], bb["yb"], bb["yb"][:, 0:W])
    return P.finish()


def prep_rwkv(up, p, l, core):
    j = core
    o = 576 + 1544 + 1536
    prev = np.concatenate([np.zeros((1, up.shape[1]), np.float32), up[:-1]], 0)
    hs = slice(j * 64, (j + 1) * 64)
    mu = p['rwkv_mu'][l]
    rep = lambda v, n=128: C_(np.tile(v[None, :].astype(np.float32), (n, 1)))
    idx = np.arange(128)
    TriU = (idx[:, None] <= idx[None, :]).astype(np.float32)
    m = {}
    for nm, off in (("r", 0), ("k", 512), ("v", 1024)):
        cs_ = slice(o + off + j * 64, o + off + (j + 1) * 64)
        m[nm + "_tok"] = tok_layout(up[:, cs_]); m[nm + "p_tok"] = tok_layout(prev[:, cs_])
        m["mu_" + nm] = rep(np.tile(mu[off + j * 64: off + (j + 1) * 64], 4))
    for nm, off, wdt in (("wd", 1536, 32), ("ad", 1568, 32), ("gd", 1600, 64)):
        cs_ = slice(o + off, o + off + wdt)
        m[nm + "T"] = C_(up[:, cs_].T); m[nm + "pT"] = C_(prev[:, cs_].T)
        m["mu_" + nm] = C_(mu[off:off + wdt][:, None].astype(np.float32))
    m["w2h"] = C_(p['rwkv_w2'][l][:, hs]); m["a2h"] = C_(p['rwkv_a2'][l][:, hs]); m["g2h"] = C_(p['rwkv_g2'][l][:, hs])
    m["w0t"] = rep(p['rwkv_w0'][l][hs]); m["a0t"] = rep(p['rwkv_a0'][l][hs]); m["kkt"] = rep(p['rwkv_k_k'][l][hs])
    m["kat"] = rep(p['rwkv_k_a'][l][hs]); m["rkt"] = rep(p['rwkv_r_k'][l][j]); m["lng"] = C_(p['rwkv_ln'][l][j][:, None].astype(np.float32))
    m["TriU"] = C_(TriU); m["SL"] = C_((idx[:, None] > idx[None, :]).astype(np.float32))
    m["SU"] = C_((idx[:, None] < idx[None, :]).astype(np.float32)); m["Ident"] = np.eye(128, dtype=np.float32)
    return m


_PROGS = {}


def _prog(name, fn):
    if name not in _PROGS:
        r = fn()
        _PROGS[name] = r[0] if isinstance(r, tuple) else r
    return _PROGS[name]


def _run(nc, maps):
    res = run_bass_kernel_spmd(nc, maps, core_ids=list(range(NCORES)))
    return res.results


def tile_w(W, nk=KC):
    ncols = W.shape[1]
    nt = (ncols + 127) // 128
    if nt * 128 != ncols:
        W = np.concatenate([W, np.zeros((W.shape[0], nt * 128 - ncols), np.float32)], 1)
    return C_(W.reshape(nk, 128, nt, 128).transpose(2, 1, 0, 3).reshape(nt, 128, nk * 128))


def tile_wd(W):
    return C_(W.reshape(FC // 4, 4, 128, KC, 128).transpose(0, 3, 2, 1, 4).reshape((FC // 4) * KC, 128, 4 * 128))


def _gl(v, n):
    return C_(np.asarray(v, np.float32).reshape(n, 128).T)


def kernel(**inp):
    p = {k: np.asarray(v, np.float32) for k, v in inp.items()}
    x = p['x'][0]
    meta = p['meta_tokens']
    depth = p['w_in'].shape[0]
    hT = [C_(np.concatenate([meta, x[c * 1024:(c + 1) * 1024]], 0).T) for c in range(NCORES)]
    nc_a = _prog("Ta", build_Ta)
    nc_b = _prog("Tb", build_Tb)
    nc_mla = _prog("mla", build_mla)
    nc_ssd = _prog("ssd", build_ssd)
    nc_ret = _prog("ret", build_ret)
    nc_rwkv = _prog("rwkv", build_rwkv)
    for l in range(depth):
        g1 = _gl(p['ffn1_norm'][l], KC); gm = _gl(p['mix_norm'][l], KC)
        wg1 = tile_w(p['ffn1_w_gate'][l]); wu1 = tile_w(p['ffn1_w_up'][l]); wd1 = tile_wd(p['ffn1_w_down'][l])
        win = tile_w(p['w_in'][l])
        res = _run(nc_a, [{"h_in": hT[c], "g1": g1, "wg": wg1, "wu": wu1,
                           "wd": wd1, "gm": gm, "win": win} for c in range(NCORES)])
        hT = [C_(res[c]['h_out']) for c in range(NCORES)]
        up = np.zeros((LP, INW), np.float32)
        up[PAD:PAD + 16] = res[0]['uT'][:, 0:16].T
        for c in range(NCORES):
            up[PAD + 16 + c * 1024:PAD + 16 + (c + 1) * 1024] = res[c]['uT'][:, 16:].T
        del res
        y = np.zeros((LP, D), np.float32)
        r = _run(nc_mla, [prep_mla(up, p, l, c) for c in range(NCORES)])
        for h in range(4):
            y[:, h * 128:(h + 1) * 128] = r[h]['yT'].T
        r = _run(nc_ssd, [prep_ssd(up, p, l, c) for c in range(NCORES)])
        for j in range(8):
            y[:, 512 + j * 64:512 + (j + 1) * 64] = r[j]['yT'].T
        r = _run(nc_ret, [prep_ret(up, p['ret_norm'][l], c) for c in range(NCORES)])
        for h in range(4):
            y[:, 1024 + h * 128:1024 + (h + 1) * 128] = r[h]['yT'].T
        r = _run(nc_rwkv, [prep_rwkv(up, p, l, c) for c in range(NCORES)])
        for j in range(8):
            y[:, 1536 + j * 64:1536 + (j + 1) * 64] = r[j]['yT'].T
        del r, up
        g2 = _gl(p['ffn2_norm'][l], KC); gs = _gl(p['ssd_norm'][l], 4)
        del wg1, wu1, wd1, win
        wg2 = tile_w(p['ffn2_w_gate'][l]); wu2 = tile_w(p['ffn2_w_up'][l]); wd2 = tile_wd(p['ffn2_w_down'][l])
        wo = tile_w(p['w_out'][l])
        maps = []
        for c in range(NCORES):
            yc = np.concatenate([y[PAD:PAD + 16], y[PAD + 16 + c * 1024:PAD + 16 + (c + 1) * 1024]], 0)
            maps.append({"h_in": hT[c], "yT": C_(yc.T), "gs": gs, "wout": wo, "g2": g2,
                         "wg": wg2, "wu": wu2, "wd": wd2})
        res = _run(nc_b, maps)
        hT = [C_(res[c]['h_out']) for c in range(NCORES)]
        del res, maps, y, wg2, wu2, wd2, wo
    out = np.concatenate([hT[c][:, 16:].T for c in range(NCORES)], 0)
    return C_(out[None].astype(np.float32))
```

```python
import numpy as np
import concourse.bass as bass
import concourse.mybir as mybir
from concourse.bass_utils import run_bass_kernel_spmd

F32 = mybir.dt.float32
BF16 = mybir.dt.bfloat16
AF = mybir.ActivationFunctionType
ALU = mybir.AluOpType
AX = mybir.AxisListType

NCORES = 8
D = 2048
KC = 16
DFF = 5632
FC = 44
NT = 1040
CGS = [(0, 16), (16, 528), (528, 1040)]
INW = 5320
EPS = 1e-6
EPOCH = 30000


class Buf:
    __slots__ = ("ap", "w", "r", "name", "dsem")

    def __init__(self, ap, name=""):
        self.ap = ap
        self.w = None
        self.r = {}
        self.name = name
        self.dsem = None

    def __getitem__(self, idx):
        return self.ap[idx]


class Sched:
    def __init__(self, nc):
        self.nc = nc
        self.engs = {"pe": nc.tensor, "dve": nc.vector, "act": nc.scalar,
                     "pool": nc.gpsimd, "sp": nc.sync}
        self.sems = []
        self.owner = {}
        self.cur = {}
        self.cnt = {}
        self.seen = {e: {} for e in self.engs}
        self.ninst = {e: 0 for e in self.engs}
        for e in ("pe", "dve", "act", "pool"):
            self._new_epoch(e)
        self.uid = 0

    def _alloc_sem(self, name, owner=None):
        h = self.nc.alloc_semaphore(name)
        self.sems.append(h)
        k = len(self.sems) - 1
        self.cnt[k] = 0
        self.owner[k] = owner
        return k

    def _new_epoch(self, e):
        self.cur[e] = self._alloc_sem("c_%s_%d" % (e, len(self.sems)), e)

    def new_dsem(self, name="d"):
        return self._alloc_sem("%s_%d" % (name, len(self.sems)))

    def _waits(self, e, reads, writes):
        need = {}
        for b in reads:
            if b.w is not None:
                k, v = b.w
                if need.get(k, 0) < v:
                    need[k] = v
        for b in writes:
            if b.w is not None:
                k, v = b.w
                if need.get(k, 0) < v:
                    need[k] = v
            for k, v in b.r.items():
                if need.get(k, 0) < v:
                    need[k] = v
        eng = self.engs[e]
        seen = self.seen[e]
        for k, v in need.items():
            if e == "pe" and self.owner[k] == "pe":
                continue
            if seen.get(k, 0) >= v:
                continue
            eng.wait_ge(self.sems[k], v)
            seen[k] = v

    def record(self):
        self._rec = []
        return self._rec

    def stop_record(self):
        self._rec = None

    def replay_weighted(self, lanes):
        self._rec = None
        n = max(len(l) for l in lanes) if lanes else 0
        pos = [0] * len(lanes)
        for i in range(n):
            for k, l in enumerate(lanes):
                tgt = ((i + 1) * len(l) + n - 1) // n
                while pos[k] < min(tgt, len(l)):
                    kind, a, kw = l[pos[k]]
                    (self.op if kind == "op" else self.dma)(*a, **kw)
                    pos[k] += 1

    def replay(self, lanes, skew=0):
        self._rec = None
        n = max(len(l) + k * skew for k, l in enumerate(lanes)) if lanes else 0
        for i in range(n):
            for k, l in enumerate(lanes):
                j = i - k * skew
                if 0 <= j < len(l):
                    kind, a, kw = l[j]
                    (self.op if kind == "op" else self.dma)(*a, **kw)

    def op(self, e, fn, reads=(), writes=(), inc=True):
        if getattr(self, "_rec", None) is not None:
            self._rec.append(("op", (e, fn, tuple(reads), tuple(writes), inc), {}))
            return None
        self._waits(e, reads, writes)
        ins = fn(self.engs[e])
        k = self.cur[e]
        self.ninst[e] += 1
        if inc:
            self.cnt[k] += 1
            ins.then_inc(self.sems[k], 1)
            tok = (k, self.cnt[k])
        else:
            tok = (k, self.cnt[k] + 1)
        for b in reads:
            if b.r.get(tok[0], 0) < tok[1]:
                b.r[tok[0]] = tok[1]
        for b in writes:
            b.w = tok
            b.r = {}
        if inc and self.cnt[k] >= EPOCH:
            self._new_epoch(e)
        return ins

    def dma(self, q, out_ap, in_ap, reads=(), writes=(), sem=None, is_out=False, **kw):
        if getattr(self, "_rec", None) is not None:
            kw2 = dict(kw); kw2.update(reads=tuple(reads), writes=tuple(writes), sem=sem, is_out=is_out)
            self._rec.append(("dma", (q, out_ap, in_ap), kw2))
            return None
        self._waits(q, reads, writes)
        if sem is None:
            for b in list(writes) + list(reads):
                if b.dsem is None:
                    b.dsem = self.new_dsem()
                sem = b.dsem
                break
        ins = self.engs[q].dma_start(out=out_ap, in_=in_ap, **kw)
        ins.then_inc(self.sems[sem], 16)
        self.cnt[sem] += 16
        self.ninst[q] += 1
        tok = (sem, self.cnt[sem])
        if is_out:
            if not hasattr(self, "out_sems"):
                self.out_sems = set()
            self.out_sems.add(sem)
        for b in reads:
            if b.r.get(tok[0], 0) < tok[1]:
                b.r[tok[0]] = tok[1]
        for b in writes:
            b.w = tok
            b.r = {}
        return sem

    def wait_all(self, e, semkeys):
        eng = self.engs[e]
        for k in semkeys:
            if self.cnt[k] > 0:
                eng.wait_ge(self.sems[k], self.cnt[k])

    def sb(self, shape, dt, name=None):
        self.uid += 1
        name = "%s_%d" % (name or "sb", self.uid)
        return Buf(self.nc.alloc_sbuf_tensor(name, list(shape), dt).ap(), name)

    def ps(self, shape, dt=F32, name=None):
        self.uid += 1
        name = "%s_%d" % (name or "ps", self.uid)
        return Buf(self.nc.alloc_psum_tensor(name, list(shape), dt).ap(), name)

    def sub(self, buf, ap):
        return Buf(ap, buf.name + "_s")


class Ring:
    def __init__(self, bufs):
        self.bufs = bufs
        self.i = 0

    def next(self):
        b = self.bufs[self.i % len(self.bufs)]
        self.i += 1
        return b


class TCtx:
    def __init__(self, S, n_ost=2):
        self.S = S
        nc = S.nc
        self.hT = [S.sb([128, NT], F32, "hT") for _ in range(KC)]
        self.xnT = [S.sb([128, NT], BF16, "xnT") for _ in range(KC)]
        self.actT = Ring([S.sb([128, NT], BF16, "actT") for _ in range(8)])
        self.wst = Ring([S.sb([128, KC, 128], F32, "wst") for _ in range(4)])
        self.wbf = Ring([S.sb([128, KC, 128], BF16, "wbf") for _ in range(6)])
        self.wdst = Ring([S.sb([128, 4, 128], F32, "wdst") for _ in range(3)])
        self.wdbf = Ring([S.sb([128, 4, 128], BF16, "wdbf") for _ in range(3)])
        self.tmp = Ring([S.sb([128, 512], F32, "tmp") for _ in range(3)])
        self.ost = Ring([S.sb([128, NT], F32, "ost") for _ in range(n_ost)])
        self.rstd = S.sb([128, NT], F32, "rstd")
        self.gcol = S.sb([128, KC], F32, "gcol")
        self.ones = S.sb([128, 128], F32, "ones")
        S.op("pool", lambda e: e.memset(self.ones[:], 1.0), writes=[self.ones])
        banks = [S.ps([128, 512], F32, "bank") for _ in range(8)]
        self.G = [banks[0], banks[1]]
        self.U = [banks[2], banks[3]]
        self.Dn = [banks[4], banks[5]]
        self.N = banks[6]
        self.M = banks[7]

    def pG(self, ci):
        return (self.M, self.M.ap[:, 0:16]) if ci == 0 else (self.G[ci - 1], self.G[ci - 1].ap[:, :])

    def pU(self, ci):
        return (self.M, self.M.ap[:, 16:32]) if ci == 0 else (self.U[ci - 1], self.U[ci - 1].ap[:, :])

    def pD(self, ci):
        return (self.M, self.M.ap[:, 32:48]) if ci == 0 else (self.Dn[ci - 1], self.Dn[ci - 1].ap[:, :])

    def pN(self, ci):
        return (self.M, self.M.ap[:, 48:64]) if ci == 0 else (self.N, self.N.ap[:, :])


def t_load_h(T, h_dram):
    S = T.S
    for kc in range(KC):
        S.dma("sp", T.hT[kc][:], h_dram[kc * 128:(kc + 1) * 128, :], writes=[T.hT[kc]])


def t_store_h(T, h_dram, sems):
    S = T.S
    for kc in range(KC):
        sems.append(S.dma("sp", h_dram[kc * 128:(kc + 1) * 128, :], T.hT[kc][:], reads=[T.hT[kc]]))


def t_rmsnorm(T, g_dram, src=None, nk=KC, dst=None, width=D):
    S = T.S
    src = src or T.hT
    dst = dst or T.xnT
    S.dma("sp", T.gcol[:, 0:nk], g_dram[:, 0:nk], writes=[T.gcol])
    for ci, (c0, c1) in enumerate(CGS):
        pb, pap = T.pN(ci)
        for kc in range(nk):
            t = T.tmp.next()
            S.op("act", lambda e, t=t, kc=kc: e.activation(out=t[:, 0:c1 - c0], in_=src[kc][:, c0:c1], func=AF.Square),
                 reads=[src[kc]], writes=[t])
            S.op("pe", lambda e, t=t, kc=kc: e.matmul(pap[:, 0:c1 - c0], lhsT=T.ones[:], rhs=t[:, 0:c1 - c0],
                                                       start=(kc == 0), stop=(kc == nk - 1)),
                 reads=[T.ones, t], writes=[pb], inc=True)
        S.op("act", lambda e: e.activation(out=T.rstd[:, c0:c1], in_=pap[:, 0:c1 - c0], func=AF.Sqrt,
                                           scale=1.0 / width, bias=EPS),
             writes=[T.rstd, pb])
    S.op("dve", lambda e: e.reciprocal(T.rstd[:], T.rstd[:]), reads=[T.rstd], writes=[T.rstd])
    for kc in range(nk):
        S.op("dve", lambda e, kc=kc: e.scalar_tensor_tensor(out=dst[kc][:], in0=src[kc][:], scalar=T.gcol[:, kc:kc + 1],
                                                            in1=T.rstd[:], op0=ALU.mult, op1=ALU.mult),
             reads=[src[kc], T.gcol, T.rstd], writes=[dst[kc]])


PROBE_NODMA = False
W_TWO_QUEUES = True
PROBE_NOMM = False


def t_load_w(T, w_dram, idx, ncol, nk=KC, cast_eng="act"):
    S = T.S
    if PROBE_NODMA and getattr(T, "_w0", None) is not None:
        return T._w0
    st = T.wst.next()
    bf = T.wbf.next()
    T._wq = getattr(T, "_wq", 0) + 1
    q = "sp" if (T._wq % 2 == 0 or not W_TWO_QUEUES) else "pool"
    S.dma(q, st[:, 0:nk, :], w_dram[idx].rearrange("p (kc f) -> p kc f", kc=nk), writes=[st])
    if cast_eng == "act":
        S.op("act", lambda e: e.activation(out=bf[:, 0:nk, :], in_=st[:, 0:nk, :], func=AF.Copy), reads=[st], writes=[bf])
    else:
        S.op(cast_eng, lambda e: e.tensor_copy(bf[:, 0:nk, :], st[:, 0:nk, :]), reads=[st], writes=[bf])
    T._w0 = bf
    return bf


def t_ffn(T, g_dram, wg, wu, wd):
    S = T.S
    t_rmsnorm(T, g_dram)
    SG = 4
    for sg in range(FC // SG):
        acts = []
        for fi in range(SG):
            fc = sg * SG + fi
            bg = t_load_w(T, wg, fc, 128)
            bu = t_load_w(T, wu, fc, 128)
            a = T.actT.next()
            acts.append(a)
            for ci, (c0, c1) in enumerate(CGS):
                n = c1 - c0
                gb, gap = T.pG(ci)
                ub, uap = T.pU(ci)
                for kc in range(KC):
                    S.op("pe", lambda e, kc=kc: e.matmul(gap[:, 0:n], lhsT=bg[:, kc, :], rhs=T.xnT[kc][:, c0:c1],
                                                          start=(kc == 0), stop=(kc == KC - 1)),
                         reads=[bg, T.xnT[kc]], writes=[gb], inc=(kc == KC - 1))
                for kc in range(KC):
                    S.op("pe", lambda e, kc=kc: e.matmul(uap[:, 0:n], lhsT=bu[:, kc, :], rhs=T.xnT[kc][:, c0:c1],
                                                          start=(kc == 0), stop=(kc == KC - 1)),
                         reads=[bu, T.xnT[kc]], writes=[ub], inc=(kc == KC - 1))
                t = T.tmp.next()
                S.op("act", lambda e, t=t: e.activation(out=t[:, 0:n], in_=gap[:, 0:n], func=AF.Silu),
                     writes=[t, gb])
                S.op("dve", lambda e, t=t: e.tensor_tensor(out=a[:, c0:c1], in0=t[:, 0:n], in1=uap[:, 0:n], op=ALU.mult),
                     reads=[t], writes=[a, ub])
        for dc in range(KC):
            st = T.wdst.next()
            bf = T.wdbf.next()
            S.dma("sp", st[:], wd[sg * KC + dc].rearrange("p (fi d) -> p fi d", fi=SG), writes=[st])
            S.op("pool", lambda e: e.tensor_copy(bf[:], st[:]), reads=[st], writes=[bf])
            for ci, (c0, c1) in enumerate(CGS):
                n = c1 - c0
                db, dap = T.pD(ci)
                for fi in range(SG):
                    S.op("pe", lambda e, fi=fi: e.matmul(dap[:, 0:n], lhsT=bf[:, fi, :], rhs=acts[fi][:, c0:c1],
                                                          start=(fi == 0), stop=(fi == SG - 1)),
                         reads=[bf, acts[fi]], writes=[db], inc=(fi == SG - 1))
                S.op("dve", lambda e: e.scalar_tensor_tensor(out=T.hT[dc][:, c0:c1], in0=dap[:, 0:n], scalar=0.5,
                                                            in1=T.hT[dc][:, c0:c1], op0=ALU.mult, op1=ALU.add),
                     writes=[T.hT[dc], db])


def t_proj(T, w_dram, ncols, rhsT, nk, sink):
    S = T.S
    nchunks = (ncols + 127) // 128
    for uc in range(nchunks):
        m = min(128, ncols - uc * 128)
        bw = t_load_w(T, w_dram, uc, m, nk=nk, cast_eng="pool")
        for ci, (c0, c1) in enumerate(CGS):
            n = c1 - c0
            gb, gap = T.pG(ci)
            for kc in range(nk):
                S.op("pe", lambda e, kc=kc: e.matmul(gap[0:m, 0:n], lhsT=bw[:, kc, 0:m], rhs=rhsT[kc][:, c0:c1],
                                                      start=(kc == 0), stop=(kc == nk - 1)),
                     reads=[bw, rhsT[kc]], writes=[gb], inc=(kc == nk - 1))
            sink(ci, c0, c1, uc, m, gb, gap)


def build_Ta():
    nc = bass.Bass("TRN2", target_bir_lowering=False)
    dt = lambda n, s, k: nc.dram_tensor(n, list(s), F32, kind=k).ap()
    h_in = dt("h_in", [D, NT], "ExternalInput")
    g1 = dt("g1", [128, KC], "ExternalInput")
    wg = dt("wg", [FC, 128, KC * 128], "ExternalInput")
    wu = dt("wu", [FC, 128, KC * 128], "ExternalInput")
    wd = dt("wd", [(FC // 4) * KC, 128, 4 * 128], "ExternalInput")
    gm = dt("gm", [128, KC], "ExternalInput")
    win = dt("win", [42, 128, KC * 128], "ExternalInput")
    h_out = dt("h_out", [D, NT], "ExternalOutput")
    uT = dt("uT", [INW, NT], "ExternalOutput")
    S = Sched(nc)
    T = TCtx(S)
    outs = []
    t_load_h(T, h_in)
    t_ffn(T, g1, wg, wu, wd)
    t_store_h(T, h_out, outs)
    t_rmsnorm(T, gm)
    cur = {}

    def sink(ci, c0, c1, uc, m, pb, pap):
        if ci == 0:
            cur["o"] = T.ost.next()
        o = cur["o"]
        n = c1 - c0
        S.op("act", lambda e: e.activation(out=o[0:m, c0:c1], in_=pap[0:m, 0:n], func=AF.Copy), writes=[o, pb])
        if ci == len(CGS) - 1:
            outs.append(S.dma("act", uT[uc * 128:uc * 128 + m, :], o[0:m, :], reads=[o]))

    t_proj(T, win, INW, T.xnT, KC, sink)
    S.wait_all("sp", sorted(set(outs)))
    return nc, S


def build_Tb():
    nc = bass.Bass("TRN2", target_bir_lowering=False)
    dt = lambda n, s, k: nc.dram_tensor(n, list(s), F32, kind=k).ap()
    h_in = dt("h_in", [D, NT], "ExternalInput")
    yT = dt("yT", [D, NT], "ExternalInput")
    gs = dt("gs", [128, 4], "ExternalInput")
    wout = dt("wout", [KC, 128, KC * 128], "ExternalInput")
    g2 = dt("g2", [128, KC], "ExternalInput")
    wg = dt("wg", [FC, 128, KC * 128], "ExternalInput")
    wu = dt("wu", [FC, 128, KC * 128], "ExternalInput")
    wd = dt("wd", [(FC // 4) * KC, 128, 4 * 128], "ExternalInput")
    h_out = dt("h_out", [D, NT], "ExternalOutput")
    S = Sched(nc)
    T = TCtx(S, n_ost=0)
    outs = []
    t_load_h(T, h_in)
    yst = [S.sb([128, NT], F32, "yst") for _ in range(4)]
    for j in range(4):
        S.dma("sp", yst[j][:], yT[(4 + j) * 128:(5 + j) * 128, :], writes=[yst[j]])
    t_rmsnorm(T, gs, src=yst, nk=4, dst=T.xnT[4:8], width=512)
    for kc in list(range(0, 4)) + list(range(8, 16)):
        st = yst[kc % 4]
        S.dma("sp", st[:], yT[kc * 128:(kc + 1) * 128, :], writes=[st])
        S.op("pool", lambda e, kc=kc, st=st: e.tensor_copy(T.xnT[kc][:], st[:]), reads=[st], writes=[T.xnT[kc]])

    def sink(ci, c0, c1, dc, m, pb, pap):
        n = c1 - c0
        S.op("dve", lambda e: e.tensor_tensor(out=T.hT[dc][:, c0:c1], in0=pap[:, 0:n], in1=T.hT[dc][:, c0:c1], op=ALU.add),
             writes=[T.hT[dc], pb])

    t_proj(T, wout, D, T.xnT, KC, sink)
    t_ffn(T, g2, wg, wu, wd)
    t_store_h(T, h_out, outs)
    S.wait_all("sp", sorted(set(outs)))
    return nc, S


LP = 8320
NCH = 65


class MProg:
    def __init__(self):
        self.nc = bass.Bass("TRN2", target_bir_lowering=False)
        self.S = Sched(self.nc)
        self.outs = []
        self.banks = [self.S.ps([128, 512], F32, "bank") for _ in range(8)]

    def din(self, name, shape):
        return self.nc.dram_tensor(name, list(shape), F32, kind="ExternalInput").ap()

    def dout(self, name, shape):
        return self.nc.dram_tensor(name, list(shape), F32, kind="ExternalOutput").ap()

    def load(self, name, shape, dt=F32, dram=None):
        d = dram if dram is not None else self.din(name, shape)
        b = self.S.sb(shape, F32, name)
        if len(shape) == 2 and shape[1] > 2048:
            step = 2080
            for c0 in range(0, shape[1], step):
                c1 = min(shape[1], c0 + step)
                self.S.dma("sp", b[:, c0:c1], d[:, c0:c1], writes=[b])
        else:
            self.S.dma("sp", b[:], d, writes=[b])
        return b

    def store(self, dram_ap, buf, ap):
        self.S.dma("pool", dram_ap, ap, reads=[buf], is_out=True)

    def finish(self):
        self.S.wait_all("sp", sorted(getattr(self.S, "out_sems", set())))
        return self.nc


class Stream:
    def __init__(self, P, name, rows, total, width, nbuf=2):
        self.P = P
        self.d = P.din(name, [rows, total])
        self.rows = rows
        self.ring = Ring([P.S.sb([rows, width], F32, name) for _ in range(nbuf)])

    def get(self, c0, w):
        b = self.ring.next()
        self.P.S.dma("sp", b[:, 0:w], self.d[:, c0:c0 + w], writes=[b])
        return b


def build_ret():
    P = MProg()
    S = P.S
    B = P.banks
    qT_s = Stream(P, "qT", 64, LP, 512); qsT_s = Stream(P, "qsT", 64, LP, 512)
    kT_s = Stream(P, "kT", 64, LP, 512); ksT_s = Stream(P, "ksT", 64, LP, 512)
    cosT_s = Stream(P, "cosT", 64, LP, 512); sinT_s = Stream(P, "sinT", 64, LP, 512)
    ktok_s = Stream(P, "k_tok", 128, NCH * 64, 256); kstok_s = Stream(P, "ks_tok", 128, NCH * 64, 256)
    costok_s = Stream(P, "cos_tok", 128, NCH * 64, 256); sintok_s = Stream(P, "sin_tok", 128, NCH * 64, 256)
    vtok_s = Stream(P, "v_tok", 128, NCH * 128, 512)
    gT_s = Stream(P, "gT", 128, LP, 512)
    qdec = P.load("qdecT", [64, 512])
    kdec = P.load("kdec", [128, 1])
    dmatT = P.load("dmatT", [128, 128])
    cdec = P.load("cdec", [64, 1])
    gcol = P.load("gcol", [128, 1])
    yT_d = P.dout("yT", [128, LP])
    tmp64 = S.sb([64, 512], F32, "tmp64")
    qd = S.sb([64, 512], F32, "qd")

    def mul(o, a, b_, w, rows=64):
        S.op("dve", lambda e: e.tensor_tensor(out=o[0:rows, 0:w], in0=a[0:rows, 0:w], in1=b_[0:rows, 0:w], op=ALU.mult),
             reads=[a, b_], writes=[o])

    def add(o, a, b_, w, rows=64):
        S.op("dve", lambda e: e.tensor_tensor(out=o[0:rows, 0:w], in0=a[0:rows, 0:w], in1=b_[0:rows, 0:w], op=ALU.add),
             reads=[a, b_], writes=[o])
    ones = S.sb([128, 128], F32, "ones")
    S.op("pool", lambda e: e.memset(ones[:], 1.0 / 128), writes=[ones])
    Sst = [S.sb([64, 128], F32, "Sst") for _ in range(2)]
    S.op("pool", lambda e: e.memset(Sst[0][:], 0.0), writes=[Sst[0]])
    atm = Ring([S.sb([128, 128], F32, "atm") for _ in range(2)])
    ybuf = Ring([S.sb([128, 512], F32, "ybuf") for _ in range(2)])
    yc = S.sb([128, 512], F32, "yc"); sq = S.sb([128, 512], F32, "sq"); rs = S.sb([128, 512], F32, "rs")
    AT = Ring([B[0], B[1]]); YT = Ring([B[2], B[3]]); SP_ = B[4]; LN1 = B[5]; LN2 = B[6]
    blocks = [(0, 1)] + [(1 + 4 * i, 4) for i in range(16)]
    for (cb, ncb) in blocks:
        yb = ybuf.next()
        W = ncb * 128
        p0 = cb * 128
        qT = qT_s.get(p0, W); qsT = qsT_s.get(p0, W); kT = kT_s.get(p0, W); ksT = ksT_s.get(p0, W)
        cosT = cosT_s.get(p0, W); sinT = sinT_s.get(p0, W)
        k_tok = ktok_s.get(cb * 64, ncb * 64); ks_tok = kstok_s.get(cb * 64, ncb * 64)
        cos_tok = costok_s.get(cb * 64, ncb * 64); sin_tok = sintok_s.get(cb * 64, ncb * 64)
        v_tok = vtok_s.get(p0, W)
        mul(qT, qT, cosT, W); mul(tmp64, qsT, sinT, W); add(qT, qT, tmp64, W)
        mul(kT, kT, cosT, W); mul(tmp64, ksT, sinT, W); add(kT, kT, tmp64, W)
        S.op("dve", lambda e: e.tensor_scalar(out=kT[:, 0:W], in0=kT[:, 0:W], scalar1=float(64 ** -0.5), scalar2=None,
                                              op0=ALU.mult), reads=[kT], writes=[kT])
        mul(qd, qT, qdec, W)
        w2 = ncb * 64
        mul(k_tok, k_tok, cos_tok, w2, 128); mul(ks_tok, ks_tok, sin_tok, w2, 128); add(k_tok, k_tok, ks_tok, w2, 128)
        S.op("dve", lambda e: e.tensor_scalar(out=k_tok[:, 0:w2], in0=k_tok[:, 0:w2], scalar1=kdec[:, 0:1], scalar2=None,
                                              op0=ALU.mult), reads=[k_tok, kdec], writes=[k_tok])
        for ci in range(ncb):
            c = cb + ci
            sl = slice(ci * 128, (ci + 1) * 128)
            Sp = Sst[c % 2]; Sn = Sst[(c + 1) % 2]
            at = AT.next(); yt = YT.next(); am = atm.next()
            S.op("pe", lambda e: e.matmul(at[:, 0:128], lhsT=kT[:, sl], rhs=qT[:, sl], start=True, stop=True),
                 reads=[kT, qT], writes=[at])
            S.op("dve", lambda e: e.tensor_tensor(out=am[:], in0=at[:, 0:128], in1=dmatT[:], op=ALU.mult),
                 reads=[dmatT], writes=[am, at])
            S.op("pe", lambda e: e.matmul(yt[:, 0:128], lhsT=v_tok[:, sl], rhs=am[:], start=True, stop=False),
                 reads=[v_tok, am], writes=[yt], inc=False)
            S.op("pe", lambda e: e.matmul(yt[:, 0:128], lhsT=Sp[:], rhs=qd[:, sl], start=False, stop=True),
                 reads=[Sp, qd], writes=[yt])
            S.op("act", lambda e: e.activation(out=yb[:, sl], in_=yt[:, 0:128], func=AF.Copy),
                 writes=[yb, yt])
            S.op("pe", lambda e: e.matmul(SP_[0:64, 0:128], lhsT=k_tok[:, ci * 64:(ci + 1) * 64], rhs=v_tok[:, sl],
                                          start=True, stop=True), reads=[k_tok, v_tok], writes=[SP_])
            S.op("dve", lambda e: e.scalar_tensor_tensor(out=Sn[:], in0=Sp[:], scalar=cdec[:, 0:1], in1=SP_[0:64, 0:128],
                                                         op0=ALU.mult, op1=ALU.add),
                 reads=[Sp, cdec], writes=[Sn, SP_])
        gb = gT_s.get(p0, W)
        S.op("act", lambda e: e.activation(out=gb[:, 0:W], in_=gb[:, 0:W], func=AF.Silu), reads=[gb], writes=[gb])
        S.op("pe", lambda e: e.matmul(LN1[:, 0:W], lhsT=ones[:], rhs=yb[:, 0:W], start=True, stop=True),
             reads=[ones, yb], writes=[LN1])
        S.op("dve", lambda e: e.tensor_tensor(out=yc[:, 0:W], in0=yb[:, 0:W], in1=LN1[:, 0:W], op=ALU.subtract),
             reads=[yb], writes=[yc, LN1])
        S.op("act", lambda e: e.activation(out=sq[:, 0:W], in_=yc[:, 0:W], func=AF.Square), reads=[yc], writes=[sq])
        S.op("pe", lambda e: e.matmul(LN2[:, 0:W], lhsT=ones[:], rhs=sq[:, 0:W], start=True, stop=True),
             reads=[ones, sq], writes=[LN2])
        S.op("act", lambda e: e.activation(out=rs[:, 0:W], in_=LN2[:, 0:W], func=AF.Sqrt, scale=1.0, bias=EPS),
             writes=[rs, LN2])
        S.op("dve", lambda e: e.reciprocal(rs[:, 0:W], rs[:, 0:W]), reads=[rs], writes=[rs])
        S.op("dve", lambda e: e.scalar_tensor_tensor(out=yc[:, 0:W], in0=yc[:, 0:W], scalar=gcol[:, 0:1], in1=rs[:, 0:W],
                                                     op0=ALU.mult, op1=ALU.mult), reads=[yc, gcol, rs], writes=[yc])
        S.op("dve", lambda e: e.tensor_tensor(out=gb[:, 0:W], in0=yc[:, 0:W], in1=gb[:, 0:W], op=ALU.mult),
             reads=[yc, gb], writes=[gb])
        P.store(yT_d[:, p0:p0 + W], gb, gb[:, 0:W])
    return P.finish()


PAD = 112
C_ = np.ascontiguousarray


def tok_layout(a):
    F_ = a.shape[1]
    return C_(a.reshape(NCH, 128, F_).transpose(1, 0, 2).reshape(128, NCH * F_))


def rope_tables(dim):
    half = dim // 2
    inv = (10000.0 ** (-np.arange(half, dtype=np.float32) / half)).astype(np.float32)
    pos = (np.arange(LP, dtype=np.float32) - PAD).astype(np.float32)
    ang = (pos[:, None] * inv[None, :]).astype(np.float32)
    cos = np.cos(ang).astype(np.float32)
    sin = np.sin(ang).astype(np.float32)
    cos2 = np.concatenate([cos, cos], 1)
    sin2 = np.concatenate([-sin, sin], 1)
    return cos2, sin2


def swap_halves(a):
    h = a.shape[1] // 2
    return np.concatenate([a[:, h:], a[:, :h]], 1)


def prep_ret(up, ret_norm_l, core):
    h = core % 4
    o = 576 + 1544
    q = up[:, o + h * 64:o + (h + 1) * 64]
    k = up[:, o + 256 + h * 64:o + 256 + (h + 1) * 64]
    v = up[:, o + 512 + h * 128:o + 512 + (h + 1) * 128]
    g = up[:, o + 1024 + h * 128:o + 1024 + (h + 1) * 128]
    cos2, sin2 = rope_tables(64)
    log_g = np.log(np.float32(1.0) - np.float32(2.0) ** np.float32(-5.0 - h)).astype(np.float32)
    idx = np.arange(128, dtype=np.float32)
    qdec = np.exp((idx + 1.0) * log_g).astype(np.float32)
    kdec = (np.exp((127.0 - idx) * log_g) * (64.0 ** -0.5)).astype(np.float32)
    diff = idx[None, :] - idx[:, None]
    dmatT = np.where(diff >= 0, np.exp(diff * log_g), 0.0).astype(np.float32)
    return {
        "qT": C_(q.T), "qsT": C_(swap_halves(q).T), "kT": C_(k.T), "ksT": C_(swap_halves(k).T),
        "cosT": C_(cos2.T), "sinT": C_(sin2.T),
        "k_tok": tok_layout(k), "ks_tok": tok_layout(swap_halves(k)),
        "cos_tok": tok_layout(cos2), "sin_tok": tok_layout(sin2),
        "v_tok": tok_layout(v), "gT": C_(g.T),
        "qdecT": C_(np.tile(np.tile(qdec, 4)[None, :], (64, 1))),
        "kdec": C_(kdec[:, None]), "dmatT": C_(dmatT),
        "cdec": np.full((64, 1), np.exp(128.0 * log_g), np.float32),
        "gcol": C_(ret_norm_l[h][:, None].astype(np.float32)),
    }


def build_ssd():
    P = MProg()
    S = P.S
    B = P.banks
    xsT_s = Stream(P, "xsT_pad", 64, LP + 3, 515); BT_s = Stream(P, "BT_pad", 128, LP + 3, 515)
    CT_s = Stream(P, "CT_pad", 128, LP + 3, 515); zT_s = Stream(P, "zT", 64, LP, 512)
    xtap = [Stream(P, "xs_tok%d" % j, 128, NCH * 64, 256) for j in range(4)]
    btap = [Stream(P, "B_tok%d" % j, 128, NCH * 128, 512) for j in range(4)]
    cw_xs = P.load("cw_xs", [64, 4]); cb_xs = P.load("cb_xs", [64, 1])
    cw_B = P.load("cw_B", [128, 4]); cb_B = P.load("cb_B", [128, 1])
    cw_C = P.load("cw_C", [128, 4]); cb_C = P.load("cb_C", [128, 1])
    cwt_xs = P.load("cwt_xs", [128, 4 * 256]); cbt_xs = P.load("cbt_xs", [128, 256])
    cwt_B = P.load("cwt_B", [128, 4 * 512]); cbt_B = P.load("cbt_B", [128, 512])
    dt = P.load("dt_tok", [128, NCH]); dtb = P.load("dtb", [128, 1]); alog = P.load("alog", [128, 1])
    valid = P.load("valid_tok", [128, NCH]); dcol = P.load("dcol", [64, 1])
    TriU = P.load("TriU", [128, 128]); UTs = P.load("UTs", [128, 128])
    yT_d = P.dout("yT", [64, LP])
    onesF = S.sb([128, 128], F32, "onesF")
    S.op("pool", lambda e: e.memset(onesF[:], 1.0), writes=[onesF])
    S.op("act", lambda e: e.activation(out=dt[:], in_=dt[:], func=AF.Exp, bias=dtb[:, 0:1]), reads=[dt, dtb], writes=[dt])
    S.op("act", lambda e: e.activation(out=dt[:], in_=dt[:], func=AF.Ln, bias=1.0), reads=[dt], writes=[dt])
    S.op("dve", lambda e: e.tensor_tensor(out=dt[:], in0=dt[:], in1=valid[:], op=ALU.mult), reads=[dt, valid], writes=[dt])
    S.op("act", lambda e: e.activation(out=alog[:], in_=alog[:], func=AF.Exp), reads=[alog], writes=[alog])
    la = S.sb([128, NCH], F32, "la"); cs = S.sb([128, NCH], F32, "cs"); dte = S.sb([128, NCH], F32, "dte")
    S.op("dve", lambda e: e.tensor_scalar(out=la[:], in0=dt[:], scalar1=alog[:, 0:1], scalar2=-1.0, op0=ALU.mult,
                                          op1=ALU.mult), reads=[dt, alog], writes=[la])
    S.op("pe", lambda e: e.matmul(B[5][:, 0:NCH], lhsT=TriU[:], rhs=la[:], start=True, stop=True),
         reads=[TriU, la], writes=[B[5]])
    S.op("act", lambda e: e.activation(out=cs[:], in_=B[5][:, 0:NCH], func=AF.Copy), writes=[cs, B[5]])
    S.op("pe", lambda e: e.matmul(B[6][:, 0:NCH], lhsT=onesF[:], rhs=la[:], start=True, stop=True),
         reads=[onesF, la], writes=[B[6]])
    S.op("dve", lambda e: e.tensor_tensor(out=dte[:], in0=B[6][:, 0:NCH], in1=cs[:], op=ALU.subtract),
         reads=[cs], writes=[dte, B[6]])
    S.op("act", lambda e: e.activation(out=dte[:], in_=dte[:], func=AF.Exp), reads=[dte], writes=[dte])

    def conv_fm(src, rows, W, cw, cb, dst):
        S.op("dve", lambda e: e.tensor_scalar(out=dst[0:rows, 0:W], in0=src[0:rows, 0:W], scalar1=cw[:, 0:1], scalar2=None,
                                              op0=ALU.mult), reads=[src, cw], writes=[dst])
        for j in range(1, 4):
            S.op("dve", lambda e, j=j: e.scalar_tensor_tensor(out=dst[0:rows, 0:W], in0=src[0:rows, j:W + j],
                                                              scalar=cw[:, j:j + 1], in1=dst[0:rows, 0:W],
                                                              op0=ALU.mult, op1=ALU.add), reads=[src, cw, dst], writes=[dst])
        S.op("act", lambda e: e.activation(out=dst[0:rows, 0:W], in_=dst[0:rows, 0:W], func=AF.Silu, bias=cb[:, 0:1]),
             reads=[dst, cb], writes=[dst])

    def conv_tm(taps, w, cwt, cbt, full, dst, tmp):
        for j in range(4):
            o = dst if j == 0 else tmp
            S.op("dve", lambda e, j=j, o=o: e.tensor_tensor(out=o[:, 0:w], in0=taps[j][:, 0:w],
                                                            in1=cwt[:, j * full:j * full + w], op=ALU.mult),
                 reads=[taps[j], cwt], writes=[o])
            if j > 0:
                S.op("dve", lambda e: e.tensor_tensor(out=dst[:, 0:w], in0=dst[:, 0:w], in1=tmp[:, 0:w], op=ALU.add),
                     reads=[dst, tmp], writes=[dst])
        S.op("dve", lambda e: e.tensor_tensor(out=dst[:, 0:w], in0=dst[:, 0:w], in1=cbt[:, 0:w], op=ALU.add),
             reads=[dst, cbt], writes=[dst])
        S.op("act", lambda e: e.activation(out=dst[:, 0:w], in_=dst[:, 0:w], func=AF.Silu), reads=[dst], writes=[dst])

    BTc = Ring([S.sb([128, 512], F32, "BTc") for _ in range(2)])
    CTc = Ring([S.sb([128, 512], F32, "CTc") for _ in range(2)])
    xsTc = Ring([S.sb([64, 512], F32, "xsTc") for _ in range(2)])
    xtok = Ring([S.sb([128, 256], F32, "xtok") for _ in range(2)])
    btok = Ring([S.sb([128, 512], F32, "btok") for _ in range(2)])
    tmpx = S.sb([128, 256], F32, "tmpx"); tmpb = S.sb([128, 512], F32, "tmpb")
    lam = Ring([S.sb([128, 128], F32, "lam") for _ in range(2)])
    laf = Ring([S.sb([128, 128], F32, "laf") for _ in range(2)])
    LT = Ring([S.sb([128, 128], F32, "LT") for _ in range(2)])
    Er = Ring([S.sb([128, 128], F32, "Er") for _ in range(2)])
    CsT = Ring([S.sb([128, 128], F32, "CsT") for _ in range(2)])
    xdt = Ring([S.sb([128, 64], F32, "xdt") for _ in range(2)])
    bd = Ring([S.sb([128, 128], F32, "bd") for _ in range(2)])
    ybuf = Ring([S.sb([64, 512], F32, "ybuf") for _ in range(2)])
    Sst = [S.sb([128, 64], F32, "Sst") for _ in range(2)]
    S.op("pool", lambda e: e.memset(Sst[0][:], 0.0), writes=[Sst[0]])
    GT = B[0]; SEG = B[1]; CSR = B[2]; YT = B[3]; SC = B[4]
    blocks = [(0, 1)] + [(1 + 4 * i, 4) for i in range(16)]
    for (cb, ncb) in blocks:
        W = ncb * 128
        p0 = cb * 128
        xs_in = xsT_s.get(p0, W + 3); B_in = BT_s.get(p0, W + 3); C_in = CT_s.get(p0, W + 3); zT = zT_s.get(p0, W)
        xt = [xtap[j].get(cb * 64, ncb * 64) for j in range(4)]
        bt = [btap[j].get(cb * 128, ncb * 128) for j in range(4)]
        BT = BTc.next(); CT = CTc.next(); xsT = xsTc.next(); xk = xtok.next(); bk = btok.next(); yb = ybuf.next()
        conv_fm(B_in, 128, W, cw_B, cb_B, BT)
        conv_fm(C_in, 128, W, cw_C, cb_C, CT)
        conv_fm(xs_in, 64, W, cw_xs, cb_xs, xsT)
        conv_tm(xt, ncb * 64, cwt_xs, cbt_xs, 256, xk, tmpx)
        conv_tm(bt, ncb * 128, cwt_B, cbt_B, 512, bk, tmpb)
        S.op("act", lambda e: e.activation(out=zT[:, 0:W], in_=zT[:, 0:W], func=AF.Silu), reads=[zT], writes=[zT])
        for ci in range(ncb):
            c = cb + ci
            sl = slice(ci * 128, (ci + 1) * 128)
            Sp = Sst[c % 2]; Sn = Sst[(c + 1) % 2]
            lm = lam.next(); lf = laf.next(); lt = LT.next(); er = Er.next(); cst = CsT.next(); xd = xdt.next(); bdd = bd.next()
            lac = la[:, c:c + 1]
            S.op("pe", lambda e: e.matmul(GT[:, 0:128], lhsT=BT[:, sl], rhs=CT[:, sl], start=True, stop=True),
                 reads=[BT, CT], writes=[GT])
            S.op("dve", lambda e: e.tensor_scalar(out=lm[:], in0=UTs[:], scalar1=lac, scalar2=None, op0=ALU.mult),
                 reads=[UTs, la], writes=[lm])
            S.op("dve", lambda e: e.tensor_scalar(out=lf[:], in0=onesF[:], scalar1=lac, scalar2=None, op0=ALU.mult),
                 reads=[onesF, la], writes=[lf])
            S.op("pe", lambda e: e.matmul(SEG[:, 0:128], lhsT=lm[:], rhs=TriU[:], start=True, stop=True),
                 reads=[lm, TriU], writes=[SEG])
            S.op("pe", lambda e: e.matmul(CSR[:, 0:128], lhsT=lf[:], rhs=TriU[:], start=True, stop=True),
                 reads=[lf, TriU], writes=[CSR])
            S.op("act", lambda e: e.activation(out=lt[:], in_=SEG[:, 0:128], func=AF.Exp), writes=[lt, SEG])
            S.op("dve", lambda e: e.tensor_tensor(out=lt[:], in0=lt[:], in1=TriU[:], op=ALU.mult), reads=[lt, TriU], writes=[lt])
            S.op("dve", lambda e: e.tensor_tensor(out=lt[:], in0=lt[:], in1=GT[:, 0:128], op=ALU.mult), reads=[lt], writes=[lt, GT])
            S.op("act", lambda e: e.activation(out=er[:], in_=CSR[:, 0:128], func=AF.Exp), writes=[er, CSR])
            S.op("dve", lambda e: e.tensor_tensor(out=cst[:], in0=CT[:, sl], in1=er[:], op=ALU.mult), reads=[CT, er], writes=[cst])
            S.op("dve", lambda e: e.tensor_scalar(out=xd[:], in0=xk[:, ci * 64:(ci + 1) * 64], scalar1=dt[:, c:c + 1],
                                                  scalar2=None, op0=ALU.mult), reads=[xk, dt], writes=[xd])
            S.op("dve", lambda e: e.tensor_scalar(out=bdd[:], in0=bk[:, sl], scalar1=dte[:, c:c + 1], scalar2=None,
                                                  op0=ALU.mult), reads=[bk, dte], writes=[bdd])
            S.op("pe", lambda e: e.matmul(YT[0:64, 0:128], lhsT=xd[:], rhs=lt[:], start=True, stop=False),
                 reads=[xd, lt], writes=[YT], inc=False)
            S.op("pe", lambda e: e.matmul(YT[0:64, 0:128], lhsT=Sp[:], rhs=cst[:], start=False, stop=True),
                 reads=[Sp, cst], writes=[YT])
            S.op("dve", lambda e: e.scalar_tensor_tensor(out=yb[:, sl], in0=xsT[:, sl], scalar=dcol[:, 0:1],
                                                         in1=YT[0:64, 0:128], op0=ALU.mult, op1=ALU.add),
                 reads=[xsT, dcol], writes=[yb, YT])
            S.op("pe", lambda e: e.matmul(SC[:, 0:64], lhsT=bdd[:], rhs=xd[:], start=True, stop=True),
                 reads=[bdd, xd], writes=[SC])
            S.op("dve", lambda e: e.scalar_tensor_tensor(out=Sn[:], in0=Sp[:], scalar=er[:, 127:128], in1=SC[:, 0:64],
                                                         op0=ALU.mult, op1=ALU.add), reads=[Sp, er], writes=[Sn, SC])
        S.op("dve", lambda e: e.tensor_tensor(out=yb[:, 0:W], in0=yb[:, 0:W], in1=zT[:, 0:W], op=ALU.mult),
             reads=[yb, zT], writes=[yb])
        P.store(yT_d[:, p0:p0 + W], yb, yb[:, 0:W])
    return P.finish()


def prep_ssd(up, p, l, core):
    j = core
    g = j // 4
    o = 576
    z = up[:, o + j * 64:o + (j + 1) * 64]
    xo = o + 512
    xs = up[:, xo + j * 64:xo + (j + 1) * 64]
    Bm = up[:, xo + 512 + g * 128:xo + 512 + (g + 1) * 128]
    Cm = up[:, xo + 768 + g * 128:xo + 768 + (g + 1) * 128]
    dtr = up[:, xo + 1024 + j:xo + 1024 + j + 1]
    cw = p['ssd_conv_w'][l]
    cb = p['ssd_conv_b'][l]
    ch_xs = slice(j * 64, (j + 1) * 64)
    ch_B = slice(512 + g * 128, 512 + (g + 1) * 128)
    ch_C = slice(768 + g * 128, 768 + (g + 1) * 128)
    pad3 = lambda a: np.concatenate([np.zeros((3, a.shape[1]), np.float32), a], 0)
    xs_p = pad3(xs); B_p = pad3(Bm); C_p = pad3(Cm)
    valid = np.ones((LP, 1), np.float32); valid[:PAD] = 0
    idx = np.arange(128)
    TriU = (idx[:, None] <= idx[None, :]).astype(np.float32)
    m = {
        "xsT_pad": C_(xs_p.T), "BT_pad": C_(B_p.T), "CT_pad": C_(C_p.T), "zT": C_(z.T),
        "cw_xs": C_(cw[:, ch_xs].T), "cb_xs": C_(cb[ch_xs][:, None]),
        "cw_B": C_(cw[:, ch_B].T), "cb_B": C_(cb[ch_B][:, None]),
        "cw_C": C_(cw[:, ch_C].T), "cb_C": C_(cb[ch_C][:, None]),
        "cwt_xs": C_(np.tile(np.tile(cw[:, ch_xs], (1, 4)).reshape(1, 4 * 256), (128, 1))),
        "cbt_xs": C_(np.tile(np.tile(cb[ch_xs], 4)[None, :], (128, 1))),
        "cwt_B": C_(np.tile(np.tile(cw[:, ch_B], (1, 4)).reshape(1, 4 * 512), (128, 1))),
        "cbt_B": C_(np.tile(np.tile(cb[ch_B], 4)[None, :], (128, 1))),
        "dt_tok": tok_layout(dtr), "dtb": np.full((128, 1), p['ssd_dt_bias'][l][j], np.float32),
        "alog": np.full((128, 1), p['ssd_a_log'][l][j], np.float32),
        "valid_tok": tok_layout(valid), "dcol": np.full((64, 1), p['ssd_d'][l][j], np.float32),
        "TriU": C_(TriU), "UTs": C_(1.0 - TriU),
    }
    for t in range(4):
        m["xs_tok%d" % t] = tok_layout(xs_p[t:t + LP])
        m["B_tok%d" % t] = tok_layout(B_p[t:t + LP])
    return m


def build_mla():
    P = MProg()
    S = P.S
    B = P.banks
    cq_s = [Stream(P, "cqT%d" % i, 128, LP, 512) for i in range(3)]
    ckv_s = Stream(P, "ckvT", 128, LP, 512)
    kpe_s = Stream(P, "kpeT", 64, LP, 512); kpes_s = Stream(P, "kpesT", 64, LP, 512)
    cos_s = Stream(P, "cosT", 64, LP, 512); sin_s = Stream(P, "sinT", 64, LP, 512)
    yT_d = P.dout("yT", [128, LP])

    def loadbf(name, shape):
        f = P.load(name, shape)
        b = S.sb(shape, BF16, name + "b")
        S.op("dve", lambda e: e.tensor_copy(b[:], f[:]), reads=[f], writes=[b])
        return b
    wqn = loadbf("wq_n", [128, 3 * 128]); wqr = loadbf("wq_r", [128, 3 * 64]); wqrs = loadbf("wq_rs", [128, 3 * 64])
    wk = loadbf("wk", [128, 128]); wv = loadbf("wv", [128, 128])
    ones0b = loadbf("ones0", [128, 128])
    gqn = P.load("gqn", [128, 3]); gkv = P.load("gkv", [128, 1])
    gq_n = P.load("gq_n", [128, 1]); gq_r = P.load("gq_r", [64, 1]); gq_rs = P.load("gq_rs", [64, 1])
    gk_n = P.load("gk_n", [128, 1]); gk_r = P.load("gk_r", [64, 1]); gk_rs = P.load("gk_rs", [64, 1])
    mblk = P.load("mblk", [128, 4 * 512])
    onesF = S.sb([128, 128], F32, "onesF"); onesb = S.sb([128, 128], BF16, "onesb")
    S.op("pool", lambda e: e.memset(onesF[:], 1.0), writes=[onesF])
    S.op("pool", lambda e: e.memset(onesb[:], 1.0), writes=[onesb])
    KnT = S.sb([128, LP], BF16, "KnT"); KrT = S.sb([64, LP], BF16, "KrT"); Vt = S.sb([128, LP], BF16, "Vt")
    sqa = Ring([S.sb([128, 512], F32, "sqa") for _ in range(2)])
    rstd = S.sb([128, 512], F32, "rstd"); rstdq = S.sb([128, 512], F32, "rstdq"); rstdk = S.sb([128, 512], F32, "rstdk")
    cqn = [S.sb([128, 512], BF16, "cqn") for _ in range(3)]
    ckvn = S.sb([128, 512], BF16, "ckvn")
    qn_fs = [S.sb([128, 512], BF16, "qn_f") for _ in range(2)]; qr_fs = [S.sb([64, 512], BF16, "qr_f") for _ in range(2)]
    t1 = S.sb([64, 512], F32, "t1"); t2 = S.sb([64, 512], F32, "t2")
    PTr = Ring([S.sb([128, 512], BF16, "PT") for _ in range(3)])
    dacc = S.sb([128, 512], F32, "dacc"); vcol = P.load("vcol", [128, 1])
    rden = S.sb([128, 512], F32, "rden"); yo = Ring([S.sb([128, 512], F32, "yo") for _ in range(2)])
    STb = Ring([B[0], B[1]]); OB = B[2]; DB = B[3]; NB = B[4]; Q1 = B[5]; Q2 = B[6]; Q3 = B[7]
    SCALE = float(192 ** -0.5)

    def rms(parts, W, width, dst, post_scale=1.0):
        n = len(parts)
        for i, (buf, ap, rows, is_ps) in enumerate(parts):
            sq = sqa.next()
            if is_ps:
                S.op("act", lambda e, sq=sq, rows=rows, ap=ap: e.activation(out=sq[0:rows, 0:W], in_=ap, func=AF.Square), writes=[sq, buf])
            else:
                S.op("act", lambda e, sq=sq, rows=rows, ap=ap: e.activation(out=sq[0:rows, 0:W], in_=ap, func=AF.Square), reads=[buf], writes=[sq])
            S.op("pe", lambda e, sq=sq, rows=rows, i=i: e.matmul(NB[:, 0:W], lhsT=onesF[0:rows, :], rhs=sq[0:rows, 0:W], start=(i == 0),
                                                            stop=(i == n - 1)), reads=[onesF, sq], writes=[NB])
        S.op("act", lambda e: e.activation(out=dst[:, 0:W], in_=NB[:, 0:W], func=AF.Sqrt, scale=1.0 / width, bias=EPS),
             writes=[dst, NB])
        S.op("dve", lambda e: e.reciprocal(dst[:, 0:W], dst[:, 0:W]), reads=[dst], writes=[dst])
        if post_scale != 1.0:
            S.op("dve", lambda e: e.tensor_scalar(out=dst[:, 0:W], in0=dst[:, 0:W], scalar1=post_scale, scalar2=None,
                                                  op0=ALU.mult), reads=[dst], writes=[dst])

    blocks = [(0, 1)] + [(1 + 4 * i, 4) for i in range(16)]
    Kn_b = [Buf(KnT.ap[:, cb * 128:(cb + ncb) * 128], "Kn") for (cb, ncb) in blocks]
    Kr_b = [Buf(KrT.ap[:, cb * 128:(cb + ncb) * 128], "Kr") for (cb, ncb) in blocks]
    V_b = [Buf(Vt.ap[:, cb * 128:(cb + ncb) * 128], "V") for (cb, ncb) in blocks]
    blk_of = {}
    for bi_, (cb_, ncb_) in enumerate(blocks):
        for c_ in range(cb_, cb_ + ncb_):
            blk_of[c_] = (bi_, (c_ - cb_) * 128)

    def prep(bi):
        cb, ncb = blocks[bi]
        W = ncb * 128
        p0 = cb * 128
        qn_f = qn_fs[bi % 2]; qr_f = qr_fs[bi % 2]
        KnB = Kn_b[bi]; KrB = Kr_b[bi]; VB = V_b[bi]
        cq = [s.get(p0, W) for s in cq_s]
        ckv = ckv_s.get(p0, W); kpe = kpe_s.get(p0, W); kpes = kpes_s.get(p0, W)
        cosb = cos_s.get(p0, W); sinb = sin_s.get(p0, W)
        rms([(cq[i], cq[i][:, 0:W], 128, False) for i in range(3)], W, 384.0, rstd)
        for i in range(3):
            S.op("dve", lambda e, i=i: e.scalar_tensor_tensor(out=cqn[i][:, 0:W], in0=cq[i][:, 0:W], scalar=gqn[:, i:i + 1],
                                                              in1=rstd[:, 0:W], op0=ALU.mult, op1=ALU.mult),
                 reads=[cq[i], gqn, rstd], writes=[cqn[i]])
        for (pb, wt, m) in ((Q1, wqn, 128), (Q2, wqr, 64), (Q3, wqrs, 64)):
            for i in range(3):
                S.op("pe", lambda e, i=i, pb=pb, wt=wt, m=m: e.matmul(pb[0:m, 0:W], lhsT=wt[:, i * m:(i + 1) * m], rhs=cqn[i][:, 0:W],
                                                                      start=(i == 0), stop=(i == 2)), reads=[wt, cqn[i]], writes=[pb],
                     inc=(i == 2))
        rms([(Q1, Q1[:, 0:W], 128, True), (Q2, Q2[0:64, 0:W], 64, True)], W, 192.0, rstdq, SCALE)
        S.op("dve", lambda e: e.scalar_tensor_tensor(out=qn_f[:, 0:W], in0=Q1[:, 0:W], scalar=gq_n[:, 0:1], in1=rstdq[:, 0:W],
                                                     op0=ALU.mult, op1=ALU.mult), reads=[gq_n, rstdq], writes=[qn_f, Q1])
        S.op("dve", lambda e: e.scalar_tensor_tensor(out=t1[:, 0:W], in0=Q2[0:64, 0:W], scalar=gq_r[:, 0:1], in1=cosb[:, 0:W],
                                                     op0=ALU.mult, op1=ALU.mult), reads=[gq_r, cosb], writes=[t1, Q2])
        S.op("dve", lambda e: e.scalar_tensor_tensor(out=t2[:, 0:W], in0=Q3[0:64, 0:W], scalar=gq_rs[:, 0:1], in1=sinb[:, 0:W],
                                                     op0=ALU.mult, op1=ALU.mult), reads=[gq_rs, sinb], writes=[t2, Q3])
        S.op("dve", lambda e: e.tensor_tensor(out=t1[:, 0:W], in0=t1[:, 0:W], in1=t2[:, 0:W], op=ALU.add), reads=[t1, t2], writes=[t1])
        S.op("dve", lambda e: e.tensor_tensor(out=qr_f[:, 0:W], in0=t1[:, 0:W], in1=rstdq[0:64, 0:W], op=ALU.mult),
             reads=[t1, rstdq], writes=[qr_f])
        rms([(ckv, ckv[:, 0:W], 128, False)], W, 128.0, rstd)
        S.op("dve", lambda e: e.scalar_tensor_tensor(out=ckvn[:, 0:W], in0=ckv[:, 0:W], scalar=gkv[:, 0:1], in1=rstd[:, 0:W],
                                                     op0=ALU.mult, op1=ALU.mult), reads=[ckv, gkv, rstd], writes=[ckvn])
        S.op("pe", lambda e: e.matmul(Q1[:, 0:W], lhsT=wk[:], rhs=ckvn[:, 0:W], start=True, stop=True),
             reads=[wk, ckvn], writes=[Q1])
        for ci in range(ncb):
            c = cb + ci
            S.op("pe", lambda e, ci=ci: e.matmul(Q3[:, 0:128], lhsT=ckvn[:, ci * 128:(ci + 1) * 128], rhs=wv[:], start=True, stop=True),
                 reads=[ckvn, wv], writes=[Q3])
            S.op("act", lambda e, ci=ci: e.activation(out=VB[:, ci * 128:(ci + 1) * 128], in_=Q3[:, 0:128], func=AF.Copy),
                 writes=[VB, Q3])
        rms([(Q1, Q1[:, 0:W], 128, True), (kpe, kpe[:, 0:W], 64, False)], W, 192.0, rstdk)
        S.op("dve", lambda e: e.scalar_tensor_tensor(out=KnB[:, 0:W], in0=Q1[:, 0:W], scalar=gk_n[:, 0:1], in1=rstdk[:, 0:W],
                                                     op0=ALU.mult, op1=ALU.mult), reads=[gk_n, rstdk], writes=[KnB, Q1])
        S.op("dve", lambda e: e.scalar_tensor_tensor(out=t1[:, 0:W], in0=kpe[:, 0:W], scalar=gk_r[:, 0:1], in1=cosb[:, 0:W],
                                                     op0=ALU.mult, op1=ALU.mult), reads=[kpe, gk_r, cosb], writes=[t1])
        S.op("dve", lambda e: e.scalar_tensor_tensor(out=t2[:, 0:W], in0=kpes[:, 0:W], scalar=gk_rs[:, 0:1], in1=sinb[:, 0:W],
                                                     op0=ALU.mult, op1=ALU.mult), reads=[kpes, gk_rs, sinb], writes=[t2])
        S.op("dve", lambda e: e.tensor_tensor(out=t1[:, 0:W], in0=t1[:, 0:W], in1=t2[:, 0:W], op=ALU.add), reads=[t1, t2], writes=[t1])
        S.op("dve", lambda e: e.tensor_tensor(out=KrB[:, 0:W], in0=t1[:, 0:W], in1=rstdk[0:64, 0:W], op=ALU.mult),
             reads=[t1, rstdk], writes=[KrB])

    def attn(bi):
        cb, ncb = blocks[bi]
        W = ncb * 128
        p0 = cb * 128
        qn_f = qn_fs[bi % 2]; qr_f = qr_fs[bi % 2]
        nk = cb + ncb
        def scores(kc):
            kb, ko = blk_of[kc]
            ks = slice(ko, ko + 128)
            st = STb.next(); pt = PTr.next()
            S.op("pe", lambda e: e.matmul(st[:, 0:W], lhsT=Kn_b[kb][:, ks], rhs=qn_f[:, 0:W], start=True, stop=False),
                 reads=[Kn_b[kb], qn_f], writes=[st], inc=False)
            S.op("pe", lambda e: e.matmul(st[:, 0:W], lhsT=Kr_b[kb][:, ks], rhs=qr_f[:, 0:W], start=False, stop=True),
                 reads=[Kr_b[kb], qr_f], writes=[st])
            S.op("act", lambda e: e.activation(out=pt[:, 0:W], in_=st[:, 0:W], func=AF.Exp), writes=[pt, st])
            if kc >= cb:
                k_ = kc - cb
                S.op("pool", lambda e: e.tensor_tensor(out=pt[:, 0:W], in0=pt[:, 0:W], in1=mblk[:, k_ * 512:k_ * 512 + W],
                                                       op=ALU.mult), reads=[pt, mblk], writes=[pt])
            return pt

        def pv(kc, pt):
            kb, ko = blk_of[kc]
            ks = slice(ko, ko + 128)
            S.op("pe", lambda e: e.matmul(OB[:, 0:W], lhsT=V_b[kb][:, ks], rhs=pt[:, 0:W], start=(kc == 0), stop=(kc == nk - 1)),
                 reads=[V_b[kb], pt], writes=[OB], inc=False)
            S.op("pe", lambda e: e.matmul(DB[:, 0:W], lhsT=(ones0b if kc == 0 else onesb)[:], rhs=pt[:, 0:W],
                                          start=(kc == 0), stop=(kc == nk - 1)), reads=[ones0b, onesb, pt], writes=[DB])
        pend = scores(0)
        for kc in range(nk):
            nxt = scores(kc + 1) if kc + 1 < nk else None
            pv(kc, pend)
            pend = nxt
        y = yo.next()
        S.op("dve", lambda e: e.tensor_scalar(out=rden[:, 0:W], in0=DB[:, 0:W], scalar1=1e-30, scalar2=None, op0=ALU.max),
             writes=[rden, DB])
        S.op("dve", lambda e: e.reciprocal(rden[:, 0:W], rden[:, 0:W]), reads=[rden], writes=[rden])
        S.op("dve", lambda e: e.tensor_tensor(out=y[:, 0:W], in0=OB[:, 0:W], in1=rden[:, 0:W], op=ALU.mult),
             reads=[rden], writes=[y, OB])
        P.store(yT_d[:, p0:p0 + W], y, y[:, 0:W])

    prep(0)
    for bi in range(len(blocks)):
        la = S.record(); attn(bi); S.stop_record()
        if bi + 1 < len(blocks):
            lp = S.record(); prep(bi + 1); S.stop_record()
            S.replay_weighted([la, lp])
        else:
            S.replay_weighted([la])
    return P.finish()


def prep_mla(up, p, l, core):
    h = core % 4
    cq = up[:, 0:384]; ckv = up[:, 384:512]; kpe = up[:, 512:576]
    cos2, sin2 = rope_tables(64)
    wq = p['mla_w_q_up'][l][:, h * 192:(h + 1) * 192]
    wkv = p['mla_w_kv_up'][l][:, h * 256:(h + 1) * 256]
    kcl = lambda w: C_(w.reshape(3, 128, w.shape[1]).transpose(1, 0, 2).reshape(128, 3 * w.shape[1]))
    gq = p['mla_qk_norm_q'][l]; gk = p['mla_qk_norm_k'][l]
    col = lambda v: C_(v[:, None].astype(np.float32))
    idx = np.arange(128)
    tri = (idx[:, None] <= idx[None, :]).astype(np.float32)
    mblk = np.zeros((4, 128, 4, 128), np.float32)
    for k in range(4):
        for qi in range(4):
            if qi > k:
                mblk[k, :, qi, :] = 1.0
            elif qi == k:
                mblk[k, :, qi, :] = tri
    mblk = mblk.reshape(4, 128, 512).transpose(1, 0, 2).reshape(128, 2048)
    ones0 = np.ones((128, 128), np.float32); ones0[:PAD] = 0
    return {
        "cqT0": C_(cq[:, 0:128].T), "cqT1": C_(cq[:, 128:256].T), "cqT2": C_(cq[:, 256:384].T),
        "ckvT": C_(ckv.T), "kpeT": C_(kpe.T), "kpesT": C_(swap_halves(kpe).T),
        "cosT": C_(cos2.T), "sinT": C_(sin2.T),
        "wq_n": kcl(wq[:, 0:128]), "wq_r": kcl(wq[:, 128:192]), "wq_rs": kcl(swap_halves(wq[:, 128:192])),
        "wk": C_(wkv[:, 0:128]), "wv": C_(wkv[:, 128:256]), "ones0": ones0,
        "gqn": C_(p['mla_q_norm'][l].reshape(3, 128).T), "gkv": col(p['mla_kv_norm'][l]),
        "gq_n": col(gq[0:128]), "gq_r": col(gq[128:192]), "gq_rs": col(swap_halves(gq[None, 128:192])[0]),
        "gk_n": col(gk[0:128]), "gk_r": col(gk[128:192]), "gk_rs": col(swap_halves(gk[None, 128:192])[0]),
        "mblk": C_(mblk), "vcol": C_(ones0[:, 0:1]),
    }


RWKV_FP32R = False


def build_rwkv():
    P = MProg()
    S = P.S
    B = P.banks
    st = {}
    for nm, rows, tot, w in (("r", 128, NCH * 64, 256), ("k", 128, NCH * 64, 256), ("v", 128, NCH * 64, 256)):
        st[nm] = Stream(P, nm + "_tok", rows, tot, w)
        st[nm + "p"] = Stream(P, nm + "p_tok", rows, tot, w)
    for nm, rows in (("wd", 32), ("ad", 32), ("gd", 64)):
        st[nm] = Stream(P, nm + "T", rows, LP, 512)
        st[nm + "p"] = Stream(P, nm + "pT", rows, LP, 512)
    mu_r = P.load("mu_r", [128, 256]); mu_k = P.load("mu_k", [128, 256]); mu_v = P.load("mu_v", [128, 256])
    mu_wd = P.load("mu_wd", [32, 1]); mu_ad = P.load("mu_ad", [32, 1]); mu_gd = P.load("mu_gd", [64, 1])
    w2h = P.load("w2h", [32, 64]); a2h = P.load("a2h", [32, 64]); g2h = P.load("g2h", [64, 64])
    w0t = P.load("w0t", [128, 64]); a0t = P.load("a0t", [128, 64]); kkt = P.load("kkt", [128, 64])
    kat = P.load("kat", [128, 64]); rkt = P.load("rkt", [128, 64]); lng = P.load("lng", [64, 1])
    TriU = P.load("TriU", [128, 128]); SL = P.load("SL", [128, 128]); SU = P.load("SU", [128, 128])
    Id = P.load("Ident", [128, 128])
    yT_d = P.dout("yT", [64, LP])
    ones64 = S.sb([64, 64], F32, "ones64")
    S.op("pool", lambda e: e.memset(ones64[:], 1.0 / 64), writes=[ones64])
    NX = 6
    X = [S.sb([64, 64], F32, "X") for _ in range(NX)]
    S.op("pool", lambda e: e.memset(X[0][:], 0.0), writes=[X[0]])

    def T(shape, name):
        return S.sb(shape, F32, name)
    blkbufs = []
    for _ in range(2):
        blkbufs.append(dict(rs=T([128, 256], "rs"), ks=T([128, 256], "ks"), vs=T([128, 256], "vs"),
                            tw=T([32, 512], "tw"), ads=T([32, 512], "ads"), sg=T([64, 512], "sg"),
                            gT=T([64, 512], "gT"), yb=T([64, 512], "yblk")))
    dtm = T([128, 256], "dtm"); d32 = T([64, 512], "d32")
    names = ["ld", "a", "kk", "kkn", "kmod", "bb", "cs_e", "Eg", "Eneg", "Ege", "Kt", "Bh", "Kh", "Rt", "t3", "Vs", "SA", "U_"]
    lanes_buf = []
    for ln in range(4):
        L = dict(c64={n: T([128, 64], n) for n in names},
                 col={n: T([128, 1], n) for n in ["ss", "rn", "sbon"]},
                 fm={n: T([64, 128], n) for n in ["KtT", "BhT", "KhT", "RtT", "WT", "bonT", "oT", "oc", "osq", "ors"]},
                 sq={n: T([128, 128], n) for n in ["Pa", "PTa", "Pb", "PTb", "A", "MakT", "MrbT", "MrkT"]},
                 gam=T([64, 1], "gam"), Xg=T([64, 64], "Xg"), a=B[2 * ln], b=B[2 * ln + 1])
        lanes_buf.append(L)

    F32R = mybir.dt.float32r

    def rr(ap):
        return ap.bitcast(F32R) if RWKV_FP32R else ap

    def mm(pb, pap, lhsT, rhs, reads, start=True, stop=True, inc=True, fast=False):
        if fast and RWKV_FP32R:
            lhsT = lhsT.bitcast(F32R); rhs = rhs.bitcast(F32R)
        S.op("pe", lambda e: e.matmul(pap, lhsT=lhsT, rhs=rhs, start=start, stop=stop), reads=reads, writes=[pb], inc=inc)

    def tt(o, oap, a, aap, b_, bap, op, extra_w=()):
        S.op("dve", lambda e: e.tensor_tensor(out=oap, in0=aap, in1=bap, op=op), reads=[a, b_], writes=[o] + list(extra_w))

    def chunk(L, bb, ci, c):
        D_ = L["c64"]; col = L["col"]; fm = L["fm"]; sq = L["sq"]; gam = L["gam"]; Xg = L["Xg"]
        Ga = L["a"]; Gb = L["b"]
        rs_, ks_, vs_, tw, ads, gT, yb = bb["rs"], bb["ks"], bb["vs"], bb["tw"], bb["ads"], bb["gT"], bb["yb"]
        s64 = slice(ci * 64, (ci + 1) * 64)
        sl = slice(ci * 128, (ci + 1) * 128)
        r_ap, k_ap, v_ap = rs_[:, s64], ks_[:, s64], vs_[:, s64]
        mm(Ga, Ga[:, 0:64], tw[:, sl], w2h[:], [tw, w2h])
        tt(D_["ld"], D_["ld"][:], w0t, w0t[:], w0t, Ga[:, 0:64], ALU.add, extra_w=[Ga])
        S.op("act", lambda e: e.activation(out=D_["ld"][:], in_=D_["ld"][:], func=AF.Sigmoid), reads=[D_["ld"]], writes=[D_["ld"]])
        S.op("dve", lambda e: e.tensor_scalar(out=D_["ld"][:], in0=D_["ld"][:], scalar1=float(-np.exp(-0.5)), scalar2=None,
                                              op0=ALU.mult), reads=[D_["ld"]], writes=[D_["ld"]])
        mm(Ga, Ga[:, 64:128], ads[:, sl], a2h[:], [ads, a2h])
        tt(D_["a"], D_["a"][:], a0t, a0t[:], a0t, Ga[:, 64:128], ALU.add, extra_w=[Ga])
        S.op("act", lambda e: e.activation(out=D_["a"][:], in_=D_["a"][:], func=AF.Sigmoid), reads=[D_["a"]], writes=[D_["a"]])
        tt(D_["kk"], D_["kk"][:], ks_, k_ap, kkt, kkt[:], ALU.mult)
        S.op("act", lambda e: e.activation(out=D_["t3"][:], in_=D_["kk"][:], func=AF.Square, accum_out=col["ss"][:, 0:1]),
             reads=[D_["kk"]], writes=[D_["t3"], col["ss"]])
        S.op("act", lambda e: e.activation(out=col["rn"][:], in_=col["ss"][:], func=AF.Sqrt), reads=[col["ss"]], writes=[col["rn"]])
        S.op("dve", lambda e: e.tensor_scalar(out=col["rn"][:], in0=col["rn"][:], scalar1=1e-12, scalar2=None, op0=ALU.max),
             reads=[col["rn"]], writes=[col["rn"]])
        S.op("dve", lambda e: e.reciprocal(col["rn"][:], col["rn"][:]), reads=[col["rn"]], writes=[col["rn"]])
        S.op("dve", lambda e: e.tensor_scalar(out=D_["kkn"][:], in0=D_["kk"][:], scalar1=col["rn"][:, 0:1], scalar2=None,
                                              op0=ALU.mult), reads=[D_["kk"], col["rn"]], writes=[D_["kkn"]])
        S.op("dve", lambda e: e.scalar_tensor_tensor(out=D_["kmod"][:], in0=D_["a"][:], scalar=-1.0, in1=kat[:],
                                                     op0=ALU.add, op1=ALU.mult), reads=[D_["a"], kat], writes=[D_["kmod"]])
        S.op("dve", lambda e: e.scalar_tensor_tensor(out=D_["kmod"][:], in0=D_["kmod"][:], scalar=1.0, in1=k_ap,
                                                     op0=ALU.add, op1=ALU.mult), reads=[D_["kmod"], ks_], writes=[D_["kmod"]])
        tt(D_["bb"], D_["bb"][:], D_["kkn"], D_["kkn"][:], D_["a"], D_["a"][:], ALU.mult)
        tt(D_["t3"], D_["t3"][:], rs_, r_ap, rkt, rkt[:], ALU.mult)
        S.op("dve", lambda e: e.scalar_tensor_tensor(out=D_["t3"][:], in0=D_["t3"][:], scalar=1.0, in1=D_["kmod"][:],
                                                     op0=ALU.mult, op1=ALU.mult, accum_out=col["sbon"][:, 0:1]),
             reads=[D_["t3"], D_["kmod"]], writes=[D_["t3"], col["sbon"]])
        S.op("dve", lambda e: e.tensor_scalar(out=D_["Vs"][:], in0=v_ap, scalar1=col["sbon"][:, 0:1], scalar2=None,
                                              op0=ALU.mult), reads=[vs_, col["sbon"]], writes=[D_["Vs"]])
        mm(Ga, Ga[:, 128:192], TriU[:], D_["ld"][:], [TriU, D_["ld"]])
        mm(Ga, Ga[0:64, 192:193], D_["ld"][:], TriU[:, 127:128], [TriU, D_["ld"]])
        S.op("act", lambda e: e.activation(out=D_["Eg"][:], in_=Ga[:, 128:192], func=AF.Exp), writes=[D_["Eg"], Ga])
        S.op("act", lambda e: e.activation(out=D_["Eneg"][:], in_=Ga[:, 128:192], func=AF.Exp, scale=-1.0), writes=[D_["Eneg"], Ga])
        S.op("act", lambda e: e.activation(out=gam[:], in_=Ga[0:64, 192:193], func=AF.Exp), writes=[gam, Ga])
        tt(D_["cs_e"], D_["cs_e"][:], D_["ld"], Ga[:, 128:192], D_["ld"], D_["ld"][:], ALU.subtract, extra_w=[Ga])
        S.op("act", lambda e: e.activation(out=D_["Ege"][:], in_=D_["cs_e"][:], func=AF.Exp), reads=[D_["cs_e"]], writes=[D_["Ege"]])
        tt(D_["Kt"], D_["Kt"][:], D_["kkn"], D_["kkn"][:], D_["Ege"], D_["Ege"][:], ALU.mult)
        tt(D_["Bh"], D_["Bh"][:], D_["bb"], D_["bb"][:], D_["Eneg"], D_["Eneg"][:], ALU.mult)
        tt(D_["Kh"], D_["Kh"][:], D_["kmod"], D_["kmod"][:], D_["Eneg"], D_["Eneg"][:], ALU.mult)
        tt(D_["Rt"], D_["Rt"][:], rs_, r_ap, D_["Eg"], D_["Eg"][:], ALU.mult)
        for i, src in enumerate(("Kt", "Bh", "Kh", "Rt")):
            mm(Gb, Gb[0:64, i * 128:(i + 1) * 128], D_[src][:], Id[:], [D_[src], Id])
        for i, dst in enumerate(("KtT", "BhT", "KhT", "RtT")):
            S.op("act", lambda e, i=i, dst=dst: e.activation(out=fm[dst][:], in_=Gb[0:64, i * 128:(i + 1) * 128], func=AF.Copy),
                 writes=[fm[dst], Gb])
        mm(Ga, Ga[:, 0:128], fm["BhT"][:], fm["KtT"][:], [fm["BhT"], fm["KtT"]])
        mm(Ga, Ga[:, 128:256], fm["KtT"][:], fm["BhT"][:], [fm["BhT"], fm["KtT"]])
        mm(Ga, Ga[:, 256:384], fm["KhT"][:], fm["KtT"][:], [fm["KhT"], fm["KtT"]])
        mm(Gb, Gb[:, 0:128], fm["BhT"][:], fm["RtT"][:], [fm["BhT"], fm["RtT"]])
        mm(Gb, Gb[:, 128:256], fm["KhT"][:], fm["RtT"][:], [fm["KhT"], fm["RtT"]])
        mm(Gb, Gb[0:64, 256:384], D_["Vs"][:], Id[:], [D_["Vs"], Id])
        S.op("dve", lambda e: e.scalar_tensor_tensor(out=rr(sq["Pa"][:]), in0=Ga[:, 0:128], scalar=-1.0, in1=SU[:],
                                                     op0=ALU.mult, op1=ALU.mult), reads=[SU], writes=[sq["Pa"], Ga])
        S.op("dve", lambda e: e.scalar_tensor_tensor(out=rr(sq["PTa"][:]), in0=Ga[:, 128:256], scalar=-1.0, in1=SL[:],
                                                     op0=ALU.mult, op1=ALU.mult), reads=[SL], writes=[sq["PTa"], Ga])
        tt(sq["MakT"], sq["MakT"][:], SU, Ga[:, 256:384], SU, SU[:], ALU.mult, extra_w=[Ga])
        tt(sq["MrbT"], sq["MrbT"][:], TriU, Gb[:, 0:128], TriU, TriU[:], ALU.mult, extra_w=[Gb])
        tt(sq["MrkT"], sq["MrkT"][:], TriU, Gb[:, 128:256], TriU, TriU[:], ALU.mult, extra_w=[Gb])
        S.op("act", lambda e: e.activation(out=fm["bonT"][:], in_=Gb[0:64, 256:384], func=AF.Copy), writes=[fm["bonT"], Gb])
        tt(sq["A"], rr(sq["A"][:]), Id, Id[:], sq["Pa"], sq["Pa"][:], ALU.add)
        Pc, PTc, Pn, PTn = "Pa", "PTa", "Pb", "PTb"
        for lvl in range(6):
            G = Ga if lvl % 2 == 0 else Gb
            mm(G, G[:, 0:128], sq[PTc][:], sq[Pc][:], [sq[PTc], sq[Pc]], fast=True)
            mm(G, G[:, 128:256], sq[Pc][:], sq[PTc][:], [sq[PTc], sq[Pc]], fast=True)
            S.op("act", lambda e, Pn=Pn, G=G: e.activation(out=rr(sq[Pn][:]), in_=G[:, 0:128], func=AF.Copy), writes=[sq[Pn], G])
            S.op("act", lambda e, PTn=PTn, G=G: e.activation(out=rr(sq[PTn][:]), in_=G[:, 128:256], func=AF.Copy), writes=[sq[PTn], G])
            mm(G, G[:, 256:384], sq[PTn][:], sq["A"][:], [sq[PTn], sq["A"]], fast=True)
            tt(sq["A"], rr(sq["A"][:]), sq["A"], sq["A"][:], sq["A"], G[:, 256:384], ALU.add, extra_w=[G])
            Pc, PTc, Pn, PTn = Pn, PTn, Pc, PTc
        mm(Gb, Gb[:, 0:64], sq["MakT"][:], v_ap, [sq["MakT"], vs_])
        S.op("act", lambda e: e.activation(out=D_["t3"][:], in_=Gb[:, 0:64], func=AF.Copy), writes=[D_["t3"], Gb])
        mm(Gb, Gb[:, 64:128], sq["A"][:], D_["t3"][:], [sq["A"], D_["t3"]])
        S.op("act", lambda e: e.activation(out=D_["U_"][:], in_=Gb[:, 64:128], func=AF.Copy, scale=-1.0), writes=[D_["U_"], Gb])
        mm(Gb, Gb[0:64, 128:256], D_["Kt"][:], sq["A"][:], [D_["Kt"], sq["A"]])
        S.op("act", lambda e: e.activation(out=fm["WT"][:], in_=Gb[0:64, 128:256], func=AF.Copy), writes=[fm["WT"], Gb])
        Xp = X[c % NX]; Xn = X[(c + 1) % NX]
        mm(Ga, Ga[:, 0:64], fm["WT"][:], Xp[:], [fm["WT"], Xp])
        tt(D_["SA"], D_["SA"][:], D_["U_"], D_["U_"][:], D_["U_"], Ga[:, 0:64], ALU.subtract, extra_w=[Ga])
        S.op("dve", lambda e: e.tensor_scalar(out=Xg[:], in0=Xp[:], scalar1=gam[:, 0:1], scalar2=None, op0=ALU.mult),
             reads=[Xp, gam], writes=[Xg])
        mm(Ga, Ga[0:64, 64:128], D_["Kh"][:], v_ap, [D_["Kh"], vs_], start=True, stop=False, inc=False)
        mm(Ga, Ga[0:64, 64:128], D_["Bh"][:], D_["SA"][:], [D_["Bh"], D_["SA"]], start=False, stop=True)
        S.op("dve", lambda e: e.scalar_tensor_tensor(out=Xn[:], in0=Ga[0:64, 64:128], scalar=gam[:, 0:1], in1=Xg[:],
                                                     op0=ALU.mult, op1=ALU.add), reads=[gam, Xg], writes=[Xn, Ga])
        mm(Gb, Gb[0:64, 0:128], Xp[:], fm["RtT"][:], [Xp, fm["RtT"]], start=True, stop=False, inc=False)
        mm(Gb, Gb[0:64, 0:128], D_["SA"][:], sq["MrbT"][:], [D_["SA"], sq["MrbT"]], start=False, stop=False, inc=False)
        mm(Gb, Gb[0:64, 0:128], v_ap, sq["MrkT"][:], [vs_, sq["MrkT"]], start=False, stop=True)
        S.op("act", lambda e: e.activation(out=fm["oT"][:], in_=Gb[0:64, 0:128], func=AF.Copy), writes=[fm["oT"], Gb])
        mm(Gb, Gb[0:64, 128:256], ones64[:], fm["oT"][:], [ones64, fm["oT"]])
        tt(fm["oc"], fm["oc"][:], fm["oT"], fm["oT"][:], fm["oT"], Gb[0:64, 128:256], ALU.subtract, extra_w=[Gb])
        S.op("act", lambda e: e.activation(out=fm["osq"][:], in_=fm["oc"][:], func=AF.Square), reads=[fm["oc"]], writes=[fm["osq"]])
        mm(Gb, Gb[0:64, 256:384], ones64[:], fm["osq"][:], [ones64, fm["osq"]])
        S.op("act", lambda e: e.activation(out=fm["ors"][:], in_=Gb[0:64, 256:384], func=AF.Sqrt, bias=64e-5), writes=[fm["ors"], Gb])
        S.op("dve", lambda e: e.reciprocal(fm["ors"][:], fm["ors"][:]), reads=[fm["ors"]], writes=[fm["ors"]])
        S.op("dve", lambda e: e.scalar_tensor_tensor(out=fm["oc"][:], in0=fm["oc"][:], scalar=lng[:, 0:1], in1=fm["ors"][:],
                                                     op0=ALU.mult, op1=ALU.mult), reads=[fm["oc"], lng, fm["ors"]], writes=[fm["oc"]])
        tt(fm["oc"], fm["oc"][:], fm["oc"], fm["oc"][:], fm["bonT"], fm["bonT"][:], ALU.add)
        tt(fm["oc"], fm["oc"][:], fm["oc"], fm["oc"][:], gT, gT[:, sl], ALU.mult)
        S.op("pool", lambda e: e.tensor_copy(yb[:, sl], fm["oc"][:]), reads=[fm["oc"]], writes=[yb])

    blocks = [(0, 1)] + [(1 + 4 * i, 4) for i in range(16)]
    for bi, (cb, ncb) in enumerate(blocks):
        W = ncb * 128
        w2 = ncb * 64
        p0 = cb * 128
        bb = blkbufs[bi % 2]
        for nm, dst, mu in (("r", bb["rs"], mu_r), ("k", bb["ks"], mu_k), ("v", bb["vs"], mu_v)):
            x = st[nm].get(cb * 64, w2); xp = st[nm + "p"].get(cb * 64, w2)
            tt(dtm, dtm[:, 0:w2], xp, xp[:, 0:w2], x, x[:, 0:w2], ALU.subtract)
            tt(dtm, dtm[:, 0:w2], dtm, dtm[:, 0:w2], mu, mu[:, 0:w2], ALU.mult)
            tt(dst, dst[:, 0:w2], dtm, dtm[:, 0:w2], x, x[:, 0:w2], ALU.add)
        for nm, dst, mu, rows, fn in (("wd", bb["tw"], mu_wd, 32, AF.Tanh), ("ad", bb["ads"], mu_ad, 32, None),
                                      ("gd", bb["sg"], mu_gd, 64, AF.Sigmoid)):
            x = st[nm].get(p0, W); xp = st[nm + "p"].get(p0, W)
            tt(d32, d32[0:rows, 0:W], xp, xp[:, 0:W], x, x[:, 0:W], ALU.subtract)
            S.op("dve", lambda e, dst=dst, mu=mu, x=x, rows=rows: e.scalar_tensor_tensor(
                out=dst[:, 0:W], in0=d32[0:rows, 0:W], scalar=mu[:, 0:1], in1=x[:, 0:W], op0=ALU.mult, op1=ALU.add),
                reads=[d32, mu, x], writes=[dst])
            if fn is not None:
                S.op("act", lambda e, dst=dst, fn=fn: e.activation(out=dst[:, 0:W], in_=dst[:, 0:W], func=fn), reads=[dst], writes=[dst])
        G0 = lanes_buf[0]["a"]
        mm(G0, G0[0:64, 0:W], g2h[:], bb["sg"][:, 0:W], [g2h, bb["sg"]])
        S.op("act", lambda e: e.activation(out=bb["gT"][:, 0:W], in_=G0[0:64, 0:W], func=AF.Copy), writes=[bb["gT"], G0])
        lanes = []
        for ci in range(ncb):
            rec = S.record()
            chunk(lanes_buf[ci], bb, ci, cb + ci)
            S.stop_record()
            lanes.append(rec)
        S.replay(lanes, skew=10)
        P.store(yT_d[:, p0:p0 + W], bb["yb"], bb["yb"][:, 0:W])
    return P.finish()


def prep_rwkv(up, p, l, core):
    j = core
    o = 576 + 1544 + 1536
    prev = np.concatenate([np.zeros((1, up.shape[1]), np.float32), up[:-1]], 0)
    hs = slice(j * 64, (j + 1) * 64)
    mu = p['rwkv_mu'][l]
    rep = lambda v, n=128: C_(np.tile(v[None, :].astype(np.float32), (n, 1)))
    idx = np.arange(128)
    TriU = (idx[:, None] <= idx[None, :]).astype(np.float32)
    m = {}
    for nm, off in (("r", 0), ("k", 512), ("v", 1024)):
        cs_ = slice(o + off + j * 64, o + off + (j + 1) * 64)
        m[nm + "_tok"] = tok_layout(up[:, cs_]); m[nm + "p_tok"] = tok_layout(prev[:, cs_])
        m["mu_" + nm] = rep(np.tile(mu[off + j * 64: off + (j + 1) * 64], 4))
    for nm, off, wdt in (("wd", 1536, 32), ("ad", 1568, 32), ("gd", 1600, 64)):
        cs_ = slice(o + off, o + off + wdt)
        m[nm + "T"] = C_(up[:, cs_].T); m[nm + "pT"] = C_(prev[:, cs_].T)
        m["mu_" + nm] = C_(mu[off:off + wdt][:, None].astype(np.float32))
    m["w2h"] = C_(p['rwkv_w2'][l][:, hs]); m["a2h"] = C_(p['rwkv_a2'][l][:, hs]); m["g2h"] = C_(p['rwkv_g2'][l][:, hs])
    m["w0t"] = rep(p['rwkv_w0'][l][hs]); m["a0t"] = rep(p['rwkv_a0'][l][hs]); m["kkt"] = rep(p['rwkv_k_k'][l][hs])
    m["kat"] = rep(p['rwkv_k_a'][l][hs]); m["rkt"] = rep(p['rwkv_r_k'][l][j]); m["lng"] = C_(p['rwkv_ln'][l][j][:, None].astype(np.float32))
    m["TriU"] = C_(TriU); m["SL"] = C_((idx[:, None] > idx[None, :]).astype(np.float32))
    m["SU"] = C_((idx[:, None] < idx[None, :]).astype(np.float32)); m["Ident"] = np.eye(128, dtype=np.float32)
    return m


_PROGS = {}


def _prog(name, fn):
    if name not in _PROGS:
        r = fn()
        _PROGS[name] = r[0] if isinstance(r, tuple) else r
    return _PROGS[name]


def _run(nc, maps):
    res = run_bass_kernel_spmd(nc, maps, core_ids=list(range(NCORES)))
    return res.results


def tile_w(W, nk=KC):
    ncols = W.shape[1]
    nt = (ncols + 127) // 128
    if nt * 128 != ncols:
        W = np.concatenate([W, np.zeros((W.shape[0], nt * 128 - ncols), np.float32)], 1)
    return C_(W.reshape(nk, 128, nt, 128).transpose(2, 1, 0, 3).reshape(nt, 128, nk * 128))


def tile_wd(W):
    return C_(W.reshape(FC // 4, 4, 128, KC, 128).transpose(0, 3, 2, 1, 4).reshape((FC // 4) * KC, 128, 4 * 128))


def _gl(v, n):
    return C_(np.asarray(v, np.float32).reshape(n, 128).T)


def kernel(**inp):
    p = {k: np.asarray(v, np.float32) for k, v in inp.items()}
    x = p['x'][0]
    meta = p['meta_tokens']
    depth = p['w_in'].shape[0]
    hT = [C_(np.concatenate([meta, x[c * 1024:(c + 1) * 1024]], 0).T) for c in range(NCORES)]
    nc_a = _prog("Ta", build_Ta)
    nc_b = _prog("Tb", build_Tb)
    nc_mla = _prog("mla", build_mla)
    nc_ssd = _prog("ssd", build_ssd)
    nc_ret = _prog("ret", build_ret)
    nc_rwkv = _prog("rwkv", build_rwkv)
    for l in range(depth):
        g1 = _gl(p['ffn1_norm'][l], KC); gm = _gl(p['mix_norm'][l], KC)
        wg1 = tile_w(p['ffn1_w_gate'][l]); wu1 = tile_w(p['ffn1_w_up'][l]); wd1 = tile_wd(p['ffn1_w_down'][l])
        win = tile_w(p['w_in'][l])
        res = _run(nc_a, [{"h_in": hT[c], "g1": g1, "wg": wg1, "wu": wu1,
                           "wd": wd1, "gm": gm, "win": win} for c in range(NCORES)])
        hT = [C_(res[c]['h_out']) for c in range(NCORES)]
        up = np.zeros((LP, INW), np.float32)
        up[PAD:PAD + 16] = res[0]['uT'][:, 0:16].T
        for c in range(NCORES):
            up[PAD + 16 + c * 1024:PAD + 16 + (c + 1) * 1024] = res[c]['uT'][:, 16:].T
        del res
        y = np.zeros((LP, D), np.float32)
        r = _run(nc_mla, [prep_mla(up, p, l, c) for c in range(NCORES)])
        for h in range(4):
            y[:, h * 128:(h + 1) * 128] = r[h]['yT'].T
        r = _run(nc_ssd, [prep_ssd(up, p, l, c) for c in range(NCORES)])
        for j in range(8):
            y[:, 512 + j * 64:512 + (j + 1) * 64] = r[j]['yT'].T
        r = _run(nc_ret, [prep_ret(up, p['ret_norm'][l], c) for c in range(NCORES)])
        for h in range(4):
            y[:, 1024 + h * 128:1024 + (h + 1) * 128] = r[h]['yT'].T
        r = _run(nc_rwkv, [prep_rwkv(up, p, l, c) for c in range(NCORES)])
        for j in range(8):
            y[:, 1536 + j * 64:1536 + (j + 1) * 64] = r[j]['yT'].T
        del r, up
        g2 = _gl(p['ffn2_norm'][l], KC); gs = _gl(p['ssd_norm'][l], 4)
        del wg1, wu1, wd1, win
        wg2 = tile_w(p['ffn2_w_gate'][l]); wu2 = tile_w(p['ffn2_w_up'][l]); wd2 = tile_wd(p['ffn2_w_down'][l])
        wo = tile_w(p['w_out'][l])
        maps = []
        for c in range(NCORES):
            yc = np.concatenate([y[PAD:PAD + 16], y[PAD + 16 + c * 1024:PAD + 16 + (c + 1) * 1024]], 0)
            maps.append({"h_in": hT[c], "yT": C_(yc.T), "gs": gs, "wout": wo, "g2": g2,
                         "wg": wg2, "wu": wu2, "wd": wd2})
        res = _run(nc_b, maps)
        hT = [C_(res[c]['h_out']) for c in range(NCORES)]
        del res, maps, y, wg2, wu2, wd2, wo
    out = np.concatenate([hT[c][:, 16:].T for c in range(NCORES)], 0)
    return C_(out[None].astype(np.float32))
```

```python
import numpy as np
import concourse.bass as bass
import concourse.mybir as mybir
from concourse.bass_utils import run_bass_kernel_spmd

F32 = mybir.dt.float32
BF16 = mybir.dt.bfloat16
AF = mybir.ActivationFunctionType
ALU = mybir.AluOpType
AX = mybir.AxisListType

NCORES = 8
D = 2048
KC = 16
DFF = 5632
FC = 44
NT = 1040
CGS = [(0, 16), (16, 528), (528, 1040)]
INW = 5320
EPS = 1e-6
EPOCH = 30000


class Buf:
    __slots__ = ("ap", "w", "r", "name", "dsem")

    def __init__(self, ap, name=""):
        self.ap = ap
        self.w = None
        self.r = {}
        self.name = name
        self.dsem = None

    def __getitem__(self, idx):
        return self.ap[idx]


class Sched:
    def __init__(self, nc):
        self.nc = nc
        self.engs = {"pe": nc.tensor, "dve": nc.vector, "act": nc.scalar,
                     "pool": nc.gpsimd, "sp": nc.sync}
        self.sems = []
        self.owner = {}
        self.cur = {}
        self.cnt = {}
        self.seen = {e: {} for e in self.engs}
        self.ninst = {e: 0 for e in self.engs}
        for e in ("pe", "dve", "act", "pool"):
            self._new_epoch(e)
        self.uid = 0

    def _alloc_sem(self, name, owner=None):
        h = self.nc.alloc_semaphore(name)
        self.sems.append(h)
        k = len(self.sems) - 1
        self.cnt[k] = 0
        self.owner[k] = owner
        return k

    def _new_epoch(self, e):
        self.cur[e] = self._alloc_sem("c_%s_%d" % (e, len(self.sems)), e)

    def new_dsem(self, name="d"):
        return self._alloc_sem("%s_%d" % (name, len(self.sems)))

    def _waits(self, e, reads, writes):
        need = {}
        for b in reads:
            if b.w is not None:
                k, v = b.w
                if need.get(k, 0) < v:
                    need[k] = v
        for b in writes:
            if b.w is not None:
                k, v = b.w
                if need.get(k, 0) < v:
                    need[k] = v
            for k, v in b.r.items():
                if need.get(k, 0) < v:
                    need[k] = v
        eng = self.engs[e]
        seen = self.seen[e]
        for k, v in need.items():
            if e == "pe" and self.owner[k] == "pe":
                continue
            if seen.get(k, 0) >= v:
                continue
            eng.wait_ge(self.sems[k], v)
            seen[k] = v

    def record(self):
        self._rec = []
        return self._rec

    def stop_record(self):
        self._rec = None

    def replay_weighted(self, lanes):
        self._rec = None
        n = max(len(l) for l in lanes) if lanes else 0
        pos = [0] * len(lanes)
        for i in range(n):
            for k, l in enumerate(lanes):
                tgt = ((i + 1) * len(l) + n - 1) // n
                while pos[k] < min(tgt, len(l)):
                    kind, a, kw = l[pos[k]]
                    (self.op if kind == "op" else self.dma)(*a, **kw)
                    pos[k] += 1

    def replay(self, lanes, skew=0):
        self._rec = None
        n = max(len(l) + k * skew for k, l in enumerate(lanes)) if lanes else 0
        for i in range(n):
            for k, l in enumerate(lanes):
                j = i - k * skew
                if 0 <= j < len(l):
                    kind, a, kw = l[j]
                    (self.op if kind == "op" else self.dma)(*a, **kw)

    def op(self, e, fn, reads=(), writes=(), inc=True):
        if getattr(self, "_rec", None) is not None:
            self._rec.append(("op", (e, fn, tuple(reads), tuple(writes), inc), {}))
            return None
        self._waits(e, reads, writes)
        ins = fn(self.engs[e])
        k = self.cur[e]
        self.ninst[e] += 1
        if inc:
            self.cnt[k] += 1
            ins.then_inc(self.sems[k], 1)
            tok = (k, self.cnt[k])
        else:
            tok = (k, self.cnt[k] + 1)
        for b in reads:
            if b.r.get(tok[0], 0) < tok[1]:
                b.r[tok[0]] = tok[1]
        for b in writes:
            b.w = tok
            b.r = {}
        if inc and self.cnt[k] >= EPOCH:
            self._new_epoch(e)
        return ins

    def dma(self, q, out_ap, in_ap, reads=(), writes=(), sem=None, is_out=False, **kw):
        if getattr(self, "_rec", None) is not None:
            kw2 = dict(kw); kw2.update(reads=tuple(reads), writes=tuple(writes), sem=sem, is_out=is_out)
            self._rec.append(("dma", (q, out_ap, in_ap), kw2))
            return None
        self._waits(q, reads, writes)
        if sem is None:
            for b in list(writes) + list(reads):
                if b.dsem is None:
                    b.dsem = self.new_dsem()
                sem = b.dsem
                break
        ins = self.engs[q].dma_start(out=out_ap, in_=in_ap, **kw)
        ins.then_inc(self.sems[sem], 16)
        self.cnt[sem] += 16
        self.ninst[q] += 1
        tok = (sem, self.cnt[sem])
        if is_out:
            if not hasattr(self, "out_sems"):
                self.out_sems = set()
            self.out_sems.add(sem)
        for b in reads:
            if b.r.get(tok[0], 0) < tok[1]:
                b.r[tok[0]] = tok[1]
        for b in writes:
            b.w = tok
            b.r = {}
        return sem

    def wait_all(self, e, semkeys):
        eng = self.engs[e]
        for k in semkeys:
            if self.cnt[k] > 0:
                eng.wait_ge(self.sems[k], self.cnt[k])

    def sb(self, shape, dt, name=None):
        self.uid += 1
        name = "%s_%d" % (name or "sb", self.uid)
        return Buf(self.nc.alloc_sbuf_tensor(name, list(shape), dt).ap(), name)

    def ps(self, shape, dt=F32, name=None):
        self.uid += 1
        name = "%s_%d" % (name or "ps", self.uid)
        return Buf(self.nc.alloc_psum_tensor(name, list(shape), dt).ap(), name)

    def sub(self, buf, ap):
        return Buf(ap, buf.name + "_s")


class Ring:
    def __init__(self, bufs):
        self.bufs = bufs
        self.i = 0

    def next(self):
        b = self.bufs[self.i % len(self.bufs)]
        self.i += 1
        return b


class TCtx:
    def __init__(self, S, n_ost=2):
        self.S = S
        nc = S.nc
        self.hT = [S.sb([128, NT], F32, "hT") for _ in range(KC)]
        self.xnT = [S.sb([128, NT], BF16, "xnT") for _ in range(KC)]
        self.actT = Ring([S.sb([128, NT], BF16, "actT") for _ in range(8)])
        self.wst = Ring([S.sb([128, KC, 128], F32, "wst") for _ in range(4)])
        self.wbf = Ring([S.sb([128, KC, 128], BF16, "wbf") for _ in range(6)])
        self.wdst = Ring([S.sb([128, 4, 128], F32, "wdst") for _ in range(3)])
        self.wdbf = Ring([S.sb([128, 4, 128], BF16, "wdbf") for _ in range(3)])
        self.tmp = Ring([S.sb([128, 512], F32, "tmp") for _ in range(3)])
        self.ost = Ring([S.sb([128, NT], F32, "ost") for _ in range(n_ost)])
        self.rstd = S.sb([128, NT], F32, "rstd")
        self.gcol = S.sb([128, KC], F32, "gcol")
        self.ones = S.sb([128, 128], F32, "ones")
        S.op("pool", lambda e: e.memset(self.ones[:], 1.0), writes=[self.ones])
        banks = [S.ps([128, 512], F32, "bank") for _ in range(8)]
        self.G = [banks[0], banks[1]]
        self.U = [banks[2], banks[3]]
        self.Dn = [banks[4], banks[5]]
        self.N = banks[6]
        self.M = banks[7]

    def pG(self, ci):
        return (self.M, self.M.ap[:, 0:16]) if ci == 0 else (self.G[ci - 1], self.G[ci - 1].ap[:, :])

    def pU(self, ci):
        return (self.M, self.M.ap[:, 16:32]) if ci == 0 else (self.U[ci - 1], self.U[ci - 1].ap[:, :])

    def pD(self, ci):
        return (self.M, self.M.ap[:, 32:48]) if ci == 0 else (self.Dn[ci - 1], self.Dn[ci - 1].ap[:, :])

    def pN(self, ci):
        return (self.M, self.M.ap[:, 48:64]) if ci == 0 else (self.N, self.N.ap[:, :])


def t_load_h(T, h_dram):
    S = T.S
    for kc in range(KC):
        S.dma("sp", T.hT[kc][:], h_dram[kc * 128:(kc + 1) * 128, :], writes=[T.hT[kc]])


def t_store_h(T, h_dram, sems):
    S = T.S
    for kc in range(KC):
        sems.append(S.dma("sp", h_dram[kc * 128:(kc + 1) * 128, :], T.hT[kc][:], reads=[T.hT[kc]]))


def t_rmsnorm(T, g_dram, src=None, nk=KC, dst=None, width=D):
    S = T.S
    src = src or T.hT
    dst = dst or T.xnT
    S.dma("sp", T.gcol[:, 0:nk], g_dram[:, 0:nk], writes=[T.gcol])
    for ci, (c0, c1) in enumerate(CGS):
        pb, pap = T.pN(ci)
        for kc in range(nk):
            t = T.tmp.next()
            S.op("act", lambda e, t=t, kc=kc: e.activation(out=t[:, 0:c1 - c0], in_=src[kc][:, c0:c1], func=AF.Square),
                 reads=[src[kc]], writes=[t])
            S.op("pe", lambda e, t=t, kc=kc: e.matmul(pap[:, 0:c1 - c0], lhsT=T.ones[:], rhs=t[:, 0:c1 - c0],
                                                       start=(kc == 0), stop=(kc == nk - 1)),
                 reads=[T.ones, t], writes=[pb], inc=True)
        S.op("act", lambda e: e.activation(out=T.rstd[:, c0:c1], in_=pap[:, 0:c1 - c0], func=AF.Sqrt,
                                           scale=1.0 / width, bias=EPS),
             writes=[T.rstd, pb])
    S.op("dve", lambda e: e.reciprocal(T.rstd[:], T.rstd[:]), reads=[T.rstd], writes=[T.rstd])
    for kc in range(nk):
        S.op("dve", lambda e, kc=kc: e.scalar_tensor_tensor(out=dst[kc][:], in0=src[kc][:], scalar=T.gcol[:, kc:kc + 1],
                                                            in1=T.rstd[:], op0=ALU.mult, op1=ALU.mult),
             reads=[src[kc], T.gcol, T.rstd], writes=[dst[kc]])


PROBE_NODMA = False
W_TWO_QUEUES = True
PROBE_NOMM = False


def t_load_w(T, w_dram, idx, ncol, nk=KC, cast_eng="act"):
    S = T.S
    if PROBE_NODMA and getattr(T, "_w0", None) is not None:
        return T._w0
    st = T.wst.next()
    bf = T.wbf.next()
    T._wq = getattr(T, "_wq", 0) + 1
    q = "sp" if (T._wq % 2 == 0 or not W_TWO_QUEUES) else "pool"
    S.dma(q, st[:, 0:nk, :], w_dram[idx].rearrange("p (kc f) -> p kc f", kc=nk), writes=[st])
    if cast_eng == "act":
        S.op("act", lambda e: e.activation(out=bf[:, 0:nk, :], in_=st[:, 0:nk, :], func=AF.Copy), reads=[st], writes=[bf])
    else:
        S.op(cast_eng, lambda e: e.tensor_copy(bf[:, 0:nk, :], st[:, 0:nk, :]), reads=[st], writes=[bf])
    T._w0 = bf
    return bf


def t_ffn(T, g_dram, wg, wu, wd):
    S = T.S
    t_rmsnorm(T, g_dram)
    SG = 4
    for sg in range(FC // SG):
        acts = []
        for fi in range(SG):
            fc = sg * SG + fi
            bg = t_load_w(T, wg, fc, 128)
            bu = t_load_w(T, wu, fc, 128)
            a = T.actT.next()
            acts.append(a)
            for ci, (c0, c1) in enumerate(CGS):
                n = c1 - c0
                gb, gap = T.pG(ci)
                ub, uap = T.pU(ci)
                for kc in range(KC):
                    S.op("pe", lambda e, kc=kc: e.matmul(gap[:, 0:n], lhsT=bg[:, kc, :], rhs=T.xnT[kc][:, c0:c1],
                                                          start=(kc == 0), stop=(kc == KC - 1)),
                         reads=[bg, T.xnT[kc]], writes=[gb], inc=(kc == KC - 1))
                for kc in range(KC):
                    S.op("pe", lambda e, kc=kc: e.matmul(uap[:, 0:n], lhsT=bu[:, kc, :], rhs=T.xnT[kc][:, c0:c1],
                                                          start=(kc == 0), stop=(kc == KC - 1)),
                         reads=[bu, T.xnT[kc]], writes=[ub], inc=(kc == KC - 1))
                t = T.tmp.next()
                S.op("act", lambda e, t=t: e.activation(out=t[:, 0:n], in_=gap[:, 0:n], func=AF.Silu),
                     writes=[t, gb])
                S.op("dve", lambda e, t=t: e.tensor_tensor(out=a[:, c0:c1], in0=t[:, 0:n], in1=uap[:, 0:n], op=ALU.mult),
                     reads=[t], writes=[a, ub])
        for dc in range(KC):
            st = T.wdst.next()
            bf = T.wdbf.next()
            S.dma("sp", st[:], wd[sg * KC + dc].rearrange("p (fi d) -> p fi d", fi=SG), writes=[st])
            S.op("pool", lambda e: e.tensor_copy(bf[:], st[:]), reads=[st], writes=[bf])
            for ci, (c0, c1) in enumerate(CGS):
                n = c1 - c0
                db, dap = T.pD(ci)
                for fi in range(SG):
                    S.op("pe", lambda e, fi=fi: e.matmul(dap[:, 0:n], lhsT=bf[:, fi, :], rhs=acts[fi][:, c0:c1],
                                                          start=(fi == 0), stop=(fi == SG - 1)),
                         reads=[bf, acts[fi]], writes=[db], inc=(fi == SG - 1))
                S.op("dve", lambda e: e.scalar_tensor_tensor(out=T.hT[dc][:, c0:c1], in0=dap[:, 0:n], scalar=0.5,
                                                            in1=T.hT[dc][:, c0:c1], op0=ALU.mult, op1=ALU.add),
                     writes=[T.hT[dc], db])


def t_proj(T, w_dram, ncols, rhsT, nk, sink):
    S = T.S
    nchunks = (ncols + 127) // 128
    for uc in range(nchunks):
        m = min(128, ncols - uc * 128)
        bw = t_load_w(T, w_dram, uc, m, nk=nk, cast_eng="pool")
        for ci, (c0, c1) in enumerate(CGS):
            n = c1 - c0
            gb, gap = T.pG(ci)
            for kc in range(nk):
                S.op("pe", lambda e, kc=kc: e.matmul(gap[0:m, 0:n], lhsT=bw[:, kc, 0:m], rhs=rhsT[kc][:, c0:c1],
                                                      start=(kc == 0), stop=(kc == nk - 1)),
                     reads=[bw, rhsT[kc]], writes=[gb], inc=(kc == nk - 1))
            sink(ci, c0, c1, uc, m, gb, gap)


def build_Ta():
    nc = bass.Bass("TRN2", target_bir_lowering=False)
    dt = lambda n, s, k: nc.dram_tensor(n, list(s), F32, kind=k).ap()
    h_in = dt("h_in", [D, NT], "ExternalInput")
    g1 = dt("g1", [128, KC], "ExternalInput")
    wg = dt("wg", [FC, 128, KC * 128], "ExternalInput")
    wu = dt("wu", [FC, 128, KC * 128], "ExternalInput")
    wd = dt("wd", [(FC // 4) * KC, 128, 4 * 128], "ExternalInput")
    gm = dt("gm", [128, KC], "ExternalInput")
    win = dt("win", [42, 128, KC * 128], "ExternalInput")
    h_out = dt("h_out", [D, NT], "ExternalOutput")
    uT = dt("uT", [INW, NT], "ExternalOutput")
    S = Sched(nc)
    T = TCtx(S)
    outs = []
    t_load_h(T, h_in)
    t_ffn(T, g1, wg, wu, wd)
    t_store_h(T, h_out, outs)
    t_rmsnorm(T, gm)
    cur = {}

    def sink(ci, c0, c1, uc, m, pb, pap):
        if ci == 0:
            cur["o"] = T.ost.next()
        o = cur["o"]
        n = c1 - c0
        S.op("act", lambda e: e.activation(out=o[0:m, c0:c1], in_=pap[0:m, 0:n], func=AF.Copy), writes=[o, pb])
        if ci == len(CGS) - 1:
            outs.append(S.dma("act", uT[uc * 128:uc * 128 + m, :], o[0:m, :], reads=[o]))

    t_proj(T, win, INW, T.xnT, KC, sink)
    S.wait_all("sp", sorted(set(outs)))
    return nc, S


def build_Tb():
    nc = bass.Bass("TRN2", target_bir_lowering=False)
    dt = lambda n, s, k: nc.dram_tensor(n, list(s), F32, kind=k).ap()
    h_in = dt("h_in", [D, NT], "ExternalInput")
    yT = dt("yT", [D, NT], "ExternalInput")
    gs = dt("gs", [128, 4], "ExternalInput")
    wout = dt("wout", [KC, 128, KC * 128], "ExternalInput")
    g2 = dt("g2", [128, KC], "ExternalInput")
    wg = dt("wg", [FC, 128, KC * 128], "ExternalInput")
    wu = dt("wu", [FC, 128, KC * 128], "ExternalInput")
    wd = dt("wd", [(FC // 4) * KC, 128, 4 * 128], "ExternalInput")
    h_out = dt("h_out", [D, NT], "ExternalOutput")
    S = Sched(nc)
    T = TCtx(S, n_ost=0)
    outs = []
    t_load_h(T, h_in)
    yst = [S.sb([128, NT], F32, "yst") for _ in range(4)]
    for j in range(4):
        S.dma("sp", yst[j][:], yT[(4 + j) * 128:(5 + j) * 128, :], writes=[yst[j]])
    t_rmsnorm(T, gs, src=yst, nk=4, dst=T.xnT[4:8], width=512)
    for kc in list(range(0, 4)) + list(range(8, 16)):
        st = yst[kc % 4]
        S.dma("sp", st[:], yT[kc * 128:(kc + 1) * 128, :], writes=[st])
        S.op("pool", lambda e, kc=kc, st=st: e.tensor_copy(T.xnT[kc][:], st[:]), reads=[st], writes=[T.xnT[kc]])

    def sink(ci, c0, c1, dc, m, pb, pap):
        n = c1 - c0
        S.op("dve", lambda e: e.tensor_tensor(out=T.hT[dc][:, c0:c1], in0=pap[:, 0:n], in1=T.hT[dc][:, c0:c1], op=ALU.add),
             writes=[T.hT[dc], pb])

    t_proj(T, wout, D, T.xnT, KC, sink)
    t_ffn(T, g2, wg, wu, wd)
    t_store_h(T, h_out, outs)
    S.wait_all("sp", sorted(set(outs)))
    return nc, S


LP = 8320
NCH = 65


class MProg:
    def __init__(self):
        self.nc = bass.Bass("TRN2", target_bir_lowering=False)
        self.S = Sched(self.nc)
        self.outs = []
        self.banks = [self.S.ps([128, 512], F32, "bank") for _ in range(8)]

    def din(self, name, shape):
        return self.nc.dram_tensor(name, list(shape), F32, kind="ExternalInput").ap()

    def dout(self, name, shape):
        return self.nc.dram_tensor(name, list(shape), F32, kind="ExternalOutput").ap()

    def load(self, name, shape, dt=F32, dram=None):
        d = dram if dram is not None else self.din(name, shape)
        b = self.S.sb(shape, F32, name)
        if len(shape) == 2 and shape[1] > 2048:
            step = 2080
            for c0 in range(0, shape[1], step):
                c1 = min(shape[1], c0 + step)
                self.S.dma("sp", b[:, c0:c1], d[:, c0:c1], writes=[b])
        else:
            self.S.dma("sp", b[:], d, writes=[b])
        return b

    def store(self, dram_ap, buf, ap):
        self.S.dma("pool", dram_ap, ap, reads=[buf], is_out=True)

    def finish(self):
        self.S.wait_all("sp", sorted(getattr(self.S, "out_sems", set())))
        return self.nc


class Stream:
    def __init__(self, P, name, rows, total, width, nbuf=2):
        self.P = P
        self.d = P.din(name, [rows, total])
        self.rows = rows
        self.ring = Ring([P.S.sb([rows, width], F32, name) for _ in range(nbuf)])

    def get(self, c0, w):
        b = self.ring.next()
        self.P.S.dma("sp", b[:, 0:w], self.d[:, c0:c0 + w], writes=[b])
        return b


def build_ret():
    P = MProg()
    S = P.S
    B = P.banks
    qT_s = Stream(P, "qT", 64, LP, 512); qsT_s = Stream(P, "qsT", 64, LP, 512)
    kT_s = Stream(P, "kT", 64, LP, 512); ksT_s = Stream(P, "ksT", 64, LP, 512)
    cosT_s = Stream(P, "cosT", 64, LP, 512); sinT_s = Stream(P, "sinT", 64, LP, 512)
    ktok_s = Stream(P, "k_tok", 128, NCH * 64, 256); kstok_s = Stream(P, "ks_tok", 128, NCH * 64, 256)
    costok_s = Stream(P, "cos_tok", 128, NCH * 64, 256); sintok_s = Stream(P, "sin_tok", 128, NCH * 64, 256)
    vtok_s = Stream(P, "v_tok", 128, NCH * 128, 512)
    gT_s = Stream(P, "gT", 128, LP, 512)
    qdec = P.load("qdecT", [64, 512])
    kdec = P.load("kdec", [128, 1])
    dmatT = P.load("dmatT", [128, 128])
    cdec = P.load("cdec", [64, 1])
    gcol = P.load("gcol", [128, 1])
    yT_d = P.dout("yT", [128, LP])
    tmp64 = S.sb([64, 512], F32, "tmp64")
    qd = S.sb([64, 512], F32, "qd")

    def mul(o, a, b_, w, rows=64):
        S.op("dve", lambda e: e.tensor_tensor(out=o[0:rows, 0:w], in0=a[0:rows, 0:w], in1=b_[0:rows, 0:w], op=ALU.mult),
             reads=[a, b_], writes=[o])

    def add(o, a, b_, w, rows=64):
        S.op("dve", lambda e: e.tensor_tensor(out=o[0:rows, 0:w], in0=a[0:rows, 0:w], in1=b_[0:rows, 0:w], op=ALU.add),
             reads=[a, b_], writes=[o])
    ones = S.sb([128, 128], F32, "ones")
    S.op("pool", lambda e: e.memset(ones[:], 1.0 / 128), writes=[ones])
    Sst = [S.sb([64, 128], F32, "Sst") for _ in range(2)]
    S.op("pool", lambda e: e.memset(Sst[0][:], 0.0), writes=[Sst[0]])
    atm = Ring([S.sb([128, 128], F32, "atm") for _ in range(2)])
    ybuf = Ring([S.sb([128, 512], F32, "ybuf") for _ in range(2)])
    yc = S.sb([128, 512], F32, "yc"); sq = S.sb([128, 512], F32, "sq"); rs = S.sb([128, 512], F32, "rs")
    AT = Ring([B[0], B[1]]); YT = Ring([B[2], B[3]]); SP_ = B[4]; LN1 = B[5]; LN2 = B[6]
    blocks = [(0, 1)] + [(1 + 4 * i, 4) for i in range(16)]
    for (cb, ncb) in blocks:
        yb = ybuf.next()
        W = ncb * 128
        p0 = cb * 128
        qT = qT_s.get(p0, W); qsT = qsT_s.get(p0, W); kT = kT_s.get(p0, W); ksT = ksT_s.get(p0, W)
        cosT = cosT_s.get(p0, W); sinT = sinT_s.get(p0, W)
        k_tok = ktok_s.get(cb * 64, ncb * 64); ks_tok = kstok_s.get(cb * 64, ncb * 64)
        cos_tok = costok_s.get(cb * 64, ncb * 64); sin_tok = sintok_s.get(cb * 64, ncb * 64)
        v_tok = vtok_s.get(p0, W)
        mul(qT, qT, cosT, W); mul(tmp64, qsT, sinT, W); add(qT, qT, tmp64, W)
        mul(kT, kT, cosT, W); mul(tmp64, ksT, sinT, W); add(kT, kT, tmp64, W)
        S.op("dve", lambda e: e.tensor_scalar(out=kT[:, 0:W], in0=kT[:, 0:W], scalar1=float(64 ** -0.5), scalar2=None,
                                              op0=ALU.mult), reads=[kT], writes=[kT])
        mul(qd, qT, qdec, W)
        w2 = ncb * 64
        mul(k_tok, k_tok, cos_tok, w2, 128); mul(ks_tok, ks_tok, sin_tok, w2, 128); add(k_tok, k_tok, ks_tok, w2, 128)
        S.op("dve", lambda e: e.tensor_scalar(out=k_tok[:, 0:w2], in0=k_tok[:, 0:w2], scalar1=kdec[:, 0:1], scalar2=None,
                                              op0=ALU.mult), reads=[k_tok, kdec], writes=[k_tok])
        for ci in range(ncb):
            c = cb + ci
            sl = slice(ci * 128, (ci + 1) * 128)
            Sp = Sst[c % 2]; Sn = Sst[(c + 1) % 2]
            at = AT.next(); yt = YT.next(); am = atm.next()
            S.op("pe", lambda e: e.matmul(at[:, 0:128], lhsT=kT[:, sl], rhs=qT[:, sl], start=True, stop=True),
                 reads=[kT, qT], writes=[at])
            S.op("dve", lambda e: e.tensor_tensor(out=am[:], in0=at[:, 0:128], in1=dmatT[:], op=ALU.mult),
                 reads=[dmatT], writes=[am, at])
            S.op("pe", lambda e: e.matmul(yt[:, 0:128], lhsT=v_tok[:, sl], rhs=am[:], start=True, stop=False),
                 reads=[v_tok, am], writes=[yt], inc=False)
            S.op("pe", lambda e: e.matmul(yt[:, 0:128], lhsT=Sp[:], rhs=qd[:, sl], start=False, stop=True),
                 reads=[Sp, qd], writes=[yt])
            S.op("act", lambda e: e.activation(out=yb[:, sl], in_=yt[:, 0:128], func=AF.Copy),
                 writes=[yb, yt])
            S.op("pe", lambda e: e.matmul(SP_[0:64, 0:128], lhsT=k_tok[:, ci * 64:(ci + 1) * 64], rhs=v_tok[:, sl],
                                          start=True, stop=True), reads=[k_tok, v_tok], writes=[SP_])
            S.op("dve", lambda e: e.scalar_tensor_tensor(out=Sn[:], in0=Sp[:], scalar=cdec[:, 0:1], in1=SP_[0:64, 0:128],
                                                         op0=ALU.mult, op1=ALU.add),
                 reads=[Sp, cdec], writes=[Sn, SP_])
        gb = gT_s.get(p0, W)
        S.op("act", lambda e: e.activation(out=gb[:, 0:W], in_=gb[:, 0:W], func=AF.Silu), reads=[gb], writes=[gb])
        S.op("pe", lambda e: e.matmul(LN1[:, 0:W], lhsT=ones[:], rhs=yb[:, 0:W], start=True, stop=True),
             reads=[ones, yb], writes=[LN1])
        S.op("dve", lambda e: e.tensor_tensor(out=yc[:, 0:W], in0=yb[:, 0:W], in1=LN1[:, 0:W], op=ALU.subtract),
             reads=[yb], writes=[yc, LN1])
        S.op("act", lambda e: e.activation(out=sq[:, 0:W], in_=yc[:, 0:W], func=AF.Square), reads=[yc], writes=[sq])
        S.op("pe", lambda e: e.matmul(LN2[:, 0:W], lhsT=ones[:], rhs=sq[:, 0:W], start=True, stop=True),
             reads=[ones, sq], writes=[LN2])
        S.op("act", lambda e: e.activation(out=rs[:, 0:W], in_=LN2[:, 0:W], func=AF.Sqrt, scale=1.0, bias=EPS),
             writes=[rs, LN2])
        S.op("dve", lambda e: e.reciprocal(rs[:, 0:W], rs[:, 0:W]), reads=[rs], writes=[rs])
        S.op("dve", lambda e: e.scalar_tensor_tensor(out=yc[:, 0:W], in0=yc[:, 0:W], scalar=gcol[:, 0:1], in1=rs[:, 0:W],
                                                     op0=ALU.mult, op1=ALU.mult), reads=[yc, gcol, rs], writes=[yc])
        S.op("dve", lambda e: e.tensor_tensor(out=gb[:, 0:W], in0=yc[:, 0:W], in1=gb[:, 0:W], op=ALU.mult),
             reads=[yc, gb], writes=[gb])
        P.store(yT_d[:, p0:p0 + W], gb, gb[:, 0:W])
    return P.finish()


PAD = 112
C_ = np.ascontiguousarray


def tok_layout(a):
    F_ = a.shape[1]
    return C_(a.reshape(NCH, 128, F_).transpose(1, 0, 2).reshape(128, NCH * F_))


def rope_tables(dim):
    half = dim // 2
    inv = (10000.0 ** (-np.arange(half, dtype=np.float32) / half)).astype(np.float32)
    pos = (np.arange(LP, dtype=np.float32) - PAD).astype(np.float32)
    ang = (pos[:, None] * inv[None, :]).astype(np.float32)
    cos = np.cos(ang).astype(np.float32)
    sin = np.sin(ang).astype(np.float32)
    cos2 = np.concatenate([cos, cos], 1)
    sin2 = np.concatenate([-sin, sin], 1)
    return cos2, sin2


def swap_halves(a):
    h = a.shape[1] // 2
    return np.concatenate([a[:, h:], a[:, :h]], 1)


def prep_ret(up, ret_norm_l, core):
    h = core % 4
    o = 576 + 1544
    q = up[:, o + h * 64:o + (h + 1) * 64]
    k = up[:, o + 256 + h * 64:o + 256 + (h + 1) * 64]
    v = up[:, o + 512 + h * 128:o + 512 + (h + 1) * 128]
    g = up[:, o + 1024 + h * 128:o + 1024 + (h + 1) * 128]
    cos2, sin2 = rope_tables(64)
    log_g = np.log(np.float32(1.0) - np.float32(2.0) ** np.float32(-5.0 - h)).astype(np.float32)
    idx = np.arange(128, dtype=np.float32)
    qdec = np.exp((idx + 1.0) * log_g).astype(np.float32)
    kdec = (np.exp((127.0 - idx) * log_g) * (64.0 ** -0.5)).astype(np.float32)
    diff = idx[None, :] - idx[:, None]
    dmatT = np.where(diff >= 0, np.exp(diff * log_g), 0.0).astype(np.float32)
    return {
        "qT": C_(q.T), "qsT": C_(swap_halves(q).T), "kT": C_(k.T), "ksT": C_(swap_halves(k).T),
        "cosT": C_(cos2.T), "sinT": C_(sin2.T),
        "k_tok": tok_layout(k), "ks_tok": tok_layout(swap_halves(k)),
        "cos_tok": tok_layout(cos2), "sin_tok": tok_layout(sin2),
        "v_tok": tok_layout(v), "gT": C_(g.T),
        "qdecT": C_(np.tile(np.tile(qdec, 4)[None, :], (64, 1))),
        "kdec": C_(kdec[:, None]), "dmatT": C_(dmatT),
        "cdec": np.full((64, 1), np.exp(128.0 * log_g), np.float32),
        "gcol": C_(ret_norm_l[h][:, None].astype(np.float32)),
    }


def build_ssd():
    P = MProg()
    S = P.S
    B = P.banks
    xsT_s = Stream(P, "xsT_pad", 64, LP + 3, 515); BT_s = Stream(P, "BT_pad", 128, LP + 3, 515)
    CT_s = Stream(P, "CT_pad", 128, LP + 3, 515); zT_s = Stream(P, "zT", 64, LP, 512)
    xtap = [Stream(P, "xs_tok%d" % j, 128, NCH * 64, 256) for j in range(4)]
    btap = [Stream(P, "B_tok%d" % j, 128, NCH * 128, 512) for j in range(4)]
    cw_xs = P.load("cw_xs", [64, 4]); cb_xs = P.load("cb_xs", [64, 1])
    cw_B = P.load("cw_B", [128, 4]); cb_B = P.load("cb_B", [128, 1])
    cw_C = P.load("cw_C", [128, 4]); cb_C = P.load("cb_C", [128, 1])
    cwt_xs = P.load("cwt_xs", [128, 4 * 256]); cbt_xs = P.load("cbt_xs", [128, 256])
    cwt_B = P.load("cwt_B", [128, 4 * 512]); cbt_B = P.load("cbt_B", [128, 512])
    dt = P.load("dt_tok", [128, NCH]); dtb = P.load("dtb", [128, 1]); alog = P.load("alog", [128, 1])
    valid = P.load("valid_tok", [128, NCH]); dcol = P.load("dcol", [64, 1])
    TriU = P.load("TriU", [128, 128]); UTs = P.load("UTs", [128, 128])
    yT_d = P.dout("yT", [64, LP])
    onesF = S.sb([128, 128], F32, "onesF")
    S.op("pool", lambda e: e.memset(onesF[:], 1.0), writes=[onesF])
    S.op("act", lambda e: e.activation(out=dt[:], in_=dt[:], func=AF.Exp, bias=dtb[:, 0:1]), reads=[dt, dtb], writes=[dt])
    S.op("act", lambda e: e.activation(out=dt[:], in_=dt[:], func=AF.Ln, bias=1.0), reads=[dt], writes=[dt])
    S.op("dve", lambda e: e.tensor_tensor(out=dt[:], in0=dt[:], in1=valid[:], op=ALU.mult), reads=[dt, valid], writes=[dt])
    S.op("act", lambda e: e.activation(out=alog[:], in_=alog[:], func=AF.Exp), reads=[alog], writes=[alog])
    la = S.sb([128, NCH], F32, "la"); cs = S.sb([128, NCH], F32, "cs"); dte = S.sb([128, NCH], F32, "dte")
    S.op("dve", lambda e: e.tensor_scalar(out=la[:], in0=dt[:], scalar1=alog[:, 0:1], scalar2=-1.0, op0=ALU.mult,
                                          op1=ALU.mult), reads=[dt, alog], writes=[la])
    S.op("pe", lambda e: e.matmul(B[5][:, 0:NCH], lhsT=TriU[:], rhs=la[:], start=True, stop=True),
         reads=[TriU, la], writes=[B[5]])
    S.op("act", lambda e: e.activation(out=cs[:], in_=B[5][:, 0:NCH], func=AF.Copy), writes=[cs, B[5]])
    S.op("pe", lambda e: e.matmul(B[6][:, 0:NCH], lhsT=onesF[:], rhs=la[:], start=True, stop=True),
         reads=[onesF, la], writes=[B[6]])
    S.op("dve", lambda e: e.tensor_tensor(out=dte[:], in0=B[6][:, 0:NCH], in1=cs[:], op=ALU.subtract),
         reads=[cs], writes=[dte, B[6]])
    S.op("act", lambda e: e.activation(out=dte[:], in_=dte[:], func=AF.Exp), reads=[dte], writes=[dte])

    def conv_fm(src, rows, W, cw, cb, dst):
        S.op("dve", lambda e: e.tensor_scalar(out=dst[0:rows, 0:W], in0=src[0:rows, 0:W], scalar1=cw[:, 0:1], scalar2=None,
                                              op0=ALU.mult), reads=[src, cw], writes=[dst])
        for j in range(1, 4):
            S.op("dve", lambda e, j=j: e.scalar_tensor_tensor(out=dst[0:rows, 0:W], in0=src[0:rows, j:W + j],
                                                              scalar=cw[:, j:j + 1], in1=dst[0:rows, 0:W],
                                                              op0=ALU.mult, op1=ALU.add), reads=[src, cw, dst], writes=[dst])
        S.op("act", lambda e: e.activation(out=dst[0:rows, 0:W], in_=dst[0:rows, 0:W], func=AF.Silu, bias=cb[:, 0:1]),
             reads=[dst, cb], writes=[dst])

    def conv_tm(taps, w, cwt, cbt, full, dst, tmp):
        for j in range(4):
            o = dst if j == 0 else tmp
            S.op("dve", lambda e, j=j, o=o: e.tensor_tensor(out=o[:, 0:w], in0=taps[j][:, 0:w],
                                                            in1=cwt[:, j * full:j * full + w], op=ALU.mult),
                 reads=[taps[j], cwt], writes=[o])
            if j > 0:
                S.op("dve", lambda e: e.tensor_tensor(out=dst[:, 0:w], in0=dst[:, 0:w], in1=tmp[:, 0:w], op=ALU.add),
                     reads=[dst, tmp], writes=[dst])
        S.op("dve", lambda e: e.tensor_tensor(out=dst[:, 0:w], in0=dst[:, 0:w], in1=cbt[:, 0:w], op=ALU.add),
             reads=[dst, cbt], writes=[dst])
        S.op("act", lambda e: e.activation(out=dst[:, 0:w], in_=dst[:, 0:w], func=AF.Silu), reads=[dst], writes=[dst])

    BTc = Ring([S.sb([128, 512], F32, "BTc") for _ in range(2)])
    CTc = Ring([S.sb([128, 512], F32, "CTc") for _ in range(2)])
    xsTc = Ring([S.sb([64, 512], F32, "xsTc") for _ in range(2)])
    xtok = Ring([S.sb([128, 256], F32, "xtok") for _ in range(2)])
    btok = Ring([S.sb([128, 512], F32, "btok") for _ in range(2)])
    tmpx = S.sb([128, 256], F32, "tmpx"); tmpb = S.sb([128, 512], F32, "tmpb")
    lam = Ring([S.sb([128, 128], F32, "lam") for _ in range(2)])
    laf = Ring([S.sb([128, 128], F32, "laf") for _ in range(2)])
    LT = Ring([S.sb([128, 128], F32, "LT") for _ in range(2)])
    Er = Ring([S.sb([128, 128], F32, "Er") for _ in range(2)])
    CsT = Ring([S.sb([128, 128], F32, "CsT") for _ in range(2)])
    xdt = Ring([S.sb([128, 64], F32, "xdt") for _ in range(2)])
    bd = Ring([S.sb([128, 128], F32, "bd") for _ in range(2)])
    ybuf = Ring([S.sb([64, 512], F32, "ybuf") for _ in range(2)])
    Sst = [S.sb([128, 64], F32, "Sst") for _ in range(2)]
    S.op("pool", lambda e: e.memset(Sst[0][:], 0.0), writes=[Sst[0]])
    GT = B[0]; SEG = B[1]; CSR = B[2]; YT = B[3]; SC = B[4]
    blocks = [(0, 1)] + [(1 + 4 * i, 4) for i in range(16)]
    for (cb, ncb) in blocks:
        W = ncb * 128
        p0 = cb * 128
        xs_in = xsT_s.get(p0, W + 3); B_in = BT_s.get(p0, W + 3); C_in = CT_s.get(p0, W + 3); zT = zT_s.get(p0, W)
        xt = [xtap[j].get(cb * 64, ncb * 64) for j in range(4)]
        bt = [btap[j].get(cb * 128, ncb * 128) for j in range(4)]
        BT = BTc.next(); CT = CTc.next(); xsT = xsTc.next(); xk = xtok.next(); bk = btok.next(); yb = ybuf.next()
        conv_fm(B_in, 128, W, cw_B, cb_B, BT)
        conv_fm(C_in, 128, W, cw_C, cb_C, CT)
        conv_fm(xs_in, 64, W, cw_xs, cb_xs, xsT)
        conv_tm(xt, ncb * 64, cwt_xs, cbt_xs, 256, xk, tmpx)
        conv_tm(bt, ncb * 128, cwt_B, cbt_B, 512, bk, tmpb)
        S.op("act", lambda e: e.activation(out=zT[:, 0:W], in_=zT[:, 0:W], func=AF.Silu), reads=[zT], writes=[zT])
        for ci in range(ncb):
            c = cb + ci
            sl = slice(ci * 128, (ci + 1) * 128)
            Sp = Sst[c % 2]; Sn = Sst[(c + 1) % 2]
            lm = lam.next(); lf = laf.next(); lt = LT.next(); er = Er.next(); cst = CsT.next(); xd = xdt.next(); bdd = bd.next()
            lac = la[:, c:c + 1]
            S.op("pe", lambda e: e.matmul(GT[:, 0:128], lhsT=BT[:, sl], rhs=CT[:, sl], start=True, stop=True),
                 reads=[BT, CT], writes=[GT])
            S.op("dve", lambda e: e.tensor_scalar(out=lm[:], in0=UTs[:], scalar1=lac, scalar2=None, op0=ALU.mult),
                 reads=[UTs, la], writes=[lm])
            S.op("dve", lambda e: e.tensor_scalar(out=lf[:], in0=onesF[:], scalar1=lac, scalar2=None, op0=ALU.mult),
                 reads=[onesF, la], writes=[lf])
            S.op("pe", lambda e: e.matmul(SEG[:, 0:128], lhsT=lm[:], rhs=TriU[:], start=True, stop=True),
                 reads=[lm, TriU], writes=[SEG])
            S.op("pe", lambda e: e.matmul(CSR[:, 0:128], lhsT=lf[:], rhs=TriU[:], start=True, stop=True),
                 reads=[lf, TriU], writes=[CSR])
            S.op("act", lambda e: e.activation(out=lt[:], in_=SEG[:, 0:128], func=AF.Exp), writes=[lt, SEG])
            S.op("dve", lambda e: e.tensor_tensor(out=lt[:], in0=lt[:], in1=TriU[:], op=ALU.mult), reads=[lt, TriU], writes=[lt])
            S.op("dve", lambda e: e.tensor_tensor(out=lt[:], in0=lt[:], in1=GT[:, 0:128], op=ALU.mult), reads=[lt], writes=[lt, GT])
            S.op("act", lambda e: e.activation(out=er[:], in_=CSR[:, 0:128], func=AF.Exp), writes=[er, CSR])
            S.op("dve", lambda e: e.tensor_tensor(out=cst[:], in0=CT[:, sl], in1=er[:], op=ALU.mult), reads=[CT, er], writes=[cst])
            S.op("dve", lambda e: e.tensor_scalar(out=xd[:], in0=xk[:, ci * 64:(ci + 1) * 64], scalar1=dt[:, c:c + 1],
                                                  scalar2=None, op0=ALU.mult), reads=[xk, dt], writes=[xd])
            S.op("dve", lambda e: e.tensor_scalar(out=bdd[:], in0=bk[:, sl], scalar1=dte[:, c:c + 1], scalar2=None,
                                                  op0=ALU.mult), reads=[bk, dte], writes=[bdd])
            S.op("pe", lambda e: e.matmul(YT[0:64, 0:128], lhsT=xd[:], rhs=lt[:], start=True, stop=False),
                 reads=[xd, lt], writes=[YT], inc=False)
            S.op("pe", lambda e: e.matmul(YT[0:64, 0:128], lhsT=Sp[:], rhs=cst[:], start=False, stop=True),
                 reads=[Sp, cst], writes=[YT])
            S.op("dve", lambda e: e.scalar_tensor_tensor(out=yb[:, sl], in0=xsT[:, sl], scalar=dcol[:, 0:1],
                                                         in1=YT[0:64, 0:128], op0=ALU.mult, op1=ALU.add),
                 reads=[xsT, dcol], writes=[yb, YT])
            S.op("pe", lambda e: e.matmul(SC[:, 0:64], lhsT=bdd[:], rhs=xd[:], start=True, stop=True),
                 reads=[bdd, xd], writes=[SC])
            S.op("dve", lambda e: e.scalar_tensor_tensor(out=Sn[:], in0=Sp[:], scalar=er[:, 127:128], in1=SC[:, 0:64],
                                                         op0=ALU.mult, op1=ALU.add), reads=[Sp, er], writes=[Sn, SC])
        S.op("dve", lambda e: e.tensor_tensor(out=yb[:, 0:W], in0=yb[:, 0:W], in1=zT[:, 0:W], op=ALU.mult),
             reads=[yb, zT], writes=[yb])
        P.store(yT_d[:, p0:p0 + W], yb, yb[:, 0:W])
    return P.finish()


def prep_ssd(up, p, l, core):
    j = core
    g = j // 4
    o = 576
    z = up[:, o + j * 64:o + (j + 1) * 64]
    xo = o + 512
    xs = up[:, xo + j * 64:xo + (j + 1) * 64]
    Bm = up[:, xo + 512 + g * 128:xo + 512 + (g + 1) * 128]
    Cm = up[:, xo + 768 + g * 128:xo + 768 + (g + 1) * 128]
    dtr = up[:, xo + 1024 + j:xo + 1024 + j + 1]
    cw = p['ssd_conv_w'][l]
    cb = p['ssd_conv_b'][l]
    ch_xs = slice(j * 64, (j + 1) * 64)
    ch_B = slice(512 + g * 128, 512 + (g + 1) * 128)
    ch_C = slice(768 + g * 128, 768 + (g + 1) * 128)
    pad3 = lambda a: np.concatenate([np.zeros((3, a.shape[1]), np.float32), a], 0)
    xs_p = pad3(xs); B_p = pad3(Bm); C_p = pad3(Cm)
    valid = np.ones((LP, 1), np.float32); valid[:PAD] = 0
    idx = np.arange(128)
    TriU = (idx[:, None] <= idx[None, :]).astype(np.float32)
    m = {
        "xsT_pad": C_(xs_p.T), "BT_pad": C_(B_p.T), "CT_pad": C_(C_p.T), "zT": C_(z.T),
        "cw_xs": C_(cw[:, ch_xs].T), "cb_xs": C_(cb[ch_xs][:, None]),
        "cw_B": C_(cw[:, ch_B].T), "cb_B": C_(cb[ch_B][:, None]),
        "cw_C": C_(cw[:, ch_C].T), "cb_C": C_(cb[ch_C][:, None]),
        "cwt_xs": C_(np.tile(np.tile(cw[:, ch_xs], (1, 4)).reshape(1, 4 * 256), (128, 1))),
        "cbt_xs": C_(np.tile(np.tile(cb[ch_xs], 4)[None, :], (128, 1))),
        "cwt_B": C_(np.tile(np.tile(cw[:, ch_B], (1, 4)).reshape(1, 4 * 512), (128, 1))),
        "cbt_B": C_(np.tile(np.tile(cb[ch_B], 4)[None, :], (128, 1))),
        "dt_tok": tok_layout(dtr), "dtb": np.full((128, 1), p['ssd_dt_bias'][l][j], np.float32),
        "alog": np.full((128, 1), p['ssd_a_log'][l][j], np.float32),
        "valid_tok": tok_layout(valid), "dcol": np.full((64, 1), p['ssd_d'][l][j], np.float32),
        "TriU": C_(TriU), "UTs": C_(1.0 - TriU),
    }
    for t in range(4):
        m["xs_tok%d" % t] = tok_layout(xs_p[t:t + LP])
        m["B_tok%d" % t] = tok_layout(B_p[t:t + LP])
    return m


def build_mla():
    P = MProg()
    S = P.S
    B = P.banks
    NQ = 128 + 8 * 512
    cq_s = [Stream(P, "cqT%d" % i, 128, NQ, 512) for i in range(3)]
    cosq_s = Stream(P, "cosqT", 64, NQ, 512); sinq_s = Stream(P, "sinqT", 64, NQ, 512)
    ckv_s = Stream(P, "ckvT", 128, LP, 512)
    kpe_s = Stream(P, "kpeT", 64, LP, 512); kpes_s = Stream(P, "kpesT", 64, LP, 512)
    cos_s = Stream(P, "cosT", 64, LP, 512); sin_s = Stream(P, "sinT", 64, LP, 512)
    yT_d = P.dout("yT", [128, NQ])

    def loadbf(name, shape):
        f = P.load(name, shape)
        b = S.sb(shape, BF16, name + "b")
        S.op("dve", lambda e: e.tensor_copy(b[:], f[:]), reads=[f], writes=[b])
        return b
    wqn = loadbf("wq_n", [128, 3 * 128]); wqr = loadbf("wq_r", [128, 3 * 64]); wqrs = loadbf("wq_rs", [128, 3 * 64])
    wk = loadbf("wk", [128, 128]); wv = loadbf("wv", [128, 128])
    ones0b = loadbf("ones0", [128, 128])
    gqn = P.load("gqn", [128, 3]); gkv = P.load("gkv", [128, 1])
    gq_n = P.load("gq_n", [128, 1]); gq_r = P.load("gq_r", [64, 1]); gq_rs = P.load("gq_rs", [64, 1])
    gk_n = P.load("gk_n", [128, 1]); gk_r = P.load("gk_r", [64, 1]); gk_rs = P.load("gk_rs", [64, 1])
    mask8 = P.load("mask8", [128, 8 * 512]); mtri = P.load("mtri", [128, 128])
    onesF = S.sb([128, 128], F32, "onesF"); onesb = S.sb([128, 128], BF16, "onesb")
    S.op("pool", lambda e: e.memset(onesF[:], 1.0), writes=[onesF])
    S.op("pool", lambda e: e.memset(onesb[:], 1.0), writes=[onesb])
    KnT = S.sb([128, LP], BF16, "KnT"); KrT = S.sb([64, LP], BF16, "KrT"); Vt = S.sb([128, LP], BF16, "Vt")
    sqa = Ring([S.sb([128, 512], F32, "sqa") for _ in range(2)])
    rstd = S.sb([128, 512], F32, "rstd"); rstdq = S.sb([128, 512], F32, "rstdq"); rstdk = S.sb([128, 512], F32, "rstdk")
    cqn = [S.sb([128, 512], BF16, "cqn") for _ in range(3)]
    ckvn = S.sb([128, 512], BF16, "ckvn")
    qn_fs = [S.sb([128, 512], BF16, "qn_f") for _ in range(2)]; qr_fs = [S.sb([64, 512], BF16, "qr_f") for _ in range(2)]
    t1 = S.sb([64, 512], F32, "t1"); t2 = S.sb([64, 512], F32, "t2")
    PTr = Ring([S.sb([128, 512], BF16, "PT") for _ in range(3)])
    dacc = S.sb([128, 512], F32, "dacc"); vcol = P.load("vcol", [128, 1])
    rden = S.sb([128, 512], F32, "rden"); yo = Ring([S.sb([128, 512], F32, "yo") for _ in range(2)])
    STb = Ring([B[0], B[1]]); OB = B[2]; DB = B[3]; NB = B[4]; Q1 = B[5]; Q2 = B[6]; Q3 = B[7]
    SCALE = float(192 ** -0.5)

    def rms(parts, W, width, dst, post_scale=1.0):
        n = len(parts)
        for i, (buf, ap, rows, is_ps) in enumerate(parts):
            sq = sqa.next()
            if is_ps:
                S.op("act", lambda e, sq=sq, rows=rows, ap=ap: e.activation(out=sq[0:rows, 0:W], in_=ap, func=AF.Square), writes=[sq, buf])
            else:
                S.op("act", lambda e, sq=sq, rows=rows, ap=ap: e.activation(out=sq[0:rows, 0:W], in_=ap, func=AF.Square), reads=[buf], writes=[sq])
            S.op("pe", lambda e, sq=sq, rows=rows, i=i: e.matmul(NB[:, 0:W], lhsT=onesF[0:rows, :], rhs=sq[0:rows, 0:W], start=(i == 0),
                                                            stop=(i == n - 1)), reads=[onesF, sq], writes=[NB])
        S.op("act", lambda e: e.activation(out=dst[:, 0:W], in_=NB[:, 0:W], func=AF.Sqrt, scale=1.0 / width, bias=EPS),
             writes=[dst, NB])
        S.op("dve", lambda e: e.reciprocal(dst[:, 0:W], dst[:, 0:W]), reads=[dst], writes=[dst])
        if post_scale != 1.0:
            S.op("dve", lambda e: e.tensor_scalar(out=dst[:, 0:W], in0=dst[:, 0:W], scalar1=post_scale, scalar2=None,
                                                  op0=ALU.mult), reads=[dst], writes=[dst])

    blocks = [(0, 1)] + [(1 + 4 * i, 4) for i in range(16)]
    Kn_b = [Buf(KnT.ap[:, cb * 128:(cb + ncb) * 128], "Kn") for (cb, ncb) in blocks]
    Kr_b = [Buf(KrT.ap[:, cb * 128:(cb + ncb) * 128], "Kr") for (cb, ncb) in blocks]
    V_b = [Buf(Vt.ap[:, cb * 128:(cb + ncb) * 128], "V") for (cb, ncb) in blocks]
    blk_of = {}
    for bi_, (cb_, ncb_) in enumerate(blocks):
        for c_ in range(cb_, cb_ + ncb_):
            blk_of[c_] = (bi_, (c_ - cb_) * 128)

    def prep_q(it):
        W = 128 if it < 0 else 512
        q0 = 0 if it < 0 else 128 + it * 512
        qn_f = qn_fs[it % 2]; qr_f = qr_fs[it % 2]
        cq = [s.get(q0, W) for s in cq_s]
        cosb = cosq_s.get(q0, W); sinb = sinq_s.get(q0, W)
        rms([(cq[i], cq[i][:, 0:W], 128, False) for i in range(3)], W, 384.0, rstd)
        for i in range(3):
            S.op("dve", lambda e, i=i: e.scalar_tensor_tensor(out=cqn[i][:, 0:W], in0=cq[i][:, 0:W], scalar=gqn[:, i:i + 1],
                                                              in1=rstd[:, 0:W], op0=ALU.mult, op1=ALU.mult),
                 reads=[cq[i], gqn, rstd], writes=[cqn[i]])
        for (pb, wt, m) in ((Q1, wqn, 128), (Q2, wqr, 64), (Q3, wqrs, 64)):
            for i in range(3):
                S.op("pe", lambda e, i=i, pb=pb, wt=wt, m=m: e.matmul(pb[0:m, 0:W], lhsT=wt[:, i * m:(i + 1) * m], rhs=cqn[i][:, 0:W],
                                                                      start=(i == 0), stop=(i == 2)), reads=[wt, cqn[i]], writes=[pb],
                     inc=(i == 2))
        rms([(Q1, Q1[:, 0:W], 128, True), (Q2, Q2[0:64, 0:W], 64, True)], W, 192.0, rstdq, SCALE)
        S.op("dve", lambda e: e.scalar_tensor_tensor(out=qn_f[:, 0:W], in0=Q1[:, 0:W], scalar=gq_n[:, 0:1], in1=rstdq[:, 0:W],
                                                     op0=ALU.mult, op1=ALU.mult), reads=[gq_n, rstdq], writes=[qn_f, Q1])
        S.op("dve", lambda e: e.scalar_tensor_tensor(out=t1[:, 0:W], in0=Q2[0:64, 0:W], scalar=gq_r[:, 0:1], in1=cosb[:, 0:W],
                                                     op0=ALU.mult, op1=ALU.mult), reads=[gq_r, cosb], writes=[t1, Q2])
        S.op("dve", lambda e: e.scalar_tensor_tensor(out=t2[:, 0:W], in0=Q3[0:64, 0:W], scalar=gq_rs[:, 0:1], in1=sinb[:, 0:W],
                                                     op0=ALU.mult, op1=ALU.mult), reads=[gq_rs, sinb], writes=[t2, Q3])
        S.op("dve", lambda e: e.tensor_tensor(out=t1[:, 0:W], in0=t1[:, 0:W], in1=t2[:, 0:W], op=ALU.add), reads=[t1, t2], writes=[t1])
        S.op("dve", lambda e: e.tensor_tensor(out=qr_f[:, 0:W], in0=t1[:, 0:W], in1=rstdq[0:64, 0:W], op=ALU.mult),
             reads=[t1, rstdq], writes=[qr_f])

    def prep_kv(bi):
        cb, ncb = blocks[bi]
        W = ncb * 128
        p0 = cb * 128
        KnB = Kn_b[bi]; KrB = Kr_b[bi]; VB = V_b[bi]
        ckv = ckv_s.get(p0, W); kpe = kpe_s.get(p0, W); kpes = kpes_s.get(p0, W)
        cosb = cos_s.get(p0, W); sinb = sin_s.get(p0, W)
        rms([(ckv, ckv[:, 0:W], 128, False)], W, 128.0, rstd)
        S.op("dve", lambda e: e.scalar_tensor_tensor(out=ckvn[:, 0:W], in0=ckv[:, 0:W], scalar=gkv[:, 0:1], in1=rstd[:, 0:W],
                                                     op0=ALU.mult, op1=ALU.mult), reads=[ckv, gkv, rstd], writes=[ckvn])
        S.op("pe", lambda e: e.matmul(Q1[:, 0:W], lhsT=wk[:], rhs=ckvn[:, 0:W], start=True, stop=True),
             reads=[wk, ckvn], writes=[Q1])
        for ci in range(ncb):
            c = cb + ci
            S.op("pe", lambda e, ci=ci: e.matmul(Q3[:, 0:128], lhsT=ckvn[:, ci * 128:(ci + 1) * 128], rhs=wv[:], start=True, stop=True),
                 reads=[ckvn, wv], writes=[Q3])
            S.op("act", lambda e, ci=ci: e.activation(out=VB[:, ci * 128:(ci + 1) * 128], in_=Q3[:, 0:128], func=AF.Copy),
                 writes=[VB, Q3])
        rms([(Q1, Q1[:, 0:W], 128, True), (kpe, kpe[:, 0:W], 64, False)], W, 192.0, rstdk)
        S.op("dve", lambda e: e.scalar_tensor_tensor(out=KnB[:, 0:W], in0=Q1[:, 0:W], scalar=gk_n[:, 0:1], in1=rstdk[:, 0:W],
                                                     op0=ALU.mult, op1=ALU.mult), reads=[gk_n, rstdk], writes=[KnB, Q1])
        S.op("dve", lambda e: e.scalar_tensor_tensor(out=t1[:, 0:W], in0=kpe[:, 0:W], scalar=gk_r[:, 0:1], in1=cosb[:, 0:W],
                                                     op0=ALU.mult, op1=ALU.mult), reads=[kpe, gk_r, cosb], writes=[t1])
        S.op("dve", lambda e: e.scalar_tensor_tensor(out=t2[:, 0:W], in0=kpes[:, 0:W], scalar=gk_rs[:, 0:1], in1=sinb[:, 0:W],
                                                     op0=ALU.mult, op1=ALU.mult), reads=[kpes, gk_rs, sinb], writes=[t2])
        S.op("dve", lambda e: e.tensor_tensor(out=t1[:, 0:W], in0=t1[:, 0:W], in1=t2[:, 0:W], op=ALU.add), reads=[t1, t2], writes=[t1])
        S.op("dve", lambda e: e.tensor_tensor(out=KrB[:, 0:W], in0=t1[:, 0:W], in1=rstdk[0:64, 0:W], op=ALU.mult),
             reads=[t1, rstdk], writes=[KrB])

    def attn(it):
        W = 128 if it < 0 else 512
        q0 = 0 if it < 0 else 128 + it * 512
        qn_f = qn_fs[it % 2]; qr_f = qr_fs[it % 2]
        nk = 1 if it < 0 else 8 * it + 9
        def scores(kc):
            kb, ko = blk_of[kc]
            ks = slice(ko, ko + 128)
            st = STb.next(); pt = PTr.next()
            S.op("pe", lambda e: e.matmul(st[:, 0:W], lhsT=Kn_b[kb][:, ks], rhs=qn_f[:, 0:W], start=True, stop=False),
                 reads=[Kn_b[kb], qn_f], writes=[st], inc=False)
            S.op("pe", lambda e: e.matmul(st[:, 0:W], lhsT=Kr_b[kb][:, ks], rhs=qr_f[:, 0:W], start=False, stop=True),
                 reads=[Kr_b[kb], qr_f], writes=[st])
            S.op("act", lambda e: e.activation(out=pt[:, 0:W], in_=st[:, 0:W], func=AF.Exp), writes=[pt, st])
            if it < 0:
                S.op("pool", lambda e: e.tensor_tensor(out=pt[:, 0:W], in0=pt[:, 0:W], in1=mtri[:, 0:W], op=ALU.mult),
                     reads=[pt, mtri], writes=[pt])
            elif kc >= nk - 8:
                k_ = kc - (nk - 8)
                S.op("pool", lambda e: e.tensor_tensor(out=pt[:, 0:W], in0=pt[:, 0:W], in1=mask8[:, k_ * 512:k_ * 512 + W],
                                                       op=ALU.mult), reads=[pt, mask8], writes=[pt])
            return pt

        def pv(kc, pt):
            kb, ko = blk_of[kc]
            ks = slice(ko, ko + 128)
            S.op("pe", lambda e: e.matmul(OB[:, 0:W], lhsT=V_b[kb][:, ks], rhs=pt[:, 0:W], start=(kc == 0), stop=(kc == nk - 1)),
                 reads=[V_b[kb], pt], writes=[OB], inc=False)
            S.op("pe", lambda e: e.matmul(DB[:, 0:W], lhsT=(ones0b if kc == 0 else onesb)[:], rhs=pt[:, 0:W],
                                          start=(kc == 0), stop=(kc == nk - 1)), reads=[ones0b, onesb, pt], writes=[DB])
        pend = scores(0)
        for kc in range(nk):
            nxt = scores(kc + 1) if kc + 1 < nk else None
            pv(kc, pend)
            pend = nxt
        y = yo.next()
        S.op("dve", lambda e: e.tensor_scalar(out=rden[:, 0:W], in0=DB[:, 0:W], scalar1=1e-30, scalar2=None, op0=ALU.max),
             writes=[rden, DB])
        S.op("dve", lambda e: e.reciprocal(rden[:, 0:W], rden[:, 0:W]), reads=[rden], writes=[rden])
        S.op("dve", lambda e: e.tensor_tensor(out=y[:, 0:W], in0=OB[:, 0:W], in1=rden[:, 0:W], op=ALU.mult),
             reads=[rden], writes=[y, OB])
        P.store(yT_d[:, q0:q0 + W], y, y[:, 0:W])

    def prep_iter(it):
        prep_kv(2 * it + 1); prep_kv(2 * it + 2); prep_q(it)
    prep_kv(0); prep_q(-1)
    la = S.record(); attn(-1); S.stop_record()
    lp = S.record(); prep_iter(0); S.stop_record()
    S.replay_weighted([la, lp])
    for it in range(8):
        la = S.record(); attn(it); S.stop_record()
        if it + 1 < 8:
            lp = S.record(); prep_iter(it + 1); S.stop_record()
            S.replay_weighted([la, lp])
        else:
            S.replay_weighted([la])
    return P.finish()


def mla_qpos(core):
    par = core // 4
    pos = [np.arange(128)]
    for it in range(8):
        cb = 8 * it + 1 + 4 * par
        pos.append(cb * 128 + np.arange(512))
    return np.concatenate(pos)


def prep_mla(up, p, l, core):
    h = core % 4
    par = core // 4
    cq = up[:, 0:384]; ckv = up[:, 384:512]; kpe = up[:, 512:576]
    cos2, sin2 = rope_tables(64)
    qpos = mla_qpos(core)
    wq = p['mla_w_q_up'][l][:, h * 192:(h + 1) * 192]
    wkv = p['mla_w_kv_up'][l][:, h * 256:(h + 1) * 256]
    kcl = lambda w: C_(w.reshape(3, 128, w.shape[1]).transpose(1, 0, 2).reshape(128, 3 * w.shape[1]))
    gq = p['mla_qk_norm_q'][l]; gk = p['mla_qk_norm_k'][l]
    col = lambda v: C_(v[:, None].astype(np.float32))
    idx = np.arange(128)
    tri = (idx[:, None] <= idx[None, :]).astype(np.float32)
    d = np.zeros((4, 128, 4, 128), np.float32)
    for k in range(4):
        for qi in range(4):
            if qi > k:
                d[k, :, qi, :] = 1.0
            elif qi == k:
                d[k, :, qi, :] = tri
    d = d.reshape(4, 128, 512)
    Z = np.zeros((4, 128, 512), np.float32); O = np.ones((4, 128, 512), np.float32)
    m8 = np.concatenate([d, Z], 0) if par == 0 else np.concatenate([O, d], 0)
    mask8 = C_(m8.transpose(1, 0, 2).reshape(128, 8 * 512))
    ones0 = np.ones((128, 128), np.float32); ones0[:PAD] = 0
    cqq = cq[qpos]
    return {
        "cqT0": C_(cqq[:, 0:128].T), "cqT1": C_(cqq[:, 128:256].T), "cqT2": C_(cqq[:, 256:384].T),
        "cosqT": C_(cos2[qpos].T), "sinqT": C_(sin2[qpos].T),
        "ckvT": C_(ckv.T), "kpeT": C_(kpe.T), "kpesT": C_(swap_halves(kpe).T),
        "cosT": C_(cos2.T), "sinT": C_(sin2.T),
        "wq_n": kcl(wq[:, 0:128]), "wq_r": kcl(wq[:, 128:192]), "wq_rs": kcl(swap_halves(wq[:, 128:192])),
        "wk": C_(wkv[:, 0:128]), "wv": C_(wkv[:, 128:256]), "ones0": ones0,
        "gqn": C_(p['mla_q_norm'][l].reshape(3, 128).T), "gkv": col(p['mla_kv_norm'][l]),
        "gq_n": col(gq[0:128]), "gq_r": col(gq[128:192]), "gq_rs": col(swap_halves(gq[None, 128:192])[0]),
        "gk_n": col(gk[0:128]), "gk_r": col(gk[128:192]), "gk_rs": col(swap_halves(gk[None, 128:192])[0]),
        "mask8": mask8, "mtri": C_(tri), "vcol": C_(ones0[:, 0:1]),
    }


RWKV_FP32R = False


def build_rwkv():
    P = MProg()
    S = P.S
    B = P.banks
    st = {}
    for nm, rows, tot, w in (("r", 128, NCH * 64, 256), ("k", 128, NCH * 64, 256), ("v", 128, NCH * 64, 256)):
        st[nm] = Stream(P, nm + "_tok", rows, tot, w)
        st[nm + "p"] = Stream(P, nm + "p_tok", rows, tot, w)
    for nm, rows in (("wd", 32), ("ad", 32), ("gd", 64)):
        st[nm] = Stream(P, nm + "T", rows, LP, 512)
        st[nm + "p"] = Stream(P, nm + "pT", rows, LP, 512)
    mu_r = P.load("mu_r", [128, 256]); mu_k = P.load("mu_k", [128, 256]); mu_v = P.load("mu_v", [128, 256])
    mu_wd = P.load("mu_wd", [32, 1]); mu_ad = P.load("mu_ad", [32, 1]); mu_gd = P.load("mu_gd", [64, 1])
    w2h = P.load("w2h", [32, 64]); a2h = P.load("a2h", [32, 64]); g2h = P.load("g2h", [64, 64])
    w0t = P.load("w0t", [128, 64]); a0t = P.load("a0t", [128, 64]); kkt = P.load("kkt", [128, 64])
    kat = P.load("kat", [128, 64]); rkt = P.load("rkt", [128, 64]); lng = P.load("lng", [64, 1])
    TriU = P.load("TriU", [128, 128]); SL = P.load("SL", [128, 128]); SU = P.load("SU", [128, 128])
    Id = P.load("Ident", [128, 128])
    yT_d = P.dout("yT", [64, LP])
    ones64 = S.sb([64, 64], F32, "ones64")
    S.op("pool", lambda e: e.memset(ones64[:], 1.0 / 64), writes=[ones64])
    NX = 6
    X = [S.sb([64, 64], F32, "X") for _ in range(NX)]
    S.op("pool", lambda e: e.memset(X[0][:], 0.0), writes=[X[0]])

    def T(shape, name):
        return S.sb(shape, F32, name)
    blkbufs = []
    for _ in range(2):
        blkbufs.append(dict(rs=T([128, 256], "rs"), ks=T([128, 256], "ks"), vs=T([128, 256], "vs"),
                            tw=T([32, 512], "tw"), ads=T([32, 512], "ads"), sg=T([64, 512], "sg"),
                            gT=T([64, 512], "gT"), yb=T([64, 512], "yblk")))
    dtm = T([128, 256], "dtm"); d32 = T([64, 512], "d32")
    names = ["ld", "a", "kk", "kkn", "kmod", "bb", "cs_e", "Eg", "Eneg", "Ege", "Kt", "Bh", "Kh", "Rt", "t3", "Vs", "SA", "U_"]
    lanes_buf = []
    for ln in range(4):
        L = dict(c64={n: T([128, 64], n) for n in names},
                 col={n: T([128, 1], n) for n in ["ss", "rn", "sbon"]},
                 fm={n: T([64, 128], n) for n in ["KtT", "BhT", "KhT", "RtT", "WT", "bonT", "oT", "oc", "osq", "ors"]},
                 sq={n: T([128, 128], n) for n in ["Pa", "PTa", "Pb", "PTb", "A", "MakT", "MrbT", "MrkT"]},
                 gam=T([64, 1], "gam"), Xg=T([64, 64], "Xg"), a=B[2 * ln], b=B[2 * ln + 1])
        lanes_buf.append(L)

    F32R = mybir.dt.float32r

    def rr(ap):
        return ap.bitcast(F32R) if RWKV_FP32R else ap

    def mm(pb, pap, lhsT, rhs, reads, start=True, stop=True, inc=True, fast=False):
        if fast and RWKV_FP32R:
            lhsT = lhsT.bitcast(F32R); rhs = rhs.bitcast(F32R)
        S.op("pe", lambda e: e.matmul(pap, lhsT=lhsT, rhs=rhs, start=start, stop=stop), reads=reads, writes=[pb], inc=inc)

    def tt(o, oap, a, aap, b_, bap, op, extra_w=()):
        S.op("dve", lambda e: e.tensor_tensor(out=oap, in0=aap, in1=bap, op=op), reads=[a, b_], writes=[o] + list(extra_w))

    def chunk(L, bb, ci, c):
        D_ = L["c64"]; col = L["col"]; fm = L["fm"]; sq = L["sq"]; gam = L["gam"]; Xg = L["Xg"]
        Ga = L["a"]; Gb = L["b"]
        rs_, ks_, vs_, tw, ads, gT, yb = bb["rs"], bb["ks"], bb["vs"], bb["tw"], bb["ads"], bb["gT"], bb["yb"]
        s64 = slice(ci * 64, (ci + 1) * 64)
        sl = slice(ci * 128, (ci + 1) * 128)
        r_ap, k_ap, v_ap = rs_[:, s64], ks_[:, s64], vs_[:, s64]
        mm(Ga, Ga[:, 0:64], tw[:, sl], w2h[:], [tw, w2h])
        tt(D_["ld"], D_["ld"][:], w0t, w0t[:], w0t, Ga[:, 0:64], ALU.add, extra_w=[Ga])
        S.op("act", lambda e: e.activation(out=D_["ld"][:], in_=D_["ld"][:], func=AF.Sigmoid), reads=[D_["ld"]], writes=[D_["ld"]])
        S.op("dve", lambda e: e.tensor_scalar(out=D_["ld"][:], in0=D_["ld"][:], scalar1=float(-np.exp(-0.5)), scalar2=None,
                                              op0=ALU.mult), reads=[D_["ld"]], writes=[D_["ld"]])
        mm(Ga, Ga[:, 64:128], ads[:, sl], a2h[:], [ads, a2h])
        tt(D_["a"], D_["a"][:], a0t, a0t[:], a0t, Ga[:, 64:128], ALU.add, extra_w=[Ga])
        S.op("act", lambda e: e.activation(out=D_["a"][:], in_=D_["a"][:], func=AF.Sigmoid), reads=[D_["a"]], writes=[D_["a"]])
        tt(D_["kk"], D_["kk"][:], ks_, k_ap, kkt, kkt[:], ALU.mult)
        S.op("act", lambda e: e.activation(out=D_["t3"][:], in_=D_["kk"][:], func=AF.Square, accum_out=col["ss"][:, 0:1]),
             reads=[D_["kk"]], writes=[D_["t3"], col["ss"]])
        S.op("act", lambda e: e.activation(out=col["rn"][:], in_=col["ss"][:], func=AF.Sqrt), reads=[col["ss"]], writes=[col["rn"]])
        S.op("dve", lambda e: e.tensor_scalar(out=col["rn"][:], in0=col["rn"][:], scalar1=1e-12, scalar2=None, op0=ALU.max),
             reads=[col["rn"]], writes=[col["rn"]])
        S.op("dve", lambda e: e.reciprocal(col["rn"][:], col["rn"][:]), reads=[col["rn"]], writes=[col["rn"]])
        S.op("dve", lambda e: e.tensor_scalar(out=D_["kkn"][:], in0=D_["kk"][:], scalar1=col["rn"][:, 0:1], scalar2=None,
                                              op0=ALU.mult), reads=[D_["kk"], col["rn"]], writes=[D_["kkn"]])
        S.op("dve", lambda e: e.scalar_tensor_tensor(out=D_["kmod"][:], in0=D_["a"][:], scalar=-1.0, in1=kat[:],
                                                     op0=ALU.add, op1=ALU.mult), reads=[D_["a"], kat], writes=[D_["kmod"]])
        S.op("dve", lambda e: e.scalar_tensor_tensor(out=D_["kmod"][:], in0=D_["kmod"][:], scalar=1.0, in1=k_ap,
                                                     op0=ALU.add, op1=ALU.mult), reads=[D_["kmod"], ks_], writes=[D_["kmod"]])
        tt(D_["bb"], D_["bb"][:], D_["kkn"], D_["kkn"][:], D_["a"], D_["a"][:], ALU.mult)
        tt(D_["t3"], D_["t3"][:], rs_, r_ap, rkt, rkt[:], ALU.mult)
        S.op("dve", lambda e: e.scalar_tensor_tensor(out=D_["t3"][:], in0=D_["t3"][:], scalar=1.0, in1=D_["kmod"][:],
                                                     op0=ALU.mult, op1=ALU.mult, accum_out=col["sbon"][:, 0:1]),
             reads=[D_["t3"], D_["kmod"]], writes=[D_["t3"], col["sbon"]])
        S.op("dve", lambda e: e.tensor_scalar(out=D_["Vs"][:], in0=v_ap, scalar1=col["sbon"][:, 0:1], scalar2=None,
                                              op0=ALU.mult), reads=[vs_, col["sbon"]], writes=[D_["Vs"]])
        mm(Ga, Ga[:, 128:192], TriU[:], D_["ld"][:], [TriU, D_["ld"]])
        mm(Ga, Ga[0:64, 192:193], D_["ld"][:], TriU[:, 127:128], [TriU, D_["ld"]])
        S.op("act", lambda e: e.activation(out=D_["Eg"][:], in_=Ga[:, 128:192], func=AF.Exp), writes=[D_["Eg"], Ga])
        S.op("act", lambda e: e.activation(out=D_["Eneg"][:], in_=Ga[:, 128:192], func=AF.Exp, scale=-1.0), writes=[D_["Eneg"], Ga])
        S.op("act", lambda e: e.activation(out=gam[:], in_=Ga[0:64, 192:193], func=AF.Exp), writes=[gam, Ga])
        tt(D_["cs_e"], D_["cs_e"][:], D_["ld"], Ga[:, 128:192], D_["ld"], D_["ld"][:], ALU.subtract, extra_w=[Ga])
        S.op("act", lambda e: e.activation(out=D_["Ege"][:], in_=D_["cs_e"][:], func=AF.Exp), reads=[D_["cs_e"]], writes=[D_["Ege"]])
        tt(D_["Kt"], D_["Kt"][:], D_["kkn"], D_["kkn"][:], D_["Ege"], D_["Ege"][:], ALU.mult)
        tt(D_["Bh"], D_["Bh"][:], D_["bb"], D_["bb"][:], D_["Eneg"], D_["Eneg"][:], ALU.mult)
        tt(D_["Kh"], D_["Kh"][:], D_["kmod"], D_["kmod"][:], D_["Eneg"], D_["Eneg"][:], ALU.mult)
        tt(D_["Rt"], D_["Rt"][:], rs_, r_ap, D_["Eg"], D_["Eg"][:], ALU.mult)
        for i, src in enumerate(("Kt", "Bh", "Kh", "Rt")):
            mm(Gb, Gb[0:64, i * 128:(i + 1) * 128], D_[src][:], Id[:], [D_[src], Id])
        for i, dst in enumerate(("KtT", "BhT", "KhT", "RtT")):
            S.op("act", lambda e, i=i, dst=dst: e.activation(out=fm[dst][:], in_=Gb[0:64, i * 128:(i + 1) * 128], func=AF.Copy),
                 writes=[fm[dst], Gb])
        mm(Ga, Ga[:, 0:128], fm["BhT"][:], fm["KtT"][:], [fm["BhT"], fm["KtT"]])
        mm(Ga, Ga[:, 128:256], fm["KtT"][:], fm["BhT"][:], [fm["BhT"], fm["KtT"]])
        mm(Ga, Ga[:, 256:384], fm["KhT"][:], fm["KtT"][:], [fm["KhT"], fm["KtT"]])
        mm(Gb, Gb[:, 0:128], fm["BhT"][:], fm["RtT"][:], [fm["BhT"], fm["RtT"]])
        mm(Gb, Gb[:, 128:256], fm["KhT"][:], fm["RtT"][:], [fm["KhT"], fm["RtT"]])
        mm(Gb, Gb[0:64, 256:384], D_["Vs"][:], Id[:], [D_["Vs"], Id])
        S.op("dve", lambda e: e.scalar_tensor_tensor(out=rr(sq["Pa"][:]), in0=Ga[:, 0:128], scalar=-1.0, in1=SU[:],
                                                     op0=ALU.mult, op1=ALU.mult), reads=[SU], writes=[sq["Pa"], Ga])
        S.op("dve", lambda e: e.scalar_tensor_tensor(out=rr(sq["PTa"][:]), in0=Ga[:, 128:256], scalar=-1.0, in1=SL[:],
                                                     op0=ALU.mult, op1=ALU.mult), reads=[SL], writes=[sq["PTa"], Ga])
        tt(sq["MakT"], sq["MakT"][:], SU, Ga[:, 256:384], SU, SU[:], ALU.mult, extra_w=[Ga])
        tt(sq["MrbT"], sq["MrbT"][:], TriU, Gb[:, 0:128], TriU, TriU[:], ALU.mult, extra_w=[Gb])
        tt(sq["MrkT"], sq["MrkT"][:], TriU, Gb[:, 128:256], TriU, TriU[:], ALU.mult, extra_w=[Gb])
        S.op("act", lambda e: e.activation(out=fm["bonT"][:], in_=Gb[0:64, 256:384], func=AF.Copy), writes=[fm["bonT"], Gb])
        tt(sq["A"], rr(sq["A"][:]), Id, Id[:], sq["Pa"], sq["Pa"][:], ALU.add)
        Pc, PTc, Pn, PTn = "Pa", "PTa", "Pb", "PTb"
        for lvl in range(6):
            G = Ga if lvl % 2 == 0 else Gb
            mm(G, G[:, 0:128], sq[PTc][:], sq[Pc][:], [sq[PTc], sq[Pc]], fast=True)
            mm(G, G[:, 128:256], sq[Pc][:], sq[PTc][:], [sq[PTc], sq[Pc]], fast=True)
            S.op("act", lambda e, Pn=Pn, G=G: e.activation(out=rr(sq[Pn][:]), in_=G[:, 0:128], func=AF.Copy), writes=[sq[Pn], G])
            S.op("act", lambda e, PTn=PTn, G=G: e.activation(out=rr(sq[PTn][:]), in_=G[:, 128:256], func=AF.Copy), writes=[sq[PTn], G])
            mm(G, G[:, 256:384], sq[PTn][:], sq["A"][:], [sq[PTn], sq["A"]], fast=True)
            tt(sq["A"], rr(sq["A"][:]), sq["A"], sq["A"][:], sq["A"], G[:, 256:384], ALU.add, extra_w=[G])
            Pc, PTc, Pn, PTn = Pn, PTn, Pc, PTc
        mm(Gb, Gb[:, 0:64], sq["MakT"][:], v_ap, [sq["MakT"], vs_])
        S.op("act", lambda e: e.activation(out=D_["t3"][:], in_=Gb[:, 0:64], func=AF.Copy), writes=[D_["t3"], Gb])
        mm(Gb, Gb[:, 64:128], sq["A"][:], D_["t3"][:], [sq["A"], D_["t3"]])
        S.op("act", lambda e: e.activation(out=D_["U_"][:], in_=Gb[:, 64:128], func=AF.Copy, scale=-1.0), writes=[D_["U_"], Gb])
        mm(Gb, Gb[0:64, 128:256], D_["Kt"][:], sq["A"][:], [D_["Kt"], sq["A"]])
        S.op("act", lambda e: e.activation(out=fm["WT"][:], in_=Gb[0:64, 128:256], func=AF.Copy), writes=[fm["WT"], Gb])
        Xp = X[c % NX]; Xn = X[(c + 1) % NX]
        mm(Ga, Ga[:, 0:64], fm["WT"][:], Xp[:], [fm["WT"], Xp])
        tt(D_["SA"], D_["SA"][:], D_["U_"], D_["U_"][:], D_["U_"], Ga[:, 0:64], ALU.subtract, extra_w=[Ga])
        S.op("dve", lambda e: e.tensor_scalar(out=Xg[:], in0=Xp[:], scalar1=gam[:, 0:1], scalar2=None, op0=ALU.mult),
             reads=[Xp, gam], writes=[Xg])
        mm(Ga, Ga[0:64, 64:128], D_["Kh"][:], v_ap, [D_["Kh"], vs_], start=True, stop=False, inc=False)
        mm(Ga, Ga[0:64, 64:128], D_["Bh"][:], D_["SA"][:], [D_["Bh"], D_["SA"]], start=False, stop=True)
        S.op("dve", lambda e: e.scalar_tensor_tensor(out=Xn[:], in0=Ga[0:64, 64:128], scalar=gam[:, 0:1], in1=Xg[:],
                                                     op0=ALU.mult, op1=ALU.add), reads=[gam, Xg], writes=[Xn, Ga])
        mm(Gb, Gb[0:64, 0:128], Xp[:], fm["RtT"][:], [Xp, fm["RtT"]], start=True, stop=False, inc=False)
        mm(Gb, Gb[0:64, 0:128], D_["SA"][:], sq["MrbT"][:], [D_["SA"], sq["MrbT"]], start=False, stop=False, inc=False)
        mm(Gb, Gb[0:64, 0:128], v_ap, sq["MrkT"][:], [vs_, sq["MrkT"]], start=False, stop=True)
        S.op("act", lambda e: e.activation(out=fm["oT"][:], in_=Gb[0:64, 0:128], func=AF.Copy), writes=[fm["oT"], Gb])
        mm(Gb, Gb[0:64, 128:256], ones64[:], fm["oT"][:], [ones64, fm["oT"]])
        tt(fm["oc"], fm["oc"][:], fm["oT"], fm["oT"][:], fm["oT"], Gb[0:64, 128:256], ALU.subtract, extra_w=[Gb])
        S.op("act", lambda e: e.activation(out=fm["osq"][:], in_=fm["oc"][:], func=AF.Square), reads=[fm["oc"]], writes=[fm["osq"]])
        mm(Gb, Gb[0:64, 256:384], ones64[:], fm["osq"][:], [ones64, fm["osq"]])
        S.op("act", lambda e: e.activation(out=fm["ors"][:], in_=Gb[0:64, 256:384], func=AF.Sqrt, bias=64e-5), writes=[fm["ors"], Gb])
        S.op("dve", lambda e: e.reciprocal(fm["ors"][:], fm["ors"][:]), reads=[fm["ors"]], writes=[fm["ors"]])
        S.op("dve", lambda e: e.scalar_tensor_tensor(out=fm["oc"][:], in0=fm["oc"][:], scalar=lng[:, 0:1], in1=fm["ors"][:],
                                                     op0=ALU.mult, op1=ALU.mult), reads=[fm["oc"], lng, fm["ors"]], writes=[fm["oc"]])
        tt(fm["oc"], fm["oc"][:], fm["oc"], fm["oc"][:], fm["bonT"], fm["bonT"][:], ALU.add)
        tt(fm["oc"], fm["oc"][:], fm["oc"], fm["oc"][:], gT, gT[:, sl], ALU.mult)
        S.op("pool", lambda e: e.tensor_copy(yb[:, sl], fm["oc"][:]), reads=[fm["oc"]], writes=[yb])

    blocks = [(0, 1)] + [(1 + 4 * i, 4) for i in range(16)]
    for bi, (cb, ncb) in enumerate(blocks):
        W = ncb * 128
        w2 = ncb * 64
        p0 = cb * 128
        bb = blkbufs[bi % 2]
        for nm, dst, mu in (("r", bb["rs"], mu_r), ("k", bb["ks"], mu_k), ("v", bb["vs"], mu_v)):
            x = st[nm].get(cb * 64, w2); xp = st[nm + "p"].get(cb * 64, w2)
            tt(dtm, dtm[:, 0:w2], xp, xp[:, 0:w2], x, x[:, 0:w2], ALU.subtract)
            tt(dtm, dtm[:, 0:w2], dtm, dtm[:, 0:w2], mu, mu[:, 0:w2], ALU.mult)
            tt(dst, dst[:, 0:w2], dtm, dtm[:, 0:w2], x, x[:, 0:w2], ALU.add)
        for nm, dst, mu, rows, fn in (("wd", bb["tw"], mu_wd, 32, AF.Tanh), ("ad", bb["ads"], mu_ad, 32, None),
                                      ("gd", bb["sg"], mu_gd, 64, AF.Sigmoid)):
            x = st[nm].get(p0, W); xp = st[nm + "p"].get(p0, W)
            tt(d32, d32[0:rows, 0:W], xp, xp[:, 0:W], x, x[:, 0:W], ALU.subtract)
            S.op("dve", lambda e, dst=dst, mu=mu, x=x, rows=rows: e.scalar_tensor_tensor(
                out=dst[:, 0:W], in0=d32[0:rows, 0:W], scalar=mu[:, 0:1], in1=x[:, 0:W], op0=ALU.mult, op1=ALU.add),
                reads=[d32, mu, x], writes=[dst])
            if fn is not None:
                S.op("act", lambda e, dst=dst, fn=fn: e.activation(out=dst[:, 0:W], in_=dst[:, 0:W], func=fn), reads=[dst], writes=[dst])
        G0 = lanes_buf[0]["a"]
        mm(G0, G0[0:64, 0:W], g2h[:], bb["sg"][:, 0:W], [g2h, bb["sg"]])
        S.op("act", lambda e: e.activation(out=bb["gT"][:, 0:W], in_=G0[0:64, 0:W], func=AF.Copy), writes=[bb["gT"], G0])
        lanes = []
        for ci in range(ncb):
            rec = S.record()
            chunk(lanes_buf[ci], bb, ci, cb + ci)
            S.stop_record()
            lanes.append(rec)
        S.replay(lanes, skew=10)
        P.store(yT_d[:, p0:p0 + W], bb["yb"], bb["yb"][:, 0:W])
    return P.finish()


def prep_rwkv(up, p, l, core):
    j = core
    o = 576 + 1544 + 1536
    prev = np.concatenate([np.zeros((1, up.shape[1]), np.float32), up[:-1]], 0)
    hs = slice(j * 64, (j + 1) * 64)
    mu = p['rwkv_mu'][l]
    rep = lambda v, n=128: C_(np.tile(v[None, :].astype(np.float32), (n, 1)))
    idx = np.arange(128)
    TriU = (idx[:, None] <= idx[None, :]).astype(np.float32)
    m = {}
    for nm, off in (("r", 0), ("k", 512), ("v", 1024)):
        cs_ = slice(o + off + j * 64, o + off + (j + 1) * 64)
        m[nm + "_tok"] = tok_layout(up[:, cs_]); m[nm + "p_tok"] = tok_layout(prev[:, cs_])
        m["mu_" + nm] = rep(np.tile(mu[off + j * 64: off + (j + 1) * 64], 4))
    for nm, off, wdt in (("wd", 1536, 32), ("ad", 1568, 32), ("gd", 1600, 64)):
        cs_ = slice(o + off, o + off + wdt)
        m[nm + "T"] = C_(up[:, cs_].T); m[nm + "pT"] = C_(prev[:, cs_].T)
        m["mu_" + nm] = C_(mu[off:off + wdt][:, None].astype(np.float32))
    m["w2h"] = C_(p['rwkv_w2'][l][:, hs]); m["a2h"] = C_(p['rwkv_a2'][l][:, hs]); m["g2h"] = C_(p['rwkv_g2'][l][:, hs])
    m["w0t"] = rep(p['rwkv_w0'][l][hs]); m["a0t"] = rep(p['rwkv_a0'][l][hs]); m["kkt"] = rep(p['rwkv_k_k'][l][hs])
    m["kat"] = rep(p['rwkv_k_a'][l][hs]); m["rkt"] = rep(p['rwkv_r_k'][l][j]); m["lng"] = C_(p['rwkv_ln'][l][j][:, None].astype(np.float32))
    m["TriU"] = C_(TriU); m["SL"] = C_((idx[:, None] > idx[None, :]).astype(np.float32))
    m["SU"] = C_((idx[:, None] < idx[None, :]).astype(np.float32)); m["Ident"] = np.eye(128, dtype=np.float32)
    return m


_PROGS = {}


def _prog(name, fn):
    if name not in _PROGS:
        r = fn()
        _PROGS[name] = r[0] if isinstance(r, tuple) else r
    return _PROGS[name]


def _run(nc, maps):
    res = run_bass_kernel_spmd(nc, maps, core_ids=list(range(NCORES)))
    return res.results


def tile_w(W, nk=KC):
    ncols = W.shape[1]
    nt = (ncols + 127) // 128
    if nt * 128 != ncols:
        W = np.concatenate([W, np.zeros((W.shape[0], nt * 128 - ncols), np.float32)], 1)
    return C_(W.reshape(nk, 128, nt, 128).transpose(2, 1, 0, 3).reshape(nt, 128, nk * 128))


def tile_wd(W):
    return C_(W.reshape(FC // 4, 4, 128, KC, 128).transpose(0, 3, 2, 1, 4).reshape((FC // 4) * KC, 128, 4 * 128))


def _gl(v, n):
    return C_(np.asarray(v, np.float32).reshape(n, 128).T)


def kernel(**inp):
    p = {k: np.asarray(v, np.float32) for k, v in inp.items()}
    x = p['x'][0]
    meta = p['meta_tokens']
    depth = p['w_in'].shape[0]
    hT = [C_(np.concatenate([meta, x[c * 1024:(c + 1) * 1024]], 0).T) for c in range(NCORES)]
    nc_a = _prog("Ta", build_Ta)
    nc_b = _prog("Tb", build_Tb)
    nc_mla = _prog("mla", build_mla)
    nc_ssd = _prog("ssd", build_ssd)
    nc_ret = _prog("ret", build_ret)
    nc_rwkv = _prog("rwkv", build_rwkv)
    for l in range(depth):
        g1 = _gl(p['ffn1_norm'][l], KC); gm = _gl(p['mix_norm'][l], KC)
        wg1 = tile_w(p['ffn1_w_gate'][l]); wu1 = tile_w(p['ffn1_w_up'][l]); wd1 = tile_wd(p['ffn1_w_down'][l])
        win = tile_w(p['w_in'][l])
        res = _run(nc_a, [{"h_in": hT[c], "g1": g1, "wg": wg1, "wu": wu1,
                           "wd": wd1, "gm": gm, "win": win} for c in range(NCORES)])
        hT = [C_(res[c]['h_out']) for c in range(NCORES)]
        up = np.zeros((LP, INW), np.float32)
        up[PAD:PAD + 16] = res[0]['uT'][:, 0:16].T
        for c in range(NCORES):
            up[PAD + 16 + c * 1024:PAD + 16 + (c + 1) * 1024] = res[c]['uT'][:, 16:].T
        del res
        y = np.zeros((LP, D), np.float32)
        r = _run(nc_mla, [prep_mla(up, p, l, c) for c in range(NCORES)])
        for c in range(NCORES):
            h = c % 4
            qp = mla_qpos(c)
            yt = r[c]['yT'].T
            if c < 4:
                y[qp, h * 128:(h + 1) * 128] = yt
            else:
                y[qp[128:], h * 128:(h + 1) * 128] = yt[128:]
        r = _run(nc_ssd, [prep_ssd(up, p, l, c) for c in range(NCORES)])
        for j in range(8):
            y[:, 512 + j * 64:512 + (j + 1) * 64] = r[j]['yT'].T
        r = _run(nc_ret, [prep_ret(up, p['ret_norm'][l], c) for c in range(NCORES)])
        for h in range(4):
            y[:, 1024 + h * 128:1024 + (h + 1) * 128] = r[h]['yT'].T
        r = _run(nc_rwkv, [prep_rwkv(up, p, l, c) for c in range(NCORES)])
        for j in range(8):
            y[:, 1536 + j * 64:1536 + (j + 1) * 64] = r[j]['yT'].T
        del r, up
        g2 = _gl(p['ffn2_norm'][l], KC); gs = _gl(p['ssd_norm'][l], 4)
        del wg1, wu1, wd1, win
        wg2 = tile_w(p['ffn2_w_gate'][l]); wu2 = tile_w(p['ffn2_w_up'][l]); wd2 = tile_wd(p['ffn2_w_down'][l])
        wo = tile_w(p['w_out'][l])
        maps = []
        for c in range(NCORES):
            yc = np.concatenate([y[PAD:PAD + 16], y[PAD + 16 + c * 1024:PAD + 16 + (c + 1) * 1024]], 0)
            maps.append({"h_in": hT[c], "yT": C_(yc.T), "gs": gs, "wout": wo, "g2": g2,
                         "wg": wg2, "wu": wu2, "wd": wd2})
        res = _run(nc_b, maps)
        hT = [C_(res[c]['h_out']) for c in range(NCORES)]
        del res, maps, y, wg2, wu2, wd2, wo
    out = np.concatenate([hT[c][:, 16:].T for c in range(NCORES)], 0)
    return C_(out[None].astype(np.float32))
```

```python
import numpy as np
import concourse.bass as bass
import concourse.mybir as mybir
from concourse.bass_utils import run_bass_kernel_spmd

F32 = mybir.dt.float32
BF16 = mybir.dt.bfloat16
AF = mybir.ActivationFunctionType
ALU = mybir.AluOpType
AX = mybir.AxisListType

NCORES = 8
D = 2048
KC = 16
DFF = 5632
FC = 44
NT = 1040
CGS = [(0, 16), (16, 528), (528, 1040)]
INW = 5320
EPS = 1e-6
EPOCH = 30000


class Buf:
    __slots__ = ("ap", "w", "r", "name", "dsem")

    def __init__(self, ap, name=""):
        self.ap = ap
        self.w = None
        self.r = {}
        self.name = name
        self.dsem = None

    def __getitem__(self, idx):
        return self.ap[idx]


class Sched:
    def __init__(self, nc):
        self.nc = nc
        self.engs = {"pe": nc.tensor, "dve": nc.vector, "act": nc.scalar,
                     "pool": nc.gpsimd, "sp": nc.sync}
        self.sems = []
        self.owner = {}
        self.cur = {}
        self.cnt = {}
        self.seen = {e: {} for e in self.engs}
        self.ninst = {e: 0 for e in self.engs}
        for e in ("pe", "dve", "act", "pool"):
            self._new_epoch(e)
        self.uid = 0

    def _alloc_sem(self, name, owner=None):
        h = self.nc.alloc_semaphore(name)
        self.sems.append(h)
        k = len(self.sems) - 1
        self.cnt[k] = 0
        self.owner[k] = owner
        return k

    def _new_epoch(self, e):
        self.cur[e] = self._alloc_sem("c_%s_%d" % (e, len(self.sems)), e)

    def new_dsem(self, name="d"):
        return self._alloc_sem("%s_%d" % (name, len(self.sems)))

    def _waits(self, e, reads, writes):
        need = {}
        for b in reads:
            if b.w is not None:
                k, v = b.w
                if need.get(k, 0) < v:
                    need[k] = v
        for b in writes:
            if b.w is not None:
                k, v = b.w
                if need.get(k, 0) < v:
                    need[k] = v
            for k, v in b.r.items():
                if need.get(k, 0) < v:
                    need[k] = v
        eng = self.engs[e]
        seen = self.seen[e]
        for k, v in need.items():
            if e == "pe" and self.owner[k] == "pe":
                continue
            if seen.get(k, 0) >= v:
                continue
            eng.wait_ge(self.sems[k], v)
            seen[k] = v

    def record(self):
        self._rec = []
        return self._rec

    def stop_record(self):
        self._rec = None

    def replay_weighted(self, lanes):
        self._rec = None
        n = max(len(l) for l in lanes) if lanes else 0
        pos = [0] * len(lanes)
        for i in range(n):
            for k, l in enumerate(lanes):
                tgt = ((i + 1) * len(l) + n - 1) // n
                while pos[k] < min(tgt, len(l)):
                    kind, a, kw = l[pos[k]]
                    (self.op if kind == "op" else self.dma)(*a, **kw)
                    pos[k] += 1

    def replay(self, lanes, skew=0):
        self._rec = None
        n = max(len(l) + k * skew for k, l in enumerate(lanes)) if lanes else 0
        for i in range(n):
            for k, l in enumerate(lanes):
                j = i - k * skew
                if 0 <= j < len(l):
                    kind, a, kw = l[j]
                    (self.op if kind == "op" else self.dma)(*a, **kw)

    def op(self, e, fn, reads=(), writes=(), inc=True):
        if getattr(self, "_rec", None) is not None:
            self._rec.append(("op", (e, fn, tuple(reads), tuple(writes), inc), {}))
            return None
        self._waits(e, reads, writes)
        ins = fn(self.engs[e])
        k = self.cur[e]
        self.ninst[e] += 1
        if inc:
            self.cnt[k] += 1
            ins.then_inc(self.sems[k], 1)
            tok = (k, self.cnt[k])
        else:
            tok = (k, self.cnt[k] + 1)
        for b in reads:
            if b.r.get(tok[0], 0) < tok[1]:
                b.r[tok[0]] = tok[1]
        for b in writes:
            b.w = tok
            b.r = {}
        if inc and self.cnt[k] >= EPOCH:
            self._new_epoch(e)
        return ins

    def dma(self, q, out_ap, in_ap, reads=(), writes=(), sem=None, is_out=False, **kw):
        if getattr(self, "_rec", None) is not None:
            kw2 = dict(kw); kw2.update(reads=tuple(reads), writes=tuple(writes), sem=sem, is_out=is_out)
            self._rec.append(("dma", (q, out_ap, in_ap), kw2))
            return None
        self._waits(q, reads, writes)
        if sem is None:
            for b in list(writes) + list(reads):
                if b.dsem is None:
                    b.dsem = self.new_dsem()
                sem = b.dsem
                break
        ins = self.engs[q].dma_start(out=out_ap, in_=in_ap, **kw)
        ins.then_inc(self.sems[sem], 16)
        self.cnt[sem] += 16
        self.ninst[q] += 1
        tok = (sem, self.cnt[sem])
        if is_out:
            if not hasattr(self, "out_sems"):
                self.out_sems = set()
            self.out_sems.add(sem)
        for b in reads:
            if b.r.get(tok[0], 0) < tok[1]:
                b.r[tok[0]] = tok[1]
        for b in writes:
            b.w = tok
            b.r = {}
        return sem

    def wait_all(self, e, semkeys):
        eng = self.engs[e]
        for k in semkeys:
            if self.cnt[k] > 0:
                eng.wait_ge(self.sems[k], self.cnt[k])

    def sb(self, shape, dt, name=None):
        self.uid += 1
        name = "%s_%d" % (name or "sb", self.uid)
        return Buf(self.nc.alloc_sbuf_tensor(name, list(shape), dt).ap(), name)

    def ps(self, shape, dt=F32, name=None):
        self.uid += 1
        name = "%s_%d" % (name or "ps", self.uid)
        return Buf(self.nc.alloc_psum_tensor(name, list(shape), dt).ap(), name)

    def sub(self, buf, ap):
        return Buf(ap, buf.name + "_s")


class Ring:
    def __init__(self, bufs):
        self.bufs = bufs
        self.i = 0

    def next(self):
        b = self.bufs[self.i % len(self.bufs)]
        self.i += 1
        return b


class TCtx:
    def __init__(self, S, n_ost=2):
        self.S = S
        nc = S.nc
        self.hT = [S.sb([128, NT], F32, "hT") for _ in range(KC)]
        self.xnT = [S.sb([128, NT], BF16, "xnT") for _ in range(KC)]
        self.actT = Ring([S.sb([128, NT], BF16, "actT") for _ in range(8)])
        self.wst = Ring([S.sb([128, KC, 128], F32, "wst") for _ in range(4)])
        self.wbf = Ring([S.sb([128, KC, 128], BF16, "wbf") for _ in range(6)])
        self.wdst = Ring([S.sb([128, 4, 128], F32, "wdst") for _ in range(3)])
        self.wdbf = Ring([S.sb([128, 4, 128], BF16, "wdbf") for _ in range(3)])
        self.tmp = Ring([S.sb([128, 512], F32, "tmp") for _ in range(3)])
        self.ost = Ring([S.sb([128, NT], F32, "ost") for _ in range(n_ost)]) if n_ost else None
        self.rstd = S.sb([128, NT], F32, "rstd")
        self.gcol = S.sb([128, KC], F32, "gcol")
        self.ones = S.sb([128, 128], F32, "ones")
        S.op("pool", lambda e: e.memset(self.ones[:], 1.0), writes=[self.ones])
        banks = [S.ps([128, 512], F32, "bank") for _ in range(8)]
        self.G = [banks[0], banks[1]]
        self.U = [banks[2], banks[3]]
        self.Dn = [banks[4], banks[5]]
        self.N = banks[6]
        self.M = banks[7]

    def pG(self, ci):
        return (self.M, self.M.ap[:, 0:16]) if ci == 0 else (self.G[ci - 1], self.G[ci - 1].ap[:, :])

    def pU(self, ci):
        return (self.M, self.M.ap[:, 16:32]) if ci == 0 else (self.U[ci - 1], self.U[ci - 1].ap[:, :])

    def pD(self, ci):
        return (self.M, self.M.ap[:, 32:48]) if ci == 0 else (self.Dn[ci - 1], self.Dn[ci - 1].ap[:, :])

    def pN(self, ci):
        return (self.M, self.M.ap[:, 48:64]) if ci == 0 else (self.N, self.N.ap[:, :])


def t_load_h(T, h_dram):
    S = T.S
    for kc in range(KC):
        S.dma("sp", T.hT[kc][:], h_dram[kc * 128:(kc + 1) * 128, :], writes=[T.hT[kc]])


def t_store_h(T, h_dram, sems):
    S = T.S
    for kc in range(KC):
        sems.append(S.dma("sp", h_dram[kc * 128:(kc + 1) * 128, :], T.hT[kc][:], reads=[T.hT[kc]]))


def t_rmsnorm(T, g_dram, src=None, nk=KC, dst=None, width=D):
    S = T.S
    src = src or T.hT
    dst = dst or T.xnT
    S.dma("sp", T.gcol[:, 0:nk], g_dram[:, 0:nk], writes=[T.gcol])
    for ci, (c0, c1) in enumerate(CGS):
        pb, pap = T.pN(ci)
        for kc in range(nk):
            t = T.tmp.next()
            S.op("act", lambda e, t=t, kc=kc: e.activation(out=t[:, 0:c1 - c0], in_=src[kc][:, c0:c1], func=AF.Square),
                 reads=[src[kc]], writes=[t])
            S.op("pe", lambda e, t=t, kc=kc: e.matmul(pap[:, 0:c1 - c0], lhsT=T.ones[:], rhs=t[:, 0:c1 - c0],
                                                       start=(kc == 0), stop=(kc == nk - 1)),
                 reads=[T.ones, t], writes=[pb], inc=True)
        S.op("act", lambda e: e.activation(out=T.rstd[:, c0:c1], in_=pap[:, 0:c1 - c0], func=AF.Sqrt,
                                           scale=1.0 / width, bias=EPS),
             writes=[T.rstd, pb])
    S.op("dve", lambda e: e.reciprocal(T.rstd[:], T.rstd[:]), reads=[T.rstd], writes=[T.rstd])
    for kc in range(nk):
        S.op("dve", lambda e, kc=kc: e.scalar_tensor_tensor(out=dst[kc][:], in0=src[kc][:], scalar=T.gcol[:, kc:kc + 1],
                                                            in1=T.rstd[:], op0=ALU.mult, op1=ALU.mult),
             reads=[src[kc], T.gcol, T.rstd], writes=[dst[kc]])


PROBE_NODMA = False
W_TWO_QUEUES = True
PROBE_NOMM = False


def t_load_w(T, w_dram, idx, ncol, nk=KC, cast_eng="act"):
    S = T.S
    if PROBE_NODMA and getattr(T, "_w0", None) is not None:
        return T._w0
    st = T.wst.next()
    bf = T.wbf.next()
    T._wq = getattr(T, "_wq", 0) + 1
    q = "sp" if (T._wq % 2 == 0 or not W_TWO_QUEUES) else "pool"
    S.dma(q, st[:, 0:nk, :], w_dram[idx].rearrange("p (kc f) -> p kc f", kc=nk), writes=[st])
    if cast_eng == "act":
        S.op("act", lambda e: e.activation(out=bf[:, 0:nk, :], in_=st[:, 0:nk, :], func=AF.Copy), reads=[st], writes=[bf])
    else:
        S.op(cast_eng, lambda e: e.tensor_copy(bf[:, 0:nk, :], st[:, 0:nk, :]), reads=[st], writes=[bf])
    T._w0 = bf
    return bf


def t_ffn(T, g_dram, wg, wu, wd):
    S = T.S
    t_rmsnorm(T, g_dram)
    SG = 4
    for sg in range(FC // SG):
        acts = []
        for fi in range(SG):
            fc = sg * SG + fi
            bg = t_load_w(T, wg, fc, 128)
            bu = t_load_w(T, wu, fc, 128)
            a = T.actT.next()
            acts.append(a)
            for ci, (c0, c1) in enumerate(CGS):
                n = c1 - c0
                gb, gap = T.pG(ci)
                ub, uap = T.pU(ci)
                for kc in range(KC):
                    S.op("pe", lambda e, kc=kc: e.matmul(gap[:, 0:n], lhsT=bg[:, kc, :], rhs=T.xnT[kc][:, c0:c1],
                                                          start=(kc == 0), stop=(kc == KC - 1)),
                         reads=[bg, T.xnT[kc]], writes=[gb], inc=(kc == KC - 1))
                for kc in range(KC):
                    S.op("pe", lambda e, kc=kc: e.matmul(uap[:, 0:n], lhsT=bu[:, kc, :], rhs=T.xnT[kc][:, c0:c1],
                                                          start=(kc == 0), stop=(kc == KC - 1)),
                         reads=[bu, T.xnT[kc]], writes=[ub], inc=(kc == KC - 1))
                t = T.tmp.next()
                S.op("act", lambda e, t=t: e.activation(out=t[:, 0:n], in_=gap[:, 0:n], func=AF.Silu),
                     writes=[t, gb])
                S.op("dve", lambda e, t=t: e.tensor_tensor(out=a[:, c0:c1], in0=t[:, 0:n], in1=uap[:, 0:n], op=ALU.mult),
                     reads=[t], writes=[a, ub])
        for dc in range(KC):
            st = T.wdst.next()
            bf = T.wdbf.next()
            S.dma("sp", st[:], wd[sg * KC + dc].rearrange("p (fi d) -> p fi d", fi=SG), writes=[st])
            S.op("pool", lambda e: e.tensor_copy(bf[:], st[:]), reads=[st], writes=[bf])
            for ci, (c0, c1) in enumerate(CGS):
                n = c1 - c0
                db, dap = T.pD(ci)
                for fi in range(SG):
                    S.op("pe", lambda e, fi=fi: e.matmul(dap[:, 0:n], lhsT=bf[:, fi, :], rhs=acts[fi][:, c0:c1],
                                                          start=(fi == 0), stop=(fi == SG - 1)),
                         reads=[bf, acts[fi]], writes=[db], inc=(fi == SG - 1))
                S.op("dve", lambda e: e.scalar_tensor_tensor(out=T.hT[dc][:, c0:c1], in0=dap[:, 0:n], scalar=0.5,
                                                            in1=T.hT[dc][:, c0:c1], op0=ALU.mult, op1=ALU.add),
                     writes=[T.hT[dc], db])


def t_proj(T, w_dram, ncols, rhsT, nk, sink):
    S = T.S
    nchunks = (ncols + 127) // 128
    for uc in range(nchunks):
        m = min(128, ncols - uc * 128)
        bw = t_load_w(T, w_dram, uc, m, nk=nk, cast_eng="pool")
        for ci, (c0, c1) in enumerate(CGS):
            n = c1 - c0
            gb, gap = T.pG(ci)
            for kc in range(nk):
                S.op("pe", lambda e, kc=kc: e.matmul(gap[0:m, 0:n], lhsT=bw[:, kc, 0:m], rhs=rhsT[kc][:, c0:c1],
                                                      start=(kc == 0), stop=(kc == nk - 1)),
                     reads=[bw, rhsT[kc]], writes=[gb], inc=(kc == nk - 1))
            sink(ci, c0, c1, uc, m, gb, gap)


def build_Ta():
    nc = bass.Bass("TRN2", target_bir_lowering=False)
    dt = lambda n, s, k: nc.dram_tensor(n, list(s), F32, kind=k).ap()
    h_in = dt("h_in", [D, NT], "ExternalInput")
    g1 = dt("g1", [128, KC], "ExternalInput")
    wg = dt("wg", [FC, 128, KC * 128], "ExternalInput")
    wu = dt("wu", [FC, 128, KC * 128], "ExternalInput")
    wd = dt("wd", [(FC // 4) * KC, 128, 4 * 128], "ExternalInput")
    gm = dt("gm", [128, KC], "ExternalInput")
    win = dt("win", [42, 128, KC * 128], "ExternalInput")
    h_out = dt("h_out", [D, NT], "ExternalOutput")
    uT = dt("uT", [INW, NT], "ExternalOutput")
    S = Sched(nc)
    T = TCtx(S)
    outs = []
    t_load_h(T, h_in)
    t_ffn(T, g1, wg, wu, wd)
    t_store_h(T, h_out, outs)
    t_rmsnorm(T, gm)
    cur = {}

    def sink(ci, c0, c1, uc, m, pb, pap):
        if ci == 0:
            cur["o"] = T.ost.next()
        o = cur["o"]
        n = c1 - c0
        S.op("act", lambda e: e.activation(out=o[0:m, c0:c1], in_=pap[0:m, 0:n], func=AF.Copy), writes=[o, pb])
        if ci == len(CGS) - 1:
            outs.append(S.dma("act", uT[uc * 128:uc * 128 + m, :], o[0:m, :], reads=[o]))

    t_proj(T, win, INW, T.xnT, KC, sink)
    S.wait_all("sp", sorted(set(outs)))
    return nc, S


def build_Tb():
    nc = bass.Bass("TRN2", target_bir_lowering=False)
    dt = lambda n, s, k: nc.dram_tensor(n, list(s), F32, kind=k).ap()
    h_in = dt("h_in", [D, NT], "ExternalInput")
    yT = dt("yT", [D, NT], "ExternalInput")
    gs = dt("gs", [128, 4], "ExternalInput")
    wout = dt("wout", [KC, 128, KC * 128], "ExternalInput")
    g2 = dt("g2", [128, KC], "ExternalInput")
    wg = dt("wg", [FC, 128, KC * 128], "ExternalInput")
    wu = dt("wu", [FC, 128, KC * 128], "ExternalInput")
    wd = dt("wd", [(FC // 4) * KC, 128, 4 * 128], "ExternalInput")
    h_out = dt("h_out", [D, NT], "ExternalOutput")
    S = Sched(nc)
    T = TCtx(S, n_ost=0)
    outs = []
    t_load_h(T, h_in)
    yst = [S.sb([128, NT], F32, "yst") for _ in range(4)]
    for j in range(4):
        S.dma("sp", yst[j][:], yT[(4 + j) * 128:(5 + j) * 128, :], writes=[yst[j]])
    t_rmsnorm(T, gs, src=yst, nk=4, dst=T.xnT[4:8], width=512)
    for kc in list(range(0, 4)) + list(range(8, 16)):
        st = yst[kc % 4]
        S.dma("sp", st[:], yT[kc * 128:(kc + 1) * 128, :], writes=[st])
        S.op("pool", lambda e, kc=kc, st=st: e.tensor_copy(T.xnT[kc][:], st[:]), reads=[st], writes=[T.xnT[kc]])

    def sink(ci, c0, c1, dc, m, pb, pap):
        n = c1 - c0
        S.op("dve", lambda e: e.tensor_tensor(out=T.hT[dc][:, c0:c1], in0=pap[:, 0:n], in1=T.hT[dc][:, c0:c1], op=ALU.add),
             writes=[T.hT[dc], pb])

    t_proj(T, wout, D, T.xnT, KC, sink)
    t_ffn(T, g2, wg, wu, wd)
    t_store_h(T, h_out, outs)
    S.wait_all("sp", sorted(set(outs)))
    return nc, S


def build_Tba():
    nc = bass.Bass("TRN2", target_bir_lowering=False)
    dt = lambda n, s, k: nc.dram_tensor(n, list(s), F32, kind=k).ap()
    h_in = dt("h_in", [D, NT], "ExternalInput")
    yT = dt("yT", [D, NT], "ExternalInput")
    gs = dt("gs", [128, 4], "ExternalInput")
    wout = dt("wout", [KC, 128, KC * 128], "ExternalInput")
    g2 = dt("g2", [128, KC], "ExternalInput")
    wg = dt("wg", [FC, 128, KC * 128], "ExternalInput")
    wu = dt("wu", [FC, 128, KC * 128], "ExternalInput")
    wd = dt("wd", [(FC // 4) * KC, 128, 4 * 128], "ExternalInput")
    g1 = dt("g1", [128, KC], "ExternalInput")
    wg1 = dt("wg1", [FC, 128, KC * 128], "ExternalInput")
    wu1 = dt("wu1", [FC, 128, KC * 128], "ExternalInput")
    wd1 = dt("wd1", [(FC // 4) * KC, 128, 4 * 128], "ExternalInput")
    gm = dt("gm", [128, KC], "ExternalInput")
    win = dt("win", [42, 128, KC * 128], "ExternalInput")
    h_out = dt("h_out", [D, NT], "ExternalOutput")
    uT = dt("uT", [INW, NT], "ExternalOutput")
    S = Sched(nc)
    T = TCtx(S, n_ost=0)
    outs = []
    t_load_h(T, h_in)
    yst = [S.sb([128, NT], F32, "yst") for _ in range(4)]
    for j in range(4):
        S.dma("sp", yst[j][:], yT[(4 + j) * 128:(5 + j) * 128, :], writes=[yst[j]])
    t_rmsnorm(T, gs, src=yst, nk=4, dst=T.xnT[4:8], width=512)
    for kc in list(range(0, 4)) + list(range(8, 16)):
        st = yst[kc % 4]
        S.dma("sp", st[:], yT[kc * 128:(kc + 1) * 128, :], writes=[st])
        S.op("pool", lambda e, kc=kc, st=st: e.tensor_copy(T.xnT[kc][:], st[:]), reads=[st], writes=[T.xnT[kc]])

    def sink_o(ci, c0, c1, dc, m, pb, pap):
        n = c1 - c0
        S.op("dve", lambda e: e.tensor_tensor(out=T.hT[dc][:, c0:c1], in0=pap[:, 0:n], in1=T.hT[dc][:, c0:c1], op=ALU.add),
             writes=[T.hT[dc], pb])

    t_proj(T, wout, D, T.xnT, KC, sink_o)
    t_ffn(T, g2, wg, wu, wd)
    t_ffn(T, g1, wg1, wu1, wd1)
    t_store_h(T, h_out, outs)
    t_rmsnorm(T, gm)
    ost = Ring(yst)
    cur = {}

    def sink_u(ci, c0, c1, uc, m, pb, pap):
        if ci == 0:
            cur["o"] = ost.next()
        o = cur["o"]
        n = c1 - c0
        S.op("act", lambda e: e.activation(out=o[0:m, c0:c1], in_=pap[0:m, 0:n], func=AF.Copy), writes=[o, pb])
        if ci == len(CGS) - 1:
            outs.append(S.dma("act", uT[uc * 128:uc * 128 + m, :], o[0:m, :], reads=[o]))

    t_proj(T, win, INW, T.xnT, KC, sink_u)
    S.wait_all("sp", sorted(set(outs)))
    return nc, S


LP = 8320
NCH = 65


class MProg:
    def __init__(self):
        self.nc = bass.Bass("TRN2", target_bir_lowering=False)
        self.S = Sched(self.nc)
        self.outs = []
        self.banks = [self.S.ps([128, 512], F32, "bank") for _ in range(8)]

    def din(self, name, shape):
        return self.nc.dram_tensor(name, list(shape), F32, kind="ExternalInput").ap()

    def dout(self, name, shape):
        return self.nc.dram_tensor(name, list(shape), F32, kind="ExternalOutput").ap()

    def load(self, name, shape, dt=F32, dram=None):
        d = dram if dram is not None else self.din(name, shape)
        b = self.S.sb(shape, F32, name)
        if len(shape) == 2 and shape[1] > 2048:
            step = 2080
            for c0 in range(0, shape[1], step):
                c1 = min(shape[1], c0 + step)
                self.S.dma("sp", b[:, c0:c1], d[:, c0:c1], writes=[b])
        else:
            self.S.dma("sp", b[:], d, writes=[b])
        return b

    def store(self, dram_ap, buf, ap):
        self.S.dma("pool", dram_ap, ap, reads=[buf], is_out=True)

    def finish(self):
        self.S.wait_all("sp", sorted(getattr(self.S, "out_sems", set())))
        return self.nc


class Stream:
    def __init__(self, P, name, rows, total, width, nbuf=2):
        self.P = P
        self.d = P.din(name, [rows, total])
        self.rows = rows
        self.ring = Ring([P.S.sb([rows, width], F32, name) for _ in range(nbuf)])

    def get(self, c0, w):
        b = self.ring.next()
        self.P.S.dma("sp", b[:, 0:w], self.d[:, c0:c0 + w], writes=[b])
        return b


def build_ret():
    P = MProg()
    S = P.S
    B = P.banks
    qT_s = Stream(P, "qT", 64, LP, 512); qsT_s = Stream(P, "qsT", 64, LP, 512)
    kT_s = Stream(P, "kT", 64, LP, 512); ksT_s = Stream(P, "ksT", 64, LP, 512)
    cosT_s = Stream(P, "cosT", 64, LP, 512); sinT_s = Stream(P, "sinT", 64, LP, 512)
    ktok_s = Stream(P, "k_tok", 128, NCH * 64, 256); kstok_s = Stream(P, "ks_tok", 128, NCH * 64, 256)
    costok_s = Stream(P, "cos_tok", 128, NCH * 64, 256); sintok_s = Stream(P, "sin_tok", 128, NCH * 64, 256)
    vtok_s = Stream(P, "v_tok", 128, NCH * 128, 512)
    gT_s = Stream(P, "gT", 128, LP, 512)
    qdec = P.load("qdecT", [64, 512])
    kdec = P.load("kdec", [128, 1])
    dmatT = P.load("dmatT", [128, 128])
    cdec = P.load("cdec", [64, 1])
    gcol = P.load("gcol", [128, 1])
    yT_d = P.dout("yT", [128, LP])
    tmp64 = S.sb([64, 512], F32, "tmp64")
    qd = S.sb([64, 512], F32, "qd")

    def mul(o, a, b_, w, rows=64):
        S.op("dve", lambda e: e.tensor_tensor(out=o[0:rows, 0:w], in0=a[0:rows, 0:w], in1=b_[0:rows, 0:w], op=ALU.mult),
             reads=[a, b_], writes=[o])

    def add(o, a, b_, w, rows=64):
        S.op("dve", lambda e: e.tensor_tensor(out=o[0:rows, 0:w], in0=a[0:rows, 0:w], in1=b_[0:rows, 0:w], op=ALU.add),
             reads=[a, b_], writes=[o])
    ones = S.sb([128, 128], F32, "ones")
    S.op("pool", lambda e: e.memset(ones[:], 1.0 / 128), writes=[ones])
    Sst = [S.sb([64, 128], F32, "Sst") for _ in range(2)]
    S.op("pool", lambda e: e.memset(Sst[0][:], 0.0), writes=[Sst[0]])
    atm = Ring([S.sb([128, 128], F32, "atm") for _ in range(2)])
    ybuf = Ring([S.sb([128, 512], F32, "ybuf") for _ in range(2)])
    yc = S.sb([128, 512], F32, "yc"); sq = S.sb([128, 512], F32, "sq"); rs = S.sb([128, 512], F32, "rs")
    AT = Ring([B[0], B[1]]); YT = Ring([B[2], B[3]]); SP_ = B[4]; LN1 = B[5]; LN2 = B[6]
    blocks = [(0, 1)] + [(1 + 4 * i, 4) for i in range(16)]
    for (cb, ncb) in blocks:
        yb = ybuf.next()
        W = ncb * 128
        p0 = cb * 128
        qT = qT_s.get(p0, W); qsT = qsT_s.get(p0, W); kT = kT_s.get(p0, W); ksT = ksT_s.get(p0, W)
        cosT = cosT_s.get(p0, W); sinT = sinT_s.get(p0, W)
        k_tok = ktok_s.get(cb * 64, ncb * 64); ks_tok = kstok_s.get(cb * 64, ncb * 64)
        cos_tok = costok_s.get(cb * 64, ncb * 64); sin_tok = sintok_s.get(cb * 64, ncb * 64)
        v_tok = vtok_s.get(p0, W)
        mul(qT, qT, cosT, W); mul(tmp64, qsT, sinT, W); add(qT, qT, tmp64, W)
        mul(kT, kT, cosT, W); mul(tmp64, ksT, sinT, W); add(kT, kT, tmp64, W)
        S.op("dve", lambda e: e.tensor_scalar(out=kT[:, 0:W], in0=kT[:, 0:W], scalar1=float(64 ** -0.5), scalar2=None,
                                              op0=ALU.mult), reads=[kT], writes=[kT])
        mul(qd, qT, qdec, W)
        w2 = ncb * 64
        mul(k_tok, k_tok, cos_tok, w2, 128); mul(ks_tok, ks_tok, sin_tok, w2, 128); add(k_tok, k_tok, ks_tok, w2, 128)
        S.op("dve", lambda e: e.tensor_scalar(out=k_tok[:, 0:w2], in0=k_tok[:, 0:w2], scalar1=kdec[:, 0:1], scalar2=None,
                                              op0=ALU.mult), reads=[k_tok, kdec], writes=[k_tok])
        for ci in range(ncb):
            c = cb + ci
            sl = slice(ci * 128, (ci + 1) * 128)
            Sp = Sst[c % 2]; Sn = Sst[(c + 1) % 2]
            at = AT.next(); yt = YT.next(); am = atm.next()
            S.op("pe", lambda e: e.matmul(at[:, 0:128], lhsT=kT[:, sl], rhs=qT[:, sl], start=True, stop=True),
                 reads=[kT, qT], writes=[at])
            S.op("dve", lambda e: e.tensor_tensor(out=am[:], in0=at[:, 0:128], in1=dmatT[:], op=ALU.mult),
                 reads=[dmatT], writes=[am, at])
            S.op("pe", lambda e: e.matmul(yt[:, 0:128], lhsT=v_tok[:, sl], rhs=am[:], start=True, stop=False),
                 reads=[v_tok, am], writes=[yt], inc=False)
            S.op("pe", lambda e: e.matmul(yt[:, 0:128], lhsT=Sp[:], rhs=qd[:, sl], start=False, stop=True),
                 reads=[Sp, qd], writes=[yt])
            S.op("act", lambda e: e.activation(out=yb[:, sl], in_=yt[:, 0:128], func=AF.Copy),
                 writes=[yb, yt])
            S.op("pe", lambda e: e.matmul(SP_[0:64, 0:128], lhsT=k_tok[:, ci * 64:(ci + 1) * 64], rhs=v_tok[:, sl],
                                          start=True, stop=True), reads=[k_tok, v_tok], writes=[SP_])
            S.op("dve", lambda e: e.scalar_tensor_tensor(out=Sn[:], in0=Sp[:], scalar=cdec[:, 0:1], in1=SP_[0:64, 0:128],
                                                         op0=ALU.mult, op1=ALU.add),
                 reads=[Sp, cdec], writes=[Sn, SP_])
        gb = gT_s.get(p0, W)
        S.op("act", lambda e: e.activation(out=gb[:, 0:W], in_=gb[:, 0:W], func=AF.Silu), reads=[gb], writes=[gb])
        S.op("pe", lambda e: e.matmul(LN1[:, 0:W], lhsT=ones[:], rhs=yb[:, 0:W], start=True, stop=True),
             reads=[ones, yb], writes=[LN1])
        S.op("dve", lambda e: e.tensor_tensor(out=yc[:, 0:W], in0=yb[:, 0:W], in1=LN1[:, 0:W], op=ALU.subtract),
             reads=[yb], writes=[yc, LN1])
        S.op("act", lambda e: e.activation(out=sq[:, 0:W], in_=yc[:, 0:W], func=AF.Square), reads=[yc], writes=[sq])
        S.op("pe", lambda e: e.matmul(LN2[:, 0:W], lhsT=ones[:], rhs=sq[:, 0:W], start=True, stop=True),
             reads=[ones, sq], writes=[LN2])
        S.op("act", lambda e: e.activation(out=rs[:, 0:W], in_=LN2[:, 0:W], func=AF.Sqrt, scale=1.0, bias=EPS),
             writes=[rs, LN2])
        S.op("dve", lambda e: e.reciprocal(rs[:, 0:W], rs[:, 0:W]), reads=[rs], writes=[rs])
        S.op("dve", lambda e: e.scalar_tensor_tensor(out=yc[:, 0:W], in0=yc[:, 0:W], scalar=gcol[:, 0:1], in1=rs[:, 0:W],
                                                     op0=ALU.mult, op1=ALU.mult), reads=[yc, gcol, rs], writes=[yc])
        S.op("dve", lambda e: e.tensor_tensor(out=gb[:, 0:W], in0=yc[:, 0:W], in1=gb[:, 0:W], op=ALU.mult),
             reads=[yc, gb], writes=[gb])
        P.store(yT_d[:, p0:p0 + W], gb, gb[:, 0:W])
    return P.finish()


PAD = 112
C_ = np.ascontiguousarray


def tok_layout(a):
    F_ = a.shape[1]
    return C_(a.reshape(NCH, 128, F_).transpose(1, 0, 2).reshape(128, NCH * F_))


def rope_tables(dim):
    half = dim // 2
    inv = (10000.0 ** (-np.arange(half, dtype=np.float32) / half)).astype(np.float32)
    pos = (np.arange(LP, dtype=np.float32) - PAD).astype(np.float32)
    ang = (pos[:, None] * inv[None, :]).astype(np.float32)
    cos = np.cos(ang).astype(np.float32)
    sin = np.sin(ang).astype(np.float32)
    cos2 = np.concatenate([cos, cos], 1)
    sin2 = np.concatenate([-sin, sin], 1)
    return cos2, sin2


def swap_halves(a):
    h = a.shape[1] // 2
    return np.concatenate([a[:, h:], a[:, :h]], 1)


def prep_ret(up, ret_norm_l, core):
    h = core % 4
    o = 576 + 1544
    q = up[:, o + h * 64:o + (h + 1) * 64]
    k = up[:, o + 256 + h * 64:o + 256 + (h + 1) * 64]
    v = up[:, o + 512 + h * 128:o + 512 + (h + 1) * 128]
    g = up[:, o + 1024 + h * 128:o + 1024 + (h + 1) * 128]
    cos2, sin2 = rope_tables(64)
    log_g = np.log(np.float32(1.0) - np.float32(2.0) ** np.float32(-5.0 - h)).astype(np.float32)
    idx = np.arange(128, dtype=np.float32)
    qdec = np.exp((idx + 1.0) * log_g).astype(np.float32)
    kdec = (np.exp((127.0 - idx) * log_g) * (64.0 ** -0.5)).astype(np.float32)
    diff = idx[None, :] - idx[:, None]
    dmatT = np.where(diff >= 0, np.exp(diff * log_g), 0.0).astype(np.float32)
    return {
        "qT": C_(q.T), "qsT": C_(swap_halves(q).T), "kT": C_(k.T), "ksT": C_(swap_halves(k).T),
        "cosT": C_(cos2.T), "sinT": C_(sin2.T),
        "k_tok": tok_layout(k), "ks_tok": tok_layout(swap_halves(k)),
        "cos_tok": tok_layout(cos2), "sin_tok": tok_layout(sin2),
        "v_tok": tok_layout(v), "gT": C_(g.T),
        "qdecT": C_(np.tile(np.tile(qdec, 4)[None, :], (64, 1))),
        "kdec": C_(kdec[:, None]), "dmatT": C_(dmatT),
        "cdec": np.full((64, 1), np.exp(128.0 * log_g), np.float32),
        "gcol": C_(ret_norm_l[h][:, None].astype(np.float32)),
    }


def build_ssd():
    P = MProg()
    S = P.S
    B = P.banks
    xsT_s = Stream(P, "xsT_pad", 64, LP + 3, 515); BT_s = Stream(P, "BT_pad", 128, LP + 3, 515)
    CT_s = Stream(P, "CT_pad", 128, LP + 3, 515); zT_s = Stream(P, "zT", 64, LP, 512)
    xtap = [Stream(P, "xs_tok%d" % j, 128, NCH * 64, 256) for j in range(4)]
    btap = [Stream(P, "B_tok%d" % j, 128, NCH * 128, 512) for j in range(4)]
    cw_xs = P.load("cw_xs", [64, 4]); cb_xs = P.load("cb_xs", [64, 1])
    cw_B = P.load("cw_B", [128, 4]); cb_B = P.load("cb_B", [128, 1])
    cw_C = P.load("cw_C", [128, 4]); cb_C = P.load("cb_C", [128, 1])
    cwt_xs = P.load("cwt_xs", [128, 4 * 256]); cbt_xs = P.load("cbt_xs", [128, 256])
    cwt_B = P.load("cwt_B", [128, 4 * 512]); cbt_B = P.load("cbt_B", [128, 512])
    dt = P.load("dt_tok", [128, NCH]); dtb = P.load("dtb", [128, 1]); alog = P.load("alog", [128, 1])
    valid = P.load("valid_tok", [128, NCH]); dcol = P.load("dcol", [64, 1])
    TriU = P.load("TriU", [128, 128]); UTs = P.load("UTs", [128, 128])
    yT_d = P.dout("yT", [64, LP])
    onesF = S.sb([128, 128], F32, "onesF")
    S.op("pool", lambda e: e.memset(onesF[:], 1.0), writes=[onesF])
    S.op("act", lambda e: e.activation(out=dt[:], in_=dt[:], func=AF.Exp, bias=dtb[:, 0:1]), reads=[dt, dtb], writes=[dt])
    S.op("act", lambda e: e.activation(out=dt[:], in_=dt[:], func=AF.Ln, bias=1.0), reads=[dt], writes=[dt])
    S.op("dve", lambda e: e.tensor_tensor(out=dt[:], in0=dt[:], in1=valid[:], op=ALU.mult), reads=[dt, valid], writes=[dt])
    S.op("act", lambda e: e.activation(out=alog[:], in_=alog[:], func=AF.Exp), reads=[alog], writes=[alog])
    la = S.sb([128, NCH], F32, "la"); cs = S.sb([128, NCH], F32, "cs"); dte = S.sb([128, NCH], F32, "dte")
    S.op("dve", lambda e: e.tensor_scalar(out=la[:], in0=dt[:], scalar1=alog[:, 0:1], scalar2=-1.0, op0=ALU.mult,
                                          op1=ALU.mult), reads=[dt, alog], writes=[la])
    S.op("pe", lambda e: e.matmul(B[5][:, 0:NCH], lhsT=TriU[:], rhs=la[:], start=True, stop=True),
         reads=[TriU, la], writes=[B[5]])
    S.op("act", lambda e: e.activation(out=cs[:], in_=B[5][:, 0:NCH], func=AF.Copy), writes=[cs, B[5]])
    S.op("pe", lambda e: e.matmul(B[6][:, 0:NCH], lhsT=onesF[:], rhs=la[:], start=True, stop=True),
         reads=[onesF, la], writes=[B[6]])
    S.op("dve", lambda e: e.tensor_tensor(out=dte[:], in0=B[6][:, 0:NCH], in1=cs[:], op=ALU.subtract),
         reads=[cs], writes=[dte, B[6]])
    S.op("act", lambda e: e.activation(out=dte[:], in_=dte[:], func=AF.Exp), reads=[dte], writes=[dte])

    def conv_fm(src, rows, W, cw, cb, dst):
        S.op("dve", lambda e: e.tensor_scalar(out=dst[0:rows, 0:W], in0=src[0:rows, 0:W], scalar1=cw[:, 0:1], scalar2=None,
                                              op0=ALU.mult), reads=[src, cw], writes=[dst])
        for j in range(1, 4):
            S.op("dve", lambda e, j=j: e.scalar_tensor_tensor(out=dst[0:rows, 0:W], in0=src[0:rows, j:W + j],
                                                              scalar=cw[:, j:j + 1], in1=dst[0:rows, 0:W],
                                                              op0=ALU.mult, op1=ALU.add), reads=[src, cw, dst], writes=[dst])
        S.op("act", lambda e: e.activation(out=dst[0:rows, 0:W], in_=dst[0:rows, 0:W], func=AF.Silu, bias=cb[:, 0:1]),
             reads=[dst, cb], writes=[dst])

    def conv_tm(taps, w, cwt, cbt, full, dst, tmp):
        for j in range(4):
            o = dst if j == 0 else tmp
            S.op("dve", lambda e, j=j, o=o: e.tensor_tensor(out=o[:, 0:w], in0=taps[j][:, 0:w],
                                                            in1=cwt[:, j * full:j * full + w], op=ALU.mult),
                 reads=[taps[j], cwt], writes=[o])
            if j > 0:
                S.op("dve", lambda e: e.tensor_tensor(out=dst[:, 0:w], in0=dst[:, 0:w], in1=tmp[:, 0:w], op=ALU.add),
                     reads=[dst, tmp], writes=[dst])
        S.op("dve", lambda e: e.tensor_tensor(out=dst[:, 0:w], in0=dst[:, 0:w], in1=cbt[:, 0:w], op=ALU.add),
             reads=[dst, cbt], writes=[dst])
        S.op("act", lambda e: e.activation(out=dst[:, 0:w], in_=dst[:, 0:w], func=AF.Silu), reads=[dst], writes=[dst])

    BTc = Ring([S.sb([128, 512], F32, "BTc") for _ in range(2)])
    CTc = Ring([S.sb([128, 512], F32, "CTc") for _ in range(2)])
    xsTc = Ring([S.sb([64, 512], F32, "xsTc") for _ in range(2)])
    xtok = Ring([S.sb([128, 256], F32, "xtok") for _ in range(2)])
    btok = Ring([S.sb([128, 512], F32, "btok") for _ in range(2)])
    tmpx = S.sb([128, 256], F32, "tmpx"); tmpb = S.sb([128, 512], F32, "tmpb")
    lam = Ring([S.sb([128, 128], F32, "lam") for _ in range(2)])
    laf = Ring([S.sb([128, 128], F32, "laf") for _ in range(2)])
    LT = Ring([S.sb([128, 128], F32, "LT") for _ in range(2)])
    Er = Ring([S.sb([128, 128], F32, "Er") for _ in range(2)])
    CsT = Ring([S.sb([128, 128], F32, "CsT") for _ in range(2)])
    xdt = Ring([S.sb([128, 64], F32, "xdt") for _ in range(2)])
    bd = Ring([S.sb([128, 128], F32, "bd") for _ in range(2)])
    ybuf = Ring([S.sb([64, 512], F32, "ybuf") for _ in range(2)])
    Sst = [S.sb([128, 64], F32, "Sst") for _ in range(2)]
    S.op("pool", lambda e: e.memset(Sst[0][:], 0.0), writes=[Sst[0]])
    GT = B[0]; SEG = B[1]; CSR = B[2]; YT = B[3]; SC = B[4]
    blocks = [(0, 1)] + [(1 + 4 * i, 4) for i in range(16)]
    for (cb, ncb) in blocks:
        W = ncb * 128
        p0 = cb * 128
        xs_in = xsT_s.get(p0, W + 3); B_in = BT_s.get(p0, W + 3); C_in = CT_s.get(p0, W + 3); zT = zT_s.get(p0, W)
        xt = [xtap[j].get(cb * 64, ncb * 64) for j in range(4)]
        bt = [btap[j].get(cb * 128, ncb * 128) for j in range(4)]
        BT = BTc.next(); CT = CTc.next(); xsT = xsTc.next(); xk = xtok.next(); bk = btok.next(); yb = ybuf.next()
        conv_fm(B_in, 128, W, cw_B, cb_B, BT)
        conv_fm(C_in, 128, W, cw_C, cb_C, CT)
        conv_fm(xs_in, 64, W, cw_xs, cb_xs, xsT)
        conv_tm(xt, ncb * 64, cwt_xs, cbt_xs, 256, xk, tmpx)
        conv_tm(bt, ncb * 128, cwt_B, cbt_B, 512, bk, tmpb)
        S.op("act", lambda e: e.activation(out=zT[:, 0:W], in_=zT[:, 0:W], func=AF.Silu), reads=[zT], writes=[zT])
        for ci in range(ncb):
            c = cb + ci
            sl = slice(ci * 128, (ci + 1) * 128)
            Sp = Sst[c % 2]; Sn = Sst[(c + 1) % 2]
            lm = lam.next(); lf = laf.next(); lt = LT.next(); er = Er.next(); cst = CsT.next(); xd = xdt.next(); bdd = bd.next()
            lac = la[:, c:c + 1]
            S.op("pe", lambda e: e.matmul(GT[:, 0:128], lhsT=BT[:, sl], rhs=CT[:, sl], start=True, stop=True),
                 reads=[BT, CT], writes=[GT])
            S.op("dve", lambda e: e.tensor_scalar(out=lm[:], in0=UTs[:], scalar1=lac, scalar2=None, op0=ALU.mult),
                 reads=[UTs, la], writes=[lm])
            S.op("dve", lambda e: e.tensor_scalar(out=lf[:], in0=onesF[:], scalar1=lac, scalar2=None, op0=ALU.mult),
                 reads=[onesF, la], writes=[lf])
            S.op("pe", lambda e: e.matmul(SEG[:, 0:128], lhsT=lm[:], rhs=TriU[:], start=True, stop=True),
                 reads=[lm, TriU], writes=[SEG])
            S.op("pe", lambda e: e.matmul(CSR[:, 0:128], lhsT=lf[:], rhs=TriU[:], start=True, stop=True),
                 reads=[lf, TriU], writes=[CSR])
            S.op("act", lambda e: e.activation(out=lt[:], in_=SEG[:, 0:128], func=AF.Exp), writes=[lt, SEG])
            S.op("dve", lambda e: e.tensor_tensor(out=lt[:], in0=lt[:], in1=TriU[:], op=ALU.mult), reads=[lt, TriU], writes=[lt])
            S.op("dve", lambda e: e.tensor_tensor(out=lt[:], in0=lt[:], in1=GT[:, 0:128], op=ALU.mult), reads=[lt], writes=[lt, GT])
            S.op("act", lambda e: e.activation(out=er[:], in_=CSR[:, 0:128], func=AF.Exp), writes=[er, CSR])
            S.op("dve", lambda e: e.tensor_tensor(out=cst[:], in0=CT[:, sl], in1=er[:], op=ALU.mult), reads=[CT, er], writes=[cst])
            S.op("dve", lambda e: e.tensor_scalar(out=xd[:], in0=xk[:, ci * 64:(ci + 1) * 64], scalar1=dt[:, c:c + 1],
                                                  scalar2=None, op0=ALU.mult), reads=[xk, dt], writes=[xd])
            S.op("dve", lambda e: e.tensor_scalar(out=bdd[:], in0=bk[:, sl], scalar1=dte[:, c:c + 1], scalar2=None,
                                                  op0=ALU.mult), reads=[bk, dte], writes=[bdd])
            S.op("pe", lambda e: e.matmul(YT[0:64, 0:128], lhsT=xd[:], rhs=lt[:], start=True, stop=False),
                 reads=[xd, lt], writes=[YT], inc=False)
            S.op("pe", lambda e: e.matmul(YT[0:64, 0:128], lhsT=Sp[:], rhs=cst[:], start=False, stop=True),
                 reads=[Sp, cst], writes=[YT])
            S.op("dve", lambda e: e.scalar_tensor_tensor(out=yb[:, sl], in0=xsT[:, sl], scalar=dcol[:, 0:1],
                                                         in1=YT[0:64, 0:128], op0=ALU.mult, op1=ALU.add),
                 reads=[xsT, dcol], writes=[yb, YT])
            S.op("pe", lambda e: e.matmul(SC[:, 0:64], lhsT=bdd[:], rhs=xd[:], start=True, stop=True),
                 reads=[bdd, xd], writes=[SC])
            S.op("dve", lambda e: e.scalar_tensor_tensor(out=Sn[:], in0=Sp[:], scalar=er[:, 127:128], in1=SC[:, 0:64],
                                                         op0=ALU.mult, op1=ALU.add), reads=[Sp, er], writes=[Sn, SC])
        S.op("dve", lambda e: e.tensor_tensor(out=yb[:, 0:W], in0=yb[:, 0:W], in1=zT[:, 0:W], op=ALU.mult),
             reads=[yb, zT], writes=[yb])
        P.store(yT_d[:, p0:p0 + W], yb, yb[:, 0:W])
    return P.finish()


def prep_ssd(up, p, l, core):
    j = core
    g = j // 4
    o = 576
    z = up[:, o + j * 64:o + (j + 1) * 64]
    xo = o + 512
    xs = up[:, xo + j * 64:xo + (j + 1) * 64]
    Bm = up[:, xo + 512 + g * 128:xo + 512 + (g + 1) * 128]
    Cm = up[:, xo + 768 + g * 128:xo + 768 + (g + 1) * 128]
    dtr = up[:, xo + 1024 + j:xo + 1024 + j + 1]
    cw = p['ssd_conv_w'][l]
    cb = p['ssd_conv_b'][l]
    ch_xs = slice(j * 64, (j + 1) * 64)
    ch_B = slice(512 + g * 128, 512 + (g + 1) * 128)
    ch_C = slice(768 + g * 128, 768 + (g + 1) * 128)
    pad3 = lambda a: np.concatenate([np.zeros((3, a.shape[1]), np.float32), a], 0)
    xs_p = pad3(xs); B_p = pad3(Bm); C_p = pad3(Cm)
    valid = np.ones((LP, 1), np.float32); valid[:PAD] = 0
    idx = np.arange(128)
    TriU = (idx[:, None] <= idx[None, :]).astype(np.float32)
    m = {
        "xsT_pad": C_(xs_p.T), "BT_pad": C_(B_p.T), "CT_pad": C_(C_p.T), "zT": C_(z.T),
        "cw_xs": C_(cw[:, ch_xs].T), "cb_xs": C_(cb[ch_xs][:, None]),
        "cw_B": C_(cw[:, ch_B].T), "cb_B": C_(cb[ch_B][:, None]),
        "cw_C": C_(cw[:, ch_C].T), "cb_C": C_(cb[ch_C][:, None]),
        "cwt_xs": C_(np.tile(np.tile(cw[:, ch_xs], (1, 4)).reshape(1, 4 * 256), (128, 1))),
        "cbt_xs": C_(np.tile(np.tile(cb[ch_xs], 4)[None, :], (128, 1))),
        "cwt_B": C_(np.tile(np.tile(cw[:, ch_B], (1, 4)).reshape(1, 4 * 512), (128, 1))),
        "cbt_B": C_(np.tile(np.tile(cb[ch_B], 4)[None, :], (128, 1))),
        "dt_tok": tok_layout(dtr), "dtb": np.full((128, 1), p['ssd_dt_bias'][l][j], np.float32),
        "alog": np.full((128, 1), p['ssd_a_log'][l][j], np.float32),
        "valid_tok": tok_layout(valid), "dcol": np.full((64, 1), p['ssd_d'][l][j], np.float32),
        "TriU": C_(TriU), "UTs": C_(1.0 - TriU),
    }
    for t in range(4):
        m["xs_tok%d" % t] = tok_layout(xs_p[t:t + LP])
        m["B_tok%d" % t] = tok_layout(B_p[t:t + LP])
    return m


def build_mla():
    P = MProg()
    S = P.S
    B = P.banks
    NQ = 128 + 8 * 512
    cq_s = [Stream(P, "cqT%d" % i, 128, NQ, 512) for i in range(3)]
    cosq_s = Stream(P, "cosqT", 64, NQ, 512); sinq_s = Stream(P, "sinqT", 64, NQ, 512)
    ckv_s = Stream(P, "ckvT", 128, LP, 512)
    kpe_s = Stream(P, "kpeT", 64, LP, 512); kpes_s = Stream(P, "kpesT", 64, LP, 512)
    cos_s = Stream(P, "cosT", 64, LP, 512); sin_s = Stream(P, "sinT", 64, LP, 512)
    yT_d = P.dout("yT", [128, NQ])

    def loadbf(name, shape):
        f = P.load(name, shape)
        b = S.sb(shape, BF16, name + "b")
        S.op("dve", lambda e: e.tensor_copy(b[:], f[:]), reads=[f], writes=[b])
        return b
    wqn = loadbf("wq_n", [128, 3 * 128]); wqr = loadbf("wq_r", [128, 3 * 64]); wqrs = loadbf("wq_rs", [128, 3 * 64])
    wk = loadbf("wk", [128, 128]); wv = loadbf("wv", [128, 128])
    ones0b = loadbf("ones0", [128, 128])
    gqn = P.load("gqn", [128, 3]); gkv = P.load("gkv", [128, 1])
    gq_n = P.load("gq_n", [128, 1]); gq_r = P.load("gq_r", [64, 1]); gq_rs = P.load("gq_rs", [64, 1])
    gk_n = P.load("gk_n", [128, 1]); gk_r = P.load("gk_r", [64, 1]); gk_rs = P.load("gk_rs", [64, 1])
    mask8 = P.load("mask8", [128, 8 * 512]); mtri = P.load("mtri", [128, 128])
    onesF = S.sb([128, 128], F32, "onesF"); onesb = S.sb([128, 128], BF16, "onesb")
    S.op("pool", lambda e: e.memset(onesF[:], 1.0), writes=[onesF])
    S.op("pool", lambda e: e.memset(onesb[:], 1.0), writes=[onesb])
    KnT = S.sb([128, LP], BF16, "KnT"); KrT = S.sb([64, LP], BF16, "KrT"); Vt = S.sb([128, LP], BF16, "Vt")
    sqa = Ring([S.sb([128, 512], F32, "sqa") for _ in range(2)])
    rstd = S.sb([128, 512], F32, "rstd"); rstdq = S.sb([128, 512], F32, "rstdq"); rstdk = S.sb([128, 512], F32, "rstdk")
    cqn = [S.sb([128, 512], BF16, "cqn") for _ in range(3)]
    ckvn = S.sb([128, 512], BF16, "ckvn")
    qn_fs = [S.sb([128, 512], BF16, "qn_f") for _ in range(2)]; qr_fs = [S.sb([64, 512], BF16, "qr_f") for _ in range(2)]
    t1 = S.sb([64, 512], F32, "t1"); t2 = S.sb([64, 512], F32, "t2")
    PTr = Ring([S.sb([128, 512], BF16, "PT") for _ in range(3)])
    dacc = S.sb([128, 512], F32, "dacc"); vcol = P.load("vcol", [128, 1])
    rden = S.sb([128, 512], F32, "rden"); yo = Ring([S.sb([128, 512], F32, "yo") for _ in range(2)])
    STb = Ring([B[0], B[1]]); OB = B[2]; DB = B[3]; NB = B[4]; Q1 = B[5]; Q2 = B[6]; Q3 = B[7]
    SCALE = float(192 ** -0.5)

    def rms(parts, W, width, dst, post_scale=1.0):
        n = len(parts)
        for i, (buf, ap, rows, is_ps) in enumerate(parts):
            sq = sqa.next()
            if is_ps:
                S.op("act", lambda e, sq=sq, rows=rows, ap=ap: e.activation(out=sq[0:rows, 0:W], in_=ap, func=AF.Square), writes=[sq, buf])
            else:
                S.op("act", lambda e, sq=sq, rows=rows, ap=ap: e.activation(out=sq[0:rows, 0:W], in_=ap, func=AF.Square), reads=[buf], writes=[sq])
            S.op("pe", lambda e, sq=sq, rows=rows, i=i: e.matmul(NB[:, 0:W], lhsT=onesF[0:rows, :], rhs=sq[0:rows, 0:W], start=(i == 0),
                                                            stop=(i == n - 1)), reads=[onesF, sq], writes=[NB])
        S.op("act", lambda e: e.activation(out=dst[:, 0:W], in_=NB[:, 0:W], func=AF.Sqrt, scale=1.0 / width, bias=EPS),
             writes=[dst, NB])
        S.op("dve", lambda e: e.reciprocal(dst[:, 0:W], dst[:, 0:W]), reads=[dst], writes=[dst])
        if post_scale != 1.0:
            S.op("dve", lambda e: e.tensor_scalar(out=dst[:, 0:W], in0=dst[:, 0:W], scalar1=post_scale, scalar2=None,
                                                  op0=ALU.mult), reads=[dst], writes=[dst])

    blocks = [(0, 1)] + [(1 + 4 * i, 4) for i in range(16)]
    Kn_b = [Buf(KnT.ap[:, cb * 128:(cb + ncb) * 128], "Kn") for (cb, ncb) in blocks]
    Kr_b = [Buf(KrT.ap[:, cb * 128:(cb + ncb) * 128], "Kr") for (cb, ncb) in blocks]
    V_b = [Buf(Vt.ap[:, cb * 128:(cb + ncb) * 128], "V") for (cb, ncb) in blocks]
    blk_of = {}
    for bi_, (cb_, ncb_) in enumerate(blocks):
        for c_ in range(cb_, cb_ + ncb_):
            blk_of[c_] = (bi_, (c_ - cb_) * 128)

    def prep_q(it):
        W = 128 if it < 0 else 512
        q0 = 0 if it < 0 else 128 + it * 512
        qn_f = qn_fs[it % 2]; qr_f = qr_fs[it % 2]
        cq = [s.get(q0, W) for s in cq_s]
        cosb = cosq_s.get(q0, W); sinb = sinq_s.get(q0, W)
        rms([(cq[i], cq[i][:, 0:W], 128, False) for i in range(3)], W, 384.0, rstd)
        for i in range(3):
            S.op("dve", lambda e, i=i: e.scalar_tensor_tensor(out=cqn[i][:, 0:W], in0=cq[i][:, 0:W], scalar=gqn[:, i:i + 1],
                                                              in1=rstd[:, 0:W], op0=ALU.mult, op1=ALU.mult),
                 reads=[cq[i], gqn, rstd], writes=[cqn[i]])
        for (pb, wt, m) in ((Q1, wqn, 128), (Q2, wqr, 64), (Q3, wqrs, 64)):
            for i in range(3):
                S.op("pe", lambda e, i=i, pb=pb, wt=wt, m=m: e.matmul(pb[0:m, 0:W], lhsT=wt[:, i * m:(i + 1) * m], rhs=cqn[i][:, 0:W],
                                                                      start=(i == 0), stop=(i == 2)), reads=[wt, cqn[i]], writes=[pb],
                     inc=(i == 2))
        rms([(Q1, Q1[:, 0:W], 128, True), (Q2, Q2[0:64, 0:W], 64, True)], W, 192.0, rstdq, SCALE)
        S.op("dve", lambda e: e.scalar_tensor_tensor(out=qn_f[:, 0:W], in0=Q1[:, 0:W], scalar=gq_n[:, 0:1], in1=rstdq[:, 0:W],
                                                     op0=ALU.mult, op1=ALU.mult), reads=[gq_n, rstdq], writes=[qn_f, Q1])
        S.op("dve", lambda e: e.scalar_tensor_tensor(out=t1[:, 0:W], in0=Q2[0:64, 0:W], scalar=gq_r[:, 0:1], in1=cosb[:, 0:W],
                                                     op0=ALU.mult, op1=ALU.mult), reads=[gq_r, cosb], writes=[t1, Q2])
        S.op("dve", lambda e: e.scalar_tensor_tensor(out=t2[:, 0:W], in0=Q3[0:64, 0:W], scalar=gq_rs[:, 0:1], in1=sinb[:, 0:W],
                                                     op0=ALU.mult, op1=ALU.mult), reads=[gq_rs, sinb], writes=[t2, Q3])
        S.op("dve", lambda e: e.tensor_tensor(out=t1[:, 0:W], in0=t1[:, 0:W], in1=t2[:, 0:W], op=ALU.add), reads=[t1, t2], writes=[t1])
        S.op("dve", lambda e: e.tensor_tensor(out=qr_f[:, 0:W], in0=t1[:, 0:W], in1=rstdq[0:64, 0:W], op=ALU.mult),
             reads=[t1, rstdq], writes=[qr_f])

    def prep_kv(bi):
        cb, ncb = blocks[bi]
        W = ncb * 128
        p0 = cb * 128
        KnB = Kn_b[bi]; KrB = Kr_b[bi]; VB = V_b[bi]
        ckv = ckv_s.get(p0, W); kpe = kpe_s.get(p0, W); kpes = kpes_s.get(p0, W)
        cosb = cos_s.get(p0, W); sinb = sin_s.get(p0, W)
        rms([(ckv, ckv[:, 0:W], 128, False)], W, 128.0, rstd)
        S.op("dve", lambda e: e.scalar_tensor_tensor(out=ckvn[:, 0:W], in0=ckv[:, 0:W], scalar=gkv[:, 0:1], in1=rstd[:, 0:W],
                                                     op0=ALU.mult, op1=ALU.mult), reads=[ckv, gkv, rstd], writes=[ckvn])
        S.op("pe", lambda e: e.matmul(Q1[:, 0:W], lhsT=wk[:], rhs=ckvn[:, 0:W], start=True, stop=True),
             reads=[wk, ckvn], writes=[Q1])
        for ci in range(ncb):
            c = cb + ci
            S.op("pe", lambda e, ci=ci: e.matmul(Q3[:, 0:128], lhsT=ckvn[:, ci * 128:(ci + 1) * 128], rhs=wv[:], start=True, stop=True),
                 reads=[ckvn, wv], writes=[Q3])
            S.op("act", lambda e, ci=ci: e.activation(out=VB[:, ci * 128:(ci + 1) * 128], in_=Q3[:, 0:128], func=AF.Copy),
                 writes=[VB, Q3])
        rms([(Q1, Q1[:, 0:W], 128, True), (kpe, kpe[:, 0:W], 64, False)], W, 192.0, rstdk)
        S.op("dve", lambda e: e.scalar_tensor_tensor(out=KnB[:, 0:W], in0=Q1[:, 0:W], scalar=gk_n[:, 0:1], in1=rstdk[:, 0:W],
                                                     op0=ALU.mult, op1=ALU.mult), reads=[gk_n, rstdk], writes=[KnB, Q1])
        S.op("dve", lambda e: e.scalar_tensor_tensor(out=t1[:, 0:W], in0=kpe[:, 0:W], scalar=gk_r[:, 0:1], in1=cosb[:, 0:W],
                                                     op0=ALU.mult, op1=ALU.mult), reads=[kpe, gk_r, cosb], writes=[t1])
        S.op("dve", lambda e: e.scalar_tensor_tensor(out=t2[:, 0:W], in0=kpes[:, 0:W], scalar=gk_rs[:, 0:1], in1=sinb[:, 0:W],
                                                     op0=ALU.mult, op1=ALU.mult), reads=[kpes, gk_rs, sinb], writes=[t2])
        S.op("dve", lambda e: e.tensor_tensor(out=t1[:, 0:W], in0=t1[:, 0:W], in1=t2[:, 0:W], op=ALU.add), reads=[t1, t2], writes=[t1])
        S.op("dve", lambda e: e.tensor_tensor(out=KrB[:, 0:W], in0=t1[:, 0:W], in1=rstdk[0:64, 0:W], op=ALU.mult),
             reads=[t1, rstdk], writes=[KrB])

    def attn(it):
        W = 128 if it < 0 else 512
        q0 = 0 if it < 0 else 128 + it * 512
        qn_f = qn_fs[it % 2]; qr_f = qr_fs[it % 2]
        nk = 1 if it < 0 else 8 * it + 9
        def scores(kc):
            kb, ko = blk_of[kc]
            ks = slice(ko, ko + 128)
            st = STb.next(); pt = PTr.next()
            S.op("pe", lambda e: e.matmul(st[:, 0:W], lhsT=Kn_b[kb][:, ks], rhs=qn_f[:, 0:W], start=True, stop=False),
                 reads=[Kn_b[kb], qn_f], writes=[st], inc=False)
            S.op("pe", lambda e: e.matmul(st[:, 0:W], lhsT=Kr_b[kb][:, ks], rhs=qr_f[:, 0:W], start=False, stop=True),
                 reads=[Kr_b[kb], qr_f], writes=[st])
            S.op("act", lambda e: e.activation(out=pt[:, 0:W], in_=st[:, 0:W], func=AF.Exp), writes=[pt, st])
            if it < 0:
                S.op("pool", lambda e: e.tensor_tensor(out=pt[:, 0:W], in0=pt[:, 0:W], in1=mtri[:, 0:W], op=ALU.mult),
                     reads=[pt, mtri], writes=[pt])
            elif kc >= nk - 8:
                k_ = kc - (nk - 8)
                S.op("pool", lambda e: e.tensor_tensor(out=pt[:, 0:W], in0=pt[:, 0:W], in1=mask8[:, k_ * 512:k_ * 512 + W],
                                                       op=ALU.mult), reads=[pt, mask8], writes=[pt])
            return pt

        def pv(kc, pt):
            kb, ko = blk_of[kc]
            ks = slice(ko, ko + 128)
            S.op("pe", lambda e: e.matmul(OB[:, 0:W], lhsT=V_b[kb][:, ks], rhs=pt[:, 0:W], start=(kc == 0), stop=(kc == nk - 1)),
                 reads=[V_b[kb], pt], writes=[OB], inc=False)
            S.op("pe", lambda e: e.matmul(DB[:, 0:W], lhsT=(ones0b if kc == 0 else onesb)[:], rhs=pt[:, 0:W],
                                          start=(kc == 0), stop=(kc == nk - 1)), reads=[ones0b, onesb, pt], writes=[DB])
        pend = scores(0)
        for kc in range(nk):
            nxt = scores(kc + 1) if kc + 1 < nk else None
            pv(kc, pend)
            pend = nxt
        y = yo.next()
        S.op("dve", lambda e: e.tensor_scalar(out=rden[:, 0:W], in0=DB[:, 0:W], scalar1=1e-30, scalar2=None, op0=ALU.max),
             writes=[rden, DB])
        S.op("dve", lambda e: e.reciprocal(rden[:, 0:W], rden[:, 0:W]), reads=[rden], writes=[rden])
        S.op("dve", lambda e: e.tensor_tensor(out=y[:, 0:W], in0=OB[:, 0:W], in1=rden[:, 0:W], op=ALU.mult),
             reads=[rden], writes=[y, OB])
        P.store(yT_d[:, q0:q0 + W], y, y[:, 0:W])

    def prep_iter(it):
        prep_kv(2 * it + 1); prep_kv(2 * it + 2); prep_q(it)
    prep_kv(0); prep_q(-1)
    la = S.record(); attn(-1); S.stop_record()
    lp = S.record(); prep_iter(0); S.stop_record()
    S.replay_weighted([la, lp])
    for it in range(8):
        la = S.record(); attn(it); S.stop_record()
        if it + 1 < 8:
            lp = S.record(); prep_iter(it + 1); S.stop_record()
            S.replay_weighted([la, lp])
        else:
            S.replay_weighted([la])
    return P.finish()


def mla_qpos(core):
    par = core // 4
    pos = [np.arange(128)]
    for it in range(8):
        cb = 8 * it + 1 + 4 * par
        pos.append(cb * 128 + np.arange(512))
    return np.concatenate(pos)


def prep_mla(up, p, l, core):
    h = core % 4
    par = core // 4
    cq = up[:, 0:384]; ckv = up[:, 384:512]; kpe = up[:, 512:576]
    cos2, sin2 = rope_tables(64)
    qpos = mla_qpos(core)
    wq = p['mla_w_q_up'][l][:, h * 192:(h + 1) * 192]
    wkv = p['mla_w_kv_up'][l][:, h * 256:(h + 1) * 256]
    kcl = lambda w: C_(w.reshape(3, 128, w.shape[1]).transpose(1, 0, 2).reshape(128, 3 * w.shape[1]))
    gq = p['mla_qk_norm_q'][l]; gk = p['mla_qk_norm_k'][l]
    col = lambda v: C_(v[:, None].astype(np.float32))
    idx = np.arange(128)
    tri = (idx[:, None] <= idx[None, :]).astype(np.float32)
    d = np.zeros((4, 128, 4, 128), np.float32)
    for k in range(4):
        for qi in range(4):
            if qi > k:
                d[k, :, qi, :] = 1.0
            elif qi == k:
                d[k, :, qi, :] = tri
    d = d.reshape(4, 128, 512)
    Z = np.zeros((4, 128, 512), np.float32); O = np.ones((4, 128, 512), np.float32)
    m8 = np.concatenate([d, Z], 0) if par == 0 else np.concatenate([O, d], 0)
    mask8 = C_(m8.transpose(1, 0, 2).reshape(128, 8 * 512))
    ones0 = np.ones((128, 128), np.float32); ones0[:PAD] = 0
    cqq = cq[qpos]
    return {
        "cqT0": C_(cqq[:, 0:128].T), "cqT1": C_(cqq[:, 128:256].T), "cqT2": C_(cqq[:, 256:384].T),
        "cosqT": C_(cos2[qpos].T), "sinqT": C_(sin2[qpos].T),
        "ckvT": C_(ckv.T), "kpeT": C_(kpe.T), "kpesT": C_(swap_halves(kpe).T),
        "cosT": C_(cos2.T), "sinT": C_(sin2.T),
        "wq_n": kcl(wq[:, 0:128]), "wq_r": kcl(wq[:, 128:192]), "wq_rs": kcl(swap_halves(wq[:, 128:192])),
        "wk": C_(wkv[:, 0:128]), "wv": C_(wkv[:, 128:256]), "ones0": ones0,
        "gqn": C_(p['mla_q_norm'][l].reshape(3, 128).T), "gkv": col(p['mla_kv_norm'][l]),
        "gq_n": col(gq[0:128]), "gq_r": col(gq[128:192]), "gq_rs": col(swap_halves(gq[None, 128:192])[0]),
        "gk_n": col(gk[0:128]), "gk_r": col(gk[128:192]), "gk_rs": col(swap_halves(gk[None, 128:192])[0]),
        "mask8": mask8, "mtri": C_(tri), "vcol": C_(ones0[:, 0:1]),
    }


RWKV_FP32R = False


def build_rwkv():
    P = MProg()
    S = P.S
    B = P.banks
    st = {}
    for nm, rows, tot, w in (("r", 128, NCH * 64, 256), ("k", 128, NCH * 64, 256), ("v", 128, NCH * 64, 256)):
        st[nm] = Stream(P, nm + "_tok", rows, tot, w)
        st[nm + "p"] = Stream(P, nm + "p_tok", rows, tot, w)
    for nm, rows in (("wd", 32), ("ad", 32), ("gd", 64)):
        st[nm] = Stream(P, nm + "T", rows, LP, 512)
        st[nm + "p"] = Stream(P, nm + "pT", rows, LP, 512)
    mu_r = P.load("mu_r", [128, 256]); mu_k = P.load("mu_k", [128, 256]); mu_v = P.load("mu_v", [128, 256])
    mu_wd = P.load("mu_wd", [32, 1]); mu_ad = P.load("mu_ad", [32, 1]); mu_gd = P.load("mu_gd", [64, 1])
    w2h = P.load("w2h", [32, 64]); a2h = P.load("a2h", [32, 64]); g2h = P.load("g2h", [64, 64])
    w0t = P.load("w0t", [128, 64]); a0t = P.load("a0t", [128, 64]); kkt = P.load("kkt", [128, 64])
    kat = P.load("kat", [128, 64]); rkt = P.load("rkt", [128, 64]); lng = P.load("lng", [64, 1])
    TriU = P.load("TriU", [128, 128]); SL = P.load("SL", [128, 128]); SU = P.load("SU", [128, 128])
    Id = P.load("Ident", [128, 128])
    yT_d = P.dout("yT", [64, LP])
    ones64 = S.sb([64, 64], F32, "ones64")
    S.op("pool", lambda e: e.memset(ones64[:], 1.0 / 64), writes=[ones64])
    NX = 6
    X = [S.sb([64, 64], F32, "X") for _ in range(NX)]
    S.op("pool", lambda e: e.memset(X[0][:], 0.0), writes=[X[0]])

    def T(shape, name):
        return S.sb(shape, F32, name)
    blkbufs = []
    for _ in range(2):
        blkbufs.append(dict(rs=T([128, 256], "rs"), ks=T([128, 256], "ks"), vs=T([128, 256], "vs"),
                            tw=T([32, 512], "tw"), ads=T([32, 512], "ads"), sg=T([64, 512], "sg"),
                            gT=T([64, 512], "gT"), yb=T([64, 512], "yblk")))
    dtm = T([128, 256], "dtm"); d32 = T([64, 512], "d32")
    names = ["ld", "a", "kk", "kkn", "kmod", "bb", "cs_e", "Eg", "Eneg", "Ege", "Kt", "Bh", "Kh", "Rt", "t3", "Vs", "SA", "U_"]
    lanes_buf = []
    for ln in range(4):
        L = dict(c64={n: T([128, 64], n) for n in names},
                 col={n: T([128, 1], n) for n in ["ss", "rn", "sbon"]},
                 fm={n: T([64, 128], n) for n in ["KtT", "BhT", "KhT", "RtT", "WT", "bonT", "oT", "oc", "osq", "ors"]},
                 sq={n: T([128, 128], n) for n in ["Pa", "PTa", "Pb", "PTb", "A", "MakT", "MrbT", "MrkT"]},
                 gam=T([64, 1], "gam"), Xg=T([64, 64], "Xg"), a=B[2 * ln], b=B[2 * ln + 1])
        lanes_buf.append(L)

    F32R = mybir.dt.float32r

    def rr(ap):
        return ap.bitcast(F32R) if RWKV_FP32R else ap

    def mm(pb, pap, lhsT, rhs, reads, start=True, stop=True, inc=True, fast=False):
        if fast and RWKV_FP32R:
            lhsT = lhsT.bitcast(F32R); rhs = rhs.bitcast(F32R)
        S.op("pe", lambda e: e.matmul(pap, lhsT=lhsT, rhs=rhs, start=start, stop=stop), reads=reads, writes=[pb], inc=inc)

    def tt(o, oap, a, aap, b_, bap, op, extra_w=()):
        S.op("dve", lambda e: e.tensor_tensor(out=oap, in0=aap, in1=bap, op=op), reads=[a, b_], writes=[o] + list(extra_w))

    def chunk(L, bb, ci, c):
        D_ = L["c64"]; col = L["col"]; fm = L["fm"]; sq = L["sq"]; gam = L["gam"]; Xg = L["Xg"]
        Ga = L["a"]; Gb = L["b"]
        rs_, ks_, vs_, tw, ads, gT, yb = bb["rs"], bb["ks"], bb["vs"], bb["tw"], bb["ads"], bb["gT"], bb["yb"]
        s64 = slice(ci * 64, (ci + 1) * 64)
        sl = slice(ci * 128, (ci + 1) * 128)
        r_ap, k_ap, v_ap = rs_[:, s64], ks_[:, s64], vs_[:, s64]
        mm(Ga, Ga[:, 0:64], tw[:, sl], w2h[:], [tw, w2h])
        tt(D_["ld"], D_["ld"][:], w0t, w0t[:], w0t, Ga[:, 0:64], ALU.add, extra_w=[Ga])
        S.op("act", lambda e: e.activation(out=D_["ld"][:], in_=D_["ld"][:], func=AF.Sigmoid), reads=[D_["ld"]], writes=[D_["ld"]])
        S.op("dve", lambda e: e.tensor_scalar(out=D_["ld"][:], in0=D_["ld"][:], scalar1=float(-np.exp(-0.5)), scalar2=None,
                                              op0=ALU.mult), reads=[D_["ld"]], writes=[D_["ld"]])
        mm(Ga, Ga[:, 64:128], ads[:, sl], a2h[:], [ads, a2h])
        tt(D_["a"], D_["a"][:], a0t, a0t[:], a0t, Ga[:, 64:128], ALU.add, extra_w=[Ga])
        S.op("act", lambda e: e.activation(out=D_["a"][:], in_=D_["a"][:], func=AF.Sigmoid), reads=[D_["a"]], writes=[D_["a"]])
        tt(D_["kk"], D_["kk"][:], ks_, k_ap, kkt, kkt[:], ALU.mult)
        S.op("act", lambda e: e.activation(out=D_["t3"][:], in_=D_["kk"][:], func=AF.Square, accum_out=col["ss"][:, 0:1]),
             reads=[D_["kk"]], writes=[D_["t3"], col["ss"]])
        S.op("act", lambda e: e.activation(out=col["rn"][:], in_=col["ss"][:], func=AF.Sqrt), reads=[col["ss"]], writes=[col["rn"]])
        S.op("dve", lambda e: e.tensor_scalar(out=col["rn"][:], in0=col["rn"][:], scalar1=1e-12, scalar2=None, op0=ALU.max),
             reads=[col["rn"]], writes=[col["rn"]])
        S.op("dve", lambda e: e.reciprocal(col["rn"][:], col["rn"][:]), reads=[col["rn"]], writes=[col["rn"]])
        S.op("dve", lambda e: e.tensor_scalar(out=D_["kkn"][:], in0=D_["kk"][:], scalar1=col["rn"][:, 0:1], scalar2=None,
                                              op0=ALU.mult), reads=[D_["kk"], col["rn"]], writes=[D_["kkn"]])
        S.op("dve", lambda e: e.scalar_tensor_tensor(out=D_["kmod"][:], in0=D_["a"][:], scalar=-1.0, in1=kat[:],
                                                     op0=ALU.add, op1=ALU.mult), reads=[D_["a"], kat], writes=[D_["kmod"]])
        S.op("dve", lambda e: e.scalar_tensor_tensor(out=D_["kmod"][:], in0=D_["kmod"][:], scalar=1.0, in1=k_ap,
                                                     op0=ALU.add, op1=ALU.mult), reads=[D_["kmod"], ks_], writes=[D_["kmod"]])
        tt(D_["bb"], D_["bb"][:], D_["kkn"], D_["kkn"][:], D_["a"], D_["a"][:], ALU.mult)
        tt(D_["t3"], D_["t3"][:], rs_, r_ap, rkt, rkt[:], ALU.mult)
        S.op("dve", lambda e: e.scalar_tensor_tensor(out=D_["t3"][:], in0=D_["t3"][:], scalar=1.0, in1=D_["kmod"][:],
                                                     op0=ALU.mult, op1=ALU.mult, accum_out=col["sbon"][:, 0:1]),
             reads=[D_["t3"], D_["kmod"]], writes=[D_["t3"], col["sbon"]])
        S.op("dve", lambda e: e.tensor_scalar(out=D_["Vs"][:], in0=v_ap, scalar1=col["sbon"][:, 0:1], scalar2=None,
                                              op0=ALU.mult), reads=[vs_, col["sbon"]], writes=[D_["Vs"]])
        mm(Ga, Ga[:, 128:192], TriU[:], D_["ld"][:], [TriU, D_["ld"]])
        mm(Ga, Ga[0:64, 192:193], D_["ld"][:], TriU[:, 127:128], [TriU, D_["ld"]])
        S.op("act", lambda e: e.activation(out=D_["Eg"][:], in_=Ga[:, 128:192], func=AF.Exp), writes=[D_["Eg"], Ga])
        S.op("act", lambda e: e.activation(out=D_["Eneg"][:], in_=Ga[:, 128:192], func=AF.Exp, scale=-1.0), writes=[D_["Eneg"], Ga])
        S.op("act", lambda e: e.activation(out=gam[:], in_=Ga[0:64, 192:193], func=AF.Exp), writes=[gam, Ga])
        tt(D_["cs_e"], D_["cs_e"][:], D_["ld"], Ga[:, 128:192], D_["ld"], D_["ld"][:], ALU.subtract, extra_w=[Ga])
        S.op("act", lambda e: e.activation(out=D_["Ege"][:], in_=D_["cs_e"][:], func=AF.Exp), reads=[D_["cs_e"]], writes=[D_["Ege"]])
        tt(D_["Kt"], D_["Kt"][:], D_["kkn"], D_["kkn"][:], D_["Ege"], D_["Ege"][:], ALU.mult)
        tt(D_["Bh"], D_["Bh"][:], D_["bb"], D_["bb"][:], D_["Eneg"], D_["Eneg"][:], ALU.mult)
        tt(D_["Kh"], D_["Kh"][:], D_["kmod"], D_["kmod"][:], D_["Eneg"], D_["Eneg"][:], ALU.mult)
        tt(D_["Rt"], D_["Rt"][:], rs_, r_ap, D_["Eg"], D_["Eg"][:], ALU.mult)
        for i, src in enumerate(("Kt", "Bh", "Kh", "Rt")):
            mm(Gb, Gb[0:64, i * 128:(i + 1) * 128], D_[src][:], Id[:], [D_[src], Id])
        for i, dst in enumerate(("KtT", "BhT", "KhT", "RtT")):
            S.op("act", lambda e, i=i, dst=dst: e.activation(out=fm[dst][:], in_=Gb[0:64, i * 128:(i + 1) * 128], func=AF.Copy),
                 writes=[fm[dst], Gb])
        mm(Ga, Ga[:, 0:128], fm["BhT"][:], fm["KtT"][:], [fm["BhT"], fm["KtT"]])
        mm(Ga, Ga[:, 128:256], fm["KtT"][:], fm["BhT"][:], [fm["BhT"], fm["KtT"]])
        mm(Ga, Ga[:, 256:384], fm["KhT"][:], fm["KtT"][:], [fm["KhT"], fm["KtT"]])
        mm(Gb, Gb[:, 0:128], fm["BhT"][:], fm["RtT"][:], [fm["BhT"], fm["RtT"]])
        mm(Gb, Gb[:, 128:256], fm["KhT"][:], fm["RtT"][:], [fm["KhT"], fm["RtT"]])
        mm(Gb, Gb[0:64, 256:384], D_["Vs"][:], Id[:], [D_["Vs"], Id])
        S.op("dve", lambda e: e.scalar_tensor_tensor(out=rr(sq["Pa"][:]), in0=Ga[:, 0:128], scalar=-1.0, in1=SU[:],
                                                     op0=ALU.mult, op1=ALU.mult), reads=[SU], writes=[sq["Pa"], Ga])
        S.op("dve", lambda e: e.scalar_tensor_tensor(out=rr(sq["PTa"][:]), in0=Ga[:, 128:256], scalar=-1.0, in1=SL[:],
                                                     op0=ALU.mult, op1=ALU.mult), reads=[SL], writes=[sq["PTa"], Ga])
        tt(sq["MakT"], sq["MakT"][:], SU, Ga[:, 256:384], SU, SU[:], ALU.mult, extra_w=[Ga])
        tt(sq["MrbT"], sq["MrbT"][:], TriU, Gb[:, 0:128], TriU, TriU[:], ALU.mult, extra_w=[Gb])
        tt(sq["MrkT"], sq["MrkT"][:], TriU, Gb[:, 128:256], TriU, TriU[:], ALU.mult, extra_w=[Gb])
        S.op("act", lambda e: e.activation(out=fm["bonT"][:], in_=Gb[0:64, 256:384], func=AF.Copy), writes=[fm["bonT"], Gb])
        tt(sq["A"], rr(sq["A"][:]), Id, Id[:], sq["Pa"], sq["Pa"][:], ALU.add)
        Pc, PTc, Pn, PTn = "Pa", "PTa", "Pb", "PTb"
        for lvl in range(6):
            G = Ga if lvl % 2 == 0 else Gb
            mm(G, G[:, 0:128], sq[PTc][:], sq[Pc][:], [sq[PTc], sq[Pc]], fast=True)
            mm(G, G[:, 128:256], sq[Pc][:], sq[PTc][:], [sq[PTc], sq[Pc]], fast=True)
            S.op("act", lambda e, Pn=Pn, G=G: e.activation(out=rr(sq[Pn][:]), in_=G[:, 0:128], func=AF.Copy), writes=[sq[Pn], G])
            S.op("act", lambda e, PTn=PTn, G=G: e.activation(out=rr(sq[PTn][:]), in_=G[:, 128:256], func=AF.Copy), writes=[sq[PTn], G])
            mm(G, G[:, 256:384], sq[PTn][:], sq["A"][:], [sq[PTn], sq["A"]], fast=True)
            tt(sq["A"], rr(sq["A"][:]), sq["A"], sq["A"][:], sq["A"], G[:, 256:384], ALU.add, extra_w=[G])
            Pc, PTc, Pn, PTn = Pn, PTn, Pc, PTc
        mm(Gb, Gb[:, 0:64], sq["MakT"][:], v_ap, [sq["MakT"], vs_])
        S.op("act", lambda e: e.activation(out=D_["t3"][:], in_=Gb[:, 0:64], func=AF.Copy), writes=[D_["t3"], Gb])
        mm(Gb, Gb[:, 64:128], sq["A"][:], D_["t3"][:], [sq["A"], D_["t3"]])
        S.op("act", lambda e: e.activation(out=D_["U_"][:], in_=Gb[:, 64:128], func=AF.Copy, scale=-1.0), writes=[D_["U_"], Gb])
        mm(Gb, Gb[0:64, 128:256], D_["Kt"][:], sq["A"][:], [D_["Kt"], sq["A"]])
        S.op("act", lambda e: e.activation(out=fm["WT"][:], in_=Gb[0:64, 128:256], func=AF.Copy), writes=[fm["WT"], Gb])
        Xp = X[c % NX]; Xn = X[(c + 1) % NX]
        mm(Ga, Ga[:, 0:64], fm["WT"][:], Xp[:], [fm["WT"], Xp])
        tt(D_["SA"], D_["SA"][:], D_["U_"], D_["U_"][:], D_["U_"], Ga[:, 0:64], ALU.subtract, extra_w=[Ga])
        S.op("dve", lambda e: e.tensor_scalar(out=Xg[:], in0=Xp[:], scalar1=gam[:, 0:1], scalar2=None, op0=ALU.mult),
             reads=[Xp, gam], writes=[Xg])
        mm(Ga, Ga[0:64, 64:128], D_["Kh"][:], v_ap, [D_["Kh"], vs_], start=True, stop=False, inc=False)
        mm(Ga, Ga[0:64, 64:128], D_["Bh"][:], D_["SA"][:], [D_["Bh"], D_["SA"]], start=False, stop=True)
        S.op("dve", lambda e: e.scalar_tensor_tensor(out=Xn[:], in0=Ga[0:64, 64:128], scalar=gam[:, 0:1], in1=Xg[:],
                                                     op0=ALU.mult, op1=ALU.add), reads=[gam, Xg], writes=[Xn, Ga])
        mm(Gb, Gb[0:64, 0:128], Xp[:], fm["RtT"][:], [Xp, fm["RtT"]], start=True, stop=False, inc=False)
        mm(Gb, Gb[0:64, 0:128], D_["SA"][:], sq["MrbT"][:], [D_["SA"], sq["MrbT"]], start=False, stop=False, inc=False)
        mm(Gb, Gb[0:64, 0:128], v_ap, sq["MrkT"][:], [vs_, sq["MrkT"]], start=False, stop=True)
        S.op("act", lambda e: e.activation(out=fm["oT"][:], in_=Gb[0:64, 0:128], func=AF.Copy), writes=[fm["oT"], Gb])
        mm(Gb, Gb[0:64, 128:256], ones64[:], fm["oT"][:], [ones64, fm["oT"]])
        tt(fm["oc"], fm["oc"][:], fm["oT"], fm["oT"][:], fm["oT"], Gb[0:64, 128:256], ALU.subtract, extra_w=[Gb])
        S.op("act", lambda e: e.activation(out=fm["osq"][:], in_=fm["oc"][:], func=AF.Square), reads=[fm["oc"]], writes=[fm["osq"]])
        mm(Gb, Gb[0:64, 256:384], ones64[:], fm["osq"][:], [ones64, fm["osq"]])
        S.op("act", lambda e: e.activation(out=fm["ors"][:], in_=Gb[0:64, 256:384], func=AF.Sqrt, bias=64e-5), writes=[fm["ors"], Gb])
        S.op("dve", lambda e: e.reciprocal(fm["ors"][:], fm["ors"][:]), reads=[fm["ors"]], writes=[fm["ors"]])
        S.op("dve", lambda e: e.scalar_tensor_tensor(out=fm["oc"][:], in0=fm["oc"][:], scalar=lng[:, 0:1], in1=fm["ors"][:],
                                                     op0=ALU.mult, op1=ALU.mult), reads=[fm["oc"], lng, fm["ors"]], writes=[fm["oc"]])
        tt(fm["oc"], fm["oc"][:], fm["oc"], fm["oc"][:], fm["bonT"], fm["bonT"][:], ALU.add)
        tt(fm["oc"], fm["oc"][:], fm["oc"], fm["oc"][:], gT, gT[:, sl], ALU.mult)
        S.op("pool", lambda e: e.tensor_copy(yb[:, sl], fm["oc"][:]), reads=[fm["oc"]], writes=[yb])

    blocks = [(0, 1)] + [(1 + 4 * i, 4) for i in range(16)]
    for bi, (cb, ncb) in enumerate(blocks):
        W = ncb * 128
        w2 = ncb * 64
        p0 = cb * 128
        bb = blkbufs[bi % 2]
        for nm, dst, mu in (("r", bb["rs"], mu_r), ("k", bb["ks"], mu_k), ("v", bb["vs"], mu_v)):
            x = st[nm].get(cb * 64, w2); xp = st[nm + "p"].get(cb * 64, w2)
            tt(dtm, dtm[:, 0:w2], xp, xp[:, 0:w2], x, x[:, 0:w2], ALU.subtract)
            tt(dtm, dtm[:, 0:w2], dtm, dtm[:, 0:w2], mu, mu[:, 0:w2], ALU.mult)
            tt(dst, dst[:, 0:w2], dtm, dtm[:, 0:w2], x, x[:, 0:w2], ALU.add)
        for nm, dst, mu, rows, fn in (("wd", bb["tw"], mu_wd, 32, AF.Tanh), ("ad", bb["ads"], mu_ad, 32, None),
                                      ("gd", bb["sg"], mu_gd, 64, AF.Sigmoid)):
            x = st[nm].get(p0, W); xp = st[nm + "p"].get(p0, W)
            tt(d32, d32[0:rows, 0:W], xp, xp[:, 0:W], x, x[:, 0:W], ALU.subtract)
            S.op("dve", lambda e, dst=dst, mu=mu, x=x, rows=rows: e.scalar_tensor_tensor(
                out=dst[:, 0:W], in0=d32[0:rows, 0:W], scalar=mu[:, 0:1], in1=x[:, 0:W], op0=ALU.mult, op1=ALU.add),
                reads=[d32, mu, x], writes=[dst])
            if fn is not None:
                S.op("act", lambda e, dst=dst, fn=fn: e.activation(out=dst[:, 0:W], in_=dst[:, 0:W], func=fn), reads=[dst], writes=[dst])
        G0 = lanes_buf[0]["a"]
        mm(G0, G0[0:64, 0:W], g2h[:], bb["sg"][:, 0:W], [g2h, bb["sg"]])
        S.op("act", lambda e: e.activation(out=bb["gT"][:, 0:W], in_=G0[0:64, 0:W], func=AF.Copy), writes=[bb["gT"], G0])
        lanes = []
        for ci in range(ncb):
            rec = S.record()
            chunk(lanes_buf[ci], bb, ci, cb + ci)
            S.stop_record()
            lanes.append(rec)
        S.replay(lanes, skew=10)
        P.store(yT_d[:, p0:p0 + W], bb["yb"], bb["yb"][:, 0:W])
    return P.finish()


def prep_rwkv(up, p, l, core):
    j = core
    o = 576 + 1544 + 1536
    prev = np.concatenate([np.zeros((1, up.shape[1]), np.float32), up[:-1]], 0)
    hs = slice(j * 64, (j + 1) * 64)
    mu = p['rwkv_mu'][l]
    rep = lambda v, n=128: C_(np.tile(v[None, :].astype(np.float32), (n, 1)))
    idx = np.arange(128)
    TriU = (idx[:, None] <= idx[None, :]).astype(np.float32)
    m = {}
    for nm, off in (("r", 0), ("k", 512), ("v", 1024)):
        cs_ = slice(o + off + j * 64, o + off + (j + 1) * 64)
        m[nm + "_tok"] = tok_layout(up[:, cs_]); m[nm + "p_tok"] = tok_layout(prev[:, cs_])
        m["mu_" + nm] = rep(np.tile(mu[off + j * 64: off + (j + 1) * 64], 4))
    for nm, off, wdt in (("wd", 1536, 32), ("ad", 1568, 32), ("gd", 1600, 64)):
        cs_ = slice(o + off, o + off + wdt)
        m[nm + "T"] = C_(up[:, cs_].T); m[nm + "pT"] = C_(prev[:, cs_].T)
        m["mu_" + nm] = C_(mu[off:off + wdt][:, None].astype(np.float32))
    m["w2h"] = C_(p['rwkv_w2'][l][:, hs]); m["a2h"] = C_(p['rwkv_a2'][l][:, hs]); m["g2h"] = C_(p['rwkv_g2'][l][:, hs])
    m["w0t"] = rep(p['rwkv_w0'][l][hs]); m["a0t"] = rep(p['rwkv_a0'][l][hs]); m["kkt"] = rep(p['rwkv_k_k'][l][hs])
    m["kat"] = rep(p['rwkv_k_a'][l][hs]); m["rkt"] = rep(p['rwkv_r_k'][l][j]); m["lng"] = C_(p['rwkv_ln'][l][j][:, None].astype(np.float32))
    m["TriU"] = C_(TriU); m["SL"] = C_((idx[:, None] > idx[None, :]).astype(np.float32))
    m["SU"] = C_((idx[:, None] < idx[None, :]).astype(np.float32)); m["Ident"] = np.eye(128, dtype=np.float32)
    return m


_PROGS = {}


def _prog(name, fn):
    if name not in _PROGS:
        r = fn()
        _PROGS[name] = r[0] if isinstance(r, tuple) else r
    return _PROGS[name]


def _run(nc, maps):
    res = run_bass_kernel_spmd(nc, maps, core_ids=list(range(NCORES)))
    return res.results


def tile_w(W, nk=KC):
    ncols = W.shape[1]
    nt = (ncols + 127) // 128
    if nt * 128 != ncols:
        W = np.concatenate([W, np.zeros((W.shape[0], nt * 128 - ncols), np.float32)], 1)
    return C_(W.reshape(nk, 128, nt, 128).transpose(2, 1, 0, 3).reshape(nt, 128, nk * 128))


def tile_wd(W):
    return C_(W.reshape(FC // 4, 4, 128, KC, 128).transpose(0, 3, 2, 1, 4).reshape((FC // 4) * KC, 128, 4 * 128))


def _gl(v, n):
    return C_(np.asarray(v, np.float32).reshape(n, 128).T)


def kernel(**inp):
    p = {k: np.asarray(v, np.float32) for k, v in inp.items()}
    x = p['x'][0]
    meta = p['meta_tokens']
    depth = p['w_in'].shape[0]
    hT = [C_(np.concatenate([meta, x[c * 1024:(c + 1) * 1024]], 0).T) for c in range(NCORES)]
    nc_a = _prog("Ta", build_Ta)
    nc_b = _prog("Tb", build_Tb)
    nc_ba = _prog("Tba", build_Tba)
    nc_mla = _prog("mla", build_mla)
    nc_ssd = _prog("ssd", build_ssd)
    nc_ret = _prog("ret", build_ret)
    nc_rwkv = _prog("rwkv", build_rwkv)

    def ffn1_maps(l):
        return {"g1": _gl(p['ffn1_norm'][l], KC), "gm": _gl(p['mix_norm'][l], KC),
                "wg1": tile_w(p['ffn1_w_gate'][l]), "wu1": tile_w(p['ffn1_w_up'][l]), "wd1": tile_wd(p['ffn1_w_down'][l]),
                "win": tile_w(p['w_in'][l])}

    def assemble_u(res):
        up = np.zeros((LP, INW), np.float32)
        up[PAD:PAD + 16] = res[0]['uT'][:, 0:16].T
        for c in range(NCORES):
            up[PAD + 16 + c * 1024:PAD + 16 + (c + 1) * 1024] = res[c]['uT'][:, 16:].T
        return up

    f = ffn1_maps(0)
    res = _run(nc_a, [{"h_in": hT[c], "g1": f["g1"], "wg": f["wg1"], "wu": f["wu1"], "wd": f["wd1"], "gm": f["gm"],
                       "win": f["win"]} for c in range(NCORES)])
    hT = [C_(res[c]['h_out']) for c in range(NCORES)]
    up = assemble_u(res)
    del res, f
    for l in range(depth):
        y = np.zeros((LP, D), np.float32)
        r = _run(nc_mla, [prep_mla(up, p, l, c) for c in range(NCORES)])
        for c in range(NCORES):
            h = c % 4
            qp = mla_qpos(c)
            yt = r[c]['yT'].T
            if c < 4:
                y[qp, h * 128:(h + 1) * 128] = yt
            else:
                y[qp[128:], h * 128:(h + 1) * 128] = yt[128:]
        r = _run(nc_ssd, [prep_ssd(up, p, l, c) for c in range(NCORES)])
        for j in range(8):
            y[:, 512 + j * 64:512 + (j + 1) * 64] = r[j]['yT'].T
        r = _run(nc_ret, [prep_ret(up, p['ret_norm'][l], c) for c in range(NCORES)])
        for h in range(4):
            y[:, 1024 + h * 128:1024 + (h + 1) * 128] = r[h]['yT'].T
        r = _run(nc_rwkv, [prep_rwkv(up, p, l, c) for c in range(NCORES)])
        for j in range(8):
            y[:, 1536 + j * 64:1536 + (j + 1) * 64] = r[j]['yT'].T
        del r, up
        base = {"gs": _gl(p['ssd_norm'][l], 4), "wout": tile_w(p['w_out'][l]), "g2": _gl(p['ffn2_norm'][l], KC),
                "wg": tile_w(p['ffn2_w_gate'][l]), "wu": tile_w(p['ffn2_w_up'][l]), "wd": tile_wd(p['ffn2_w_down'][l])}
        last = (l == depth - 1)
        if not last:
            base.update(ffn1_maps(l + 1))
        maps = []
        for c in range(NCORES):
            yc = np.concatenate([y[PAD:PAD + 16], y[PAD + 16 + c * 1024:PAD + 16 + (c + 1) * 1024]], 0)
            m = dict(base)
            m["h_in"] = hT[c]
            m["yT"] = C_(yc.T)
            maps.append(m)
        res = _run(nc_b if last else nc_ba, maps)
        hT = [C_(res[c]['h_out']) for c in range(NCORES)]
        if not last:
            up = assemble_u(res)
        del res, maps, y, base
    out = np.concatenate([hT[c][:, 16:].T for c in range(NCORES)], 0)
    return C_(out[None].astype(np.float32))
```

```python
import numpy as np
import concourse.bass as bass
import concourse.mybir as mybir
from concourse.bass_utils import run_bass_kernel_spmd

F32 = mybir.dt.float32
BF16 = mybir.dt.bfloat16
AF = mybir.ActivationFunctionType
ALU = mybir.AluOpType
AX = mybir.AxisListType

NCORES = 8
D = 2048
KC = 16
DFF = 5632
FC = 44
NT = 1040
CGS = [(0, 16), (16, 528), (528, 1040)]
INW = 5320
EPS = 1e-6
EPOCH = 30000


class Buf:
    __slots__ = ("ap", "w", "r", "name", "dsem")

    def __init__(self, ap, name=""):
        self.ap = ap
        self.w = None
        self.r = {}
        self.name = name
        self.dsem = None

    def __getitem__(self, idx):
        return self.ap[idx]


class Sched:
    def __init__(self, nc):
        self.nc = nc
        self.engs = {"pe": nc.tensor, "dve": nc.vector, "act": nc.scalar,
                     "pool": nc.gpsimd, "sp": nc.sync}
        self.sems = []
        self.owner = {}
        self.cur = {}
        self.cnt = {}
        self.seen = {e: {} for e in self.engs}
        self.ninst = {e: 0 for e in self.engs}
        for e in ("pe", "dve", "act", "pool"):
            self._new_epoch(e)
        self.uid = 0

    def _alloc_sem(self, name, owner=None):
        h = self.nc.alloc_semaphore(name)
        self.sems.append(h)
        k = len(self.sems) - 1
        self.cnt[k] = 0
        self.owner[k] = owner
        return k

    def _new_epoch(self, e):
        self.cur[e] = self._alloc_sem("c_%s_%d" % (e, len(self.sems)), e)

    def new_dsem(self, name="d"):
        return self._alloc_sem("%s_%d" % (name, len(self.sems)))

    def _waits(self, e, reads, writes):
        need = {}
        for b in reads:
            if b.w is not None:
                k, v = b.w
                if need.get(k, 0) < v:
                    need[k] = v
        for b in writes:
            if b.w is not None:
                k, v = b.w
                if need.get(k, 0) < v:
                    need[k] = v
            for k, v in b.r.items():
                if need.get(k, 0) < v:
                    need[k] = v
        eng = self.engs[e]
        seen = self.seen[e]
        for k, v in need.items():
            if e == "pe" and self.owner[k] == "pe":
                continue
            if seen.get(k, 0) >= v:
                continue
            eng.wait_ge(self.sems[k], v)
            seen[k] = v

    def record(self):
        self._rec = []
        return self._rec

    def stop_record(self):
        self._rec = None

    def replay_weighted(self, lanes):
        self._rec = None
        n = max(len(l) for l in lanes) if lanes else 0
        pos = [0] * len(lanes)
        for i in range(n):
            for k, l in enumerate(lanes):
                tgt = ((i + 1) * len(l) + n - 1) // n
                while pos[k] < min(tgt, len(l)):
                    kind, a, kw = l[pos[k]]
                    (self.op if kind == "op" else self.dma)(*a, **kw)
                    pos[k] += 1

    def replay(self, lanes, skew=0):
        self._rec = None
        n = max(len(l) + k * skew for k, l in enumerate(lanes)) if lanes else 0
        for i in range(n):
            for k, l in enumerate(lanes):
                j = i - k * skew
                if 0 <= j < len(l):
                    kind, a, kw = l[j]
                    (self.op if kind == "op" else self.dma)(*a, **kw)

    def op(self, e, fn, reads=(), writes=(), inc=True):
        if getattr(self, "_rec", None) is not None:
            self._rec.append(("op", (e, fn, tuple(reads), tuple(writes), inc), {}))
            return None
        self._waits(e, reads, writes)
        ins = fn(self.engs[e])
        k = self.cur[e]
        self.ninst[e] += 1
        if inc:
            self.cnt[k] += 1
            ins.then_inc(self.sems[k], 1)
            tok = (k, self.cnt[k])
        else:
            tok = (k, self.cnt[k] + 1)
        for b in reads:
            if b.r.get(tok[0], 0) < tok[1]:
                b.r[tok[0]] = tok[1]
        for b in writes:
            b.w = tok
            b.r = {}
        if inc and self.cnt[k] >= EPOCH:
            self._new_epoch(e)
        return ins

    def dma(self, q, out_ap, in_ap, reads=(), writes=(), sem=None, is_out=False, **kw):
        if getattr(self, "_rec", None) is not None:
            kw2 = dict(kw); kw2.update(reads=tuple(reads), writes=tuple(writes), sem=sem, is_out=is_out)
            self._rec.append(("dma", (q, out_ap, in_ap), kw2))
            return None
        self._waits(q, reads, writes)
        if sem is None:
            for b in list(writes) + list(reads):
                if b.dsem is None:
                    b.dsem = self.new_dsem()
                sem = b.dsem
                break
        ins = self.engs[q].dma_start(out=out_ap, in_=in_ap, **kw)
        ins.then_inc(self.sems[sem], 16)
        self.cnt[sem] += 16
        self.ninst[q] += 1
        tok = (sem, self.cnt[sem])
        if is_out:
            if not hasattr(self, "out_sems"):
                self.out_sems = set()
            self.out_sems.add(sem)
        for b in reads:
            if b.r.get(tok[0], 0) < tok[1]:
                b.r[tok[0]] = tok[1]
        for b in writes:
            b.w = tok
            b.r = {}
        return sem

    def wait_all(self, e, semkeys):
        eng = self.engs[e]
        for k in semkeys:
            if self.cnt[k] > 0:
                eng.wait_ge(self.sems[k], self.cnt[k])

    def sb(self, shape, dt, name=None):
        self.uid += 1
        name = "%s_%d" % (name or "sb", self.uid)
        return Buf(self.nc.alloc_sbuf_tensor(name, list(shape), dt).ap(), name)

    def ps(self, shape, dt=F32, name=None):
        self.uid += 1
        name = "%s_%d" % (name or "ps", self.uid)
        return Buf(self.nc.alloc_psum_tensor(name, list(shape), dt).ap(), name)

    def sub(self, buf, ap):
        return Buf(ap, buf.name + "_s")


class Ring:
    def __init__(self, bufs):
        self.bufs = bufs
        self.i = 0

    def next(self):
        b = self.bufs[self.i % len(self.bufs)]
        self.i += 1
        return b


class TCtx:
    def __init__(self, S, n_ost=2):
        self.S = S
        nc = S.nc
        self.hT = [S.sb([128, NT], F32, "hT") for _ in range(KC)]
        self.xnT = [S.sb([128, NT], BF16, "xnT") for _ in range(KC)]
        self.actT = Ring([S.sb([128, NT], BF16, "actT") for _ in range(8)])
        self.wst = Ring([S.sb([128, KC, 128], F32, "wst") for _ in range(4)])
        self.wbf = Ring([S.sb([128, KC, 128], BF16, "wbf") for _ in range(6)])
        self.wdst = Ring([S.sb([128, 4, 128], F32, "wdst") for _ in range(3)])
        self.wdbf = Ring([S.sb([128, 4, 128], BF16, "wdbf") for _ in range(3)])
        self.tmp = Ring([S.sb([128, 512], F32, "tmp") for _ in range(3)])
        self.ost = Ring([S.sb([128, NT], F32, "ost") for _ in range(n_ost)]) if n_ost else None
        self.rstd = S.sb([128, NT], F32, "rstd")
        self.gcol = S.sb([128, KC], F32, "gcol")
        self.ones = S.sb([128, 128], F32, "ones")
        S.op("pool", lambda e: e.memset(self.ones[:], 1.0), writes=[self.ones])
        banks = [S.ps([128, 512], F32, "bank") for _ in range(8)]
        self.G = [banks[0], banks[1]]
        self.U = [banks[2], banks[3]]
        self.Dn = [banks[4], banks[5]]
        self.N = banks[6]
        self.M = banks[7]

    def pG(self, ci):
        return (self.M, self.M.ap[:, 0:16]) if ci == 0 else (self.G[ci - 1], self.G[ci - 1].ap[:, :])

    def pU(self, ci):
        return (self.M, self.M.ap[:, 16:32]) if ci == 0 else (self.U[ci - 1], self.U[ci - 1].ap[:, :])

    def pD(self, ci):
        return (self.M, self.M.ap[:, 32:48]) if ci == 0 else (self.Dn[ci - 1], self.Dn[ci - 1].ap[:, :])

    def pN(self, ci):
        return (self.M, self.M.ap[:, 48:64]) if ci == 0 else (self.N, self.N.ap[:, :])


def t_load_h(T, h_dram):
    S = T.S
    for kc in range(KC):
        S.dma("sp", T.hT[kc][:], h_dram[kc * 128:(kc + 1) * 128, :], writes=[T.hT[kc]])


def t_store_h(T, h_dram, sems):
    S = T.S
    for kc in range(KC):
        sems.append(S.dma("sp", h_dram[kc * 128:(kc + 1) * 128, :], T.hT[kc][:], reads=[T.hT[kc]]))


def t_rmsnorm(T, g_dram, src=None, nk=KC, dst=None, width=D):
    S = T.S
    src = src or T.hT
    dst = dst or T.xnT
    S.dma("sp", T.gcol[:, 0:nk], g_dram[:, 0:nk], writes=[T.gcol])
    for ci, (c0, c1) in enumerate(CGS):
        pb, pap = T.pN(ci)
        for kc in range(nk):
            t = T.tmp.next()
            S.op("act", lambda e, t=t, kc=kc: e.activation(out=t[:, 0:c1 - c0], in_=src[kc][:, c0:c1], func=AF.Square),
                 reads=[src[kc]], writes=[t])
            S.op("pe", lambda e, t=t, kc=kc: e.matmul(pap[:, 0:c1 - c0], lhsT=T.ones[:], rhs=t[:, 0:c1 - c0],
                                                       start=(kc == 0), stop=(kc == nk - 1)),
                 reads=[T.ones, t], writes=[pb], inc=True)
        S.op("act", lambda e: e.activation(out=T.rstd[:, c0:c1], in_=pap[:, 0:c1 - c0], func=AF.Sqrt,
                                           scale=1.0 / width, bias=EPS),
             writes=[T.rstd, pb])
    S.op("dve", lambda e: e.reciprocal(T.rstd[:], T.rstd[:]), reads=[T.rstd], writes=[T.rstd])
    for kc in range(nk):
        S.op("dve", lambda e, kc=kc: e.scalar_tensor_tensor(out=dst[kc][:], in0=src[kc][:], scalar=T.gcol[:, kc:kc + 1],
                                                            in1=T.rstd[:], op0=ALU.mult, op1=ALU.mult),
             reads=[src[kc], T.gcol, T.rstd], writes=[dst[kc]])


PROBE_NODMA = False
W_TWO_QUEUES = True
PROBE_NOMM = False


def t_load_w(T, w_dram, idx, ncol, nk=KC, cast_eng="act"):
    S = T.S
    if PROBE_NODMA and getattr(T, "_w0", None) is not None:
        return T._w0
    st = T.wst.next()
    bf = T.wbf.next()
    T._wq = getattr(T, "_wq", 0) + 1
    q = "sp" if (T._wq % 2 == 0 or not W_TWO_QUEUES) else "pool"
    S.dma(q, st[:, 0:nk, :], w_dram[idx].rearrange("p (kc f) -> p kc f", kc=nk), writes=[st])
    if cast_eng == "act":
        S.op("act", lambda e: e.activation(out=bf[:, 0:nk, :], in_=st[:, 0:nk, :], func=AF.Copy), reads=[st], writes=[bf])
    else:
        S.op(cast_eng, lambda e: e.tensor_copy(bf[:, 0:nk, :], st[:, 0:nk, :]), reads=[st], writes=[bf])
    T._w0 = bf
    return bf


def t_ffn(T, g_dram, wg, wu, wd):
    S = T.S
    t_rmsnorm(T, g_dram)
    SG = 4
    for sg in range(FC // SG):
        acts = []
        for fi in range(SG):
            fc = sg * SG + fi
            bg = t_load_w(T, wg, fc, 128)
            bu = t_load_w(T, wu, fc, 128)
            a = T.actT.next()
            acts.append(a)
            for ci, (c0, c1) in enumerate(CGS):
                n = c1 - c0
                gb, gap = T.pG(ci)
                ub, uap = T.pU(ci)
                for kc in range(KC):
                    S.op("pe", lambda e, kc=kc: e.matmul(gap[:, 0:n], lhsT=bg[:, kc, :], rhs=T.xnT[kc][:, c0:c1],
                                                          start=(kc == 0), stop=(kc == KC - 1)),
                         reads=[bg, T.xnT[kc]], writes=[gb], inc=(kc == KC - 1))
                for kc in range(KC):
                    S.op("pe", lambda e, kc=kc: e.matmul(uap[:, 0:n], lhsT=bu[:, kc, :], rhs=T.xnT[kc][:, c0:c1],
                                                          start=(kc == 0), stop=(kc == KC - 1)),
                         reads=[bu, T.xnT[kc]], writes=[ub], inc=(kc == KC - 1))
                t = T.tmp.next()
                S.op("act", lambda e, t=t: e.activation(out=t[:, 0:n], in_=gap[:, 0:n], func=AF.Silu),
                     writes=[t, gb])
                S.op("dve", lambda e, t=t: e.tensor_tensor(out=a[:, c0:c1], in0=t[:, 0:n], in1=uap[:, 0:n], op=ALU.mult),
                     reads=[t], writes=[a, ub])
        for dc in range(KC):
            st = T.wdst.next()
            bf = T.wdbf.next()
            S.dma("sp", st[:], wd[sg * KC + dc].rearrange("p (fi d) -> p fi d", fi=SG), writes=[st])
            S.op("pool", lambda e: e.tensor_copy(bf[:], st[:]), reads=[st], writes=[bf])
            for ci, (c0, c1) in enumerate(CGS):
                n = c1 - c0
                db, dap = T.pD(ci)
                for fi in range(SG):
                    S.op("pe", lambda e, fi=fi: e.matmul(dap[:, 0:n], lhsT=bf[:, fi, :], rhs=acts[fi][:, c0:c1],
                                                          start=(fi == 0), stop=(fi == SG - 1)),
                         reads=[bf, acts[fi]], writes=[db], inc=(fi == SG - 1))
                S.op("dve", lambda e: e.scalar_tensor_tensor(out=T.hT[dc][:, c0:c1], in0=dap[:, 0:n], scalar=0.5,
                                                            in1=T.hT[dc][:, c0:c1], op0=ALU.mult, op1=ALU.add),
                     writes=[T.hT[dc], db])


def t_proj(T, w_dram, ncols, rhsT, nk, sink):
    S = T.S
    nchunks = (ncols + 127) // 128
    for uc in range(nchunks):
        m = min(128, ncols - uc * 128)
        bw = t_load_w(T, w_dram, uc, m, nk=nk, cast_eng="pool")
        for ci, (c0, c1) in enumerate(CGS):
            n = c1 - c0
            gb, gap = T.pG(ci)
            for kc in range(nk):
                S.op("pe", lambda e, kc=kc: e.matmul(gap[0:m, 0:n], lhsT=bw[:, kc, 0:m], rhs=rhsT[kc][:, c0:c1],
                                                      start=(kc == 0), stop=(kc == nk - 1)),
                     reads=[bw, rhsT[kc]], writes=[gb], inc=(kc == nk - 1))
            sink(ci, c0, c1, uc, m, gb, gap)


def build_Ta():
    nc = bass.Bass("TRN2", target_bir_lowering=False)
    dt = lambda n, s, k: nc.dram_tensor(n, list(s), F32, kind=k).ap()
    h_in = dt("h_in", [D, NT], "ExternalInput")
    g1 = dt("g1", [128, KC], "ExternalInput")
    wg = dt("wg", [FC, 128, KC * 128], "ExternalInput")
    wu = dt("wu", [FC, 128, KC * 128], "ExternalInput")
    wd = dt("wd", [(FC // 4) * KC, 128, 4 * 128], "ExternalInput")
    gm = dt("gm", [128, KC], "ExternalInput")
    win = dt("win", [42, 128, KC * 128], "ExternalInput")
    h_out = dt("h_out", [D, NT], "ExternalOutput")
    uT = dt("uT", [INW, NT], "ExternalOutput")
    S = Sched(nc)
    T = TCtx(S)
    outs = []
    t_load_h(T, h_in)
    t_ffn(T, g1, wg, wu, wd)
    t_store_h(T, h_out, outs)
    t_rmsnorm(T, gm)
    cur = {}

    def sink(ci, c0, c1, uc, m, pb, pap):
        if ci == 0:
            cur["o"] = T.ost.next()
        o = cur["o"]
        n = c1 - c0
        S.op("act", lambda e: e.activation(out=o[0:m, c0:c1], in_=pap[0:m, 0:n], func=AF.Copy), writes=[o, pb])
        if ci == len(CGS) - 1:
            outs.append(S.dma("act", uT[uc * 128:uc * 128 + m, :], o[0:m, :], reads=[o]))

    t_proj(T, win, INW, T.xnT, KC, sink)
    S.wait_all("sp", sorted(set(outs)))
    return nc, S


def build_Tb():
    nc = bass.Bass("TRN2", target_bir_lowering=False)
    dt = lambda n, s, k: nc.dram_tensor(n, list(s), F32, kind=k).ap()
    h_in = dt("h_in", [D, NT], "ExternalInput")
    yT = dt("yT", [D, NT], "ExternalInput")
    gs = dt("gs", [128, 4], "ExternalInput")
    wout = dt("wout", [KC, 128, KC * 128], "ExternalInput")
    g2 = dt("g2", [128, KC], "ExternalInput")
    wg = dt("wg", [FC, 128, KC * 128], "ExternalInput")
    wu = dt("wu", [FC, 128, KC * 128], "ExternalInput")
    wd = dt("wd", [(FC // 4) * KC, 128, 4 * 128], "ExternalInput")
    h_out = dt("h_out", [D, NT], "ExternalOutput")
    S = Sched(nc)
    T = TCtx(S, n_ost=0)
    outs = []
    t_load_h(T, h_in)
    yst = [S.sb([128, NT], F32, "yst") for _ in range(4)]
    for j in range(4):
        S.dma("sp", yst[j][:], yT[(4 + j) * 128:(5 + j) * 128, :], writes=[yst[j]])
    t_rmsnorm(T, gs, src=yst, nk=4, dst=T.xnT[4:8], width=512)
    for kc in list(range(0, 4)) + list(range(8, 16)):
        st = yst[kc % 4]
        S.dma("sp", st[:], yT[kc * 128:(kc + 1) * 128, :], writes=[st])
        S.op("pool", lambda e, kc=kc, st=st: e.tensor_copy(T.xnT[kc][:], st[:]), reads=[st], writes=[T.xnT[kc]])

    def sink(ci, c0, c1, dc, m, pb, pap):
        n = c1 - c0
        S.op("dve", lambda e: e.tensor_tensor(out=T.hT[dc][:, c0:c1], in0=pap[:, 0:n], in1=T.hT[dc][:, c0:c1], op=ALU.add),
             writes=[T.hT[dc], pb])

    t_proj(T, wout, D, T.xnT, KC, sink)
    t_ffn(T, g2, wg, wu, wd)
    t_store_h(T, h_out, outs)
    S.wait_all("sp", sorted(set(outs)))
    return nc, S


def build_Tba():
    nc = bass.Bass("TRN2", target_bir_lowering=False)
    dt = lambda n, s, k: nc.dram_tensor(n, list(s), F32, kind=k).ap()
    h_in = dt("h_in", [D, NT], "ExternalInput")
    yT = dt("yT", [D, NT], "ExternalInput")
    gs = dt("gs", [128, 4], "ExternalInput")
    wout = dt("wout", [KC, 128, KC * 128], "ExternalInput")
    g2 = dt("g2", [128, KC], "ExternalInput")
    wg = dt("wg", [FC, 128, KC * 128], "ExternalInput")
    wu = dt("wu", [FC, 128, KC * 128], "ExternalInput")
    wd = dt("wd", [(FC // 4) * KC, 128, 4 * 128], "ExternalInput")
    g1 = dt("g1", [128, KC], "ExternalInput")
    wg1 = dt("wg1", [FC, 128, KC * 128], "ExternalInput")
    wu1 = dt("wu1", [FC, 128, KC * 128], "ExternalInput")
    wd1 = dt("wd1", [(FC // 4) * KC, 128, 4 * 128], "ExternalInput")
    gm = dt("gm", [128, KC], "ExternalInput")
    win = dt("win", [42, 128, KC * 128], "ExternalInput")
    h_out = dt("h_out", [D, NT], "ExternalOutput")
    uT = dt("uT", [INW, NT], "ExternalOutput")
    S = Sched(nc)
    T = TCtx(S, n_ost=0)
    outs = []
    t_load_h(T, h_in)
    yst = [S.sb([128, NT], F32, "yst") for _ in range(4)]
    for j in range(4):
        S.dma("sp", yst[j][:], yT[(4 + j) * 128:(5 + j) * 128, :], writes=[yst[j]])
    t_rmsnorm(T, gs, src=yst, nk=4, dst=T.xnT[4:8], width=512)
    for kc in list(range(0, 4)) + list(range(8, 16)):
        st = yst[kc % 4]
        S.dma("sp", st[:], yT[kc * 128:(kc + 1) * 128, :], writes=[st])
        S.op("pool", lambda e, kc=kc, st=st: e.tensor_copy(T.xnT[kc][:], st[:]), reads=[st], writes=[T.xnT[kc]])

    def sink_o(ci, c0, c1, dc, m, pb, pap):
        n = c1 - c0
        S.op("dve", lambda e: e.tensor_tensor(out=T.hT[dc][:, c0:c1], in0=pap[:, 0:n], in1=T.hT[dc][:, c0:c1], op=ALU.add),
             writes=[T.hT[dc], pb])

    t_proj(T, wout, D, T.xnT, KC, sink_o)
    t_ffn(T, g2, wg, wu, wd)
    t_ffn(T, g1, wg1, wu1, wd1)
    t_store_h(T, h_out, outs)
    t_rmsnorm(T, gm)
    ost = Ring(yst)
    cur = {}

    def sink_u(ci, c0, c1, uc, m, pb, pap):
        if ci == 0:
            cur["o"] = ost.next()
        o = cur["o"]
        n = c1 - c0
        S.op("act", lambda e: e.activation(out=o[0:m, c0:c1], in_=pap[0:m, 0:n], func=AF.Copy), writes=[o, pb])
        if ci == len(CGS) - 1:
            outs.append(S.dma("act", uT[uc * 128:uc * 128 + m, :], o[0:m, :], reads=[o]))

    t_proj(T, win, INW, T.xnT, KC, sink_u)
    S.wait_all("sp", sorted(set(outs)))
    return nc, S


LP = 8320
NCH = 65


class MProg:
    def __init__(self):
        self.nc = bass.Bass("TRN2", target_bir_lowering=False)
        self.S = Sched(self.nc)
        self.outs = []
        self.banks = [self.S.ps([128, 512], F32, "bank") for _ in range(8)]

    def din(self, name, shape):
        return self.nc.dram_tensor(name, list(shape), F32, kind="ExternalInput").ap()

    def dout(self, name, shape):
        return self.nc.dram_tensor(name, list(shape), F32, kind="ExternalOutput").ap()

    def load(self, name, shape, dt=F32, dram=None):
        d = dram if dram is not None else self.din(name, shape)
        b = self.S.sb(shape, F32, name)
        if len(shape) == 2 and shape[1] > 2048:
            step = 2080
            for c0 in range(0, shape[1], step):
                c1 = min(shape[1], c0 + step)
                self.S.dma("sp", b[:, c0:c1], d[:, c0:c1], writes=[b])
        else:
            self.S.dma("sp", b[:], d, writes=[b])
        return b

    def store(self, dram_ap, buf, ap):
        self.S.dma("pool", dram_ap, ap, reads=[buf], is_out=True)

    def finish(self):
        self.S.wait_all("sp", sorted(getattr(self.S, "out_sems", set())))
        return self.nc


class Stream:
    def __init__(self, P, name, rows, total, width, nbuf=2):
        self.P = P
        self.d = P.din(name, [rows, total])
        self.rows = rows
        self.ring = Ring([P.S.sb([rows, width], F32, name) for _ in range(nbuf)])

    def get(self, c0, w):
        b = self.ring.next()
        self.P.S.dma("sp", b[:, 0:w], self.d[:, c0:c0 + w], writes=[b])
        return b


def build_ret(P=None, oname="yT", fin=True):
    P = P or MProg()
    S = P.S
    B = P.banks
    qT_s = Stream(P, "qT", 64, LP, 512); qsT_s = Stream(P, "qsT", 64, LP, 512)
    kT_s = Stream(P, "kT", 64, LP, 512); ksT_s = Stream(P, "ksT", 64, LP, 512)
    cosT_s = Stream(P, "cosT", 64, LP, 512); sinT_s = Stream(P, "sinT", 64, LP, 512)
    ktok_s = Stream(P, "k_tok", 128, NCH * 64, 256); kstok_s = Stream(P, "ks_tok", 128, NCH * 64, 256)
    costok_s = Stream(P, "cos_tok", 128, NCH * 64, 256); sintok_s = Stream(P, "sin_tok", 128, NCH * 64, 256)
    vtok_s = Stream(P, "v_tok", 128, NCH * 128, 512)
    gT_s = Stream(P, "gT", 128, LP, 512)
    qdec = P.load("qdecT", [64, 512])
    kdec = P.load("kdec", [128, 1])
    dmatT = P.load("dmatT", [128, 128])
    cdec = P.load("cdec", [64, 1])
    gcol = P.load("gcol", [128, 1])
    yT_d = P.dout(oname, [128, LP])
    tmp64 = S.sb([64, 512], F32, "tmp64")
    qd = S.sb([64, 512], F32, "qd")

    def mul(o, a, b_, w, rows=64):
        S.op("dve", lambda e: e.tensor_tensor(out=o[0:rows, 0:w], in0=a[0:rows, 0:w], in1=b_[0:rows, 0:w], op=ALU.mult),
             reads=[a, b_], writes=[o])

    def add(o, a, b_, w, rows=64):
        S.op("dve", lambda e: e.tensor_tensor(out=o[0:rows, 0:w], in0=a[0:rows, 0:w], in1=b_[0:rows, 0:w], op=ALU.add),
             reads=[a, b_], writes=[o])
    ones = S.sb([128, 128], F32, "ones")
    S.op("pool", lambda e: e.memset(ones[:], 1.0 / 128), writes=[ones])
    Sst = [S.sb([64, 128], F32, "Sst") for _ in range(2)]
    S.op("pool", lambda e: e.memset(Sst[0][:], 0.0), writes=[Sst[0]])
    atm = Ring([S.sb([128, 128], F32, "atm") for _ in range(2)])
    ybuf = Ring([S.sb([128, 512], F32, "ybuf") for _ in range(2)])
    yc = S.sb([128, 512], F32, "yc"); sq = S.sb([128, 512], F32, "sq"); rs = S.sb([128, 512], F32, "rs")
    AT = Ring([B[0], B[1]]); YT = Ring([B[2], B[3]]); SP_ = B[4]; LN1 = B[5]; LN2 = B[6]
    blocks = [(0, 1)] + [(1 + 4 * i, 4) for i in range(16)]
    for (cb, ncb) in blocks:
        yb = ybuf.next()
        W = ncb * 128
        p0 = cb * 128
        qT = qT_s.get(p0, W); qsT = qsT_s.get(p0, W); kT = kT_s.get(p0, W); ksT = ksT_s.get(p0, W)
        cosT = cosT_s.get(p0, W); sinT = sinT_s.get(p0, W)
        k_tok = ktok_s.get(cb * 64, ncb * 64); ks_tok = kstok_s.get(cb * 64, ncb * 64)
        cos_tok = costok_s.get(cb * 64, ncb * 64); sin_tok = sintok_s.get(cb * 64, ncb * 64)
        v_tok = vtok_s.get(p0, W)
        mul(qT, qT, cosT, W); mul(tmp64, qsT, sinT, W); add(qT, qT, tmp64, W)
        mul(kT, kT, cosT, W); mul(tmp64, ksT, sinT, W); add(kT, kT, tmp64, W)
        S.op("dve", lambda e: e.tensor_scalar(out=kT[:, 0:W], in0=kT[:, 0:W], scalar1=float(64 ** -0.5), scalar2=None,
                                              op0=ALU.mult), reads=[kT], writes=[kT])
        mul(qd, qT, qdec, W)
        w2 = ncb * 64
        mul(k_tok, k_tok, cos_tok, w2, 128); mul(ks_tok, ks_tok, sin_tok, w2, 128); add(k_tok, k_tok, ks_tok, w2, 128)
        S.op("dve", lambda e: e.tensor_scalar(out=k_tok[:, 0:w2], in0=k_tok[:, 0:w2], scalar1=kdec[:, 0:1], scalar2=None,
                                              op0=ALU.mult), reads=[k_tok, kdec], writes=[k_tok])
        for ci in range(ncb):
            c = cb + ci
            sl = slice(ci * 128, (ci + 1) * 128)
            Sp = Sst[c % 2]; Sn = Sst[(c + 1) % 2]
            at = AT.next(); yt = YT.next(); am = atm.next()
            S.op("pe", lambda e: e.matmul(at[:, 0:128], lhsT=kT[:, sl], rhs=qT[:, sl], start=True, stop=True),
                 reads=[kT, qT], writes=[at])
            S.op("dve", lambda e: e.tensor_tensor(out=am[:], in0=at[:, 0:128], in1=dmatT[:], op=ALU.mult),
                 reads=[dmatT], writes=[am, at])
            S.op("pe", lambda e: e.matmul(yt[:, 0:128], lhsT=v_tok[:, sl], rhs=am[:], start=True, stop=False),
                 reads=[v_tok, am], writes=[yt], inc=False)
            S.op("pe", lambda e: e.matmul(yt[:, 0:128], lhsT=Sp[:], rhs=qd[:, sl], start=False, stop=True),
                 reads=[Sp, qd], writes=[yt])
            S.op("act", lambda e: e.activation(out=yb[:, sl], in_=yt[:, 0:128], func=AF.Copy),
                 writes=[yb, yt])
            S.op("pe", lambda e: e.matmul(SP_[0:64, 0:128], lhsT=k_tok[:, ci * 64:(ci + 1) * 64], rhs=v_tok[:, sl],
                                          start=True, stop=True), reads=[k_tok, v_tok], writes=[SP_])
            S.op("dve", lambda e: e.scalar_tensor_tensor(out=Sn[:], in0=Sp[:], scalar=cdec[:, 0:1], in1=SP_[0:64, 0:128],
                                                         op0=ALU.mult, op1=ALU.add),
                 reads=[Sp, cdec], writes=[Sn, SP_])
        gb = gT_s.get(p0, W)
        S.op("act", lambda e: e.activation(out=gb[:, 0:W], in_=gb[:, 0:W], func=AF.Silu), reads=[gb], writes=[gb])
        S.op("pe", lambda e: e.matmul(LN1[:, 0:W], lhsT=ones[:], rhs=yb[:, 0:W], start=True, stop=True),
             reads=[ones, yb], writes=[LN1])
        S.op("dve", lambda e: e.tensor_tensor(out=yc[:, 0:W], in0=yb[:, 0:W], in1=LN1[:, 0:W], op=ALU.subtract),
             reads=[yb], writes=[yc, LN1])
        S.op("act", lambda e: e.activation(out=sq[:, 0:W], in_=yc[:, 0:W], func=AF.Square), reads=[yc], writes=[sq])
        S.op("pe", lambda e: e.matmul(LN2[:, 0:W], lhsT=ones[:], rhs=sq[:, 0:W], start=True, stop=True),
             reads=[ones, sq], writes=[LN2])
        S.op("act", lambda e: e.activation(out=rs[:, 0:W], in_=LN2[:, 0:W], func=AF.Sqrt, scale=1.0, bias=EPS),
             writes=[rs, LN2])
        S.op("dve", lambda e: e.reciprocal(rs[:, 0:W], rs[:, 0:W]), reads=[rs], writes=[rs])
        S.op("dve", lambda e: e.scalar_tensor_tensor(out=yc[:, 0:W], in0=yc[:, 0:W], scalar=gcol[:, 0:1], in1=rs[:, 0:W],
                                                     op0=ALU.mult, op1=ALU.mult), reads=[yc, gcol, rs], writes=[yc])
        S.op("dve", lambda e: e.tensor_tensor(out=gb[:, 0:W], in0=yc[:, 0:W], in1=gb[:, 0:W], op=ALU.mult),
             reads=[yc, gb], writes=[gb])
        P.store(yT_d[:, p0:p0 + W], gb, gb[:, 0:W])
    return P.finish() if fin else P


PAD = 112
C_ = np.ascontiguousarray


def tok_layout(a):
    F_ = a.shape[1]
    return C_(a.reshape(NCH, 128, F_).transpose(1, 0, 2).reshape(128, NCH * F_))


def rope_tables(dim):
    half = dim // 2
    inv = (10000.0 ** (-np.arange(half, dtype=np.float32) / half)).astype(np.float32)
    pos = (np.arange(LP, dtype=np.float32) - PAD).astype(np.float32)
    ang = (pos[:, None] * inv[None, :]).astype(np.float32)
    cos = np.cos(ang).astype(np.float32)
    sin = np.sin(ang).astype(np.float32)
    cos2 = np.concatenate([cos, cos], 1)
    sin2 = np.concatenate([-sin, sin], 1)
    return cos2, sin2


def swap_halves(a):
    h = a.shape[1] // 2
    return np.concatenate([a[:, h:], a[:, :h]], 1)


def prep_ret(up, ret_norm_l, core):
    h = core % 4
    o = 576 + 1544
    q = up[:, o + h * 64:o + (h + 1) * 64]
    k = up[:, o + 256 + h * 64:o + 256 + (h + 1) * 64]
    v = up[:, o + 512 + h * 128:o + 512 + (h + 1) * 128]
    g = up[:, o + 1024 + h * 128:o + 1024 + (h + 1) * 128]
    cos2, sin2 = rope_tables(64)
    log_g = np.log(np.float32(1.0) - np.float32(2.0) ** np.float32(-5.0 - h)).astype(np.float32)
    idx = np.arange(128, dtype=np.float32)
    qdec = np.exp((idx + 1.0) * log_g).astype(np.float32)
    kdec = (np.exp((127.0 - idx) * log_g) * (64.0 ** -0.5)).astype(np.float32)
    diff = idx[None, :] - idx[:, None]
    dmatT = np.where(diff >= 0, np.exp(diff * log_g), 0.0).astype(np.float32)
    return {
        "qT": C_(q.T), "qsT": C_(swap_halves(q).T), "kT": C_(k.T), "ksT": C_(swap_halves(k).T),
        "cosT": C_(cos2.T), "sinT": C_(sin2.T),
        "k_tok": tok_layout(k), "ks_tok": tok_layout(swap_halves(k)),
        "cos_tok": tok_layout(cos2), "sin_tok": tok_layout(sin2),
        "v_tok": tok_layout(v), "gT": C_(g.T),
        "qdecT": C_(np.tile(np.tile(qdec, 4)[None, :], (64, 1))),
        "kdec": C_(kdec[:, None]), "dmatT": C_(dmatT),
        "cdec": np.full((64, 1), np.exp(128.0 * log_g), np.float32),
        "gcol": C_(ret_norm_l[h][:, None].astype(np.float32)),
    }


def build_ssd(P=None, oname="yT", fin=True):
    P = P or MProg()
    S = P.S
    B = P.banks
    xsT_s = Stream(P, "xsT_pad", 64, LP + 3, 515); BT_s = Stream(P, "BT_pad", 128, LP + 3, 515)
    CT_s = Stream(P, "CT_pad", 128, LP + 3, 515); zT_s = Stream(P, "zT", 64, LP, 512)
    xtap = [Stream(P, "xs_tok%d" % j, 128, NCH * 64, 256) for j in range(4)]
    btap = [Stream(P, "B_tok%d" % j, 128, NCH * 128, 512) for j in range(4)]
    cw_xs = P.load("cw_xs", [64, 4]); cb_xs = P.load("cb_xs", [64, 1])
    cw_B = P.load("cw_B", [128, 4]); cb_B = P.load("cb_B", [128, 1])
    cw_C = P.load("cw_C", [128, 4]); cb_C = P.load("cb_C", [128, 1])
    cwt_xs = P.load("cwt_xs", [128, 4 * 256]); cbt_xs = P.load("cbt_xs", [128, 256])
    cwt_B = P.load("cwt_B", [128, 4 * 512]); cbt_B = P.load("cbt_B", [128, 512])
    dt = P.load("dt_tok", [128, NCH]); dtb = P.load("dtb", [128, 1]); alog = P.load("alog", [128, 1])
    valid = P.load("valid_tok", [128, NCH]); dcol = P.load("dcol", [64, 1])
    TriU = P.load("TriU", [128, 128]); UTs = P.load("UTs", [128, 128])
    yT_d = P.dout(oname, [64, LP])
    onesF = S.sb([128, 128], F32, "onesF")
    S.op("pool", lambda e: e.memset(onesF[:], 1.0), writes=[onesF])
    S.op("act", lambda e: e.activation(out=dt[:], in_=dt[:], func=AF.Exp, bias=dtb[:, 0:1]), reads=[dt, dtb], writes=[dt])
    S.op("act", lambda e: e.activation(out=dt[:], in_=dt[:], func=AF.Ln, bias=1.0), reads=[dt], writes=[dt])
    S.op("dve", lambda e: e.tensor_tensor(out=dt[:], in0=dt[:], in1=valid[:], op=ALU.mult), reads=[dt, valid], writes=[dt])
    S.op("act", lambda e: e.activation(out=alog[:], in_=alog[:], func=AF.Exp), reads=[alog], writes=[alog])
    la = S.sb([128, NCH], F32, "la"); cs = S.sb([128, NCH], F32, "cs"); dte = S.sb([128, NCH], F32, "dte")
    S.op("dve", lambda e: e.tensor_scalar(out=la[:], in0=dt[:], scalar1=alog[:, 0:1], scalar2=-1.0, op0=ALU.mult,
                                          op1=ALU.mult), reads=[dt, alog], writes=[la])
    S.op("pe", lambda e: e.matmul(B[5][:, 0:NCH], lhsT=TriU[:], rhs=la[:], start=True, stop=True),
         reads=[TriU, la], writes=[B[5]])
    S.op("act", lambda e: e.activation(out=cs[:], in_=B[5][:, 0:NCH], func=AF.Copy), writes=[cs, B[5]])
    S.op("pe", lambda e: e.matmul(B[6][:, 0:NCH], lhsT=onesF[:], rhs=la[:], start=True, stop=True),
         reads=[onesF, la], writes=[B[6]])
    S.op("dve", lambda e: e.tensor_tensor(out=dte[:], in0=B[6][:, 0:NCH], in1=cs[:], op=ALU.subtract),
         reads=[cs], writes=[dte, B[6]])
    S.op("act", lambda e: e.activation(out=dte[:], in_=dte[:], func=AF.Exp), reads=[dte], writes=[dte])

    def conv_fm(src, rows, W, cw, cb, dst):
        S.op("dve", lambda e: e.tensor_scalar(out=dst[0:rows, 0:W], in0=src[0:rows, 0:W], scalar1=cw[:, 0:1], scalar2=None,
                                              op0=ALU.mult), reads=[src, cw], writes=[dst])
        for j in range(1, 4):
            S.op("dve", lambda e, j=j: e.scalar_tensor_tensor(out=dst[0:rows, 0:W], in0=src[0:rows, j:W + j],
                                                              scalar=cw[:, j:j + 1], in1=dst[0:rows, 0:W],
                                                              op0=ALU.mult, op1=ALU.add), reads=[src, cw, dst], writes=[dst])
        S.op("act", lambda e: e.activation(out=dst[0:rows, 0:W], in_=dst[0:rows, 0:W], func=AF.Silu, bias=cb[:, 0:1]),
             reads=[dst, cb], writes=[dst])

    def conv_tm(taps, w, cwt, cbt, full, dst, tmp):
        for j in range(4):
            o = dst if j == 0 else tmp
            S.op("dve", lambda e, j=j, o=o: e.tensor_tensor(out=o[:, 0:w], in0=taps[j][:, 0:w],
                                                            in1=cwt[:, j * full:j * full + w], op=ALU.mult),
                 reads=[taps[j], cwt], writes=[o])
            if j > 0:
                S.op("dve", lambda e: e.tensor_tensor(out=dst[:, 0:w], in0=dst[:, 0:w], in1=tmp[:, 0:w], op=ALU.add),
                     reads=[dst, tmp], writes=[dst])
        S.op("dve", lambda e: e.tensor_tensor(out=dst[:, 0:w], in0=dst[:, 0:w], in1=cbt[:, 0:w], op=ALU.add),
             reads=[dst, cbt], writes=[dst])
        S.op("act", lambda e: e.activation(out=dst[:, 0:w], in_=dst[:, 0:w], func=AF.Silu), reads=[dst], writes=[dst])

    BTc = Ring([S.sb([128, 512], F32, "BTc") for _ in range(2)])
    CTc = Ring([S.sb([128, 512], F32, "CTc") for _ in range(2)])
    xsTc = Ring([S.sb([64, 512], F32, "xsTc") for _ in range(2)])
    xtok = Ring([S.sb([128, 256], F32, "xtok") for _ in range(2)])
    btok = Ring([S.sb([128, 512], F32, "btok") for _ in range(2)])
    tmpx = S.sb([128, 256], F32, "tmpx"); tmpb = S.sb([128, 512], F32, "tmpb")
    lam = Ring([S.sb([128, 128], F32, "lam") for _ in range(2)])
    laf = Ring([S.sb([128, 128], F32, "laf") for _ in range(2)])
    LT = Ring([S.sb([128, 128], F32, "LT") for _ in range(2)])
    Er = Ring([S.sb([128, 128], F32, "Er") for _ in range(2)])
    CsT = Ring([S.sb([128, 128], F32, "CsT") for _ in range(2)])
    xdt = Ring([S.sb([128, 64], F32, "xdt") for _ in range(2)])
    bd = Ring([S.sb([128, 128], F32, "bd") for _ in range(2)])
    ybuf = Ring([S.sb([64, 512], F32, "ybuf") for _ in range(2)])
    Sst = [S.sb([128, 64], F32, "Sst") for _ in range(2)]
    S.op("pool", lambda e: e.memset(Sst[0][:], 0.0), writes=[Sst[0]])
    GT = B[0]; SEG = B[1]; CSR = B[2]; YT = B[3]; SC = B[4]
    blocks = [(0, 1)] + [(1 + 4 * i, 4) for i in range(16)]
    for (cb, ncb) in blocks:
        W = ncb * 128
        p0 = cb * 128
        xs_in = xsT_s.get(p0, W + 3); B_in = BT_s.get(p0, W + 3); C_in = CT_s.get(p0, W + 3); zT = zT_s.get(p0, W)
        xt = [xtap[j].get(cb * 64, ncb * 64) for j in range(4)]
        bt = [btap[j].get(cb * 128, ncb * 128) for j in range(4)]
        BT = BTc.next(); CT = CTc.next(); xsT = xsTc.next(); xk = xtok.next(); bk = btok.next(); yb = ybuf.next()
        conv_fm(B_in, 128, W, cw_B, cb_B, BT)
        conv_fm(C_in, 128, W, cw_C, cb_C, CT)
        conv_fm(xs_in, 64, W, cw_xs, cb_xs, xsT)
        conv_tm(xt, ncb * 64, cwt_xs, cbt_xs, 256, xk, tmpx)
        conv_tm(bt, ncb * 128, cwt_B, cbt_B, 512, bk, tmpb)
        S.op("act", lambda e: e.activation(out=zT[:, 0:W], in_=zT[:, 0:W], func=AF.Silu), reads=[zT], writes=[zT])
        for ci in range(ncb):
            c = cb + ci
            sl = slice(ci * 128, (ci + 1) * 128)
            Sp = Sst[c % 2]; Sn = Sst[(c + 1) % 2]
            lm = lam.next(); lf = laf.next(); lt = LT.next(); er = Er.next(); cst = CsT.next(); xd = xdt.next(); bdd = bd.next()
            lac = la[:, c:c + 1]
            S.op("pe", lambda e: e.matmul(GT[:, 0:128], lhsT=BT[:, sl], rhs=CT[:, sl], start=True, stop=True),
                 reads=[BT, CT], writes=[GT])
            S.op("dve", lambda e: e.tensor_scalar(out=lm[:], in0=UTs[:], scalar1=lac, scalar2=None, op0=ALU.mult),
                 reads=[UTs, la], writes=[lm])
            S.op("dve", lambda e: e.tensor_scalar(out=lf[:], in0=onesF[:], scalar1=lac, scalar2=None, op0=ALU.mult),
                 reads=[onesF, la], writes=[lf])
            S.op("pe", lambda e: e.matmul(SEG[:, 0:128], lhsT=lm[:], rhs=TriU[:], start=True, stop=True),
                 reads=[lm, TriU], writes=[SEG])
            S.op("pe", lambda e: e.matmul(CSR[:, 0:128], lhsT=lf[:], rhs=TriU[:], start=True, stop=True),
                 reads=[lf, TriU], writes=[CSR])
            S.op("act", lambda e: e.activation(out=lt[:], in_=SEG[:, 0:128], func=AF.Exp), writes=[lt, SEG])
            S.op("dve", lambda e: e.tensor_tensor(out=lt[:], in0=lt[:], in1=TriU[:], op=ALU.mult), reads=[lt, TriU], writes=[lt])
            S.op("dve", lambda e: e.tensor_tensor(out=lt[:], in0=lt[:], in1=GT[:, 0:128], op=ALU.mult), reads=[lt], writes=[lt, GT])
            S.op("act", lambda e: e.activation(out=er[:], in_=CSR[:, 0:128], func=AF.Exp), writes=[er, CSR])
            S.op("dve", lambda e: e.tensor_tensor(out=cst[:], in0=CT[:, sl], in1=er[:], op=ALU.mult), reads=[CT, er], writes=[cst])
            S.op("dve", lambda e: e.tensor_scalar(out=xd[:], in0=xk[:, ci * 64:(ci + 1) * 64], scalar1=dt[:, c:c + 1],
                                                  scalar2=None, op0=ALU.mult), reads=[xk, dt], writes=[xd])
            S.op("dve", lambda e: e.tensor_scalar(out=bdd[:], in0=bk[:, sl], scalar1=dte[:, c:c + 1], scalar2=None,
                                                  op0=ALU.mult), reads=[bk, dte], writes=[bdd])
            S.op("pe", lambda e: e.matmul(YT[0:64, 0:128], lhsT=xd[:], rhs=lt[:], start=True, stop=False),
                 reads=[xd, lt], writes=[YT], inc=False)
            S.op("pe", lambda e: e.matmul(YT[0:64, 0:128], lhsT=Sp[:], rhs=cst[:], start=False, stop=True),
                 reads=[Sp, cst], writes=[YT])
            S.op("dve", lambda e: e.scalar_tensor_tensor(out=yb[:, sl], in0=xsT[:, sl], scalar=dcol[:, 0:1],
                                                         in1=YT[0:64, 0:128], op0=ALU.mult, op1=ALU.add),
                 reads=[xsT, dcol], writes=[yb, YT])
            S.op("pe", lambda e: e.matmul(SC[:, 0:64], lhsT=bdd[:], rhs=xd[:], start=True, stop=True),
                 reads=[bdd, xd], writes=[SC])
            S.op("dve", lambda e: e.scalar_tensor_tensor(out=Sn[:], in0=Sp[:], scalar=er[:, 127:128], in1=SC[:, 0:64],
                                                         op0=ALU.mult, op1=ALU.add), reads=[Sp, er], writes=[Sn, SC])
        S.op("dve", lambda e: e.tensor_tensor(out=yb[:, 0:W], in0=yb[:, 0:W], in1=zT[:, 0:W], op=ALU.mult),
             reads=[yb, zT], writes=[yb])
        P.store(yT_d[:, p0:p0 + W], yb, yb[:, 0:W])
    return P.finish() if fin else P


def build_ssd_ret():
    P = MProg()
    build_ssd(P, "yT_ssd", fin=False)
    build_ret(P, "yT_ret", fin=False)
    return P.finish()


def prep_ssd(up, p, l, core):
    j = core
    g = j // 4
    o = 576
    z = up[:, o + j * 64:o + (j + 1) * 64]
    xo = o + 512
    xs = up[:, xo + j * 64:xo + (j + 1) * 64]
    Bm = up[:, xo + 512 + g * 128:xo + 512 + (g + 1) * 128]
    Cm = up[:, xo + 768 + g * 128:xo + 768 + (g + 1) * 128]
    dtr = up[:, xo + 1024 + j:xo + 1024 + j + 1]
    cw = p['ssd_conv_w'][l]
    cb = p['ssd_conv_b'][l]
    ch_xs = slice(j * 64, (j + 1) * 64)
    ch_B = slice(512 + g * 128, 512 + (g + 1) * 128)
    ch_C = slice(768 + g * 128, 768 + (g + 1) * 128)
    pad3 = lambda a: np.concatenate([np.zeros((3, a.shape[1]), np.float32), a], 0)
    xs_p = pad3(xs); B_p = pad3(Bm); C_p = pad3(Cm)
    valid = np.ones((LP, 1), np.float32); valid[:PAD] = 0
    idx = np.arange(128)
    TriU = (idx[:, None] <= idx[None, :]).astype(np.float32)
    m = {
        "xsT_pad": C_(xs_p.T), "BT_pad": C_(B_p.T), "CT_pad": C_(C_p.T), "zT": C_(z.T),
        "cw_xs": C_(cw[:, ch_xs].T), "cb_xs": C_(cb[ch_xs][:, None]),
        "cw_B": C_(cw[:, ch_B].T), "cb_B": C_(cb[ch_B][:, None]),
        "cw_C": C_(cw[:, ch_C].T), "cb_C": C_(cb[ch_C][:, None]),
        "cwt_xs": C_(np.tile(np.tile(cw[:, ch_xs], (1, 4)).reshape(1, 4 * 256), (128, 1))),
        "cbt_xs": C_(np.tile(np.tile(cb[ch_xs], 4)[None, :], (128, 1))),
        "cwt_B": C_(np.tile(np.tile(cw[:, ch_B], (1, 4)).reshape(1, 4 * 512), (128, 1))),
        "cbt_B": C_(np.tile(np.tile(cb[ch_B], 4)[None, :], (128, 1))),
        "dt_tok": tok_layout(dtr), "dtb": np.full((128, 1), p['ssd_dt_bias'][l][j], np.float32),
        "alog": np.full((128, 1), p['ssd_a_log'][l][j], np.float32),
        "valid_tok": tok_layout(valid), "dcol": np.full((64, 1), p['ssd_d'][l][j], np.float32),
        "TriU": C_(TriU), "UTs": C_(1.0 - TriU),
    }
    for t in range(4):
        m["xs_tok%d" % t] = tok_layout(xs_p[t:t + LP])
        m["B_tok%d" % t] = tok_layout(B_p[t:t + LP])
    return m


def build_mla():
    P = MProg()
    S = P.S
    B = P.banks
    NQ = 128 + 8 * 512
    cq_s = [Stream(P, "cqT%d" % i, 128, NQ, 512) for i in range(3)]
    cosq_s = Stream(P, "cosqT", 64, NQ, 512); sinq_s = Stream(P, "sinqT", 64, NQ, 512)
    ckv_s = Stream(P, "ckvT", 128, LP, 512)
    kpe_s = Stream(P, "kpeT", 64, LP, 512); kpes_s = Stream(P, "kpesT", 64, LP, 512)
    cos_s = Stream(P, "cosT", 64, LP, 512); sin_s = Stream(P, "sinT", 64, LP, 512)
    yT_d = P.dout("yT", [128, NQ])

    def loadbf(name, shape):
        f = P.load(name, shape)
        b = S.sb(shape, BF16, name + "b")
        S.op("dve", lambda e: e.tensor_copy(b[:], f[:]), reads=[f], writes=[b])
        return b
    wqn = loadbf("wq_n", [128, 3 * 128]); wqr = loadbf("wq_r", [128, 3 * 64]); wqrs = loadbf("wq_rs", [128, 3 * 64])
    wk = loadbf("wk", [128, 128]); wv = loadbf("wv", [128, 128])
    ones0b = loadbf("ones0", [128, 128])
    gqn = P.load("gqn", [128, 3]); gkv = P.load("gkv", [128, 1])
    gq_n = P.load("gq_n", [128, 1]); gq_r = P.load("gq_r", [64, 1]); gq_rs = P.load("gq_rs", [64, 1])
    gk_n = P.load("gk_n", [128, 1]); gk_r = P.load("gk_r", [64, 1]); gk_rs = P.load("gk_rs", [64, 1])
    mask8 = P.load("mask8", [128, 8 * 512]); mtri = P.load("mtri", [128, 128])
    onesF = S.sb([128, 128], F32, "onesF"); onesb = S.sb([128, 128], BF16, "onesb")
    S.op("pool", lambda e: e.memset(onesF[:], 1.0), writes=[onesF])
    S.op("pool", lambda e: e.memset(onesb[:], 1.0), writes=[onesb])
    KnT = S.sb([128, LP], BF16, "KnT"); KrT = S.sb([64, LP], BF16, "KrT"); Vt = S.sb([128, LP], BF16, "Vt")
    sqa = Ring([S.sb([128, 512], F32, "sqa") for _ in range(2)])
    rstd = S.sb([128, 512], F32, "rstd"); rstdq = S.sb([128, 512], F32, "rstdq"); rstdk = S.sb([128, 512], F32, "rstdk")
    cqn = [S.sb([128, 512], BF16, "cqn") for _ in range(3)]
    ckvn = S.sb([128, 512], BF16, "ckvn")
    qn_fs = [S.sb([128, 512], BF16, "qn_f") for _ in range(2)]; qr_fs = [S.sb([64, 512], BF16, "qr_f") for _ in range(2)]
    t1 = S.sb([64, 512], F32, "t1"); t2 = S.sb([64, 512], F32, "t2")
    PTr = Ring([S.sb([128, 512], BF16, "PT") for _ in range(3)])
    dacc = S.sb([128, 512], F32, "dacc"); vcol = P.load("vcol", [128, 1])
    rden = S.sb([128, 512], F32, "rden"); yo = Ring([S.sb([128, 512], F32, "yo") for _ in range(2)])
    STb = Ring([B[0], B[1]]); OB = B[2]; DB = B[3]; NB = B[4]; Q1 = B[5]; Q2 = B[6]; Q3 = B[7]
    SCALE = float(192 ** -0.5)

    def rms(parts, W, width, dst, post_scale=1.0):
        n = len(parts)
        for i, (buf, ap, rows, is_ps) in enumerate(parts):
            sq = sqa.next()
            if is_ps:
                S.op("act", lambda e, sq=sq, rows=rows, ap=ap: e.activation(out=sq[0:rows, 0:W], in_=ap, func=AF.Square), writes=[sq, buf])
            else:
                S.op("act", lambda e, sq=sq, rows=rows, ap=ap: e.activation(out=sq[0:rows, 0:W], in_=ap, func=AF.Square), reads=[buf], writes=[sq])
            S.op("pe", lambda e, sq=sq, rows=rows, i=i: e.matmul(NB[:, 0:W], lhsT=onesF[0:rows, :], rhs=sq[0:rows, 0:W], start=(i == 0),
                                                            stop=(i == n - 1)), reads=[onesF, sq], writes=[NB])
        S.op("act", lambda e: e.activation(out=dst[:, 0:W], in_=NB[:, 0:W], func=AF.Sqrt, scale=1.0 / width, bias=EPS),
             writes=[dst, NB])
        S.op("dve", lambda e: e.reciprocal(dst[:, 0:W], dst[:, 0:W]), reads=[dst], writes=[dst])
        if post_scale != 1.0:
            S.op("dve", lambda e: e.tensor_scalar(out=dst[:, 0:W], in0=dst[:, 0:W], scalar1=post_scale, scalar2=None,
                                                  op0=ALU.mult), reads=[dst], writes=[dst])

    blocks = [(0, 1)] + [(1 + 4 * i, 4) for i in range(16)]
    Kn_b = [Buf(KnT.ap[:, cb * 128:(cb + ncb) * 128], "Kn") for (cb, ncb) in blocks]
    Kr_b = [Buf(KrT.ap[:, cb * 128:(cb + ncb) * 128], "Kr") for (cb, ncb) in blocks]
    V_b = [Buf(Vt.ap[:, cb * 128:(cb + ncb) * 128], "V") for (cb, ncb) in blocks]
    blk_of = {}
    for bi_, (cb_, ncb_) in enumerate(blocks):
        for c_ in range(cb_, cb_ + ncb_):
            blk_of[c_] = (bi_, (c_ - cb_) * 128)

    def prep_q(it):
        W = 128 if it < 0 else 512
        q0 = 0 if it < 0 else 128 + it * 512
        qn_f = qn_fs[it % 2]; qr_f = qr_fs[it % 2]
        cq = [s.get(q0, W) for s in cq_s]
        cosb = cosq_s.get(q0, W); sinb = sinq_s.get(q0, W)
        rms([(cq[i], cq[i][:, 0:W], 128, False) for i in range(3)], W, 384.0, rstd)
        for i in range(3):
            S.op("dve", lambda e, i=i: e.scalar_tensor_tensor(out=cqn[i][:, 0:W], in0=cq[i][:, 0:W], scalar=gqn[:, i:i + 1],
                                                              in1=rstd[:, 0:W], op0=ALU.mult, op1=ALU.mult),
                 reads=[cq[i], gqn, rstd], writes=[cqn[i]])
        for (pb, wt, m) in ((Q1, wqn, 128), (Q2, wqr, 64), (Q3, wqrs, 64)):
            for i in range(3):
                S.op("pe", lambda e, i=i, pb=pb, wt=wt, m=m: e.matmul(pb[0:m, 0:W], lhsT=wt[:, i * m:(i + 1) * m], rhs=cqn[i][:, 0:W],
                                                                      start=(i == 0), stop=(i == 2)), reads=[wt, cqn[i]], writes=[pb],
                     inc=(i == 2))
        rms([(Q1, Q1[:, 0:W], 128, True), (Q2, Q2[0:64, 0:W], 64, True)], W, 192.0, rstdq, SCALE)
        S.op("dve", lambda e: e.scalar_tensor_tensor(out=qn_f[:, 0:W], in0=Q1[:, 0:W], scalar=gq_n[:, 0:1], in1=rstdq[:, 0:W],
                                                     op0=ALU.mult, op1=ALU.mult), reads=[gq_n, rstdq], writes=[qn_f, Q1])
        S.op("dve", lambda e: e.scalar_tensor_tensor(out=t1[:, 0:W], in0=Q2[0:64, 0:W], scalar=gq_r[:, 0:1], in1=cosb[:, 0:W],
                                                     op0=ALU.mult, op1=ALU.mult), reads=[gq_r, cosb], writes=[t1, Q2])
        S.op("dve", lambda e: e.scalar_tensor_tensor(out=t2[:, 0:W], in0=Q3[0:64, 0:W], scalar=gq_rs[:, 0:1], in1=sinb[:, 0:W],
                                                     op0=ALU.mult, op1=ALU.mult), reads=[gq_rs, sinb], writes=[t2, Q3])
        S.op("dve", lambda e: e.tensor_tensor(out=t1[:, 0:W], in0=t1[:, 0:W], in1=t2[:, 0:W], op=ALU.add), reads=[t1, t2], writes=[t1])
        S.op("dve", lambda e: e.tensor_tensor(out=qr_f[:, 0:W], in0=t1[:, 0:W], in1=rstdq[0:64, 0:W], op=ALU.mult),
             reads=[t1, rstdq], writes=[qr_f])

    def prep_kv(bi):
        cb, ncb = blocks[bi]
        W = ncb * 128
        p0 = cb * 128
        KnB = Kn_b[bi]; KrB = Kr_b[bi]; VB = V_b[bi]
        ckv = ckv_s.get(p0, W); kpe = kpe_s.get(p0, W); kpes = kpes_s.get(p0, W)
        cosb = cos_s.get(p0, W); sinb = sin_s.get(p0, W)
        rms([(ckv, ckv[:, 0:W], 128, False)], W, 128.0, rstd)
        S.op("dve", lambda e: e.scalar_tensor_tensor(out=ckvn[:, 0:W], in0=ckv[:, 0:W], scalar=gkv[:, 0:1], in1=rstd[:, 0:W],
                                                     op0=ALU.mult, op1=ALU.mult), reads=[ckv, gkv, rstd], writes=[ckvn])
        S.op("pe", lambda e: e.matmul(Q1[:, 0:W], lhsT=wk[:], rhs=ckvn[:, 0:W], start=True, stop=True),
             reads=[wk, ckvn], writes=[Q1])
        for ci in range(ncb):
            c = cb + ci
            S.op("pe", lambda e, ci=ci: e.matmul(Q3[:, 0:128], lhsT=ckvn[:, ci * 128:(ci + 1) * 128], rhs=wv[:], start=True, stop=True),
                 reads=[ckvn, wv], writes=[Q3])
            S.op("act", lambda e, ci=ci: e.activation(out=VB[:, ci * 128:(ci + 1) * 128], in_=Q3[:, 0:128], func=AF.Copy),
                 writes=[VB, Q3])
        rms([(Q1, Q1[:, 0:W], 128, True), (kpe, kpe[:, 0:W], 64, False)], W, 192.0, rstdk)
        S.op("dve", lambda e: e.scalar_tensor_tensor(out=KnB[:, 0:W], in0=Q1[:, 0:W], scalar=gk_n[:, 0:1], in1=rstdk[:, 0:W],
                                                     op0=ALU.mult, op1=ALU.mult), reads=[gk_n, rstdk], writes=[KnB, Q1])
        S.op("dve", lambda e: e.scalar_tensor_tensor(out=t1[:, 0:W], in0=kpe[:, 0:W], scalar=gk_r[:, 0:1], in1=cosb[:, 0:W],
                                                     op0=ALU.mult, op1=ALU.mult), reads=[kpe, gk_r, cosb], writes=[t1])
        S.op("dve", lambda e: e.scalar_tensor_tensor(out=t2[:, 0:W], in0=kpes[:, 0:W], scalar=gk_rs[:, 0:1], in1=sinb[:, 0:W],
                                                     op0=ALU.mult, op1=ALU.mult), reads=[kpes, gk_rs, sinb], writes=[t2])
        S.op("dve", lambda e: e.tensor_tensor(out=t1[:, 0:W], in0=t1[:, 0:W], in1=t2[:, 0:W], op=ALU.add), reads=[t1, t2], writes=[t1])
        S.op("dve", lambda e: e.tensor_tensor(out=KrB[:, 0:W], in0=t1[:, 0:W], in1=rstdk[0:64, 0:W], op=ALU.mult),
             reads=[t1, rstdk], writes=[KrB])

    def attn(it):
        W = 128 if it < 0 else 512
        q0 = 0 if it < 0 else 128 + it * 512
        qn_f = qn_fs[it % 2]; qr_f = qr_fs[it % 2]
        nk = 1 if it < 0 else 8 * it + 9
        def scores(kc):
            kb, ko = blk_of[kc]
            ks = slice(ko, ko + 128)
            st = STb.next(); pt = PTr.next()
            S.op("pe", lambda e: e.matmul(st[:, 0:W], lhsT=Kn_b[kb][:, ks], rhs=qn_f[:, 0:W], start=True, stop=False),
                 reads=[Kn_b[kb], qn_f], writes=[st], inc=False)
            S.op("pe", lambda e: e.matmul(st[:, 0:W], lhsT=Kr_b[kb][:, ks], rhs=qr_f[:, 0:W], start=False, stop=True),
                 reads=[Kr_b[kb], qr_f], writes=[st])
            S.op("act", lambda e: e.activation(out=pt[:, 0:W], in_=st[:, 0:W], func=AF.Exp), writes=[pt, st])
            if it < 0:
                S.op("pool", lambda e: e.tensor_tensor(out=pt[:, 0:W], in0=pt[:, 0:W], in1=mtri[:, 0:W], op=ALU.mult),
                     reads=[pt, mtri], writes=[pt])
            elif kc >= nk - 8:
                k_ = kc - (nk - 8)
                S.op("pool", lambda e: e.tensor_tensor(out=pt[:, 0:W], in0=pt[:, 0:W], in1=mask8[:, k_ * 512:k_ * 512 + W],
                                                       op=ALU.mult), reads=[pt, mask8], writes=[pt])
            return pt

        def pv(kc, pt):
            kb, ko = blk_of[kc]
            ks = slice(ko, ko + 128)
            S.op("pe", lambda e: e.matmul(OB[:, 0:W], lhsT=V_b[kb][:, ks], rhs=pt[:, 0:W], start=(kc == 0), stop=(kc == nk - 1)),
                 reads=[V_b[kb], pt], writes=[OB], inc=False)
            S.op("pe", lambda e: e.matmul(DB[:, 0:W], lhsT=(ones0b if kc == 0 else onesb)[:], rhs=pt[:, 0:W],
                                          start=(kc == 0), stop=(kc == nk - 1)), reads=[ones0b, onesb, pt], writes=[DB])
        pend = scores(0)
        for kc in range(nk):
            nxt = scores(kc + 1) if kc + 1 < nk else None
            pv(kc, pend)
            pend = nxt
        y = yo.next()
        S.op("dve", lambda e: e.tensor_scalar(out=rden[:, 0:W], in0=DB[:, 0:W], scalar1=1e-30, scalar2=None, op0=ALU.max),
             writes=[rden, DB])
        S.op("dve", lambda e: e.reciprocal(rden[:, 0:W], rden[:, 0:W]), reads=[rden], writes=[rden])
        S.op("dve", lambda e: e.tensor_tensor(out=y[:, 0:W], in0=OB[:, 0:W], in1=rden[:, 0:W], op=ALU.mult),
             reads=[rden], writes=[y, OB])
        P.store(yT_d[:, q0:q0 + W], y, y[:, 0:W])

    def prep_iter(it):
        prep_kv(2 * it + 1); prep_kv(2 * it + 2); prep_q(it)
    prep_kv(0); prep_q(-1)
    la = S.record(); attn(-1); S.stop_record()
    lp = S.record(); prep_iter(0); S.stop_record()
    S.replay_weighted([la, lp])
    for it in range(8):
        la = S.record(); attn(it); S.stop_record()
        if it + 1 < 8:
            lp = S.record(); prep_iter(it + 1); S.stop_record()
            S.replay_weighted([la, lp])
        else:
            S.replay_weighted([la])
    return P.finish()


def mla_qpos(core):
    par = core // 4
    pos = [np.arange(128)]
    for it in range(8):
        cb = 8 * it + 1 + 4 * par
        pos.append(cb * 128 + np.arange(512))
    return np.concatenate(pos)


def prep_mla(up, p, l, core):
    h = core % 4
    par = core // 4
    cq = up[:, 0:384]; ckv = up[:, 384:512]; kpe = up[:, 512:576]
    cos2, sin2 = rope_tables(64)
    qpos = mla_qpos(core)
    wq = p['mla_w_q_up'][l][:, h * 192:(h + 1) * 192]
    wkv = p['mla_w_kv_up'][l][:, h * 256:(h + 1) * 256]
    kcl = lambda w: C_(w.reshape(3, 128, w.shape[1]).transpose(1, 0, 2).reshape(128, 3 * w.shape[1]))
    gq = p['mla_qk_norm_q'][l]; gk = p['mla_qk_norm_k'][l]
    col = lambda v: C_(v[:, None].astype(np.float32))
    idx = np.arange(128)
    tri = (idx[:, None] <= idx[None, :]).astype(np.float32)
    d = np.zeros((4, 128, 4, 128), np.float32)
    for k in range(4):
        for qi in range(4):
            if qi > k:
                d[k, :, qi, :] = 1.0
            elif qi == k:
                d[k, :, qi, :] = tri
    d = d.reshape(4, 128, 512)
    Z = np.zeros((4, 128, 512), np.float32); O = np.ones((4, 128, 512), np.float32)
    m8 = np.concatenate([d, Z], 0) if par == 0 else np.concatenate([O, d], 0)
    mask8 = C_(m8.transpose(1, 0, 2).reshape(128, 8 * 512))
    ones0 = np.ones((128, 128), np.float32); ones0[:PAD] = 0
    cqq = cq[qpos]
    return {
        "cqT0": C_(cqq[:, 0:128].T), "cqT1": C_(cqq[:, 128:256].T), "cqT2": C_(cqq[:, 256:384].T),
        "cosqT": C_(cos2[qpos].T), "sinqT": C_(sin2[qpos].T),
        "ckvT": C_(ckv.T), "kpeT": C_(kpe.T), "kpesT": C_(swap_halves(kpe).T),
        "cosT": C_(cos2.T), "sinT": C_(sin2.T),
        "wq_n": kcl(wq[:, 0:128]), "wq_r": kcl(wq[:, 128:192]), "wq_rs": kcl(swap_halves(wq[:, 128:192])),
        "wk": C_(wkv[:, 0:128]), "wv": C_(wkv[:, 128:256]), "ones0": ones0,
        "gqn": C_(p['mla_q_norm'][l].reshape(3, 128).T), "gkv": col(p['mla_kv_norm'][l]),
        "gq_n": col(gq[0:128]), "gq_r": col(gq[128:192]), "gq_rs": col(swap_halves(gq[None, 128:192])[0]),
        "gk_n": col(gk[0:128]), "gk_r": col(gk[128:192]), "gk_rs": col(swap_halves(gk[None, 128:192])[0]),
        "mask8": mask8, "mtri": C_(tri), "vcol": C_(ones0[:, 0:1]),
    }


RWKV_FP32R = False


def build_rwkv():
    P = MProg()
    S = P.S
    B = P.banks
    st = {}
    for nm, rows, tot, w in (("r", 128, NCH * 64, 256), ("k", 128, NCH * 64, 256), ("v", 128, NCH * 64, 256)):
        st[nm] = Stream(P, nm + "_tok", rows, tot, w)
        st[nm + "p"] = Stream(P, nm + "p_tok", rows, tot, w)
    for nm, rows in (("wd", 32), ("ad", 32), ("gd", 64)):
        st[nm] = Stream(P, nm + "T", rows, LP, 512)
        st[nm + "p"] = Stream(P, nm + "pT", rows, LP, 512)
    mu_r = P.load("mu_r", [128, 256]); mu_k = P.load("mu_k", [128, 256]); mu_v = P.load("mu_v", [128, 256])
    mu_wd = P.load("mu_wd", [32, 1]); mu_ad = P.load("mu_ad", [32, 1]); mu_gd = P.load("mu_gd", [64, 1])
    w2h = P.load("w2h", [32, 64]); a2h = P.load("a2h", [32, 64]); g2h = P.load("g2h", [64, 64])
    w0t = P.load("w0t", [128, 64]); a0t = P.load("a0t", [128, 64]); kkt = P.load("kkt", [128, 64])
    kat = P.load("kat", [128, 64]); rkt = P.load("rkt", [128, 64]); lng = P.load("lng", [64, 1])
    TriU = P.load("TriU", [128, 128]); SL = P.load("SL", [128, 128]); SU = P.load("SU", [128, 128])
    Id = P.load("Ident", [128, 128])
    yT_d = P.dout("yT", [64, LP])
    ones64 = S.sb([64, 64], F32, "ones64")
    S.op("pool", lambda e: e.memset(ones64[:], 1.0 / 64), writes=[ones64])
    NX = 6
    X = [S.sb([64, 64], F32, "X") for _ in range(NX)]
    S.op("pool", lambda e: e.memset(X[0][:], 0.0), writes=[X[0]])

    def T(shape, name):
        return S.sb(shape, F32, name)
    blkbufs = []
    for _ in range(2):
        blkbufs.append(dict(rs=T([128, 256], "rs"), ks=T([128, 256], "ks"), vs=T([128, 256], "vs"),
                            tw=T([32, 512], "tw"), ads=T([32, 512], "ads"), sg=T([64, 512], "sg"),
                            gT=T([64, 512], "gT"), yb=T([64, 512], "yblk")))
    dtm = T([128, 256], "dtm"); d32 = T([64, 512], "d32")
    names = ["ld", "a", "kk", "kkn", "kmod", "bb", "cs_e", "Eg", "Eneg", "Ege", "Kt", "Bh", "Kh", "Rt", "t3", "Vs", "SA", "U_"]
    lanes_buf = []
    for ln in range(4):
        L = dict(c64={n: T([128, 64], n) for n in names},
                 col={n: T([128, 1], n) for n in ["ss", "rn", "sbon"]},
                 fm={n: T([64, 128], n) for n in ["KtT", "BhT", "KhT", "RtT", "WT", "bonT", "oT", "oc", "osq", "ors"]},
                 sq={n: T([128, 128], n) for n in ["Pa", "PTa", "Pb", "PTb", "A", "MakT", "MrbT", "MrkT"]},
                 gam=T([64, 1], "gam"), Xg=T([64, 64], "Xg"), a=B[2 * ln], b=B[2 * ln + 1])
        lanes_buf.append(L)

    F32R = mybir.dt.float32r

    def rr(ap):
        return ap.bitcast(F32R) if RWKV_FP32R else ap

    def mm(pb, pap, lhsT, rhs, reads, start=True, stop=True, inc=True, fast=False):
        if fast and RWKV_FP32R:
            lhsT = lhsT.bitcast(F32R); rhs = rhs.bitcast(F32R)
        S.op("pe", lambda e: e.matmul(pap, lhsT=lhsT, rhs=rhs, start=start, stop=stop), reads=reads, writes=[pb], inc=inc)

    def tt(o, oap, a, aap, b_, bap, op, extra_w=()):
        S.op("dve", lambda e: e.tensor_tensor(out=oap, in0=aap, in1=bap, op=op), reads=[a, b_], writes=[o] + list(extra_w))

    def chunk(L, bb, ci, c):
        D_ = L["c64"]; col = L["col"]; fm = L["fm"]; sq = L["sq"]; gam = L["gam"]; Xg = L["Xg"]
        Ga = L["a"]; Gb = L["b"]
        rs_, ks_, vs_, tw, ads, gT, yb = bb["rs"], bb["ks"], bb["vs"], bb["tw"], bb["ads"], bb["gT"], bb["yb"]
        s64 = slice(ci * 64, (ci + 1) * 64)
        sl = slice(ci * 128, (ci + 1) * 128)
        r_ap, k_ap, v_ap = rs_[:, s64], ks_[:, s64], vs_[:, s64]
        mm(Ga, Ga[:, 0:64], tw[:, sl], w2h[:], [tw, w2h])
        tt(D_["ld"], D_["ld"][:], w0t, w0t[:], w0t, Ga[:, 0:64], ALU.add, extra_w=[Ga])
        S.op("act", lambda e: e.activation(out=D_["ld"][:], in_=D_["ld"][:], func=AF.Sigmoid), reads=[D_["ld"]], writes=[D_["ld"]])
        S.op("dve", lambda e: e.tensor_scalar(out=D_["ld"][:], in0=D_["ld"][:], scalar1=float(-np.exp(-0.5)), scalar2=None,
                                              op0=ALU.mult), reads=[D_["ld"]], writes=[D_["ld"]])
        mm(Ga, Ga[:, 64:128], ads[:, sl], a2h[:], [ads, a2h])
        tt(D_["a"], D_["a"][:], a0t, a0t[:], a0t, Ga[:, 64:128], ALU.add, extra_w=[Ga])
        S.op("act", lambda e: e.activation(out=D_["a"][:], in_=D_["a"][:], func=AF.Sigmoid), reads=[D_["a"]], writes=[D_["a"]])
        tt(D_["kk"], D_["kk"][:], ks_, k_ap, kkt, kkt[:], ALU.mult)
        S.op("act", lambda e: e.activation(out=D_["t3"][:], in_=D_["kk"][:], func=AF.Square, accum_out=col["ss"][:, 0:1]),
             reads=[D_["kk"]], writes=[D_["t3"], col["ss"]])
        S.op("act", lambda e: e.activation(out=col["rn"][:], in_=col["ss"][:], func=AF.Sqrt), reads=[col["ss"]], writes=[col["rn"]])
        S.op("dve", lambda e: e.tensor_scalar(out=col["rn"][:], in0=col["rn"][:], scalar1=1e-12, scalar2=None, op0=ALU.max),
             reads=[col["rn"]], writes=[col["rn"]])
        S.op("dve", lambda e: e.reciprocal(col["rn"][:], col["rn"][:]), reads=[col["rn"]], writes=[col["rn"]])
        S.op("dve", lambda e: e.tensor_scalar(out=D_["kkn"][:], in0=D_["kk"][:], scalar1=col["rn"][:, 0:1], scalar2=None,
                                              op0=ALU.mult), reads=[D_["kk"], col["rn"]], writes=[D_["kkn"]])
        S.op("dve", lambda e: e.scalar_tensor_tensor(out=D_["kmod"][:], in0=D_["a"][:], scalar=-1.0, in1=kat[:],
                                                     op0=ALU.add, op1=ALU.mult), reads=[D_["a"], kat], writes=[D_["kmod"]])
        S.op("dve", lambda e: e.scalar_tensor_tensor(out=D_["kmod"][:], in0=D_["kmod"][:], scalar=1.0, in1=k_ap,
                                                     op0=ALU.add, op1=ALU.mult), reads=[D_["kmod"], ks_], writes=[D_["kmod"]])
        tt(D_["bb"], D_["bb"][:], D_["kkn"], D_["kkn"][:], D_["a"], D_["a"][:], ALU.mult)
        tt(D_["t3"], D_["t3"][:], rs_, r_ap, rkt, rkt[:], ALU.mult)
        S.op("dve", lambda e: e.scalar_tensor_tensor(out=D_["t3"][:], in0=D_["t3"][:], scalar=1.0, in1=D_["kmod"][:],
                                                     op0=ALU.mult, op1=ALU.mult, accum_out=col["sbon"][:, 0:1]),
             reads=[D_["t3"], D_["kmod"]], writes=[D_["t3"], col["sbon"]])
        S.op("dve", lambda e: e.tensor_scalar(out=D_["Vs"][:], in0=v_ap, scalar1=col["sbon"][:, 0:1], scalar2=None,
                                              op0=ALU.mult), reads=[vs_, col["sbon"]], writes=[D_["Vs"]])
        mm(Ga, Ga[:, 128:192], TriU[:], D_["ld"][:], [TriU, D_["ld"]])
        mm(Ga, Ga[0:64, 192:193], D_["ld"][:], TriU[:, 127:128], [TriU, D_["ld"]])
        S.op("act", lambda e: e.activation(out=D_["Eg"][:], in_=Ga[:, 128:192], func=AF.Exp), writes=[D_["Eg"], Ga])
        S.op("act", lambda e: e.activation(out=D_["Eneg"][:], in_=Ga[:, 128:192], func=AF.Exp, scale=-1.0), writes=[D_["Eneg"], Ga])
        S.op("act", lambda e: e.activation(out=gam[:], in_=Ga[0:64, 192:193], func=AF.Exp), writes=[gam, Ga])
        tt(D_["cs_e"], D_["cs_e"][:], D_["ld"], Ga[:, 128:192], D_["ld"], D_["ld"][:], ALU.subtract, extra_w=[Ga])
        S.op("act", lambda e: e.activation(out=D_["Ege"][:], in_=D_["cs_e"][:], func=AF.Exp), reads=[D_["cs_e"]], writes=[D_["Ege"]])
        tt(D_["Kt"], D_["Kt"][:], D_["kkn"], D_["kkn"][:], D_["Ege"], D_["Ege"][:], ALU.mult)
        tt(D_["Bh"], D_["Bh"][:], D_["bb"], D_["bb"][:], D_["Eneg"], D_["Eneg"][:], ALU.mult)
        tt(D_["Kh"], D_["Kh"][:], D_["kmod"], D_["kmod"][:], D_["Eneg"], D_["Eneg"][:], ALU.mult)
        tt(D_["Rt"], D_["Rt"][:], rs_, r_ap, D_["Eg"], D_["Eg"][:], ALU.mult)
        for i, src in enumerate(("Kt", "Bh", "Kh", "Rt")):
            mm(Gb, Gb[0:64, i * 128:(i + 1) * 128], D_[src][:], Id[:], [D_[src], Id])
        for i, dst in enumerate(("KtT", "BhT", "KhT", "RtT")):
            S.op("act", lambda e, i=i, dst=dst: e.activation(out=fm[dst][:], in_=Gb[0:64, i * 128:(i + 1) * 128], func=AF.Copy),
                 writes=[fm[dst], Gb])
        mm(Ga, Ga[:, 0:128], fm["BhT"][:], fm["KtT"][:], [fm["BhT"], fm["KtT"]])
        mm(Ga, Ga[:, 128:256], fm["KtT"][:], fm["BhT"][:], [fm["BhT"], fm["KtT"]])
        mm(Ga, Ga[:, 256:384], fm["KhT"][:], fm["KtT"][:], [fm["KhT"], fm["KtT"]])
        mm(Gb, Gb[:, 0:128], fm["BhT"][:], fm["RtT"][:], [fm["BhT"], fm["RtT"]])
        mm(Gb, Gb[:, 128:256], fm["KhT"][:], fm["RtT"][:], [fm["KhT"], fm["RtT"]])
        mm(Gb, Gb[0:64, 256:384], D_["Vs"][:], Id[:], [D_["Vs"], Id])
        S.op("dve", lambda e: e.scalar_tensor_tensor(out=rr(sq["Pa"][:]), in0=Ga[:, 0:128], scalar=-1.0, in1=SU[:],
                                                     op0=ALU.mult, op1=ALU.mult), reads=[SU], writes=[sq["Pa"], Ga])
        S.op("dve", lambda e: e.scalar_tensor_tensor(out=rr(sq["PTa"][:]), in0=Ga[:, 128:256], scalar=-1.0, in1=SL[:],
                                                     op0=ALU.mult, op1=ALU.mult), reads=[SL], writes=[sq["PTa"], Ga])
        tt(sq["MakT"], sq["MakT"][:], SU, Ga[:, 256:384], SU, SU[:], ALU.mult, extra_w=[Ga])
        tt(sq["MrbT"], sq["MrbT"][:], TriU, Gb[:, 0:128], TriU, TriU[:], ALU.mult, extra_w=[Gb])
        tt(sq["MrkT"], sq["MrkT"][:], TriU, Gb[:, 128:256], TriU, TriU[:], ALU.mult, extra_w=[Gb])
        S.op("act", lambda e: e.activation(out=fm["bonT"][:], in_=Gb[0:64, 256:384], func=AF.Copy), writes=[fm["bonT"], Gb])
        tt(sq["A"], rr(sq["A"][:]), Id, Id[:], sq["Pa"], sq["Pa"][:], ALU.add)
        Pc, PTc, Pn, PTn = "Pa", "PTa", "Pb", "PTb"
        for lvl in range(6):
            G = Ga if lvl % 2 == 0 else Gb
            mm(G, G[:, 0:128], sq[PTc][:], sq[Pc][:], [sq[PTc], sq[Pc]], fast=True)
            mm(G, G[:, 128:256], sq[Pc][:], sq[PTc][:], [sq[PTc], sq[Pc]], fast=True)
            S.op("act", lambda e, Pn=Pn, G=G: e.activation(out=rr(sq[Pn][:]), in_=G[:, 0:128], func=AF.Copy), writes=[sq[Pn], G])
            S.op("act", lambda e, PTn=PTn, G=G: e.activation(out=rr(sq[PTn][:]), in_=G[:, 128:256], func=AF.Copy), writes=[sq[PTn], G])
            mm(G, G[:, 256:384], sq[PTn][:], sq["A"][:], [sq[PTn], sq["A"]], fast=True)
            tt(sq["A"], rr(sq["A"][:]), sq["A"], sq["A"][:], sq["A"], G[:, 256:384], ALU.add, extra_w=[G])
            Pc, PTc, Pn, PTn = Pn, PTn, Pc, PTc
        mm(Gb, Gb[:, 0:64], sq["MakT"][:], v_ap, [sq["MakT"], vs_])
        S.op("act", lambda e: e.activation(out=D_["t3"][:], in_=Gb[:, 0:64], func=AF.Copy), writes=[D_["t3"], Gb])
        mm(Gb, Gb[:, 64:128], sq["A"][:], D_["t3"][:], [sq["A"], D_["t3"]])
        S.op("act", lambda e: e.activation(out=D_["U_"][:], in_=Gb[:, 64:128], func=AF.Copy, scale=-1.0), writes=[D_["U_"], Gb])
        mm(Gb, Gb[0:64, 128:256], D_["Kt"][:], sq["A"][:], [D_["Kt"], sq["A"]])
        S.op("act", lambda e: e.activation(out=fm["WT"][:], in_=Gb[0:64, 128:256], func=AF.Copy), writes=[fm["WT"], Gb])
        Xp = X[c % NX]; Xn = X[(c + 1) % NX]
        mm(Ga, Ga[:, 0:64], fm["WT"][:], Xp[:], [fm["WT"], Xp])
        tt(D_["SA"], D_["SA"][:], D_["U_"], D_["U_"][:], D_["U_"], Ga[:, 0:64], ALU.subtract, extra_w=[Ga])
        S.op("dve", lambda e: e.tensor_scalar(out=Xg[:], in0=Xp[:], scalar1=gam[:, 0:1], scalar2=None, op0=ALU.mult),
             reads=[Xp, gam], writes=[Xg])
        mm(Ga, Ga[0:64, 64:128], D_["Kh"][:], v_ap, [D_["Kh"], vs_], start=True, stop=False, inc=False)
        mm(Ga, Ga[0:64, 64:128], D_["Bh"][:], D_["SA"][:], [D_["Bh"], D_["SA"]], start=False, stop=True)
        S.op("dve", lambda e: e.scalar_tensor_tensor(out=Xn[:], in0=Ga[0:64, 64:128], scalar=gam[:, 0:1], in1=Xg[:],
                                                     op0=ALU.mult, op1=ALU.add), reads=[gam, Xg], writes=[Xn, Ga])
        mm(Gb, Gb[0:64, 0:128], Xp[:], fm["RtT"][:], [Xp, fm["RtT"]], start=True, stop=False, inc=False)
        mm(Gb, Gb[0:64, 0:128], D_["SA"][:], sq["MrbT"][:], [D_["SA"], sq["MrbT"]], start=False, stop=False, inc=False)
        mm(Gb, Gb[0:64, 0:128], v_ap, sq["MrkT"][:], [vs_, sq["MrkT"]], start=False, stop=True)
        S.op("act", lambda e: e.activation(out=fm["oT"][:], in_=Gb[0:64, 0:128], func=AF.Copy), writes=[fm["oT"], Gb])
        mm(Gb, Gb[0:64, 128:256], ones64[:], fm["oT"][:], [ones64, fm["oT"]])
        tt(fm["oc"], fm["oc"][:], fm["oT"], fm["oT"][:], fm["oT"], Gb[0:64, 128:256], ALU.subtract, extra_w=[Gb])
        S.op("act", lambda e: e.activation(out=fm["osq"][:], in_=fm["oc"][:], func=AF.Square), reads=[fm["oc"]], writes=[fm["osq"]])
        mm(Gb, Gb[0:64, 256:384], ones64[:], fm["osq"][:], [ones64, fm["osq"]])
        S.op("act", lambda e: e.activation(out=fm["ors"][:], in_=Gb[0:64, 256:384], func=AF.Sqrt, bias=64e-5), writes=[fm["ors"], Gb])
        S.op("dve", lambda e: e.reciprocal(fm["ors"][:], fm["ors"][:]), reads=[fm["ors"]], writes=[fm["ors"]])
        S.op("dve", lambda e: e.scalar_tensor_tensor(out=fm["oc"][:], in0=fm["oc"][:], scalar=lng[:, 0:1], in1=fm["ors"][:],
                                                     op0=ALU.mult, op1=ALU.mult), reads=[fm["oc"], lng, fm["ors"]], writes=[fm["oc"]])
        tt(fm["oc"], fm["oc"][:], fm["oc"], fm["oc"][:], fm["bonT"], fm["bonT"][:], ALU.add)
        tt(fm["oc"], fm["oc"][:], fm["oc"], fm["oc"][:], gT, gT[:, sl], ALU.mult)
        S.op("pool", lambda e: e.tensor_copy(yb[:, sl], fm["oc"][:]), reads=[fm["oc"]], writes=[yb])

    blocks = [(0, 1)] + [(1 + 4 * i, 4) for i in range(16)]
    for bi, (cb, ncb) in enumerate(blocks):
        W = ncb * 128
        w2 = ncb * 64
        p0 = cb * 128
        bb = blkbufs[bi % 2]
        for nm, dst, mu in (("r", bb["rs"], mu_r), ("k", bb["ks"], mu_k), ("v", bb["vs"], mu_v)):
            x = st[nm].get(cb * 64, w2); xp = st[nm + "p"].get(cb * 64, w2)
            tt(dtm, dtm[:, 0:w2], xp, xp[:, 0:w2], x, x[:, 0:w2], ALU.subtract)
            tt(dtm, dtm[:, 0:w2], dtm, dtm[:, 0:w2], mu, mu[:, 0:w2], ALU.mult)
            tt(dst, dst[:, 0:w2], dtm, dtm[:, 0:w2], x, x[:, 0:w2], ALU.add)
        for nm, dst, mu, rows, fn in (("wd", bb["tw"], mu_wd, 32, AF.Tanh), ("ad", bb["ads"], mu_ad, 32, None),
                                      ("gd", bb["sg"], mu_gd, 64, AF.Sigmoid)):
            x = st[nm].get(p0, W); xp = st[nm + "p"].get(p0, W)
            tt(d32, d32[0:rows, 0:W], xp, xp[:, 0:W], x, x[:, 0:W], ALU.subtract)
            S.op("dve", lambda e, dst=dst, mu=mu, x=x, rows=rows: e.scalar_tensor_tensor(
                out=dst[:, 0:W], in0=d32[0:rows, 0:W], scalar=mu[:, 0:1], in1=x[:, 0:W], op0=ALU.mult, op1=ALU.add),
                reads=[d32, mu, x], writes=[dst])
            if fn is not None:
                S.op("act", lambda e, dst=dst, fn=fn: e.activation(out=dst[:, 0:W], in_=dst[:, 0:W], func=fn), reads=[dst], writes=[dst])
        G0 = lanes_buf[0]["a"]
        mm(G0, G0[0:64, 0:W], g2h[:], bb["sg"][:, 0:W], [g2h, bb["sg"]])
        S.op("act", lambda e: e.activation(out=bb["gT"][:, 0:W], in_=G0[0:64, 0:W], func=AF.Copy), writes=[bb["gT"], G0])
        lanes = []
        for ci in range(ncb):
            rec = S.record()
            chunk(lanes_buf[ci], bb, ci, cb + ci)
            S.stop_record()
            lanes.append(rec)
        S.replay(lanes, skew=10)
        P.store(yT_d[:, p0:p0 + W], bb["yb"], bb["yb"][:, 0:W])
    return P.finish()


def prep_rwkv(up, p, l, core):
    j = core
    o = 576 + 1544 + 1536
    prev = np.concatenate([np.zeros((1, up.shape[1]), np.float32), up[:-1]], 0)
    hs = slice(j * 64, (j + 1) * 64)
    mu = p['rwkv_mu'][l]
    rep = lambda v, n=128: C_(np.tile(v[None, :].astype(np.float32), (n, 1)))
    idx = np.arange(128)
    TriU = (idx[:, None] <= idx[None, :]).astype(np.float32)
    m = {}
    for nm, off in (("r", 0), ("k", 512), ("v", 1024)):
        cs_ = slice(o + off + j * 64, o + off + (j + 1) * 64)
        m[nm + "_tok"] = tok_layout(up[:, cs_]); m[nm + "p_tok"] = tok_layout(prev[:, cs_])
        m["mu_" + nm] = rep(np.tile(mu[off + j * 64: off + (j + 1) * 64], 4))
    for nm, off, wdt in (("wd", 1536, 32), ("ad", 1568, 32), ("gd", 1600, 64)):
        cs_ = slice(o + off, o + off + wdt)
        m[nm + "T"] = C_(up[:, cs_].T); m[nm + "pT"] = C_(prev[:, cs_].T)
        m["mu_" + nm] = C_(mu[off:off + wdt][:, None].astype(np.float32))
    m["w2h"] = C_(p['rwkv_w2'][l][:, hs]); m["a2h"] = C_(p['rwkv_a2'][l][:, hs]); m["g2h"] = C_(p['rwkv_g2'][l][:, hs])
    m["w0t"] = rep(p['rwkv_w0'][l][hs]); m["a0t"] = rep(p['rwkv_a0'][l][hs]); m["kkt"] = rep(p['rwkv_k_k'][l][hs])
    m["kat"] = rep(p['rwkv_k_a'][l][hs]); m["rkt"] = rep(p['rwkv_r_k'][l][j]); m["lng"] = C_(p['rwkv_ln'][l][j][:, None].astype(np.float32))
    m["TriU"] = C_(TriU); m["SL"] = C_((idx[:, None] > idx[None, :]).astype(np.float32))
    m["SU"] = C_((idx[:, None] < idx[None, :]).astype(np.float32)); m["Ident"] = np.eye(128, dtype=np.float32)
    return m


_PROGS = {}


def _prog(name, fn):
    if name not in _PROGS:
        r = fn()
        _PROGS[name] = r[0] if isinstance(r, tuple) else r
    return _PROGS[name]


def _run(nc, maps):
    res = run_bass_kernel_spmd(nc, maps, core_ids=list(range(NCORES)))
    return res.results


def tile_w(W, nk=KC):
    ncols = W.shape[1]
    nt = (ncols + 127) // 128
    if nt * 128 != ncols:
        W = np.concatenate([W, np.zeros((W.shape[0], nt * 128 - ncols), np.float32)], 1)
    return C_(W.reshape(nk, 128, nt, 128).transpose(2, 1, 0, 3).reshape(nt, 128, nk * 128))


def tile_wd(W):
    return C_(W.reshape(FC // 4, 4, 128, KC, 128).transpose(0, 3, 2, 1, 4).reshape((FC // 4) * KC, 128, 4 * 128))


def _gl(v, n):
    return C_(np.asarray(v, np.float32).reshape(n, 128).T)


def kernel(**inp):
    p = {k: np.asarray(v, np.float32) for k, v in inp.items()}
    x = p['x'][0]
    meta = p['meta_tokens']
    depth = p['w_in'].shape[0]
    hT = [C_(np.concatenate([meta, x[c * 1024:(c + 1) * 1024]], 0).T) for c in range(NCORES)]
    nc_a = _prog("Ta", build_Ta)
    nc_b = _prog("Tb", build_Tb)
    nc_ba = _prog("Tba", build_Tba)
    nc_mla = _prog("mla", build_mla)
    nc_sr = _prog("ssd_ret", build_ssd_ret)
    nc_rwkv = _prog("rwkv", build_rwkv)

    def ffn1_maps(l):
        return {"g1": _gl(p['ffn1_norm'][l], KC), "gm": _gl(p['mix_norm'][l], KC),
                "wg1": tile_w(p['ffn1_w_gate'][l]), "wu1": tile_w(p['ffn1_w_up'][l]), "wd1": tile_wd(p['ffn1_w_down'][l]),
                "win": tile_w(p['w_in'][l])}

    def assemble_u(res):
        up = np.zeros((LP, INW), np.float32)
        up[PAD:PAD + 16] = res[0]['uT'][:, 0:16].T
        for c in range(NCORES):
            up[PAD + 16 + c * 1024:PAD + 16 + (c + 1) * 1024] = res[c]['uT'][:, 16:].T
        return up

    f = ffn1_maps(0)
    res = _run(nc_a, [{"h_in": hT[c], "g1": f["g1"], "wg": f["wg1"], "wu": f["wu1"], "wd": f["wd1"], "gm": f["gm"],
                       "win": f["win"]} for c in range(NCORES)])
    hT = [C_(res[c]['h_out']) for c in range(NCORES)]
    up = assemble_u(res)
    del res, f
    for l in range(depth):
        y = np.zeros((LP, D), np.float32)
        r = _run(nc_mla, [prep_mla(up, p, l, c) for c in range(NCORES)])
        for c in range(NCORES):
            h = c % 4
            qp = mla_qpos(c)
            yt = r[c]['yT'].T
            if c < 4:
                y[qp, h * 128:(h + 1) * 128] = yt
            else:
                y[qp[128:], h * 128:(h + 1) * 128] = yt[128:]
        sr_maps = []
        for c in range(NCORES):
            m = prep_ssd(up, p, l, c)
            m.update(prep_ret(up, p['ret_norm'][l], c))
            sr_maps.append(m)
        r = _run(nc_sr, sr_maps)
        del sr_maps
        for j in range(8):
            y[:, 512 + j * 64:512 + (j + 1) * 64] = r[j]['yT_ssd'].T
        for h in range(4):
            y[:, 1024 + h * 128:1024 + (h + 1) * 128] = r[h]['yT_ret'].T
        r = _run(nc_rwkv, [prep_rwkv(up, p, l, c) for c in range(NCORES)])
        for j in range(8):
            y[:, 1536 + j * 64:1536 + (j + 1) * 64] = r[j]['yT'].T
        del r, up
        base = {"gs": _gl(p['ssd_norm'][l], 4), "wout": tile_w(p['w_out'][l]), "g2": _gl(p['ffn2_norm'][l], KC),
                "wg": tile_w(p['ffn2_w_gate'][l]), "wu": tile_w(p['ffn2_w_up'][l]), "wd": tile_wd(p['ffn2_w_down'][l])}
        last = (l == depth - 1)
        if not last:
            base.update(ffn1_maps(l + 1))
        maps = []
        for c in range(NCORES):
            yc = np.concatenate([y[PAD:PAD + 16], y[PAD + 16 + c * 1024:PAD + 16 + (c + 1) * 1024]], 0)
            m = dict(base)
            m["h_in"] = hT[c]
            m["yT"] = C_(yc.T)
            maps.append(m)
        res = _run(nc_b if last else nc_ba, maps)
        hT = [C_(res[c]['h_out']) for c in range(NCORES)]
        if not last:
            up = assemble_u(res)
        del res, maps, y, base
    out = np.concatenate([hT[c][:, 16:].T for c in range(NCORES)], 0)
    return C_(out[None].astype(np.float32))
```
